# Optimizing a Trainium2 kernel written in Bass

```python
import math
import jax, jax.numpy as jnp
from jax import lax
import numpy as np


D_MODEL = 1024
BATCH = 16
SEQ = 2048
DEPTH = 2

N_BRANCH = 3
BRANCH_WIDTH = 512
POOL_GROUPS = 4
POOL_WINDOWS = (2, 4, 8, 16)
POOL_WIDTH = 512
POOL_GC = POOL_WIDTH // POOL_GROUPS
DN_HEADS = 4
DN_HEAD_DIM = 128
DN_WIDTH = DN_HEADS * DN_HEAD_DIM
DN_CONV = 4
DN_CHUNK = 64
MOBA_HEADS = 8
MOBA_HEAD_DIM = 64
MOBA_WIDTH = MOBA_HEADS * MOBA_HEAD_DIM
MOBA_BLOCK = 256
MOBA_TOPK = 3
MOBA_Q_CHUNK = 16
REL_BUCKETS = 32
REL_MAX_DIST = 128
FFN_DIM = 2816
FFN_CONV = 3
PLE_DIM = 256
NORM_EPS = 1e-6
NEG_INF = -1e30
SPLIT_SIZES = (POOL_WIDTH, 3 * DN_WIDTH, DN_WIDTH, DN_HEADS, DN_HEADS, 3 * MOBA_WIDTH, N_BRANCH * D_MODEL)
IN_COLS = 7176

kernel_name = 'hybrid_pool_deltanet_moba_block'


def rmsnorm(x, w):
    xf = x.astype(jnp.float32)
    y = xf * lax.rsqrt(jnp.mean(xf * xf, axis=-1, keepdims=True) + NORM_EPS)
    return (y * w.astype(jnp.float32)).astype(x.dtype)


def l2norm(x):
    return x * lax.rsqrt(jnp.sum(x * x, axis=-1, keepdims=True) + NORM_EPS)


def split_cols(z):
    out, start = [], 0
    for n in SPLIT_SIZES:
        out.append(z[..., start:start + n])
        start += n
    return out


def causal_dwconv(x, w):
    k = w.shape[0]
    return lax.conv_general_dilated(x, w[:, None, :].astype(x.dtype), window_strides=(1,),
                                    padding=[(k - 1, 0)], dimension_numbers=('NWC', 'WIO', 'NWC'),
                                    feature_group_count=x.shape[-1])


def pool_mixer(u, w_group, scale):
    b, s, _ = u.shape
    ug = u.reshape(b, s, POOL_GROUPS, POOL_GC).astype(jnp.float32)
    c = jnp.pad(jnp.cumsum(ug, axis=1), ((0, 0), (1, 0), (0, 0), (0, 0)))
    t = jnp.arange(s)[:, None]
    win = jnp.array(POOL_WINDOWS, jnp.int32)[None, :]
    lo = jnp.maximum(t + 1 - win, 0)
    gi = jnp.arange(POOL_GROUPS)[None, :]
    total = c[:, 1:] - c[:, lo, gi]
    cnt = (t + 1 - lo).astype(jnp.float32)
    mixed = (total / cnt[None, :, :, None] - ug).astype(u.dtype)
    y = jnp.einsum('bsgc,gcd->bsgd', mixed, w_group).reshape(b, s, POOL_WIDTH)
    return y * scale


def chunk_gated_delta_rule(q, k, v, g, beta):
    b, h, s, dk = q.shape
    dv = v.shape[-1]
    n = s // DN_CHUNK
    q = q * dk ** -0.5
    chunks = lambda t: t.reshape(b, h, n, DN_CHUNK, *t.shape[3:])
    qc, kc, vc = chunks(q), chunks(k), chunks(v)
    gc = jnp.cumsum(chunks(g), axis=-1)
    bc = chunks(beta)
    kb = kc * bc[..., None]
    vb = vc * bc[..., None]
    incl = jnp.tril(jnp.ones((DN_CHUNK, DN_CHUNK), bool))
    strict = jnp.tril(jnp.ones((DN_CHUNK, DN_CHUNK), bool), -1)
    diff = gc[..., :, None] - gc[..., None, :]
    decay = jnp.where(incl, jnp.exp(jnp.where(incl, diff, 0.0)), 0.0)
    a_kk = jnp.where(strict, jnp.einsum('bhnid,bhnjd->bhnij', kb, kc) * decay, 0.0)
    eye = jnp.eye(DN_CHUNK, dtype=jnp.float32)
    t_mat = lax.linalg.triangular_solve(eye + a_kk, jnp.broadcast_to(eye, a_kk.shape),
                                        left_side=True, lower=True)
    u = t_mat @ vb
    w = t_mat @ (kb * jnp.exp(gc)[..., None])
    a_qk = jnp.where(incl, jnp.einsum('bhnid,bhnjd->bhnij', qc, kc) * decay, 0.0)
    q_dec = qc * jnp.exp(gc)[..., None]
    g_last = gc[..., -1]
    k_dec = kc * jnp.exp(g_last[..., None] - gc)[..., None]
    xs = tuple(jnp.moveaxis(t, 2, 0) for t in (u, w, a_qk, q_dec, k_dec, g_last))

    def step(state, inp):
        u_n, w_n, aqk_n, qd_n, kd_n, gl_n = inp
        v_new = u_n - w_n @ state
        o_n = qd_n @ state + aqk_n @ v_new
        state = state * jnp.exp(gl_n)[..., None, None] + jnp.einsum('bhcd,bhce->bhde', kd_n, v_new)
        return state, o_n

    state0 = jnp.zeros((b, h, dk, dv), jnp.float32)
    _, o = lax.scan(step, state0, xs)
    return jnp.moveaxis(o, 0, 2).reshape(b, h, s, dv)


def gated_deltanet(qkv, z, b_raw, a_raw, conv_w, a_log, dt_bias, norm_w):
    dt = qkv.dtype
    bsz, s, _ = qkv.shape
    act = jax.nn.silu(causal_dwconv(qkv, conv_w)).astype(jnp.float32)
    q, k, v = jnp.split(act, 3, axis=-1)
    heads = lambda t: t.reshape(bsz, s, DN_HEADS, DN_HEAD_DIM).transpose(0, 2, 1, 3)
    q, k, v = l2norm(heads(q)), l2norm(heads(k)), heads(v)
    beta = jax.nn.sigmoid(b_raw.astype(jnp.float32)).transpose(0, 2, 1)
    g = (-jnp.exp(a_log.astype(jnp.float32))
         * jax.nn.softplus(a_raw.astype(jnp.float32) + dt_bias.astype(jnp.float32))).transpose(0, 2, 1)
    o = chunk_gated_delta_rule(q, k, v, g, beta).transpose(0, 2, 1, 3)
    o = rmsnorm(o, norm_w) * jax.nn.silu(z.astype(jnp.float32).reshape(bsz, s, DN_HEADS, DN_HEAD_DIM))
    return o.reshape(bsz, s, DN_WIDTH).astype(dt)


def t5_bucket(rel):
    n = jnp.maximum(-rel, 0)
    exact = REL_BUCKETS // 2
    nf = jnp.maximum(n, 1).astype(jnp.float32)
    large = exact + (jnp.log(nf / exact) / math.log(REL_MAX_DIST / exact)
                     * (REL_BUCKETS - exact)).astype(jnp.int32)
    large = jnp.minimum(large, REL_BUCKETS - 1)
    return jnp.where(n < exact, n, large)


def moba_attention(q, k, v, rel_bias):
    b, h, s, dh = q.shape
    nb = -(-s // MOBA_BLOCK)
    sp = nb * MOBA_BLOCK
    padw = ((0, 0), (0, 0), (0, sp - s), (0, 0))
    q, k, v = jnp.pad(q, padw), jnp.pad(k, padw), jnp.pad(v, padw)
    kb = k.reshape(b, h, nb, MOBA_BLOCK, dh)
    vb = v.reshape(b, h, nb, MOBA_BLOCK, dh)
    k_mean = jnp.mean(kb.astype(jnp.float32), axis=3)
    q_blk = jnp.arange(sp) // MOBA_BLOCK
    gate = jnp.einsum('bhsd,bhnd->bhsn', q.astype(jnp.float32), k_mean)
    past = jnp.arange(nb)[None, :] < q_blk[:, None]
    gate = jnp.where(past, gate, NEG_INF)
    topk = min(MOBA_TOPK, nb)
    _, sel = lax.top_k(gate, topk)
    valid = sel < q_blk[:, None]
    nq = sp // MOBA_Q_CHUNK
    to_chunks = lambda t: jnp.moveaxis(t.reshape(b, h, nq, MOBA_Q_CHUNK, *t.shape[3:]), 2, 0)
    starts = jnp.arange(nq, dtype=jnp.int32) * MOBA_Q_CHUNK
    bi = jnp.arange(b)[:, None, None, None]
    hi = jnp.arange(h)[None, :, None, None]
    bias_ht = rel_bias.T.astype(jnp.float32)
    scale = dh ** -0.5
    offs = jnp.arange(MOBA_BLOCK, dtype=jnp.int32)

    def chunk(args):
        qc, selc, validc, start = args
        qpos = start + jnp.arange(MOBA_Q_CHUNK, dtype=jnp.int32)
        own = start // MOBA_BLOCK
        k_own = lax.dynamic_index_in_dim(kb, own, axis=2, keepdims=False)
        v_own = lax.dynamic_index_in_dim(vb, own, axis=2, keepdims=False)
        k_sel = kb[bi, hi, selc]
        v_sel = vb[bi, hi, selc]
        s_sel = jnp.einsum('bhqd,bhqnkd->bhqnk', qc, k_sel).astype(jnp.float32) * scale
        s_own = jnp.einsum('bhqd,bhkd->bhqk', qc, k_own).astype(jnp.float32) * scale
        kpos_sel = selc[..., None] * MOBA_BLOCK + offs
        kpos_own = own * MOBA_BLOCK + offs
        bias_sel = bias_ht[hi[..., None], t5_bucket(kpos_sel - qpos[:, None, None])]
        bias_own = bias_ht[:, t5_bucket(kpos_own[None, :] - qpos[:, None])]
        s_sel = jnp.where(validc[..., None], s_sel + bias_sel, NEG_INF)
        s_own = jnp.where(kpos_own[None, :] <= qpos[:, None], s_own + bias_own, NEG_INF)
        logits = jnp.concatenate([s_sel.reshape(b, h, MOBA_Q_CHUNK, topk * MOBA_BLOCK), s_own], axis=-1)
        probs = jax.nn.softmax(logits, axis=-1).astype(v.dtype)
        p_sel = probs[..., :topk * MOBA_BLOCK].reshape(b, h, MOBA_Q_CHUNK, topk, MOBA_BLOCK)
        p_own = probs[..., topk * MOBA_BLOCK:]
        return (jnp.einsum('bhqnk,bhqnkd->bhqd', p_sel, v_sel)
                + jnp.einsum('bhqk,bhkd->bhqd', p_own, v_own))

    out = lax.map(chunk, (to_chunks(q), to_chunks(sel), to_chunks(valid), starts))
    out = jnp.moveaxis(out, 0, 2).reshape(b, h, sp, dh)
    return out[:, :, :s]


def channel_mixer(h, w_up, w_conv, w_down):
    up = causal_dwconv(h @ w_up, w_conv)
    a, v = jnp.split(up, 2, axis=-1)
    return (jax.nn.gelu(a, approximate=True) * v) @ w_down


def setup_inputs(seed: int = 0) -> dict:
    key = jax.random.key(seed)
    ks = jax.random.split(key, 21)
    f32 = jnp.float32

    def nrm(k, shape, sc):
        return jax.random.normal(k, shape, f32) * sc

    def gain(k, shape):
        return 1.0 + 0.02 * jax.random.normal(k, shape, f32)

    dt = jnp.exp(jax.random.uniform(ks[9], (DEPTH, DN_HEADS), f32, math.log(1e-3), math.log(1e-1)))
    return {
        'x': nrm(ks[0], (BATCH, SEQ, D_MODEL), 1.0),
        'p': nrm(ks[1], (DEPTH, BATCH, SEQ, PLE_DIM), 1.0),
        'rel_bias': nrm(ks[2], (REL_BUCKETS, MOBA_HEADS), 0.5),
        'norm_mix': gain(ks[3], (DEPTH, D_MODEL)),
        'w_in': nrm(ks[4], (DEPTH, D_MODEL, IN_COLS), D_MODEL ** -0.5),
        'pool_w': nrm(ks[5], (DEPTH, POOL_GROUPS, POOL_GC, POOL_GC), POOL_GC ** -0.5),
        'pool_scale': gain(ks[6], (DEPTH, POOL_WIDTH)),
        'dn_conv': nrm(ks[7], (DEPTH, DN_CONV, 3 * DN_WIDTH), DN_CONV ** -0.5),
        'dn_a_log': jnp.log(jax.random.uniform(ks[8], (DEPTH, DN_HEADS), f32, 1.0, 16.0)),
        'dn_dt_bias': dt + jnp.log(-jnp.expm1(-dt)),
        'dn_norm': gain(ks[10], (DEPTH, DN_HEAD_DIM)),
        'w_branch': nrm(ks[11], (DEPTH, N_BRANCH, BRANCH_WIDTH, D_MODEL), BRANCH_WIDTH ** -0.5),
        'w_out': nrm(ks[12], (DEPTH, D_MODEL, D_MODEL), D_MODEL ** -0.5),
        'norm_ffn': gain(ks[13], (DEPTH, D_MODEL)),
        'ffn_up': nrm(ks[14], (DEPTH, D_MODEL, 2 * FFN_DIM), D_MODEL ** -0.5),
        'ffn_conv': nrm(ks[15], (DEPTH, FFN_CONV, 2 * FFN_DIM), FFN_CONV ** -0.5),
        'ffn_down': nrm(ks[16], (DEPTH, FFN_DIM, D_MODEL), FFN_DIM ** -0.5),
        'norm_ple': gain(ks[17], (DEPTH, D_MODEL)),
        'ple_gate': nrm(ks[18], (DEPTH, D_MODEL, D_MODEL), D_MODEL ** -0.5),
        'ple_proj': nrm(ks[19], (DEPTH, PLE_DIM, D_MODEL), PLE_DIM ** -0.5),
        'norm_final': gain(ks[20], (D_MODEL,)),
    }


def reference(x, p, rel_bias, norm_mix, w_in, pool_w, pool_scale, dn_conv, dn_a_log, dn_dt_bias,
              dn_norm, w_branch, w_out, norm_ffn, ffn_up, ffn_conv, ffn_down, norm_ple, ple_gate,
              ple_proj, norm_final):
    bsz, s, d = x.shape
    for i in range(DEPTH):
        h = rmsnorm(x, norm_mix[i])
        z = h @ w_in[i]
        u_pool, dn_qkv, dn_z, dn_b, dn_a, mb_qkv, gate_raw = split_cols(z)
        y_a = pool_mixer(u_pool, pool_w[i], pool_scale[i])
        y_b = gated_deltanet(dn_qkv, dn_z, dn_b, dn_a, dn_conv[i], dn_a_log[i], dn_dt_bias[i], dn_norm[i])
        mq, mk, mv = (t.reshape(bsz, s, MOBA_HEADS, MOBA_HEAD_DIM).transpose(0, 2, 1, 3)
                      for t in jnp.split(mb_qkv, 3, axis=-1))
        y_c = moba_attention(mq, mk, mv, rel_bias).transpose(0, 2, 1, 3).reshape(bsz, s, MOBA_WIDTH)
        branches = jnp.stack([y_a, y_b, y_c], axis=2)
        proj = jnp.einsum('bsnc,ncd->bsnd', branches, w_branch[i])
        gates = jax.nn.sigmoid(gate_raw.reshape(bsz, s, N_BRANCH, d))
        merged = jnp.sum(gates * proj, axis=2)
        x = x + merged @ w_out[i]
        x = x + channel_mixer(rmsnorm(x, norm_ffn[i]), ffn_up[i], ffn_conv[i], ffn_down[i])
        ple_g = jax.nn.sigmoid(rmsnorm(x, norm_ple[i]) @ ple_gate[i])
        x = x + (p[i] @ ple_proj[i]) * ple_g
    return rmsnorm(x, norm_final)
```

```python
import contextlib
import math
import numpy as np
import concourse.bass as bass
import concourse.mybir as mybir
from concourse.bass_utils import run_bass_kernel_spmd

F32 = mybir.dt.float32
BF16 = mybir.dt.bfloat16
AF = mybir.ActivationFunctionType
ALU = mybir.AluOpType
AX = mybir.AxisListType

D = 1024
S = 2048
DEPTH = 2
IN_COLS = 7176
FFN = 2816
EPS = 1e-6
NCORES = 8
BIG = 30000.0


class Op:
    __slots__ = ("eng", "fn", "deps", "sem", "inc", "val", "signal", "is_dma")

    def __init__(self, eng, fn, is_dma=False):
        self.eng = eng
        self.fn = fn
        self.deps = []
        self.sem = None
        self.inc = 1
        self.val = None
        self.signal = False
        self.is_dma = is_dma


class Prog:
    ENGS = ("pe", "act", "dve", "pool", "sp")

    def __init__(self, nc):
        self.nc = nc
        self.ops = {e: [] for e in self.ENGS}
        self.last_w = {}
        self.readers = {}
        self.n_ops = 0
        self.last_dma = {}
        self.bar = {e: None for e in self.ENGS}

    def barrier(self):
        deps = []
        for e in self.ENGS:
            for o in reversed(self.ops[e]):
                if not o.is_dma:
                    deps.append(o)
                    break
        deps.extend(self.last_dma.values())
        for e in self.ENGS:
            self.bar[e] = deps
        self.last_w = {}
        self.readers = {}

    def _add(self, op, reads, writes):
        deps = []
        for k in reads:
            w = self.last_w.get(k)
            if w is not None:
                deps.append((w, False))
        for k in writes:
            w = self.last_w.get(k)
            if w is not None:
                deps.append((w, False))
            for r in self.readers.get(k, ()):
                deps.append((r, True))
        if self.bar[op.eng] is not None:
            for d in self.bar[op.eng]:
                deps.append((d, False))
            self.bar[op.eng] = None
        seen = set()
        for d, war in deps:
            if d is op or id(d) in seen:
                continue
            if d.eng == op.eng and not d.is_dma and not op.is_dma:
                if op.eng == "pe" or war:
                    continue
            seen.add(id(d))
            op.deps.append(d)
            d.signal = True
        for k in reads:
            self.readers.setdefault(k, []).append(op)
        for k in writes:
            self.last_w[k] = op
            self.readers[k] = []
        self.ops[op.eng].append(op)
        self.n_ops += 1

    @staticmethod
    def _is_psum(k):
        return isinstance(k, str) and k.startswith("ps") and k[2:].isdigit()

    def op(self, eng, fn, reads=(), writes=()):
        o = Op(eng, fn)
        o.sem = ("eng", eng)
        ex = [k for k in reads if self._is_psum(k)]
        if ex:
            reads = [k for k in reads if not self._is_psum(k)]
            writes = list(writes) + ex
        self._add(o, reads, writes)
        return o

    def dma(self, fn, semkey, reads=(), writes=(), q="sp"):
        o = Op(q, fn, is_dma=True)
        o.sem = ("dma", semkey)
        o.inc = 16
        o.signal = True
        self._add(o, reads, writes)
        self.last_dma[semkey] = o
        return o

    def emit(self, final_wait_ops=()):
        nc = self.nc
        counts = {}
        for e in self.ENGS:
            for o in self.ops[e]:
                if o.signal:
                    c = counts.get(o.sem, 0) + o.inc
                    counts[o.sem] = c
                    o.val = c
        semkeys = list(counts.keys())
        with contextlib.ExitStack() as st:
            sems = {}
            for i, k in enumerate(semkeys):
                sems[k] = st.enter_context(nc.semaphore("s%d" % i))
            block = st.enter_context(nc.Block())
            engmap = {"pe": block.tensor, "act": block.scalar, "dve": block.vector,
                      "pool": block.gpsimd, "sp": block.sync}

            def make(e):
                oplist = self.ops[e]

                def body(eng):
                    waited = {}
                    for o in oplist:
                        for d in o.deps:
                            if waited.get(d.sem, 0) >= d.val:
                                continue
                            eng.wait_ge(sems[d.sem], d.val)
                            waited[d.sem] = d.val
                        ins = o.fn(eng)
                        if o.signal:
                            ins.then_inc(sems[o.sem], o.inc)
                    if e == "sp":
                        for o in final_wait_ops:
                            if waited.get(o.sem, 0) >= o.val:
                                continue
                            eng.wait_ge(sems[o.sem], o.val)
                            waited[o.sem] = o.val

                return body

            for e in self.ENGS:
                if self.ops[e] or e == "sp":
                    engmap[e](make(e))
        return len(semkeys)


ARENA_F32 = 46 * 1024


import os as _os
CSTOP = int(_os.environ.get("C_STOP", "99"))
DSTOP = int(_os.environ.get("D_STOP", "99"))


class _Stop(Exception):
    pass


class KC:
    def __init__(self, nc, NSEQ, debug):
        self.nc = nc
        self.P = Prog(nc)
        self.NSEQ = NSEQ
        self.T = NSEQ * S
        self.NT = self.T // 512
        self.debug = debug
        self.st = contextlib.ExitStack()
        self.arena = self.st.enter_context(nc.sbuf_tensor("arena", [128, ARENA_F32], F32))
        self.arena_bf = self.arena[:, :].bitcast(BF16)
        self.psb = [self.st.enter_context(nc.psum_tensor("psb%d" % i, [128, 512], F32)) for i in range(8)]
        self.bump = 0
        self.perm = 0
        self.rr = 0
        self.finals = []
        self.dram = {}
        self.uid = 0

    def alloc(self, cols, dt=F32):
        if dt == BF16:
            w = (cols + 1) // 2
            a = self.bump
            self.bump += w
            assert self.bump <= ARENA_F32, "SBUF arena overflow %d" % self.bump
            return self.arena_bf[:, 2 * a:2 * a + cols]
        a = self.bump
        self.bump += cols
        assert self.bump <= ARENA_F32, "SBUF arena overflow %d" % self.bump
        return self.arena[:, a:a + cols]

    def make_perm(self):
        self.perm = self.bump

    def reset(self):
        self.P.barrier()
        self.bump = self.perm

    def key(self, base):
        self.uid += 1
        return "%s#%d" % (base, self.uid)

    def din(self, name, shape, dt=F32):
        t = self.nc.dram_tensor(name, list(shape), dt, kind="ExternalInput").ap()
        self.dram[name] = t
        return t

    def dscr(self, name, shape, dt=F32, out=False):
        kind = "ExternalOutput" if (out or self.debug) else "Internal"
        t = self.nc.dram_tensor(name, list(shape), dt, kind=kind).ap()
        self.dram[name] = t
        return t

    def ld(self, dst, src, key, reads=(), q="sp", semkey=None, slow=False):
        if slow:
            return self.P.dma(lambda e: e.dma_start(out=dst, in_=src, allow_slow_non_contiguous=True), semkey or key,
                              reads=reads, writes=[key], q=q)
        return self.P.dma(lambda e: e.dma_start(out=dst, in_=src), semkey or key, reads=reads, writes=[key], q=q)

    def stt(self, dst, src, srckey, writes=(), q="sp", semkey=None):
        keys = list(srckey) if isinstance(srckey, list) else [srckey]
        sk = semkey or (str(keys[0]) + "_st")
        return self.P.dma(lambda e: e.dma_start(out=dst, in_=src), sk, reads=keys, writes=writes, q=q)

    def dump(self, name, ap, keys, dt=F32):
        if not self.debug:
            return
        shape = list(ap.shape)
        d = self.nc.dram_tensor("dbg_" + name, shape, dt, kind="ExternalOutput").ap()
        self.finals.append(self.P.dma(lambda e: e.dma_start(out=d, in_=ap), "dump_" + name, reads=list(keys)))

    def store(self, dst, src, srckey, writes=(), q="sp", semkey=None):
        return self.stt(dst, src, srckey, writes, q, semkey)

    def act(self, out, in_, func, reads, writes, bias=None, scale=None, accum=None):
        kw = {}
        if bias is not None:
            kw["bias"] = bias
        if scale is not None:
            kw["scale"] = scale
        if accum is not None:
            kw["accum_out"] = accum
        return self.P.op("act", lambda e: e.activation(out=out, in_=in_, func=func, **kw), reads=reads, writes=writes)

    def tt(self, eng, out, in0, in1, op, reads, writes):
        return self.P.op(eng, lambda e: e.tensor_tensor(out, in0, in1, op), reads=reads, writes=writes)

    def ts(self, eng, out, in0, s1, s2, op0, op1, reads, writes):
        if s2 is None:
            return self.P.op(eng, lambda e: e.tensor_scalar(out, in0, s1, None, op0), reads=reads, writes=writes)
        return self.P.op(eng, lambda e: e.tensor_scalar(out, in0, s1, s2, op0, op1), reads=reads, writes=writes)

    def sto(self, eng, out, in0, scalar, in1, op0, op1, reads, writes):
        return self.P.op(eng, lambda e: e.scalar_tensor_tensor(out=out, in0=in0, scalar=scalar, in1=in1, op0=op0,
                                                               op1=op1), reads=reads, writes=writes)

    def copy(self, eng, out, in_, reads, writes):
        if eng == "act":
            return self.P.op("act", lambda e: e.copy(out, in_), reads=reads, writes=writes)
        return self.P.op(eng, lambda e: e.tensor_copy(out, in_), reads=reads, writes=writes)

    def memset(self, eng, ap, val, writes):
        return self.P.op(eng, lambda e: e.memset(ap, val), writes=writes)

    def recip(self, out, in_, reads, writes):
        return self.P.op("dve", lambda e: e.reciprocal(out, in_), reads=reads, writes=writes)

    def transpose(self, out, in_, ident, reads, writes):
        return self.P.op("pe", lambda e: e.transpose(out, in_, ident), reads=reads, writes=writes)

    def mm(self, out, lhsT, rhs, start, stop, reads, writes):
        return self.P.op("pe", lambda e: e.matmul(out, lhsT, rhs, start=start, stop=stop), reads=reads, writes=writes)

    def bank(self, n=8, base=0):
        i = base + (self.rr % n)
        self.rr += 1
        return i, self.psb[i], "ps%d" % i


def r3(ap, a):
    return ap.rearrange("p (a b) -> p a b", a=a)


V_NMIX = 0
V_NFFN = 16
V_NPLE = 32
V_NFIN = 48
V_PSCALE = 56
V_CW = 64
V_FCW = 160
V_DNW = 424
V_EPS = 426
V_ONE = 427
NVEC = 428
WIN_GROUP_C0 = [0, 512, 1024, 1536, 2048, 2568, 3080, 3592] + [4104 + 512 * i for i in range(6)]
WIN_GROUP_DST = [("u", None, 0), ("dqkv", None, 0), ("dqkv", None, 512), ("dqkv", None, 1024), ("dz", None, 0),
                 ("mq", None, 0), ("mk", None, 0), ("mv", None, 0)] + [("gate", None, 512 * i) for i in range(6)]

C_POOL = 0
C_DQKV = 512
C_DZ = 2048
C_DB = 2560
C_DA = 2564
C_MQ = 2568
C_MK = 2568 + 512
C_MV = 2568 + 1024
C_GATE = 4104


def build(NSEQ=2, debug=False, layers=(0, 1), stages="ABCDEF", final=True):
    nc = bass.Bass("TRN2", target_bir_lowering=False)
    kc = KC(nc, NSEQ, debug)
    P = kc.P
    T = kc.T
    NT = kc.NT
    xT_in = kc.din("xT", [D, T])
    pT_in = kc.din("pT", [DEPTH, 256, T])
    w_in = kc.din("w_in", [DEPTH, D, IN_COLS])
    w_branch = kc.din("w_branch", [DEPTH, 3, 512, D])
    w_out = kc.din("w_out", [DEPTH, D, D])
    ffn_up = kc.din("ffn_up", [DEPTH, D, 2 * FFN])
    ffn_down = kc.din("ffn_down", [DEPTH, FFN, D])
    ple_gate = kc.din("ple_gate", [DEPTH, D, D])
    ple_proj = kc.din("ple_proj", [DEPTH, 256, D])
    pool_w = kc.din("pool_w", [DEPTH, 4, 128, 128])
    vecs_d = kc.din("vecs", [128, NVEC])
    ident_d = kc.din("ident", [128, 128])
    tri_d = kc.din("tri", [64, 4 * 64])
    rcnt_d = kc.din("rcnt", [128, 64])
    band_d = kc.din("band", [128, 8 * 1152])
    ind_d = kc.din("ind", [128, 64 * 128])
    a4_d = kc.din("a4", [4, 4])
    outT = kc.dscr("outT", [D, T], out=True)
    xs = kc.dscr("xs", [D, T])
    x1s = kc.dscr("x1s", [D, T])
    uT = kc.dscr("uT", [512, T])
    dqkvT = kc.dscr("dqkvT", [1536, T])
    dzT = kc.dscr("dzT", [512, T])
    dbg = kc.dscr("dbg", [8, T])
    mqT = kc.dscr("mqT", [512, T], BF16)
    mkT = kc.dscr("mkT", [512, T], BF16)
    mv = kc.dscr("mv", [T, 512], BF16)
    gatesT = kc.dscr("gatesT", [3072, T], BF16)
    ybT = kc.dscr("ybT", [1536, T], BF16)
    NG_IN = 15
    winb = kc.dscr("winb", [DEPTH, 15, 128, 8 * 512], BF16)
    wbrb = kc.dscr("wbrb", [DEPTH, 128, 12 * 1024], BF16)
    woutb = kc.dscr("woutb", [DEPTH, 128, 8 * 1024], BF16)
    wupb = kc.dscr("wupb", [DEPTH, 22, 128, 8 * 256], BF16)
    wdnb = kc.dscr("wdnb", [DEPTH, 128, 22 * 1024], BF16)
    pgb = kc.dscr("pgb", [DEPTH, 128, 8 * 1024], BF16)
    ppb = kc.dscr("ppb", [DEPTH, 128, 2 * 1024], BF16)
    pwb = kc.dscr("pwb", [DEPTH, 128, 4 * 128], BF16)

    vecs = kc.alloc(NVEC)
    ident = kc.alloc(128)
    ones_bf = kc.alloc(128, BF16)
    ones_f = kc.alloc(128)
    kc.ld(vecs, vecs_d, "vecs")
    kc.ld(ident, ident_d, "ident")
    P.op("pool", lambda e: e.memset(ones_bf, 1.0), writes=["ones_bf"])
    P.op("pool", lambda e: e.memset(ones_f, 1.0), writes=["ones_f"])
    kc.make_perm()
    CONST = ["vecs", "ident", "ones_bf", "ones_f"]

    def cast(dst, src, key):
        grp = ("CAST0" if key[0] == "winb" else "CAST1") + ("" if key[1] == layers[0] else "b")
        P.dma(lambda e: e.dma_start(out=dst, in_=src), grp, writes=[grp], q="pool")

    for l in layers:
        for g in range(14):
            c0 = WIN_GROUP_C0[g]
            cast(winb[l, g].rearrange("p (k c) -> p k c", k=8),
                 w_in[l, :, c0:c0 + 512].rearrange("(k p) c -> p k c", p=128), ("winb", l, g))
        cast(winb[l, 14].rearrange("p (k c) -> p k c", k=8)[:, :, 0:8],
             w_in[l, :, C_DB:C_DB + 8].rearrange("(k p) c -> p k c", p=128), ("winb", l, 14))
        cast(pwb[l].rearrange("p (g d) -> p g d", g=4), pool_w[l].rearrange("g c d -> c g d"), ("pwb", l))
        for n in range(3):
            cast(wbrb[l].rearrange("p (n k d) -> p n k d", n=3, k=4)[:, n],
                 w_branch[l, n].rearrange("(k p) d -> p k d", p=128), ("wbrb", l, n))
        cast(woutb[l].rearrange("p (k d) -> p k d", k=8), w_out[l].rearrange("(k p) d -> p k d", p=128), ("woutb", l))
        for j in range(22):
            dstv = wupb[l, j].rearrange("p (k c) -> p k c", k=8)
            cast(dstv[:, :, 0:128], ffn_up[l, :, j * 128:(j + 1) * 128].rearrange("(k p) c -> p k c", p=128),
                 ("wupb", l, j, 0))
            cast(dstv[:, :, 128:256],
                 ffn_up[l, :, FFN + j * 128:FFN + (j + 1) * 128].rearrange("(k p) c -> p k c", p=128),
                 ("wupb", l, j, 1))
        cast(wdnb[l].rearrange("p (j d) -> p j d", j=22), ffn_down[l].rearrange("(j p) d -> p j d", p=128),
             ("wdnb", l))
        cast(pgb[l].rearrange("p (k d) -> p k d", k=8), ple_gate[l].rearrange("(k p) d -> p k d", p=128), ("pgb", l))
        cast(ppb[l].rearrange("p (k d) -> p k d", k=2), ple_proj[l].rearrange("(k p) d -> p k d", p=128), ("ppb", l))

    def vcol(i):
        return vecs[:, i:i + 1]

    def rmsnorm_tile(xt, xkey, nbase, out_fn, outkeys, sqt, sqkey, rstd, rkey, engs=("dve",)):
        P.op("act", lambda e: e.activation(out=sqt, in_=xt, func=AF.Square), reads=[xkey], writes=[sqkey])
        bi, pb, pk = kc.bank()
        for k in range(8):
            kc.mm(pb[:, :], ones_bf, sqt[:, k * 512:(k + 1) * 512], k == 0, k == 7, [sqkey, "ones_bf"], [pk])
        P.op("act", lambda e: e.activation(out=rstd, in_=pb[:, :], func=AF.Sqrt, bias=vcol(V_EPS), scale=1.0 / D),
             reads=[pk, "vecs"], writes=[rkey])
        P.op("dve", lambda e: e.reciprocal(rstd, rstd), reads=[rkey], writes=[rkey])
        for k in range(8):
            o = out_fn(k)
            eng = engs[k % len(engs)]
            P.op(eng, (lambda o=o, k=k: lambda e: e.scalar_tensor_tensor(
                out=o, in0=xt[:, k * 512:(k + 1) * 512], scalar=vcol(nbase + k), in1=rstd,
                op0=ALU.mult, op1=ALU.mult))(), reads=[xkey, rkey, "vecs"], writes=[outkeys[k]])

    stages0 = stages
    for l in layers:
        stages = stages0 if l == layers[0] else _os.environ.get("L1S", stages0)
        xsrc = xT_in if l == 0 else xs
        xsrc_key = "xs"
        if "A" in stages:
            hT = kc.alloc(8 * T, BF16)
            hT3 = r3(hT, 8)
            xtb = [kc.alloc(8 * 512) for _ in range(2)]
            sqt = kc.alloc(8 * 512, BF16)
            rstd = kc.alloc(512)
            for tt in range(NT):
                xt = xtb[tt % 2]
                xk = "A_xt%d" % (tt % 2)
                kc.ld(r3(xt, 8), xsrc[:, tt * 512:(tt + 1) * 512].rearrange("(k p) t -> p k t", p=128), xk,
                      reads=[xsrc_key])
                rmsnorm_tile(xt, xk, V_NMIX + l * 8, lambda k: hT3[:, k, tt * 512:(tt + 1) * 512],
                             [("hT", tt, k) for k in range(8)], sqt, "A_sq", rstd, "A_rstd")
            HT_ALL = [("hT", tt) for tt in range(NT)]
            wgb = [kc.alloc(8 * 512, BF16) for _ in range(2)]
            ob32 = [kc.alloc(T) for _ in range(2)]
            ob16 = [kc.alloc(T, BF16) for _ in range(2)]
            ovb = [kc.alloc(512, BF16) for _ in range(2)]
            cnt = {"o32": 0, "o16": 0, "ov": 0, "ev": 0}

            def evac(kind, dst, src, pk, okey):
                if kind in ("u", "dqkv"):
                    eng = "act" if cnt["ev"] % 2 == 0 else "dve"
                    cnt["ev"] += 1
                    if eng == "act":
                        P.op("act", lambda e: e.copy(dst, src), reads=[pk], writes=[okey])
                    else:
                        P.op("dve", lambda e: e.tensor_copy(dst, src), reads=[pk], writes=[okey])
                elif kind == "dz":
                    P.op("act", lambda e: e.activation(out=dst, in_=src, func=AF.Silu), reads=[pk], writes=[okey])
                elif kind == "mq":
                    P.op("dve", lambda e: e.tensor_scalar(dst, src, 0.125, None, ALU.mult), reads=[pk], writes=[okey])
                elif kind == "mk":
                    P.op("dve", lambda e: e.tensor_copy(dst, src), reads=[pk], writes=[okey])
                elif kind == "gate":
                    P.op("act", lambda e: e.activation(out=dst, in_=src, func=AF.Sigmoid), reads=[pk], writes=[okey])
                else:
                    raise ValueError(kind)

            for g in range(14):
                wg = wgb[g % 2]
                wk = "A_wg%d" % (g % 2)
                kc.ld(wg, winb[l, g], wk, reads=["CAST0b"] if l != layers[0] else ["CAST0"])
                wg3 = r3(wg, 8)
                gkind, gdst, grow0 = WIN_GROUP_DST[g]
                if gkind == "mv":
                    for i in range(T // 128):
                        bi, pb, pk = kc.bank()
                        for k in range(8):
                            kc.mm(pb[:, :], hT3[:, k, i * 128:(i + 1) * 128], wg3[:, k, :], k == 0, k == 7,
                                  [("hT", i // 4, k), wk], [pk])
                        ov = ovb[cnt["ov"] % 2]
                        ok = "A_ov%d" % (cnt["ov"] % 2)
                        cnt["ov"] += 1
                        eng = "act" if i % 2 == 0 else "dve"
                        if eng == "act":
                            P.op("act", (lambda ov=ov, pb=pb: lambda e: e.copy(ov, pb[:, :]))(), reads=[pk], writes=[ok])
                        else:
                            P.op("dve", (lambda ov=ov, pb=pb: lambda e: e.tensor_copy(ov, pb[:, :]))(), reads=[pk],
                                 writes=[ok])
                        kc.stt(mv[i * 128:(i + 1) * 128, :], ov, ok, writes=["mv"])
                    continue
                for j in range(4):
                    is16 = gkind in ("mq", "mk", "gate")
                    if is16:
                        ob = ob16[cnt["o16"] % 2]
                        okey = "A_o16_%d" % (cnt["o16"] % 2)
                        cnt["o16"] += 1
                    else:
                        ob = ob32[cnt["o32"] % 2]
                        okey = "A_o32_%d" % (cnt["o32"] % 2)
                        cnt["o32"] += 1
                    for tt in range(NT):
                        bi, pb, pk = kc.bank()
                        for k in range(8):
                            kc.mm(pb[:, :], wg3[:, k, j * 128:(j + 1) * 128], hT3[:, k, tt * 512:(tt + 1) * 512],
                                  k == 0, k == 7, [("hT", tt, k), wk], [pk])
                        evac(gkind, ob[:, tt * 512:(tt + 1) * 512], pb[:, :], pk, (okey, tt))
                    dst = {"u": uT, "dqkv": dqkvT, "dz": dzT, "mq": mqT, "mk": mkT, "gate": gatesT}[gkind]
                    r0 = grow0 + j * 128
                    kc.stt(dst[r0:r0 + 128, :], ob, [(okey, tt) for tt in range(NT)], writes=[gkind + "_d"], semkey=okey + "_st")
            wg = wgb[0]
            wk = "A_wg0"
            kc.ld(wg, winb[l, 14], wk, reads=["CAST0b"] if l != layers[0] else ["CAST0"])
            wg3 = r3(wg, 8)
            a4 = kc.alloc(4)
            P_a4 = kc.ld(a4[0:4, :], a4_d, "A_a4")
            nexpA = kc.alloc(1)
            kc.act(nexpA[0:4, :], a4[0:4, 2 * l:2 * l + 1], AF.Exp, ["A_a4"], ["A_nexpA"])
            kc.ts("dve", nexpA[0:4, :], nexpA[0:4, :], -1.0, None, ALU.mult, None, ["A_nexpA"], ["A_nexpA"])
            obb = ob32[0]
            oba = ob32[1]
            for tt in range(NT):
                sl = slice(tt * 512, (tt + 1) * 512)
                bi, pb, pk = kc.bank()
                for k in range(8):
                    kc.mm(pb[0:4, :], wg3[:, k, 0:4], hT3[:, k, sl], k == 0, k == 7, [("hT", tt, k), wk], [pk])
                kc.act(obb[0:4, sl], pb[0:4, :], AF.Sigmoid, [pk], [("A_o32_0", tt)])
                bi, pb, pk = kc.bank()
                for k in range(8):
                    kc.mm(pb[0:4, :], wg3[:, k, 4:8], hT3[:, k, sl], k == 0, k == 7, [("hT", tt, k), wk], [pk])
                kc.act(oba[0:4, sl], pb[0:4, :], AF.Exp, [pk, "A_a4"], [("A_o32_1", tt)],
                       bias=a4[0:4, 2 * l + 1:2 * l + 2])
            AK1 = [("A_o32_1", tt) for tt in range(NT)]
            kc.act(oba[0:4, :], oba[0:4, :], AF.Ln, AK1 + ["vecs"], AK1, bias=vcol(V_ONE)[0:4, :])
            kc.ts("dve", oba[0:4, :], oba[0:4, :], nexpA[0:4, 0:1], None, ALU.mult, None, AK1 + ["A_nexpA"], AK1)
            kc.stt(dbg[0:4, :], obb[0:4, :], [("A_o32_0", tt) for tt in range(NT)], writes=["dbg"], semkey="A_o32_0_st")
            kc.stt(dbg[4:8, :], oba[0:4, :], [("A_o32_1", tt) for tt in range(NT)], writes=["dbg"], semkey="A_o32_1_st")
            kc.reset()
        if "B" in stages:
            pw = kc.alloc(4 * 128, BF16)
            kc.ld(pw, pwb[l], "B_pw", reads=["CAST1" if l == layers[0] else "CAST1b"])
            pw3 = r3(pw, 4)
            rc = kc.alloc(64)
            kc.ld(rc, rcnt_d, "B_rc")
            ub = [kc.alloc(16 + S) for _ in range(2)]
            sab = [kc.alloc(16 + S) for _ in range(2)]
            mxb = [kc.alloc(S, BF16) for _ in range(2)]
            t16 = kc.alloc(16)
            yob = [kc.alloc(S, BF16) for _ in range(2)]
            for i in range(2):
                kc.memset("pool", ub[i][:, 0:16], 0.0, ["B_u%dz" % i])
                kc.memset("pool", sab[i][:, 0:16], 0.0, ["B_s%dz" % i])
            it = 0
            for s_ in range(NSEQ):
                for g in range(4):
                    i2 = it % 2
                    u = ub[i2]
                    uk = "B_u%d" % i2
                    kc.ld(u[:, 16:], uT[g * 128:(g + 1) * 128, s_ * S:(s_ + 1) * S], uk)
                    cur, curk = u, uk
                    for j in range(g + 1):
                        dst = sab[j % 2]
                        dk = "B_s%d" % (j % 2)
                        sh = 1 << j
                        kc.tt("dve" if j % 2 == 0 else "pool", dst[:, 16:], cur[:, 16:], cur[:, 16 - sh:16 - sh + S],
                              ALU.add, [curk, curk + "z"], [dk])
                        cur, curk = dst, dk
                    w = 1 << (g + 1)
                    m = mxb[i2]
                    mk_ = "B_mx%d" % i2
                    kc.sto("dve", m, cur[:, 16:], 1.0 / w, u[:, 16:], ALU.mult, ALU.subtract, [curk, uk], [mk_])
                    kc.tt("dve", t16, cur[:, 16:32], rc[:, g * 16:(g + 1) * 16], ALU.mult, [curk, "B_rc"], ["B_t16"])
                    kc.tt("dve", m[:, 0:16], t16, u[:, 16:32], ALU.subtract, ["B_t16", uk, mk_], [mk_])
                    y = yob[i2]
                    yk = "B_y%d" % i2
                    for j in range(4):
                        bi, pb, pk = kc.bank()
                        kc.mm(pb[:, :], pw3[:, g, :], m[:, j * 512:(j + 1) * 512], True, True, [mk_, "B_pw"], [pk])
                        kc.ts("dve" if j % 2 == 0 else "dve", y[:, j * 512:(j + 1) * 512], pb[:, :],
                              vcol(V_PSCALE + l * 4 + g), None, ALU.mult, None, [pk, "vecs"], [yk])
                    kc.store(ybT[g * 128:(g + 1) * 128, s_ * S:(s_ + 1) * S], y, yk, writes=["ybT"])
                    it += 1
            kc.reset()

        if "C" in stages:
            tri = kc.alloc(256)
            kc.ld(tri[0:64, :], tri_d, "C_tri")
            Ut = tri[0:64, 0:64]
            mA = tri[0:64, 64:128]
            mB = tri[0:64, 128:192]
            SU = tri[0:64, 192:256]
            identb3 = ident[0:64, 0:64].rearrange("p (o j) -> p o j", o=1).to_broadcast([64, 8, 64])
            raw = kc.alloc(3 + S)
            acc = kc.alloc(S)
            sqb = kc.alloc(S, BF16)
            rn = kc.alloc(512)
            khb = kc.alloc(S, BF16)
            qhb = kc.alloc(S, BF16)
            keT = kc.alloc(S)
            qdT = kc.alloc(S)
            kdec = kc.alloc(32 * 128, BF16)
            vtok = kc.alloc(32 * 128)
            Xb = kc.alloc(S)
            gcrow = kc.alloc(S)
            E1 = kc.alloc(S)
            brow = kc.alloc(S)
            gT = kc.alloc(32)
            bT = kc.alloc(32)
            gcc = kc.alloc(32)
            egd = kc.alloc(32)
            egl = kc.alloc(32)
            Dm = kc.alloc(512)
            Gb = kc.alloc(512)
            GTi = kc.alloc(512)
            GTb = kc.alloc(512)
            tmpD = kc.alloc(512)
            Qk = [kc.alloc(512) for _ in range(4)]
            Rk = [kc.alloc(512) for _ in range(4)]
            Gk = [kc.alloc(512) for _ in range(4)]
            aqkT = kc.alloc(S, BF16)
            TTb = kc.alloc(S, BF16)
            oT = kc.alloc(S)
            S_ = kc.alloc(128)
            Rb = kc.alloc(128, BF16)
            vn = kc.alloc(128, BF16)
            zs = kc.alloc(S)
            yout = kc.alloc(S, BF16)
            kc.memset("pool", raw[:, 0:3], 0.0, ["C_rawz"])

            def bc_mid(ap64, n):
                return ap64.rearrange("p (o j) -> p o j", o=1).to_broadcast([64, n, 64])

            def bc_last(ap, n, w):
                return ap.rearrange("p (n o) -> p n o", o=1).to_broadcast([64, n, w])

            for s_ in range(NSEQ):
              try:
                for h in range(4):
                    t0 = s_ * S
                    kc.ld(gcrow[0:64, :], dbg[4 + h:5 + h, t0:t0 + S].partition_broadcast(64), "C_gcrow")
                    kc.ld(brow[0:64, :], dbg[h:h + 1, t0:t0 + S].partition_broadcast(64), "C_brow")
                    idb32 = ident[0:64, 0:64].rearrange("p (o j) -> p o j", o=1).to_broadcast([64, 32, 64])
                    kc.tt("dve", r3(Xb[0:64, :], 32), r3(gcrow[0:64, :], 32), idb32, ALU.mult, ["C_gcrow", "ident"], ["C_X"])
                    P.op("dve", (lambda gT=gT, Xb=Xb: lambda e: e.tensor_reduce(gT[0:64, :], r3(Xb[0:64, :], 32), AX.X, ALU.add))(), reads=["C_X"],
                         writes=["C_gT"])
                    kc.tt("dve", r3(Xb[0:64, :], 32), r3(brow[0:64, :], 32), idb32, ALU.mult, ["C_brow", "ident"], ["C_X"])
                    P.op("dve", (lambda bT=bT, Xb=Xb: lambda e: e.tensor_reduce(bT[0:64, :], r3(Xb[0:64, :], 32), AX.X, ALU.add))(), reads=["C_X"],
                         writes=["C_bT"])
                    bi, pb, pk = kc.bank()
                    kc.mm(pb[0:64, 0:32], Ut, gT[0:64, :], True, True, ["C_tri", "C_gT"], [pk])
                    kc.copy("act", gcc[0:64, :], pb[0:64, 0:32], [pk], ["C_gcc"])
                    bi, pb, pk = kc.bank()
                    kc.mm(pb[:, 0:32], ones_f[0:64, 0:128], gT[0:64, :], True, True, ["ones_f", "C_gT"], [pk])
                    kc.act(egl[:, :], pb[:, 0:32], AF.Exp, [pk], ["C_egl"])
                    kc.tt("dve", egd[0:64, :], pb[0:64, 0:32], gcc[0:64, :], ALU.subtract, [pk, "C_gcc"], ["C_egd"])
                    kc.act(egd[0:64, :], egd[0:64, :], AF.Exp, ["C_egd"], ["C_egd"])
                    kc.tt("dve", r3(Xb[0:64, :], 32), bc_last(gT[0:64, :], 32, 64), bc_mid(Ut, 32), ALU.mult,
                          ["C_gT", "C_tri"], ["C_X"])
                    for q4 in range(4):
                        sl = slice(q4 * 512, (q4 + 1) * 512)
                        bi, pb, pk = kc.bank()
                        kc.mm(pb[:, :], ones_f[0:64, 0:128], Xb[0:64, sl], True, True, ["ones_f", "C_X"], [pk])
                        kc.copy("dve", gcrow[:, sl], pb[:, :], [pk], ["C_gcrow"])
                        kc.act(E1[:, sl], pb[:, :], AF.Exp, [pk], ["C_E1"])

                    if CSTOP == 1:
                        kc.dump("gcc", gcc[0:64, :], ["C_gcc"]); kc.dump("egl", egl[:, :], ["C_egl"])
                        kc.dump("egd", egd[0:64, :], ["C_egd"]); kc.dump("gcrow", gcrow[:, :], ["C_gcrow"])
                        kc.dump("E1", E1[:, :], ["C_E1"]); kc.dump("gT", gT[0:64, :], ["C_gT"])
                        raise _Stop()
                    def conv_silu(comp, eng):
                        blk = comp * 4 + h
                        kc.ld(raw[:, 3:3 + S], dqkvT[blk * 128:(blk + 1) * 128, t0:t0 + S], "C_raw")
                        cwi = V_CW + (l * 12 + blk) * 4
                        kc.ts(eng, acc, raw[:, 0:S], vcol(cwi), None, ALU.mult, None, ["C_raw", "C_rawz", "vecs"],
                              ["C_acc"])
                        for j in range(1, 4):
                            kc.sto(eng, acc, raw[:, j:j + S], vcol(cwi + j), acc, ALU.mult, ALU.add,
                                   ["C_raw", "C_rawz", "C_acc", "vecs"], ["C_acc"])
                        kc.act(acc, acc, AF.Silu, ["C_acc"], ["C_acc"])

                    def l2n(scale):
                        kc.act(sqb, acc, AF.Square, ["C_acc"], ["C_sqb"])
                        for q4 in range(4):
                            sl = slice(q4 * 512, (q4 + 1) * 512)
                            bi, pb, pk = kc.bank()
                            kc.mm(pb[:, :], ones_bf, sqb[:, sl], True, True, ["ones_bf", "C_sqb"], [pk])
                            kc.act(rn, pb[:, :], AF.Sqrt, [pk, "vecs"], ["C_rn"], bias=vcol(V_EPS))
                            kc.recip(rn, rn, ["C_rn"], ["C_rn"])
                            kc.sto("dve", acc[:, sl], acc[:, sl], scale, rn, ALU.mult, ALU.mult, ["C_acc", "C_rn"],
                                   ["C_acc"])

                    def to_tok(dst, dkey, mul_egd):
                        for n4 in range(8):
                            bi, pb, pk = kc.bank()
                            for c in range(4):
                                n = n4 * 4 + c
                                kc.transpose(pb[0:64, c * 128:(c + 1) * 128], acc[:, n * 64:(n + 1) * 64], ident,
                                             ["C_acc", "ident"], [pk])
                            dsl = dst[0:64, n4 * 512:(n4 + 1) * 512]
                            if mul_egd:
                                kc.tt("dve", r3(dsl, 4), r3(pb[0:64, :], 4), bc_last(egd[0:64, n4 * 4:n4 * 4 + 4], 4, 128),
                                      ALU.mult, [pk, "C_egd"], [dkey])
                            else:
                                kc.copy("act", dsl, pb[0:64, :], [pk], [dkey])

                    conv_silu(1, "dve")
                    l2n(1.0)
                    kc.copy("pool", khb, acc, ["C_acc"], ["C_khb"])
                    kc.tt("dve", keT, acc, E1, ALU.mult, ["C_acc", "C_E1"], ["C_keT"])
                    to_tok(kdec, "C_kdec", True)
                    conv_silu(0, "dve")
                    l2n(128.0 ** -0.5)
                    kc.copy("pool", qhb, acc, ["C_acc"], ["C_qhb"])
                    kc.tt("dve", qdT, acc, E1, ALU.mult, ["C_acc", "C_E1"], ["C_qdT"])
                    conv_silu(2, "dve")
                    to_tok(vtok, "C_vtok", False)

                    if CSTOP == 2:
                        kc.dump("keT", keT[:, :], ["C_keT"]); kc.dump("qdT", qdT[:, :], ["C_qdT"])
                        kc.dump("kdec", kdec[0:64, :], ["C_kdec"], BF16); kc.dump("vtok", vtok[0:64, :], ["C_vtok"])
                        raise _Stop()
                    for bt in range(4):
                        n0 = bt * 8
                        c0 = bt * 512
                        bsl = slice(c0, c0 + 512)
                        kc.tt("dve", r3(Dm[0:64, :], 8), r3(gcrow[0:64, bsl], 8), bc_last(gcc[0:64, n0:n0 + 8], 8, 64),
                              ALU.subtract, ["C_gcrow", "C_gcc"], ["C_Dm"])
                        kc.tt("pool", r3(tmpD[0:64, :], 8), r3(Dm[0:64, :], 8), bc_mid(mA, 8), ALU.add,
                              ["C_Dm", "C_tri"], ["C_tmpD"])
                        kc.act(GTi[0:64, :], tmpD[0:64, :], AF.Exp, ["C_tmpD"], ["C_GTi"])
                        kc.tt("dve", r3(tmpD[0:64, :], 8), bc_mid(mB, 8), r3(Dm[0:64, :], 8), ALU.subtract,
                              ["C_Dm", "C_tri", "C_tmpD"], ["C_tmpD"])
                        kc.act(Gb[0:64, :], tmpD[0:64, :], AF.Exp, ["C_tmpD"], ["C_Gb"])
                        kc.tt("dve", r3(Gb[0:64, :], 8), r3(Gb[0:64, :], 8), bc_last(bT[0:64, n0:n0 + 8], 8, 64),
                              ALU.mult, ["C_Gb", "C_bT"], ["C_Gb"])
                        kc.tt("pool", r3(GTb[0:64, :], 8), r3(GTi[0:64, :], 8), bc_mid(SU, 8), ALU.mult,
                              ["C_GTi", "C_tri"], ["C_GTb"])
                        kc.tt("pool", GTb[0:64, :], GTb[0:64, :], brow[0:64, bsl], ALU.mult, ["C_GTb", "C_brow"],
                              ["C_GTb"])
                        Q, R, G = Qk[bt], Rk[bt], Gk[bt]
                        qk_, rk_, gk_ = "C_Q%d" % bt, "C_R%d" % bt, "C_G%d" % bt
                        bi, pb, pk = kc.bank()
                        for c in range(8):
                            cs = slice((n0 + c) * 64, (n0 + c + 1) * 64)
                            kc.mm(pb[0:64, c * 64:(c + 1) * 64], khb[:, cs], khb[:, cs], True, True, ["C_khb"], [pk])
                        kc.tt("dve", R[0:64, :], pb[0:64, :], Gb[0:64, :], ALU.mult, [pk, "C_Gb"], [rk_])
                        kc.tt("dve", Q[0:64, :], pb[0:64, :], GTb[0:64, :], ALU.mult, [pk, "C_GTb"], [qk_])
                        bi, pb, pk = kc.bank()
                        for c in range(8):
                            cs = slice((n0 + c) * 64, (n0 + c + 1) * 64)
                            kc.mm(pb[0:64, c * 64:(c + 1) * 64], khb[:, cs], qhb[:, cs], True, True,
                                  ["C_khb", "C_qhb"], [pk])
                        kc.tt("dve", aqkT[0:64, bsl], pb[0:64, :], GTi[0:64, :], ALU.mult, [pk, "C_GTi"], ["C_aqkT"])
                        kc.tt("pool", r3(G[0:64, :], 8), identb3, r3(Q[0:64, :], 8), ALU.subtract, ["ident", qk_], [gk_])
                    if CSTOP == 3:
                        kc.dump("Q0", Qk[0][0:64, :], ["C_Q0"]); kc.dump("R0", Rk[0][0:64, :], ["C_R0"])
                        kc.dump("G0", Gk[0][0:64, :], ["C_G0"]); kc.dump("aqkT", aqkT[0:64, :], ["C_aqkT"], BF16)
                        raise _Stop()
                    for lev in range(1, 6):
                        for bt in range(4):
                            Q, R, G = Qk[bt], Rk[bt], Gk[bt]
                            qk_, rk_, gk_ = "C_Q%d" % bt, "C_R%d" % bt, "C_G%d" % bt
                            if lev < 5:
                                bq, pbq, pkq = kc.bank()
                                for c in range(8):
                                    cs = slice(c * 64, (c + 1) * 64)
                                    kc.mm(pbq[0:64, cs], R[0:64, cs], Q[0:64, cs], True, True, [qk_, rk_], [pkq])
                            br, pbr, pkr = kc.bank()
                            for c in range(8):
                                cs = slice(c * 64, (c + 1) * 64)
                                kc.mm(pbr[0:64, cs], Q[0:64, cs], R[0:64, cs], True, True, [qk_, rk_], [pkr])
                            if lev < 5:
                                kc.copy("act", Q[0:64, :], pbq[0:64, :], [pkq], [qk_])
                            kc.copy("act", R[0:64, :], pbr[0:64, :], [pkr], [rk_])
                            bg, pbg, pkg = kc.bank()
                            for c in range(8):
                                cs = slice(c * 64, (c + 1) * 64)
                                kc.mm(pbg[0:64, cs], R[0:64, cs], G[0:64, cs], True, True, [rk_, gk_], [pkg])
                            kc.tt("dve", G[0:64, :], G[0:64, :], pbg[0:64, :], ALU.add, [gk_, pkg], [gk_])
                    for bt in range(4):
                        n0 = bt * 8
                        kc.tt("dve", r3(TTb[0:64, bt * 512:(bt + 1) * 512], 8), r3(Gk[bt][0:64, :], 8),
                              bc_last(bT[0:64, n0:n0 + 8], 8, 64), ALU.mult, ["C_G%d" % bt, "C_bT"], ["C_TTb"])

                    if CSTOP == 4:
                        kc.dump("TTb", TTb[0:64, :], ["C_TTb"], BF16)
                        raise _Stop()
                    kc.memset("dve", S_, 0.0, ["C_S"])
                    for n in range(32):
                        cs = slice(n * 64, (n + 1) * 64)
                        ns = slice(n * 128, (n + 1) * 128)
                        b1, pb1, pk1 = kc.bank()
                        kc.mm(pb1[0:64, 0:128], keT[:, cs], S_, True, True, ["C_keT", "C_S"], [pk1])
                        b3, pb3, pk3 = kc.bank()
                        kc.mm(pb3[:, 0:64], S_, qdT[:, cs], True, False, ["C_qdT", "C_S"], [pk3])
                        kc.tt("dve", Rb[0:64, :], vtok[0:64, ns], pb1[0:64, 0:128], ALU.subtract, ["C_vtok", pk1],
                              ["C_Rb"])
                        b2, pb2, pk2 = kc.bank()
                        kc.mm(pb2[0:64, 0:128], TTb[0:64, cs], Rb[0:64, :], True, True, ["C_TTb", "C_Rb"], [pk2])
                        kc.copy("act", vn[0:64, :], pb2[0:64, 0:128], [pk2], ["C_vn"])
                        kc.mm(pb3[:, 0:64], vn[0:64, :], aqkT[0:64, cs], False, True, ["C_vn", "C_aqkT"], [pk3])
                        b4, pb4, pk4 = kc.bank()
                        kc.mm(pb4[:, 0:128], kdec[0:64, ns], vn[0:64, :], True, True, ["C_kdec", "C_vn"], [pk4])
                        kc.copy("act", oT[:, cs], pb3[:, 0:64], [pk3], ["C_oT"])
                        kc.sto("dve", S_, S_, egl[:, n:n + 1], pb4[:, 0:128], ALU.mult, ALU.add, ["C_S", "C_egl", pk4],
                               ["C_S"])

                    if CSTOP == 5:
                        kc.dump("oT", oT[:, :], ["C_oT"])
                        raise _Stop()
                    kc.ld(zs, dzT[h * 128:(h + 1) * 128, t0:t0 + S], "C_zs")
                    kc.act(sqb, oT, AF.Square, ["C_oT"], ["C_sqb"])
                    for q4 in range(4):
                        sl = slice(q4 * 512, (q4 + 1) * 512)
                        bi, pb, pk = kc.bank()
                        kc.mm(pb[:, :], ones_bf, sqb[:, sl], True, True, ["ones_bf", "C_sqb"], [pk])
                        kc.act(rn, pb[:, :], AF.Sqrt, [pk, "vecs"], ["C_rn"], bias=vcol(V_EPS), scale=1.0 / 128)
                        kc.recip(rn, rn, ["C_rn"], ["C_rn"])
                        kc.sto("dve", oT[:, sl], oT[:, sl], vcol(V_DNW + l), rn, ALU.mult, ALU.mult,
                               ["C_oT", "C_rn", "vecs"], ["C_oT"])
                        kc.tt("pool", yout[:, sl], oT[:, sl], zs[:, sl], ALU.mult, ["C_oT", "C_zs"], ["C_yout"])
                    kc.store(ybT[512 + h * 128:512 + (h + 1) * 128, t0:t0 + S], yout, "C_yout", writes=["ybT"])
              except _Stop:
                pass
            kc.reset()

        if "D" in stages:
            band = kc.alloc(8 * 1152, BF16)
            kc.ld(band, band_d, "D_band", q="pool")
            band3 = r3(band, 8)
            ind = kc.alloc(64 * 128, BF16)
            kc.ld(ind, ind_d, "D_ind", q="pool")
            ind3 = r3(ind, 64)
            identb = kc.alloc(128, BF16)
            kc.copy("dve", identb, ident, ["ident"], ["D_identb"])
            QT = kc.alloc(4 * S, BF16)
            KT = kc.alloc(4 * S, BF16)
            QT3 = r3(QT, 4)
            KT3 = r3(KT, 4)
            VP = kc.alloc(16 * 768, BF16)
            VP4 = VP.rearrange("p (i c w) -> p i c w", i=16, c=4)
            VP3 = r3(VP, 16)
            kc.memset("pool", VP, 0.0, ["D_VPz"])
            kc.memset("pool", VP4[:, :, :, 64:65], 1.0, ["D_VPz"])
            kms = kc.alloc(32)
            KM = kc.alloc(4 * 64, BF16)
            KM3 = r3(KM, 4)
            gsb = kc.alloc(128)
            gw = kc.alloc(128)
            mxt = kc.alloc(16)
            eqt = kc.alloc(128)
            Mt = kc.alloc(128)
            MallT = kc.alloc(S, BF16)
            ptb = [kc.alloc(512, BF16) for _ in range(3)]
            rl = kc.alloc(512)
            osb = kc.alloc(512)
            ycT = kc.alloc(4 * S, BF16)
            ycT3 = r3(ycT, 4)
            kc.memset("pool", KM, 0.0, ["D_KMz"])
            for s_ in range(NSEQ):
              try:
                t0 = s_ * S
                kc.ld(QT3, mqT[:, t0:t0 + S].rearrange("(c p) t -> p c t", p=128), "D_QT")
                kc.ld(KT3, mkT[:, t0:t0 + S].rearrange("(c p) t -> p c t", p=128), "D_KT")
                srcv = mv[t0:t0 + S, :].rearrange("(i p) (c two d) -> p i c two d", p=128, two=2, d=64)
                for c in range(4):
                    kc.ld(VP4[:, :, c, 0:64], srcv[:, :, c, 0, :], ("D_VP", c, 0), reads=["D_VPz"], semkey="D_VPa%d" % c)
                    kc.ld(VP4[:, :, c, 128:192], srcv[:, :, c, 1, :], ("D_VP", c, 1), reads=["D_VPz"],
                          semkey="D_VPb%d" % c)
                P.op("dve", (lambda kms=kms, KT=KT: lambda e: e.tensor_reduce(r3(kms, 4), KT.rearrange("p (c n k) -> p c n k", c=4, n=8), AX.X,
                                                      ALU.add))(), reads=["D_KT"], writes=["D_kms"])
                kms3 = r3(kms, 4)
                for c in range(4):
                    kc.copy("dve", KM3[0:64, c, (2 * c) * 8:(2 * c) * 8 + 8], kms3[0:64, c, :], ["D_kms", "D_KMz"],
                            ["D_KM"])
                    kc.copy("dve", KM3[64:128, c, (2 * c + 1) * 8:(2 * c + 1) * 8 + 8], kms3[64:128, c, :],
                            ["D_kms", "D_KMz"], ["D_KM"])
                if DSTOP == 1:
                    kc.dump("KM", KM, ["D_KM"], BF16)
                    kc.dump("VP", VP, ["D_VPz"] + [("D_VP", c, hh) for c in range(4) for hh in range(2)], BF16)
                    kc.dump("kms", kms, ["D_kms"])
                    raise _Stop()
                for q4 in range(4):
                    bm, pbm, pkm = kc.bank()
                    for qq in range(4):
                        qt = q4 * 4 + qq
                        b = qt // 2
                        if b >= 4:
                            bi, pb, pk = kc.bank()
                            for c in range(4):
                                kc.mm(pb[:, 0:64], QT3[:, c, qt * 128:(qt + 1) * 128], KM3[:, c, :], c == 0, c == 3,
                                      ["D_QT", "D_KM"], [pk])
                            g3 = r3(gsb[:, 0:64], 8)
                            w3 = r3(gw[:, 0:64], 8)
                            e3 = r3(eqt[:, 0:64], 8)
                            kc.copy("act", gsb[:, 0:64], pb[:, 0:64], [pk], ["D_gsb"])
                            kc.memset("dve", g3[:, :, b:8], -1.0e9, ["D_gsb"])
                            src, srck = g3, "D_gsb"
                            for rnd in range(3):
                                P.op("dve", (lambda src=src, mxt=mxt: lambda e: e.tensor_reduce(mxt[:, 0:8], src, AX.X, ALU.max))(),
                                     reads=[srck], writes=["D_mxt"])
                                if rnd == 2:
                                    break
                                mb = mxt[:, 0:8].rearrange("p (h o) -> p h o", o=1).to_broadcast([128, 8, 8])
                                kc.tt("dve", e3, src, mb, ALU.is_ge, [srck, "D_mxt"], ["D_eqt"])
                                kc.sto("dve", w3, e3, -1.0e9, src, ALU.mult, ALU.add, ["D_eqt", srck], ["D_gw"])
                                src, srck = w3, "D_gw"
                            mb = mxt[:, 0:8].rearrange("p (h o) -> p h o", o=1).to_broadcast([128, 8, 8])
                            kc.tt("dve", e3, g3, mb, ALU.is_ge, ["D_gsb", "D_mxt"], ["D_eqt"])
                            kc.ts("dve", Mt[:, 0:64], eqt[:, 0:64], BIG, -BIG, ALU.mult, ALU.add, ["D_eqt"], ["D_Mt"])
                            kc.memset("dve", r3(Mt[:, 0:64], 8)[:, :, b:b + 1], 0.0, ["D_Mt"])
                        else:
                            kc.memset("dve", Mt[:, 0:64], 0.0, ["D_Mt"])
                        kc.transpose(pbm[0:64, qq * 128:(qq + 1) * 128], Mt[:, 0:64], ident, ["D_Mt", "ident"], [pkm])
                    kc.copy("act", MallT[0:64, q4 * 512:(q4 + 1) * 512], pbm[0:64, :], [pkm], ["D_MallT"])
                    kc.copy("act", MallT[64:128, q4 * 512:(q4 + 1) * 512], pbm[0:64, :], [pkm], ["D_MallT"])
                if DSTOP == 2:
                    kc.dump("MallT", MallT[0:64, :], ["D_MallT"], BF16)
                    raise _Stop()
                ipt = 0
                for h in range(8):
                    c = h // 2
                    r0 = (h % 2) * 64
                    lrow = 64 if h % 2 == 0 else 0
                    for qi in range(4):
                        q0 = qi * 512
                        nkt = (qi + 1) * 4
                        bo, pbo, pko = kc.bank(n=2, base=6)
                        for kt in range(nkt):
                            k0 = kt * 128
                            nblk = kt // 2
                            bs_, pbs, pks = kc.bank(n=5, base=0)
                            kc.mm(pbs[:, :], KT3[r0:r0 + 64, c, k0:k0 + 128], QT3[r0:r0 + 64, c, q0:q0 + 512], True, False,
                                  ["D_KT", "D_QT"], [pks])
                            kc.mm(pbs[:, :], ind3[r0:r0 + 64, h * 8 + nblk, :], MallT[r0:r0 + 64, q0:q0 + 512], False, False,
                                  ["D_ind", "D_MallT"], [pks])
                            off = min(max(q0 - k0, -384), 256) + 384
                            kc.mm(pbs[:, :], identb, band3[:, h, off:off + 512], False, True, ["D_identb", "D_band"], [pks])
                            pt = ptb[ipt % 3]
                            ptk = "D_pt%d" % (ipt % 3)
                            ipt += 1
                            kc.act(pt, pbs[:, :], AF.Exp, [pks], [ptk])
                            lo = c * 192 + (h % 2) * 64
                            kc.mm(pbo[:, :], VP3[:, kt, lo:lo + 128], pt, kt == 0, kt == nkt - 1, [("D_VP", c, 0), ("D_VP", c, 1), "D_VPz", ptk],
                                  [pko])
                        kc.recip(rl[lrow:lrow + 1, :], pbo[lrow:lrow + 1, :], [pko], ["D_rl"])
                        kc.copy("act", osb[r0:r0 + 64, :], pbo[r0:r0 + 64, :], [pko], ["D_osb"])
                        br_, pbr, pkr = kc.bank(n=1, base=5)
                        kc.mm(pbr[:, :], ones_f[lrow:lrow + 1, 0:128], rl[lrow:lrow + 1, :], True, True,
                              ["ones_f", "D_rl"], [pkr])
                        kc.tt("dve", ycT3[r0:r0 + 64, c, q0:q0 + 512], osb[r0:r0 + 64, :], pbr[r0:r0 + 64, :], ALU.mult,
                              ["D_osb", pkr], ["D_ycT"])
                    if DSTOP == 3 + h:
                        kc.dump("ycT", ycT, ["D_ycT"], BF16)
                        raise _Stop()
                kc.store(ybT[1024:1536, t0:t0 + S].rearrange("(c p) t -> p c t", p=128), ycT3, "D_ycT", writes=["ybT"])
              except _Stop:
                pass
            kc.reset()

        if "E" in stages:
            wbr = kc.alloc(12 * 1024, BF16)
            kc.ld(wbr, wbrb[l], "E_wbr", reads=["CAST1" if l == layers[0] else "CAST1b"])
            wbr4 = wbr.rearrange("p (n k d) -> p n k d", n=3, k=4)
            wo = kc.alloc(8 * 1024, BF16)
            kc.ld(wo, woutb[l], "E_wo", reads=["CAST1" if l == layers[0] else "CAST1b"])
            wo3 = r3(wo, 8)
            ybb = [kc.alloc(12 * 512, BF16) for _ in range(2)]
            gtb = [kc.alloc(24 * 512, BF16) for _ in range(2)]
            xtb = [kc.alloc(8 * 512) for _ in range(2)]
            mg = kc.alloc(8 * 512, BF16)
            mg3 = r3(mg, 8)
            ta = kc.alloc(512)
            tb = kc.alloc(512)
            tc_ = kc.alloc(512)
            for tt in range(NT):
                i2 = tt % 2
                tsl = slice(tt * 512, (tt + 1) * 512)
                yb3 = r3(ybb[i2], 12)
                gt3 = r3(gtb[i2], 24)
                xt3 = r3(xtb[i2], 8)
                ybk, gtk, xk = "E_yb%d" % i2, "E_gt%d" % i2, "E_xt%d" % i2
                kc.ld(yb3, ybT[:, tsl].rearrange("(c p) t -> p c t", p=128), ybk)
                kc.ld(gt3, gatesT[:, tsl].rearrange("(c p) t -> p c t", p=128), gtk)
                kc.ld(xt3, xsrc[:, tsl].rearrange("(k p) t -> p k t", p=128), xk)
                for m in range(8):
                    pbs_ = []
                    for n in range(3):
                        bi, pb, pk = kc.bank()
                        for k in range(4):
                            kc.mm(pb[:, :], wbr4[:, n, k, m * 128:(m + 1) * 128], yb3[:, n * 4 + k, :], k == 0, k == 3,
                                  ["E_wbr", ybk], [pk])
                        pbs_.append((pb, pk))
                    kc.tt("dve", ta, pbs_[0][0][:, :], gt3[:, m, :], ALU.mult, [pbs_[0][1], gtk], ["E_ta"])
                    kc.tt("dve", tb, pbs_[1][0][:, :], gt3[:, 8 + m, :], ALU.mult, [pbs_[1][1], gtk], ["E_tb"])
                    kc.tt("dve", tc_, pbs_[2][0][:, :], gt3[:, 16 + m, :], ALU.mult, [pbs_[2][1], gtk], ["E_tc"])
                    kc.tt("pool", ta, ta, tb, ALU.add, ["E_ta", "E_tb"], ["E_ta"])
                    kc.tt("pool", mg3[:, m, :], ta, tc_, ALU.add, ["E_ta", "E_tc"], [("E_mg", m)])
                for m in range(8):
                    bi, pb, pk = kc.bank()
                    for k in range(8):
                        kc.mm(pb[:, :], wo3[:, k, m * 128:(m + 1) * 128], mg3[:, k, :], k == 0, k == 7,
                              ["E_wo", ("E_mg", k)], [pk])
                    kc.tt("dve", xt3[:, m, :], xt3[:, m, :], pb[:, :], ALU.add, [xk, pk], [xk])
                kc.store(x1s[:, tsl].rearrange("(k p) t -> p k t", p=128), xt3, xk, writes=["x1s"])
            kc.reset()

        if "F" in stages:
            wdn = kc.alloc(22 * 1024, BF16)
            kc.ld(wdn, wdnb[l], "F_wdn", reads=["CAST1" if l == layers[0] else "CAST1b"])
            wdn3 = r3(wdn, 22)
            pgw = kc.alloc(8 * 1024, BF16)
            kc.ld(pgw, pgb[l], "F_pgw", reads=["CAST1" if l == layers[0] else "CAST1b"])
            pgw3 = r3(pgw, 8)
            ppw = kc.alloc(2 * 1024, BF16)
            kc.ld(ppw, ppb[l], "F_ppw", reads=["CAST1" if l == layers[0] else "CAST1b"])
            ppw3 = r3(ppw, 2)
            hal = kc.alloc(44 * 2)
            xtb = [kc.alloc(8 * 512) for _ in range(2)]
            h2 = kc.alloc(8 * 512, BF16)
            h23 = r3(h2, 8)
            sqt = kc.alloc(8 * 512, BF16)
            rstd = kc.alloc(512)
            wub = [kc.alloc(8 * 256, BF16) for _ in range(2)]
            rawb = [kc.alloc(514) for _ in range(2)]
            yb_ = [kc.alloc(512) for _ in range(2)]
            gTt = kc.alloc(22 * 512, BF16)
            gT3 = r3(gTt, 22)
            ptb_ = [kc.alloc(2 * 512, BF16) for _ in range(2)]
            sg = kc.alloc(512)
            tp = kc.alloc(512)
            last = (l == layers[-1]) and final
            if last:
                ot = kc.alloc(8 * 512)
                ot3 = r3(ot, 8)
            iw = 0
            for tt in range(NT):
                i2 = tt % 2
                tsl = slice(tt * 512, (tt + 1) * 512)
                xt = xtb[i2]
                xt3 = r3(xt, 8)
                xk = "F_xt%d" % i2
                kc.ld(xt3, x1s[:, tsl].rearrange("(k p) t -> p k t", p=128), xk)
                pt_ = ptb_[i2]
                ptk = "F_pt%d" % i2
                kc.ld(r3(pt_, 2), pT_in[l, :, tsl].rearrange("(k p) t -> p k t", p=128), ptk, q="pool")
                if tt % 4 == 0:
                    kc.memset("pool", hal, 0.0, ["F_hal"])
                rmsnorm_tile(xt, xk, V_NFFN + l * 8, lambda k: h23[:, k, :], [("F_h", k) for k in range(8)], sqt, "F_sq",
                             rstd, "F_rstd")
                HK = [("F_h", k) for k in range(8)]
                for j in range(22):
                    wu = wub[iw % 2]
                    wuk = "F_wu%d" % (iw % 2)
                    iw += 1
                    kc.ld(wu, wupb[l, j], wuk, reads=["CAST1" if l == layers[0] else "CAST1b"])
                    wu3 = r3(wu, 8)
                    for half in range(2):
                        bi, pb, pk = kc.bank()
                        for k in range(8):
                            kc.mm(pb[:, :], wu3[:, k, half * 128:(half + 1) * 128], h23[:, k, :], k == 0, k == 7,
                                  [wuk, ("F_h", k)], [pk])
                        rb = rawb[half]
                        rk = "F_raw%d" % half
                        yb = yb_[half]
                        yk = "F_y%d" % half
                        hi = (j * 2 + half) * 2
                        kc.copy("act", rb[:, 2:514], pb[:, :], [pk], [rk])
                        kc.copy("pool", rb[:, 0:2], hal[:, hi:hi + 2], ["F_hal"], [rk + "h"])
                        kc.copy("pool", hal[:, hi:hi + 2], rb[:, 512:514], [rk], ["F_hal"])
                        cwi = V_FCW + (l * 44 + half * 22 + j) * 3
                        e1 = "dve"
                        kc.ts(e1, yb, rb[:, 0:512], vcol(cwi), None, ALU.mult, None, [rk, rk + "h", "vecs"], [yk])
                        kc.sto(e1, yb, rb[:, 1:513], vcol(cwi + 1), yb, ALU.mult, ALU.add, [rk, rk + "h", yk, "vecs"], [yk])
                        kc.sto("dve", yb, rb[:, 2:514], vcol(cwi + 2), yb, ALU.mult, ALU.add, [rk, yk, "vecs"], [yk])
                    kc.act(yb_[0], yb_[0], AF.Gelu_apprx_tanh, ["F_y0"], ["F_y0"])
                    kc.tt("dve", gT3[:, j, :], yb_[0], yb_[1], ALU.mult, ["F_y0", "F_y1"], [("F_g", j)])
                for m in range(8):
                    bi, pb, pk = kc.bank()
                    for j in range(22):
                        kc.mm(pb[:, :], wdn3[:, j, m * 128:(m + 1) * 128], gT3[:, j, :], j == 0, j == 21,
                              ["F_wdn", ("F_g", j)], [pk])
                    kc.tt("dve", xt3[:, m, :], xt3[:, m, :], pb[:, :], ALU.add, [xk, pk], [xk])
                rmsnorm_tile(xt, xk, V_NPLE + l * 8, lambda k: h23[:, k, :], HK, sqt, "F_sq", rstd, "F_rstd")
                for m in range(8):
                    bi, pb, pk = kc.bank()
                    for k in range(8):
                        kc.mm(pb[:, :], pgw3[:, k, m * 128:(m + 1) * 128], h23[:, k, :], k == 0, k == 7,
                              ["F_pgw", ("F_h", k)], [pk])
                    kc.act(sg, pb[:, :], AF.Sigmoid, [pk], ["F_sg"])
                    bi, pb, pk = kc.bank()
                    for k in range(2):
                        kc.mm(pb[:, :], ppw3[:, k, m * 128:(m + 1) * 128], r3(pt_, 2)[:, k, :], k == 0, k == 1,
                              ["F_ppw", ptk], [pk])
                    kc.tt("dve", tp, pb[:, :], sg, ALU.mult, [pk, "F_sg"], ["F_tp"])
                    kc.tt("pool", xt3[:, m, :], xt3[:, m, :], tp, ALU.add, [xk, "F_tp"], [xk])
                if last:
                    rmsnorm_tile(xt, xk, V_NFIN, lambda k: ot3[:, k, :], [("F_ot", k) for k in range(8)], sqt, "F_sq", rstd, "F_rstd")
                    kc.finals.append(kc.store(outT[:, tsl].rearrange("(k p) t -> p k t", p=128), ot3, [("F_ot", k) for k in range(8)],
                                              writes=["outT"], semkey="F_ot_st"))
                else:
                    kc.store(xs[:, tsl].rearrange("(k p) t -> p k t", p=128), xt3, xk, writes=["xs"])
            kc.reset()
    nsem = P.emit(final_wait_ops=kc.finals)
    return nc, kc, nsem


def _t5_bucket_np(n):
    n = np.maximum(n, 0)
    exact = 16
    nf = np.maximum(n, 1).astype(np.float32)
    large = exact + (np.log(nf / exact) / math.log(128 / exact) * (32 - exact)).astype(np.int32)
    large = np.minimum(large, 31)
    return np.where(n < exact, n, large)


def host_consts():
    ident = np.eye(128, dtype=np.float32)
    p = np.arange(64)[:, None]
    f = np.arange(64)[None, :]
    U = (p <= f).astype(np.float32)
    mA = np.where(f >= p, 0.0, -BIG).astype(np.float32)
    mB = np.where(p > f, 0.0, -BIG).astype(np.float32)
    SU = (f > p).astype(np.float32)
    tri = np.concatenate([U, mA, mB, SU], axis=1)
    rc = np.zeros((128, 64), np.float32)
    for g in range(4):
        w = 2 << g
        for t in range(16):
            rc[:, g * 16 + t] = 1.0 / min(t + 1, w)
    ind = np.zeros((64, 64, 128), np.float32)
    for r in range(64):
        ind[r, r, :] = 1.0
    ind = ind.reshape(64, 64 * 128)
    return ident, tri, rc, np.concatenate([ind, ind], axis=0)


def host_vecs(inp):
    v = np.zeros((128, NVEC), np.float32)

    def put(base, arr):
        a = np.asarray(arr, np.float32)
        lead = int(np.prod(a.shape[:-1])) if a.ndim > 1 else 1
        a = a.reshape(lead, -1, 128)
        a = a.transpose(2, 0, 1).reshape(128, -1)
        v[:, base:base + a.shape[1]] = a

    put(V_NMIX, inp["norm_mix"])
    put(V_NFFN, inp["norm_ffn"])
    put(V_NPLE, inp["norm_ple"])
    put(V_NFIN, inp["norm_final"])
    put(V_PSCALE, inp["pool_scale"])
    cw = np.asarray(inp["dn_conv"], np.float32).reshape(2, 4, 12, 128).transpose(3, 0, 2, 1).reshape(128, 96)
    v[:, V_CW:V_CW + 96] = cw
    fc = np.asarray(inp["ffn_conv"], np.float32).reshape(2, 3, 44, 128).transpose(3, 0, 2, 1).reshape(128, 264)
    v[:, V_FCW:V_FCW + 264] = fc
    v[:, V_DNW:V_DNW + 2] = np.asarray(inp["dn_norm"], np.float32).T
    v[:, V_EPS] = EPS
    v[:, V_ONE] = 1.0
    return v


def host_band(rel_bias):
    rb = np.asarray(rel_bias, np.float32)
    p = np.arange(128)[:, None]
    c = np.arange(1152)[None, :]
    n = c - 384 - p
    bucket = _t5_bucket_np(n)
    band = np.empty((128, 8, 1152), np.float32)
    for h in range(8):
        band[:, h, :] = np.where(n >= 0, rb[bucket, h], -BIG)
    return band.reshape(128, 8 * 1152)


_CACHE = {}


def get_program(NSEQ=2, **kw):
    key = (NSEQ, tuple(sorted(kw.items())))
    if key not in _CACHE:
        _CACHE[key] = build(NSEQ=NSEQ, **kw)
    return _CACHE[key]


def make_in_maps(inp, NSEQ, ncores):
    ident, tri, rc, ind = host_consts()
    vecs = host_vecs(inp)
    band = host_band(inp["rel_bias"])
    a4 = np.stack([np.asarray(inp["dn_a_log"], np.float32)[0], np.asarray(inp["dn_dt_bias"], np.float32)[0],
                   np.asarray(inp["dn_a_log"], np.float32)[1], np.asarray(inp["dn_dt_bias"], np.float32)[1]], axis=1)
    x = np.asarray(inp["x"], np.float32)
    p = np.asarray(inp["p"], np.float32)
    shared = {
        "w_in": np.ascontiguousarray(inp["w_in"], np.float32),
        "w_branch": np.ascontiguousarray(inp["w_branch"], np.float32),
        "w_out": np.ascontiguousarray(inp["w_out"], np.float32),
        "ffn_up": np.ascontiguousarray(inp["ffn_up"], np.float32),
        "ffn_down": np.ascontiguousarray(inp["ffn_down"], np.float32),
        "ple_gate": np.ascontiguousarray(inp["ple_gate"], np.float32),
        "ple_proj": np.ascontiguousarray(inp["ple_proj"], np.float32),
        "pool_w": np.ascontiguousarray(inp["pool_w"], np.float32),
        "vecs": vecs, "ident": ident, "tri": tri, "rcnt": rc, "band": band, "ind": ind,
        "a4": np.ascontiguousarray(a4),
    }
    maps = []
    for c in range(ncores):
        xs_ = x[c * NSEQ:(c + 1) * NSEQ].reshape(NSEQ * S, D)
        ps_ = p[:, c * NSEQ:(c + 1) * NSEQ].reshape(DEPTH, NSEQ * S, 256)
        m = dict(shared)
        m["xT"] = np.ascontiguousarray(xs_.T)
        m["pT"] = np.ascontiguousarray(ps_.transpose(0, 2, 1))
        maps.append(m)
    return maps


def kernel(**inputs):
    NSEQ = 2
    nc, kc, _ = get_program(NSEQ=NSEQ)
    maps = make_in_maps(inputs, NSEQ, NCORES)
    res = run_bass_kernel_spmd(nc, maps, core_ids=list(range(NCORES)))
    outs = []
    for c in range(NCORES):
        oT = np.asarray(res.results[c]["outT"], np.float32)
        outs.append(oT.T.reshape(NSEQ, S, D))
    return np.concatenate(outs, axis=0).astype(np.float32)
```

```python
import contextlib
import math
import numpy as np
import concourse.bass as bass
import concourse.mybir as mybir
from concourse.bass_utils import run_bass_kernel_spmd

F32 = mybir.dt.float32
BF16 = mybir.dt.bfloat16
AF = mybir.ActivationFunctionType
ALU = mybir.AluOpType
AX = mybir.AxisListType

D = 1024
S = 2048
DEPTH = 2
IN_COLS = 7176
FFN = 2816
EPS = 1e-6
NCORES = 8
BIG = 30000.0


class Op:
    __slots__ = ("eng", "fn", "deps", "sem", "inc", "val", "signal", "is_dma", "tag")

    def __init__(self, eng, fn, is_dma=False):
        self.eng = eng
        self.fn = fn
        self.deps = []
        self.sem = None
        self.inc = 1
        self.val = None
        self.signal = False
        self.is_dma = is_dma


class Prog:
    ENGS = ("pe", "act", "dve", "pool", "sp")

    def __init__(self, nc):
        self.nc = nc
        self.ops = {e: [] for e in self.ENGS}
        self.last_w = {}
        self.readers = {}
        self.n_ops = 0
        self.last_dma = {}
        self.bar = {e: None for e in self.ENGS}
        self.cur_tag = "pre"
        self.scopes = False

    def barrier(self):
        deps = []
        for e in self.ENGS:
            for o in reversed(self.ops[e]):
                if not o.is_dma:
                    deps.append(o)
                    break
        deps.extend(self.last_dma.values())
        for e in self.ENGS:
            self.bar[e] = deps
        self.last_w = {}
        self.readers = {}

    def _add(self, op, reads, writes):
        deps = []
        for k in reads:
            w = self.last_w.get(k)
            if w is not None:
                deps.append((w, False))
        for k in writes:
            w = self.last_w.get(k)
            if w is not None:
                deps.append((w, False))
            for r in self.readers.get(k, ()):
                deps.append((r, True))
        if self.bar[op.eng] is not None:
            for d in self.bar[op.eng]:
                deps.append((d, False))
            self.bar[op.eng] = None
        seen = set()
        for d, war in deps:
            if d is op or id(d) in seen:
                continue
            if d.eng == op.eng and not d.is_dma and not op.is_dma:
                if op.eng == "pe" or war:
                    continue
            seen.add(id(d))
            op.deps.append(d)
            d.signal = True
        for k in reads:
            self.readers.setdefault(k, []).append(op)
        for k in writes:
            self.last_w[k] = op
            self.readers[k] = []
        op.tag = self.cur_tag
        self.ops[op.eng].append(op)
        self.n_ops += 1

    @staticmethod
    def _is_psum(k):
        return isinstance(k, str) and k.startswith("ps") and k[2:].isdigit()

    def op(self, eng, fn, reads=(), writes=()):
        o = Op(eng, fn)
        o.sem = ("eng", eng)
        ex = [k for k in reads if self._is_psum(k)]
        if ex:
            reads = [k for k in reads if not self._is_psum(k)]
            writes = list(writes) + ex
        self._add(o, reads, writes)
        return o

    def dma(self, fn, semkey, reads=(), writes=(), q="sp"):
        o = Op(q, fn, is_dma=True)
        o.sem = ("dma", semkey)
        o.inc = 16
        o.signal = True
        self._add(o, reads, writes)
        self.last_dma[semkey] = o
        return o

    def emit(self, final_wait_ops=()):
        nc = self.nc
        counts = {}
        for e in self.ENGS:
            for o in self.ops[e]:
                if o.signal:
                    c = counts.get(o.sem, 0) + o.inc
                    counts[o.sem] = c
                    o.val = c
        semkeys = list(counts.keys())
        with contextlib.ExitStack() as st:
            sems = {}
            for i, k in enumerate(semkeys):
                sems[k] = st.enter_context(nc.semaphore("s%d" % i))
            block = st.enter_context(nc.Block())
            engmap = {"pe": block.tensor, "act": block.scalar, "dve": block.vector,
                      "pool": block.gpsimd, "sp": block.sync}

            def make(e):
                oplist = self.ops[e]

                def body(eng):
                    waited = {}
                    cur = None
                    cm = None
                    for o in oplist:
                        if self.scopes and o.tag != cur:
                            if cm is not None:
                                cm.__exit__(None, None, None)
                            cm = nc.named_scope(o.tag)
                            cm.__enter__()
                            cur = o.tag
                        for d in o.deps:
                            if waited.get(d.sem, 0) >= d.val:
                                continue
                            eng.wait_ge(sems[d.sem], d.val)
                            waited[d.sem] = d.val
                        ins = o.fn(eng)
                        if o.signal:
                            ins.then_inc(sems[o.sem], o.inc)
                    if cm is not None:
                        cm.__exit__(None, None, None)
                    if e == "sp":
                        for o in final_wait_ops:
                            if waited.get(o.sem, 0) >= o.val:
                                continue
                            eng.wait_ge(sems[o.sem], o.val)
                            waited[o.sem] = o.val

                return body

            for e in self.ENGS:
                if self.ops[e] or e == "sp":
                    engmap[e](make(e))
        return len(semkeys)


ARENA_F32 = 46 * 1024


import os as _os
CSTOP = int(_os.environ.get("C_STOP", "99"))
DSTOP = int(_os.environ.get("D_STOP", "99"))


class _Stop(Exception):
    pass


class KC:
    def __init__(self, nc, NSEQ, debug):
        self.nc = nc
        self.P = Prog(nc)
        self.NSEQ = NSEQ
        self.T = NSEQ * S
        self.NT = self.T // 512
        self.debug = debug
        self.st = contextlib.ExitStack()
        self.arena = self.st.enter_context(nc.sbuf_tensor("arena", [128, ARENA_F32], F32))
        self.arena_bf = self.arena[:, :].bitcast(BF16)
        self.psb = [self.st.enter_context(nc.psum_tensor("psb%d" % i, [128, 512], F32)) for i in range(8)]
        self.bump = 0
        self.perm = 0
        self.rr = 0
        self.finals = []
        self.dram = {}
        self.uid = 0

    def alloc(self, cols, dt=F32):
        if dt == BF16:
            w = (cols + 1) // 2
            a = self.bump
            self.bump += w
            assert self.bump <= ARENA_F32, "SBUF arena overflow %d" % self.bump
            return self.arena_bf[:, 2 * a:2 * a + cols]
        a = self.bump
        self.bump += cols
        assert self.bump <= ARENA_F32, "SBUF arena overflow %d" % self.bump
        return self.arena[:, a:a + cols]

    def make_perm(self):
        self.perm = self.bump

    def reset(self):
        self.P.barrier()
        self.bump = self.perm

    def key(self, base):
        self.uid += 1
        return "%s#%d" % (base, self.uid)

    def din(self, name, shape, dt=F32):
        t = self.nc.dram_tensor(name, list(shape), dt, kind="ExternalInput").ap()
        self.dram[name] = t
        return t

    def dscr(self, name, shape, dt=F32, out=False):
        kind = "ExternalOutput" if (out or self.debug) else "Internal"
        t = self.nc.dram_tensor(name, list(shape), dt, kind=kind).ap()
        self.dram[name] = t
        return t

    def ld(self, dst, src, key, reads=(), q="sp", semkey=None, slow=False):
        if slow:
            return self.P.dma(lambda e: e.dma_start(out=dst, in_=src, allow_slow_non_contiguous=True), semkey or key,
                              reads=reads, writes=[key], q=q)
        return self.P.dma(lambda e: e.dma_start(out=dst, in_=src), semkey or key, reads=reads, writes=[key], q=q)

    def stt(self, dst, src, srckey, writes=(), q="sp", semkey=None):
        keys = list(srckey) if isinstance(srckey, list) else [srckey]
        sk = semkey or (str(keys[0]) + "_st")
        return self.P.dma(lambda e: e.dma_start(out=dst, in_=src), sk, reads=keys, writes=writes, q=q)

    def dump(self, name, ap, keys, dt=F32):
        if not self.debug:
            return
        shape = list(ap.shape)
        d = self.nc.dram_tensor("dbg_" + name, shape, dt, kind="ExternalOutput").ap()
        self.finals.append(self.P.dma(lambda e: e.dma_start(out=d, in_=ap), "dump_" + name, reads=list(keys)))

    def store(self, dst, src, srckey, writes=(), q="sp", semkey=None):
        return self.stt(dst, src, srckey, writes, q, semkey)

    def act(self, out, in_, func, reads, writes, bias=None, scale=None, accum=None):
        kw = {}
        if bias is not None:
            kw["bias"] = bias
        if scale is not None:
            kw["scale"] = scale
        if accum is not None:
            kw["accum_out"] = accum
        return self.P.op("act", lambda e: e.activation(out=out, in_=in_, func=func, **kw), reads=reads, writes=writes)

    def tt(self, eng, out, in0, in1, op, reads, writes):
        return self.P.op(eng, lambda e: e.tensor_tensor(out, in0, in1, op), reads=reads, writes=writes)

    def ts(self, eng, out, in0, s1, s2, op0, op1, reads, writes):
        if s2 is None:
            return self.P.op(eng, lambda e: e.tensor_scalar(out, in0, s1, None, op0), reads=reads, writes=writes)
        return self.P.op(eng, lambda e: e.tensor_scalar(out, in0, s1, s2, op0, op1), reads=reads, writes=writes)

    def sto(self, eng, out, in0, scalar, in1, op0, op1, reads, writes):
        return self.P.op(eng, lambda e: e.scalar_tensor_tensor(out=out, in0=in0, scalar=scalar, in1=in1, op0=op0,
                                                               op1=op1), reads=reads, writes=writes)

    def copy(self, eng, out, in_, reads, writes):
        if eng == "act":
            return self.P.op("act", lambda e: e.copy(out, in_), reads=reads, writes=writes)
        return self.P.op(eng, lambda e: e.tensor_copy(out, in_), reads=reads, writes=writes)

    def memset(self, eng, ap, val, writes):
        return self.P.op(eng, lambda e: e.memset(ap, val), writes=writes)

    def recip(self, out, in_, reads, writes):
        return self.P.op("dve", lambda e: e.reciprocal(out, in_), reads=reads, writes=writes)

    def transpose(self, out, in_, ident, reads, writes):
        return self.P.op("pe", lambda e: e.transpose(out, in_, ident), reads=reads, writes=writes)

    def mm(self, out, lhsT, rhs, start, stop, reads, writes):
        return self.P.op("pe", lambda e: e.matmul(out, lhsT, rhs, start=start, stop=stop), reads=reads, writes=writes)

    def bank(self, n=8, base=0):
        i = base + (self.rr % n)
        self.rr += 1
        return i, self.psb[i], "ps%d" % i


def r3(ap, a):
    return ap.rearrange("p (a b) -> p a b", a=a)


V_NMIX = 0
V_NFFN = 16
V_NPLE = 32
V_NFIN = 48
V_PSCALE = 56
V_CW = 64
V_FCW = 160
V_DNW = 424
V_EPS = 426
V_ONE = 427
NVEC = 428
WIN_GROUP_C0 = [0, 512, 1024, 1536, 2048, 2568, 3080, 3592] + [4104 + 512 * i for i in range(6)]
WIN_GROUP_DST = [("u", None, 0), ("dqkv", None, 0), ("dqkv", None, 512), ("dqkv", None, 1024), ("dz", None, 0),
                 ("mq", None, 0), ("mk", None, 0), ("mv", None, 0)] + [("gate", None, 512 * i) for i in range(6)]

C_POOL = 0
C_DQKV = 512
C_DZ = 2048
C_DB = 2560
C_DA = 2564
C_MQ = 2568
C_MK = 2568 + 512
C_MV = 2568 + 1024
C_GATE = 4104


def build(NSEQ=2, debug=False, layers=(0, 1), stages="ABCDEF", final=True, scopes=False):
    nc = bass.Bass("TRN2", target_bir_lowering=False)
    kc = KC(nc, NSEQ, debug)
    P = kc.P
    P.scopes = scopes
    T = kc.T
    NT = kc.NT
    xT_in = kc.din("xT", [D, T])
    pT_in = kc.din("pT", [DEPTH, 256, T])
    w_in = kc.din("w_in", [DEPTH, D, IN_COLS])
    w_branch = kc.din("w_branch", [DEPTH, 3, 512, D])
    w_out = kc.din("w_out", [DEPTH, D, D])
    ffn_up = kc.din("ffn_up", [DEPTH, D, 2 * FFN])
    ffn_down = kc.din("ffn_down", [DEPTH, FFN, D])
    ple_gate = kc.din("ple_gate", [DEPTH, D, D])
    ple_proj = kc.din("ple_proj", [DEPTH, 256, D])
    pool_w = kc.din("pool_w", [DEPTH, 4, 128, 128])
    vecs_d = kc.din("vecs", [128, NVEC])
    ident_d = kc.din("ident", [128, 128])
    tri_d = kc.din("tri", [64, 4 * 64])
    rcnt_d = kc.din("rcnt", [128, 64])
    band_d = kc.din("band", [128, 8 * 1152])
    ind_d = kc.din("ind", [128, 64 * 128])
    a4_d = kc.din("a4", [4, 4])
    outT = kc.dscr("outT", [D, T], out=True)
    xs = kc.dscr("xs", [D, T])
    x1s = kc.dscr("x1s", [D, T])
    uT = kc.dscr("uT", [512, T])
    dqkvT = kc.dscr("dqkvT", [1536, T])
    dzT = kc.dscr("dzT", [512, T])
    dbg = kc.dscr("dbg", [8, T])
    mqT = kc.dscr("mqT", [512, T], BF16)
    mkT = kc.dscr("mkT", [512, T], BF16)
    mv = kc.dscr("mv", [T, 512], BF16)
    gatesT = kc.dscr("gatesT", [3072, T], BF16)
    ybT = kc.dscr("ybT", [1536, T], BF16)
    NG_IN = 15
    winb = kc.dscr("winb", [DEPTH, 15, 128, 8 * 512], BF16)
    wbrb = kc.dscr("wbrb", [DEPTH, 128, 12 * 1024], BF16)
    woutb = kc.dscr("woutb", [DEPTH, 128, 8 * 1024], BF16)
    wupb = kc.dscr("wupb", [DEPTH, 22, 128, 8 * 256], BF16)
    wdnb = kc.dscr("wdnb", [DEPTH, 128, 22 * 1024], BF16)
    pgb = kc.dscr("pgb", [DEPTH, 128, 8 * 1024], BF16)
    ppb = kc.dscr("ppb", [DEPTH, 128, 2 * 1024], BF16)
    pwb = kc.dscr("pwb", [DEPTH, 128, 4 * 128], BF16)

    vecs = kc.alloc(NVEC)
    ident = kc.alloc(128)
    ones_bf = kc.alloc(128, BF16)
    ones_f = kc.alloc(128)
    kc.ld(vecs, vecs_d, "vecs")
    kc.ld(ident, ident_d, "ident")
    P.op("pool", lambda e: e.memset(ones_bf, 1.0), writes=["ones_bf"])
    P.op("pool", lambda e: e.memset(ones_f, 1.0), writes=["ones_f"])
    kc.make_perm()
    CONST = ["vecs", "ident", "ones_bf", "ones_f"]

    def cast(dst, src, key):
        grp = ("CAST0" if key[0] == "winb" else "CAST1") + ("" if key[1] == layers[0] else "b")
        P.dma(lambda e: e.dma_start(out=dst, in_=src), grp, writes=[grp], q="pool")

    for l in layers:
        for g in range(14):
            c0 = WIN_GROUP_C0[g]
            cast(winb[l, g].rearrange("p (k c) -> p k c", k=8),
                 w_in[l, :, c0:c0 + 512].rearrange("(k p) c -> p k c", p=128), ("winb", l, g))
        cast(winb[l, 14].rearrange("p (k c) -> p k c", k=8)[:, :, 0:8],
             w_in[l, :, C_DB:C_DB + 8].rearrange("(k p) c -> p k c", p=128), ("winb", l, 14))
        cast(pwb[l].rearrange("p (g d) -> p g d", g=4), pool_w[l].rearrange("g c d -> c g d"), ("pwb", l))
        for n in range(3):
            cast(wbrb[l].rearrange("p (n k d) -> p n k d", n=3, k=4)[:, n],
                 w_branch[l, n].rearrange("(k p) d -> p k d", p=128), ("wbrb", l, n))
        cast(woutb[l].rearrange("p (k d) -> p k d", k=8), w_out[l].rearrange("(k p) d -> p k d", p=128), ("woutb", l))
        for j in range(22):
            dstv = wupb[l, j].rearrange("p (k c) -> p k c", k=8)
            cast(dstv[:, :, 0:128], ffn_up[l, :, j * 128:(j + 1) * 128].rearrange("(k p) c -> p k c", p=128),
                 ("wupb", l, j, 0))
            cast(dstv[:, :, 128:256],
                 ffn_up[l, :, FFN + j * 128:FFN + (j + 1) * 128].rearrange("(k p) c -> p k c", p=128),
                 ("wupb", l, j, 1))
        cast(wdnb[l].rearrange("p (j d) -> p j d", j=22), ffn_down[l].rearrange("(j p) d -> p j d", p=128),
             ("wdnb", l))
        cast(pgb[l].rearrange("p (k d) -> p k d", k=8), ple_gate[l].rearrange("(k p) d -> p k d", p=128), ("pgb", l))
        cast(ppb[l].rearrange("p (k d) -> p k d", k=2), ple_proj[l].rearrange("(k p) d -> p k d", p=128), ("ppb", l))

    def vcol(i):
        return vecs[:, i:i + 1]

    def rmsnorm_tile(xt, xkey, nbase, out_fn, outkeys, sqt, sqkey, rstd, rkey, engs=("dve",)):
        P.op("act", lambda e: e.activation(out=sqt, in_=xt, func=AF.Square), reads=[xkey], writes=[sqkey])
        bi, pb, pk = kc.bank()
        for k in range(8):
            kc.mm(pb[:, :], ones_bf, sqt[:, k * 512:(k + 1) * 512], k == 0, k == 7, [sqkey, "ones_bf"], [pk])
        P.op("act", lambda e: e.activation(out=rstd, in_=pb[:, :], func=AF.Sqrt, bias=vcol(V_EPS), scale=1.0 / D),
             reads=[pk, "vecs"], writes=[rkey])
        P.op("dve", lambda e: e.reciprocal(rstd, rstd), reads=[rkey], writes=[rkey])
        for k in range(8):
            o = out_fn(k)
            eng = engs[k % len(engs)]
            P.op(eng, (lambda o=o, k=k: lambda e: e.scalar_tensor_tensor(
                out=o, in0=xt[:, k * 512:(k + 1) * 512], scalar=vcol(nbase + k), in1=rstd,
                op0=ALU.mult, op1=ALU.mult))(), reads=[xkey, rkey, "vecs"], writes=[outkeys[k]])

    stages0 = stages
    for l in layers:
        stages = stages0 if l == layers[0] else _os.environ.get("L1S", stages0)
        xsrc = xT_in if l == 0 else xs
        xsrc_key = "xs"
        if "A" in stages:
            P.cur_tag = "L%dA" % l
            hT = kc.alloc(8 * T, BF16)
            hT3 = r3(hT, 8)
            xtb = [kc.alloc(8 * 512) for _ in range(2)]
            sqt = kc.alloc(8 * 512, BF16)
            rstd = kc.alloc(512)
            for tt in range(NT):
                xt = xtb[tt % 2]
                xk = "A_xt%d" % (tt % 2)
                kc.ld(r3(xt, 8), xsrc[:, tt * 512:(tt + 1) * 512].rearrange("(k p) t -> p k t", p=128), xk,
                      reads=[xsrc_key])
                rmsnorm_tile(xt, xk, V_NMIX + l * 8, lambda k: hT3[:, k, tt * 512:(tt + 1) * 512],
                             [("hT", tt, k) for k in range(8)], sqt, "A_sq", rstd, "A_rstd")
            HT_ALL = [("hT", tt) for tt in range(NT)]
            wgb = [kc.alloc(8 * 512, BF16) for _ in range(2)]
            ob32 = [kc.alloc(T) for _ in range(2)]
            ob16 = [kc.alloc(T, BF16) for _ in range(2)]
            ovb = [kc.alloc(512, BF16) for _ in range(2)]
            cnt = {"o32": 0, "o16": 0, "ov": 0, "ev": 0}

            def evac(kind, dst, src, pk, okey):
                if kind in ("u", "dqkv"):
                    eng = "act" if cnt["ev"] % 2 == 0 else "dve"
                    cnt["ev"] += 1
                    if eng == "act":
                        P.op("act", lambda e: e.copy(dst, src), reads=[pk], writes=[okey])
                    else:
                        P.op("dve", lambda e: e.tensor_copy(dst, src), reads=[pk], writes=[okey])
                elif kind == "dz":
                    P.op("act", lambda e: e.activation(out=dst, in_=src, func=AF.Silu), reads=[pk], writes=[okey])
                elif kind == "mq":
                    P.op("dve", lambda e: e.tensor_scalar(dst, src, 0.125, None, ALU.mult), reads=[pk], writes=[okey])
                elif kind == "mk":
                    P.op("dve", lambda e: e.tensor_copy(dst, src), reads=[pk], writes=[okey])
                elif kind == "gate":
                    P.op("act", lambda e: e.activation(out=dst, in_=src, func=AF.Sigmoid), reads=[pk], writes=[okey])
                else:
                    raise ValueError(kind)

            for g in range(14):
                wg = wgb[g % 2]
                wk = "A_wg%d" % (g % 2)
                kc.ld(wg, winb[l, g], wk, reads=["CAST0b"] if l != layers[0] else ["CAST0"])
                wg3 = r3(wg, 8)
                gkind, gdst, grow0 = WIN_GROUP_DST[g]
                if gkind == "mv":
                    for i in range(T // 128):
                        bi, pb, pk = kc.bank()
                        for k in range(8):
                            kc.mm(pb[:, :], hT3[:, k, i * 128:(i + 1) * 128], wg3[:, k, :], k == 0, k == 7,
                                  [("hT", i // 4, k), wk], [pk])
                        ov = ovb[cnt["ov"] % 2]
                        ok = "A_ov%d" % (cnt["ov"] % 2)
                        cnt["ov"] += 1
                        eng = "act" if i % 2 == 0 else "dve"
                        if eng == "act":
                            P.op("act", (lambda ov=ov, pb=pb: lambda e: e.copy(ov, pb[:, :]))(), reads=[pk], writes=[ok])
                        else:
                            P.op("dve", (lambda ov=ov, pb=pb: lambda e: e.tensor_copy(ov, pb[:, :]))(), reads=[pk],
                                 writes=[ok])
                        kc.stt(mv[i * 128:(i + 1) * 128, :], ov, ok, writes=["mv"])
                    continue
                for j in range(4):
                    is16 = gkind in ("mq", "mk", "gate")
                    if is16:
                        ob = ob16[cnt["o16"] % 2]
                        okey = "A_o16_%d" % (cnt["o16"] % 2)
                        cnt["o16"] += 1
                    else:
                        ob = ob32[cnt["o32"] % 2]
                        okey = "A_o32_%d" % (cnt["o32"] % 2)
                        cnt["o32"] += 1
                    for tt in range(NT):
                        bi, pb, pk = kc.bank()
                        for k in range(8):
                            kc.mm(pb[:, :], wg3[:, k, j * 128:(j + 1) * 128], hT3[:, k, tt * 512:(tt + 1) * 512],
                                  k == 0, k == 7, [("hT", tt, k), wk], [pk])
                        evac(gkind, ob[:, tt * 512:(tt + 1) * 512], pb[:, :], pk, (okey, tt))
                    dst = {"u": uT, "dqkv": dqkvT, "dz": dzT, "mq": mqT, "mk": mkT, "gate": gatesT}[gkind]
                    r0 = grow0 + j * 128
                    kc.stt(dst[r0:r0 + 128, :], ob, [(okey, tt) for tt in range(NT)], writes=[gkind + "_d"], semkey=okey + "_st")
            wg = wgb[0]
            wk = "A_wg0"
            kc.ld(wg, winb[l, 14], wk, reads=["CAST0b"] if l != layers[0] else ["CAST0"])
            wg3 = r3(wg, 8)
            a4 = kc.alloc(4)
            P_a4 = kc.ld(a4[0:4, :], a4_d, "A_a4")
            nexpA = kc.alloc(1)
            kc.act(nexpA[0:4, :], a4[0:4, 2 * l:2 * l + 1], AF.Exp, ["A_a4"], ["A_nexpA"])
            kc.ts("dve", nexpA[0:4, :], nexpA[0:4, :], -1.0, None, ALU.mult, None, ["A_nexpA"], ["A_nexpA"])
            obb = ob32[0]
            oba = ob32[1]
            for tt in range(NT):
                sl = slice(tt * 512, (tt + 1) * 512)
                bi, pb, pk = kc.bank()
                for k in range(8):
                    kc.mm(pb[0:4, :], wg3[:, k, 0:4], hT3[:, k, sl], k == 0, k == 7, [("hT", tt, k), wk], [pk])
                kc.act(obb[0:4, sl], pb[0:4, :], AF.Sigmoid, [pk], [("A_o32_0", tt)])
                bi, pb, pk = kc.bank()
                for k in range(8):
                    kc.mm(pb[0:4, :], wg3[:, k, 4:8], hT3[:, k, sl], k == 0, k == 7, [("hT", tt, k), wk], [pk])
                kc.act(oba[0:4, sl], pb[0:4, :], AF.Exp, [pk, "A_a4"], [("A_o32_1", tt)],
                       bias=a4[0:4, 2 * l + 1:2 * l + 2])
            AK1 = [("A_o32_1", tt) for tt in range(NT)]
            kc.act(oba[0:4, :], oba[0:4, :], AF.Ln, AK1 + ["vecs"], AK1, bias=vcol(V_ONE)[0:4, :])
            kc.ts("dve", oba[0:4, :], oba[0:4, :], nexpA[0:4, 0:1], None, ALU.mult, None, AK1 + ["A_nexpA"], AK1)
            kc.stt(dbg[0:4, :], obb[0:4, :], [("A_o32_0", tt) for tt in range(NT)], writes=["dbg"], semkey="A_o32_0_st")
            kc.stt(dbg[4:8, :], oba[0:4, :], [("A_o32_1", tt) for tt in range(NT)], writes=["dbg"], semkey="A_o32_1_st")
            kc.reset()
        if "B" in stages:
            P.cur_tag = "L%dB" % l
            pw = kc.alloc(4 * 128, BF16)
            kc.ld(pw, pwb[l], "B_pw", reads=["CAST1" if l == layers[0] else "CAST1b"])
            pw3 = r3(pw, 4)
            rc = kc.alloc(64)
            kc.ld(rc, rcnt_d, "B_rc")
            ub = [kc.alloc(16 + S) for _ in range(2)]
            sab = [kc.alloc(16 + S) for _ in range(2)]
            mxb = [kc.alloc(S, BF16) for _ in range(2)]
            t16 = kc.alloc(16)
            yob = [kc.alloc(S, BF16) for _ in range(2)]
            for i in range(2):
                kc.memset("pool", ub[i][:, 0:16], 0.0, ["B_u%dz" % i])
                kc.memset("pool", sab[i][:, 0:16], 0.0, ["B_s%dz" % i])
            it = 0
            for s_ in range(NSEQ):
                for g in range(4):
                    i2 = it % 2
                    u = ub[i2]
                    uk = "B_u%d" % i2
                    kc.ld(u[:, 16:], uT[g * 128:(g + 1) * 128, s_ * S:(s_ + 1) * S], uk)
                    cur, curk = u, uk
                    for j in range(g + 1):
                        dst = sab[j % 2]
                        dk = "B_s%d" % (j % 2)
                        sh = 1 << j
                        kc.tt("dve" if j % 2 == 0 else "pool", dst[:, 16:], cur[:, 16:], cur[:, 16 - sh:16 - sh + S],
                              ALU.add, [curk, curk + "z"], [dk])
                        cur, curk = dst, dk
                    w = 1 << (g + 1)
                    m = mxb[i2]
                    mk_ = "B_mx%d" % i2
                    kc.sto("dve", m, cur[:, 16:], 1.0 / w, u[:, 16:], ALU.mult, ALU.subtract, [curk, uk], [mk_])
                    kc.tt("dve", t16, cur[:, 16:32], rc[:, g * 16:(g + 1) * 16], ALU.mult, [curk, "B_rc"], ["B_t16"])
                    kc.tt("dve", m[:, 0:16], t16, u[:, 16:32], ALU.subtract, ["B_t16", uk, mk_], [mk_])
                    y = yob[i2]
                    yk = "B_y%d" % i2
                    for j in range(4):
                        bi, pb, pk = kc.bank()
                        kc.mm(pb[:, :], pw3[:, g, :], m[:, j * 512:(j + 1) * 512], True, True, [mk_, "B_pw"], [pk])
                        kc.ts("dve" if j % 2 == 0 else "dve", y[:, j * 512:(j + 1) * 512], pb[:, :],
                              vcol(V_PSCALE + l * 4 + g), None, ALU.mult, None, [pk, "vecs"], [yk])
                    kc.store(ybT[g * 128:(g + 1) * 128, s_ * S:(s_ + 1) * S], y, yk, writes=["ybT"])
                    it += 1
            kc.reset()

        if "C" in stages:
            P.cur_tag = "L%dC" % l
            tri = kc.alloc(256)
            kc.ld(tri[0:64, :], tri_d, "C_tri")
            Ut = tri[0:64, 0:64]
            mA = tri[0:64, 64:128]
            mB = tri[0:64, 128:192]
            SU = tri[0:64, 192:256]
            identb3 = ident[0:64, 0:64].rearrange("p (o j) -> p o j", o=1).to_broadcast([64, 8, 64])
            raw = kc.alloc(3 + S)
            acc = kc.alloc(S)
            sqb = kc.alloc(S, BF16)
            rn = kc.alloc(512)
            khb = kc.alloc(S, BF16)
            qhb = kc.alloc(S, BF16)
            keT = kc.alloc(S)
            qdT = kc.alloc(S)
            kdec = kc.alloc(32 * 128, BF16)
            vtok = kc.alloc(32 * 128)
            Xb = kc.alloc(S)
            gcrow = kc.alloc(S)
            E1 = kc.alloc(S)
            brow = kc.alloc(S)
            gT = kc.alloc(32)
            bT = kc.alloc(32)
            gcc = kc.alloc(32)
            egd = kc.alloc(32)
            egl = kc.alloc(32)
            Dm = kc.alloc(512)
            Gb = kc.alloc(512)
            GTi = kc.alloc(512)
            GTb = kc.alloc(512)
            tmpD = kc.alloc(512)
            Qk = [kc.alloc(512) for _ in range(4)]
            Rk = [kc.alloc(512) for _ in range(4)]
            Gk = [kc.alloc(512) for _ in range(4)]
            aqkT = kc.alloc(S, BF16)
            TTb = kc.alloc(S, BF16)
            oT = kc.alloc(S)
            S_ = kc.alloc(128)
            Rb = kc.alloc(128, BF16)
            vn = kc.alloc(128, BF16)
            zs = kc.alloc(S)
            yout = kc.alloc(S, BF16)
            kc.memset("pool", raw[:, 0:3], 0.0, ["C_rawz"])

            def bc_mid(ap64, n):
                return ap64.rearrange("p (o j) -> p o j", o=1).to_broadcast([64, n, 64])

            def bc_last(ap, n, w):
                return ap.rearrange("p (n o) -> p n o", o=1).to_broadcast([64, n, w])

            for s_ in range(NSEQ):
              try:
                for h in range(4):
                    t0 = s_ * S
                    kc.ld(gcrow[0:64, :], dbg[4 + h:5 + h, t0:t0 + S].partition_broadcast(64), "C_gcrow")
                    kc.ld(brow[0:64, :], dbg[h:h + 1, t0:t0 + S].partition_broadcast(64), "C_brow")
                    idb32 = ident[0:64, 0:64].rearrange("p (o j) -> p o j", o=1).to_broadcast([64, 32, 64])
                    kc.tt("dve", r3(Xb[0:64, :], 32), r3(gcrow[0:64, :], 32), idb32, ALU.mult, ["C_gcrow", "ident"], ["C_X"])
                    P.op("dve", (lambda gT=gT, Xb=Xb: lambda e: e.tensor_reduce(gT[0:64, :], r3(Xb[0:64, :], 32), AX.X, ALU.add))(), reads=["C_X"],
                         writes=["C_gT"])
                    kc.tt("dve", r3(Xb[0:64, :], 32), r3(brow[0:64, :], 32), idb32, ALU.mult, ["C_brow", "ident"], ["C_X"])
                    P.op("dve", (lambda bT=bT, Xb=Xb: lambda e: e.tensor_reduce(bT[0:64, :], r3(Xb[0:64, :], 32), AX.X, ALU.add))(), reads=["C_X"],
                         writes=["C_bT"])
                    bi, pb, pk = kc.bank()
                    kc.mm(pb[0:64, 0:32], Ut, gT[0:64, :], True, True, ["C_tri", "C_gT"], [pk])
                    kc.copy("act", gcc[0:64, :], pb[0:64, 0:32], [pk], ["C_gcc"])
                    bi, pb, pk = kc.bank()
                    kc.mm(pb[:, 0:32], ones_f[0:64, 0:128], gT[0:64, :], True, True, ["ones_f", "C_gT"], [pk])
                    kc.act(egl[:, :], pb[:, 0:32], AF.Exp, [pk], ["C_egl"])
                    kc.tt("dve", egd[0:64, :], pb[0:64, 0:32], gcc[0:64, :], ALU.subtract, [pk, "C_gcc"], ["C_egd"])
                    kc.act(egd[0:64, :], egd[0:64, :], AF.Exp, ["C_egd"], ["C_egd"])
                    kc.tt("dve", r3(Xb[0:64, :], 32), bc_last(gT[0:64, :], 32, 64), bc_mid(Ut, 32), ALU.mult,
                          ["C_gT", "C_tri"], ["C_X"])
                    for q4 in range(4):
                        sl = slice(q4 * 512, (q4 + 1) * 512)
                        bi, pb, pk = kc.bank()
                        kc.mm(pb[:, :], ones_f[0:64, 0:128], Xb[0:64, sl], True, True, ["ones_f", "C_X"], [pk])
                        kc.copy("dve", gcrow[:, sl], pb[:, :], [pk], ["C_gcrow"])
                        kc.act(E1[:, sl], pb[:, :], AF.Exp, [pk], ["C_E1"])

                    if CSTOP == 1:
                        kc.dump("gcc", gcc[0:64, :], ["C_gcc"]); kc.dump("egl", egl[:, :], ["C_egl"])
                        kc.dump("egd", egd[0:64, :], ["C_egd"]); kc.dump("gcrow", gcrow[:, :], ["C_gcrow"])
                        kc.dump("E1", E1[:, :], ["C_E1"]); kc.dump("gT", gT[0:64, :], ["C_gT"])
                        raise _Stop()
                    def conv_silu(comp, eng):
                        blk = comp * 4 + h
                        kc.ld(raw[:, 3:3 + S], dqkvT[blk * 128:(blk + 1) * 128, t0:t0 + S], "C_raw")
                        cwi = V_CW + (l * 12 + blk) * 4
                        kc.ts(eng, acc, raw[:, 0:S], vcol(cwi), None, ALU.mult, None, ["C_raw", "C_rawz", "vecs"],
                              ["C_acc"])
                        for j in range(1, 4):
                            kc.sto(eng, acc, raw[:, j:j + S], vcol(cwi + j), acc, ALU.mult, ALU.add,
                                   ["C_raw", "C_rawz", "C_acc", "vecs"], ["C_acc"])
                        kc.act(acc, acc, AF.Silu, ["C_acc"], ["C_acc"])

                    def l2n(scale):
                        kc.act(sqb, acc, AF.Square, ["C_acc"], ["C_sqb"])
                        for q4 in range(4):
                            sl = slice(q4 * 512, (q4 + 1) * 512)
                            bi, pb, pk = kc.bank()
                            kc.mm(pb[:, :], ones_bf, sqb[:, sl], True, True, ["ones_bf", "C_sqb"], [pk])
                            kc.act(rn, pb[:, :], AF.Sqrt, [pk, "vecs"], ["C_rn"], bias=vcol(V_EPS))
                            kc.recip(rn, rn, ["C_rn"], ["C_rn"])
                            kc.sto("dve", acc[:, sl], acc[:, sl], scale, rn, ALU.mult, ALU.mult, ["C_acc", "C_rn"],
                                   ["C_acc"])

                    def to_tok(dst, dkey, mul_egd):
                        for n4 in range(8):
                            bi, pb, pk = kc.bank()
                            for c in range(4):
                                n = n4 * 4 + c
                                kc.transpose(pb[0:64, c * 128:(c + 1) * 128], acc[:, n * 64:(n + 1) * 64], ident,
                                             ["C_acc", "ident"], [pk])
                            dsl = dst[0:64, n4 * 512:(n4 + 1) * 512]
                            if mul_egd:
                                kc.tt("dve", r3(dsl, 4), r3(pb[0:64, :], 4), bc_last(egd[0:64, n4 * 4:n4 * 4 + 4], 4, 128),
                                      ALU.mult, [pk, "C_egd"], [dkey])
                            else:
                                kc.copy("act", dsl, pb[0:64, :], [pk], [dkey])

                    conv_silu(1, "dve")
                    l2n(1.0)
                    kc.copy("pool", khb, acc, ["C_acc"], ["C_khb"])
                    kc.tt("dve", keT, acc, E1, ALU.mult, ["C_acc", "C_E1"], ["C_keT"])
                    to_tok(kdec, "C_kdec", True)
                    conv_silu(0, "dve")
                    l2n(128.0 ** -0.5)
                    kc.copy("pool", qhb, acc, ["C_acc"], ["C_qhb"])
                    kc.tt("dve", qdT, acc, E1, ALU.mult, ["C_acc", "C_E1"], ["C_qdT"])
                    conv_silu(2, "dve")
                    to_tok(vtok, "C_vtok", False)

                    if CSTOP == 2:
                        kc.dump("keT", keT[:, :], ["C_keT"]); kc.dump("qdT", qdT[:, :], ["C_qdT"])
                        kc.dump("kdec", kdec[0:64, :], ["C_kdec"], BF16); kc.dump("vtok", vtok[0:64, :], ["C_vtok"])
                        raise _Stop()
                    for bt in range(4):
                        n0 = bt * 8
                        c0 = bt * 512
                        bsl = slice(c0, c0 + 512)
                        kc.tt("dve", r3(Dm[0:64, :], 8), r3(gcrow[0:64, bsl], 8), bc_last(gcc[0:64, n0:n0 + 8], 8, 64),
                              ALU.subtract, ["C_gcrow", "C_gcc"], ["C_Dm"])
                        kc.tt("pool", r3(tmpD[0:64, :], 8), r3(Dm[0:64, :], 8), bc_mid(mA, 8), ALU.add,
                              ["C_Dm", "C_tri"], ["C_tmpD"])
                        kc.act(GTi[0:64, :], tmpD[0:64, :], AF.Exp, ["C_tmpD"], ["C_GTi"])
                        kc.tt("dve", r3(tmpD[0:64, :], 8), bc_mid(mB, 8), r3(Dm[0:64, :], 8), ALU.subtract,
                              ["C_Dm", "C_tri", "C_tmpD"], ["C_tmpD"])
                        kc.act(Gb[0:64, :], tmpD[0:64, :], AF.Exp, ["C_tmpD"], ["C_Gb"])
                        kc.tt("dve", r3(Gb[0:64, :], 8), r3(Gb[0:64, :], 8), bc_last(bT[0:64, n0:n0 + 8], 8, 64),
                              ALU.mult, ["C_Gb", "C_bT"], ["C_Gb"])
                        kc.tt("pool", r3(GTb[0:64, :], 8), r3(GTi[0:64, :], 8), bc_mid(SU, 8), ALU.mult,
                              ["C_GTi", "C_tri"], ["C_GTb"])
                        kc.tt("pool", GTb[0:64, :], GTb[0:64, :], brow[0:64, bsl], ALU.mult, ["C_GTb", "C_brow"],
                              ["C_GTb"])
                        Q, R, G = Qk[bt], Rk[bt], Gk[bt]
                        qk_, rk_, gk_ = "C_Q%d" % bt, "C_R%d" % bt, "C_G%d" % bt
                        bi, pb, pk = kc.bank()
                        for c in range(8):
                            cs = slice((n0 + c) * 64, (n0 + c + 1) * 64)
                            kc.mm(pb[0:64, c * 64:(c + 1) * 64], khb[:, cs], khb[:, cs], True, True, ["C_khb"], [pk])
                        kc.tt("dve", R[0:64, :], pb[0:64, :], Gb[0:64, :], ALU.mult, [pk, "C_Gb"], [rk_])
                        kc.tt("dve", Q[0:64, :], pb[0:64, :], GTb[0:64, :], ALU.mult, [pk, "C_GTb"], [qk_])
                        bi, pb, pk = kc.bank()
                        for c in range(8):
                            cs = slice((n0 + c) * 64, (n0 + c + 1) * 64)
                            kc.mm(pb[0:64, c * 64:(c + 1) * 64], khb[:, cs], qhb[:, cs], True, True,
                                  ["C_khb", "C_qhb"], [pk])
                        kc.tt("dve", aqkT[0:64, bsl], pb[0:64, :], GTi[0:64, :], ALU.mult, [pk, "C_GTi"], ["C_aqkT"])
                        kc.tt("pool", r3(G[0:64, :], 8), identb3, r3(Q[0:64, :], 8), ALU.subtract, ["ident", qk_], [gk_])
                    if CSTOP == 3:
                        kc.dump("Q0", Qk[0][0:64, :], ["C_Q0"]); kc.dump("R0", Rk[0][0:64, :], ["C_R0"])
                        kc.dump("G0", Gk[0][0:64, :], ["C_G0"]); kc.dump("aqkT", aqkT[0:64, :], ["C_aqkT"], BF16)
                        raise _Stop()
                    for lev in range(1, 6):
                        for bt in range(4):
                            Q, R, G = Qk[bt], Rk[bt], Gk[bt]
                            qk_, rk_, gk_ = "C_Q%d" % bt, "C_R%d" % bt, "C_G%d" % bt
                            if lev < 5:
                                bq, pbq, pkq = kc.bank()
                                for c in range(8):
                                    cs = slice(c * 64, (c + 1) * 64)
                                    kc.mm(pbq[0:64, cs], R[0:64, cs], Q[0:64, cs], True, True, [qk_, rk_], [pkq])
                            br, pbr, pkr = kc.bank()
                            for c in range(8):
                                cs = slice(c * 64, (c + 1) * 64)
                                kc.mm(pbr[0:64, cs], Q[0:64, cs], R[0:64, cs], True, True, [qk_, rk_], [pkr])
                            if lev < 5:
                                kc.copy("act", Q[0:64, :], pbq[0:64, :], [pkq], [qk_])
                            kc.copy("act", R[0:64, :], pbr[0:64, :], [pkr], [rk_])
                            bg, pbg, pkg = kc.bank()
                            for c in range(8):
                                cs = slice(c * 64, (c + 1) * 64)
                                kc.mm(pbg[0:64, cs], R[0:64, cs], G[0:64, cs], True, True, [rk_, gk_], [pkg])
                            kc.tt("dve", G[0:64, :], G[0:64, :], pbg[0:64, :], ALU.add, [gk_, pkg], [gk_])
                    for bt in range(4):
                        n0 = bt * 8
                        kc.tt("dve", r3(TTb[0:64, bt * 512:(bt + 1) * 512], 8), r3(Gk[bt][0:64, :], 8),
                              bc_last(bT[0:64, n0:n0 + 8], 8, 64), ALU.mult, ["C_G%d" % bt, "C_bT"], ["C_TTb"])

                    if CSTOP == 4:
                        kc.dump("TTb", TTb[0:64, :], ["C_TTb"], BF16)
                        raise _Stop()
                    kc.memset("dve", S_, 0.0, ["C_S"])
                    for n in range(32):
                        cs = slice(n * 64, (n + 1) * 64)
                        ns = slice(n * 128, (n + 1) * 128)
                        b1, pb1, pk1 = kc.bank()
                        kc.mm(pb1[0:64, 0:128], keT[:, cs], S_, True, True, ["C_keT", "C_S"], [pk1])
                        b3, pb3, pk3 = kc.bank()
                        kc.mm(pb3[:, 0:64], S_, qdT[:, cs], True, False, ["C_qdT", "C_S"], [pk3])
                        kc.tt("dve", Rb[0:64, :], vtok[0:64, ns], pb1[0:64, 0:128], ALU.subtract, ["C_vtok", pk1],
                              ["C_Rb"])
                        b2, pb2, pk2 = kc.bank()
                        kc.mm(pb2[0:64, 0:128], TTb[0:64, cs], Rb[0:64, :], True, True, ["C_TTb", "C_Rb"], [pk2])
                        kc.copy("act", vn[0:64, :], pb2[0:64, 0:128], [pk2], ["C_vn"])
                        kc.mm(pb3[:, 0:64], vn[0:64, :], aqkT[0:64, cs], False, True, ["C_vn", "C_aqkT"], [pk3])
                        b4, pb4, pk4 = kc.bank()
                        kc.mm(pb4[:, 0:128], kdec[0:64, ns], vn[0:64, :], True, True, ["C_kdec", "C_vn"], [pk4])
                        kc.copy("act", oT[:, cs], pb3[:, 0:64], [pk3], ["C_oT"])
                        kc.sto("dve", S_, S_, egl[:, n:n + 1], pb4[:, 0:128], ALU.mult, ALU.add, ["C_S", "C_egl", pk4],
                               ["C_S"])

                    if CSTOP == 5:
                        kc.dump("oT", oT[:, :], ["C_oT"])
                        raise _Stop()
                    kc.ld(zs, dzT[h * 128:(h + 1) * 128, t0:t0 + S], "C_zs")
                    kc.act(sqb, oT, AF.Square, ["C_oT"], ["C_sqb"])
                    for q4 in range(4):
                        sl = slice(q4 * 512, (q4 + 1) * 512)
                        bi, pb, pk = kc.bank()
                        kc.mm(pb[:, :], ones_bf, sqb[:, sl], True, True, ["ones_bf", "C_sqb"], [pk])
                        kc.act(rn, pb[:, :], AF.Sqrt, [pk, "vecs"], ["C_rn"], bias=vcol(V_EPS), scale=1.0 / 128)
                        kc.recip(rn, rn, ["C_rn"], ["C_rn"])
                        kc.sto("dve", oT[:, sl], oT[:, sl], vcol(V_DNW + l), rn, ALU.mult, ALU.mult,
                               ["C_oT", "C_rn", "vecs"], ["C_oT"])
                        kc.tt("pool", yout[:, sl], oT[:, sl], zs[:, sl], ALU.mult, ["C_oT", "C_zs"], ["C_yout"])
                    kc.store(ybT[512 + h * 128:512 + (h + 1) * 128, t0:t0 + S], yout, "C_yout", writes=["ybT"])
              except _Stop:
                pass
            kc.reset()

        if "D" in stages:
            P.cur_tag = "L%dD" % l
            band = kc.alloc(8 * 1152, BF16)
            kc.ld(band, band_d, "D_band", q="pool")
            band3 = r3(band, 8)
            ind = kc.alloc(64 * 128, BF16)
            kc.ld(ind, ind_d, "D_ind", q="pool")
            ind3 = r3(ind, 64)
            identb = kc.alloc(128, BF16)
            kc.copy("dve", identb, ident, ["ident"], ["D_identb"])
            QT = kc.alloc(4 * S, BF16)
            KT = kc.alloc(4 * S, BF16)
            QT3 = r3(QT, 4)
            KT3 = r3(KT, 4)
            VP = kc.alloc(16 * 768, BF16)
            VP4 = VP.rearrange("p (i c w) -> p i c w", i=16, c=4)
            VP3 = r3(VP, 16)
            kc.memset("pool", VP, 0.0, ["D_VPz"])
            kc.memset("pool", VP4[:, :, :, 64:65], 1.0, ["D_VPz"])
            kms = kc.alloc(32)
            KM = kc.alloc(4 * 64, BF16)
            KM3 = r3(KM, 4)
            gsb = kc.alloc(128)
            gw = kc.alloc(128)
            mxt = kc.alloc(16)
            eqt = kc.alloc(128)
            Mt = kc.alloc(128)
            MallT = kc.alloc(S, BF16)
            ptb = [kc.alloc(512, BF16) for _ in range(4)]
            cfar = kc.alloc(8)
            kc.copy("dve", cfar, band3[:, :, 1151], ["D_band"], ["D_cfar"])
            rl = kc.alloc(512)
            osb = kc.alloc(512)
            ycT = kc.alloc(4 * S, BF16)
            ycT3 = r3(ycT, 4)
            kc.memset("pool", KM, 0.0, ["D_KMz"])
            for s_ in range(NSEQ):
              try:
                t0 = s_ * S
                kc.ld(QT3, mqT[:, t0:t0 + S].rearrange("(c p) t -> p c t", p=128), "D_QT")
                kc.ld(KT3, mkT[:, t0:t0 + S].rearrange("(c p) t -> p c t", p=128), "D_KT")
                srcv = mv[t0:t0 + S, :].rearrange("(i p) (c two d) -> p i c two d", p=128, two=2, d=64)
                for c in range(4):
                    kc.ld(VP4[:, :, c, 0:64], srcv[:, :, c, 0, :], ("D_VP", c, 0), reads=["D_VPz"], semkey="D_VPa%d" % c)
                    kc.ld(VP4[:, :, c, 128:192], srcv[:, :, c, 1, :], ("D_VP", c, 1), reads=["D_VPz"],
                          semkey="D_VPb%d" % c)
                P.op("dve", (lambda kms=kms, KT=KT: lambda e: e.tensor_reduce(r3(kms, 4), KT.rearrange("p (c n k) -> p c n k", c=4, n=8), AX.X,
                                                      ALU.add))(), reads=["D_KT"], writes=["D_kms"])
                kms3 = r3(kms, 4)
                for c in range(4):
                    kc.copy("dve", KM3[0:64, c, (2 * c) * 8:(2 * c) * 8 + 8], kms3[0:64, c, :], ["D_kms", "D_KMz"],
                            ["D_KM"])
                    kc.copy("dve", KM3[64:128, c, (2 * c + 1) * 8:(2 * c + 1) * 8 + 8], kms3[64:128, c, :],
                            ["D_kms", "D_KMz"], ["D_KM"])
                if DSTOP == 1:
                    kc.dump("KM", KM, ["D_KM"], BF16)
                    kc.dump("VP", VP, ["D_VPz"] + [("D_VP", c, hh) for c in range(4) for hh in range(2)], BF16)
                    kc.dump("kms", kms, ["D_kms"])
                    raise _Stop()
                for q4 in range(4):
                    bm, pbm, pkm = kc.bank()
                    for qq in range(4):
                        qt = q4 * 4 + qq
                        b = qt // 2
                        if b >= 4:
                            bi, pb, pk = kc.bank()
                            for c in range(4):
                                kc.mm(pb[:, 0:64], QT3[:, c, qt * 128:(qt + 1) * 128], KM3[:, c, :], c == 0, c == 3,
                                      ["D_QT", "D_KM"], [pk])
                            g3 = r3(gsb[:, 0:64], 8)
                            w3 = r3(gw[:, 0:64], 8)
                            e3 = r3(eqt[:, 0:64], 8)
                            kc.copy("act", gsb[:, 0:64], pb[:, 0:64], [pk], ["D_gsb"])
                            kc.memset("dve", g3[:, :, b:8], -1.0e9, ["D_gsb"])
                            src, srck = g3, "D_gsb"
                            for rnd in range(3):
                                P.op("dve", (lambda src=src, mxt=mxt: lambda e: e.tensor_reduce(mxt[:, 0:8], src, AX.X, ALU.max))(),
                                     reads=[srck], writes=["D_mxt"])
                                if rnd == 2:
                                    break
                                mb = mxt[:, 0:8].rearrange("p (h o) -> p h o", o=1).to_broadcast([128, 8, 8])
                                kc.tt("dve", e3, src, mb, ALU.is_ge, [srck, "D_mxt"], ["D_eqt"])
                                kc.sto("dve", w3, e3, -1.0e9, src, ALU.mult, ALU.add, ["D_eqt", srck], ["D_gw"])
                                src, srck = w3, "D_gw"
                            mb = mxt[:, 0:8].rearrange("p (h o) -> p h o", o=1).to_broadcast([128, 8, 8])
                            kc.tt("dve", e3, g3, mb, ALU.is_ge, ["D_gsb", "D_mxt"], ["D_eqt"])
                            kc.ts("dve", Mt[:, 0:64], eqt[:, 0:64], BIG, -BIG, ALU.mult, ALU.add, ["D_eqt"], ["D_Mt"])
                            kc.memset("dve", r3(Mt[:, 0:64], 8)[:, :, b:b + 1], 0.0, ["D_Mt"])
                        else:
                            kc.memset("dve", Mt[:, 0:64], 0.0, ["D_Mt"])
                        kc.transpose(pbm[0:64, qq * 128:(qq + 1) * 128], Mt[:, 0:64], ident, ["D_Mt", "ident"], [pkm])
                    kc.copy("act", MallT[0:64, q4 * 512:(q4 + 1) * 512], pbm[0:64, :], [pkm], ["D_MallT"])
                    kc.copy("act", MallT[64:128, q4 * 512:(q4 + 1) * 512], pbm[0:64, :], [pkm], ["D_MallT"])
                if DSTOP == 2:
                    kc.dump("MallT", MallT[0:64, :], ["D_MallT"], BF16)
                    raise _Stop()
                ipt = 0
                pend = []

                def flush(keep):
                    while len(pend) > keep:
                        pend.pop(0)()

                for h in range(8):
                    c = h // 2
                    r0 = (h % 2) * 64
                    lrow = 64 if h % 2 == 0 else 0
                    for qi in range(4):
                        q0 = qi * 512
                        nkt = (qi + 1) * 4
                        bo, pbo, pko = kc.bank(n=2, base=6)
                        for kt in range(nkt):
                            k0 = kt * 128
                            nblk = kt // 2
                            bs_, pbs, pks = kc.bank(n=5, base=0)
                            mms = [(KT3[r0:r0 + 64, c, k0:k0 + 128], QT3[r0:r0 + 64, c, q0:q0 + 512], ["D_KT", "D_QT"])]
                            if qi >= 2 and nblk < 2 * qi + 1:
                                mms.append((ind3[r0:r0 + 64, h * 8 + nblk, :], MallT[r0:r0 + 64, q0:q0 + 512],
                                            ["D_ind", "D_MallT"]))
                            far = (q0 - k0) >= 256
                            if not far:
                                off = min(max(q0 - k0, -384), 256) + 384
                                mms.append((identb, band3[:, h, off:off + 512], ["D_identb", "D_band"]))
                            for i_, (a_, b_, rd_) in enumerate(mms):
                                kc.mm(pbs[:, :], a_, b_, i_ == 0, i_ == len(mms) - 1, rd_, [pks])
                            pt = ptb[ipt % 4]
                            ptk = "D_pt%d" % (ipt % 4)
                            ipt += 1
                            if far:
                                kc.act(pt, pbs[:, :], AF.Exp, [pks, "D_cfar"], [ptk], bias=cfar[:, h:h + 1])
                            else:
                                kc.act(pt, pbs[:, :], AF.Exp, [pks], [ptk])
                            lo = c * 192 + (h % 2) * 64

                            def pv(pbo=pbo, pko=pko, kt=kt, nkt=nkt, lo=lo, pt=pt, ptk=ptk, c=c, r0=r0, lrow=lrow, q0=q0):
                                kc.mm(pbo[:, :], VP3[:, kt, lo:lo + 128], pt, kt == 0, kt == nkt - 1,
                                      [("D_VP", c, 0), ("D_VP", c, 1), "D_VPz", ptk], [pko])
                                if kt == nkt - 1:
                                    kc.recip(rl[lrow:lrow + 1, :], pbo[lrow:lrow + 1, :], [pko], ["D_rl"])
                                    kc.copy("act", osb[r0:r0 + 64, :], pbo[r0:r0 + 64, :], [pko], ["D_osb"])
                                    br_, pbr, pkr = kc.bank(n=1, base=5)
                                    kc.mm(pbr[:, :], ones_f[lrow:lrow + 1, 0:128], rl[lrow:lrow + 1, :], True, True,
                                          ["ones_f", "D_rl"], [pkr])
                                    kc.tt("dve", ycT3[r0:r0 + 64, c, q0:q0 + 512], osb[r0:r0 + 64, :], pbr[r0:r0 + 64, :],
                                          ALU.mult, ["D_osb", pkr], ["D_ycT"])

                            pend.append(pv)
                            flush(1)
                    if DSTOP == 3 + h:
                        flush(0)
                        kc.dump("ycT", ycT, ["D_ycT"], BF16)
                        raise _Stop()
                flush(0)
                kc.store(ybT[1024:1536, t0:t0 + S].rearrange("(c p) t -> p c t", p=128), ycT3, "D_ycT", writes=["ybT"])
              except _Stop:
                pass
            kc.reset()

        if "E" in stages:
            P.cur_tag = "L%dE" % l
            wbr = kc.alloc(12 * 1024, BF16)
            kc.ld(wbr, wbrb[l], "E_wbr", reads=["CAST1" if l == layers[0] else "CAST1b"])
            wbr4 = wbr.rearrange("p (n k d) -> p n k d", n=3, k=4)
            wo = kc.alloc(8 * 1024, BF16)
            kc.ld(wo, woutb[l], "E_wo", reads=["CAST1" if l == layers[0] else "CAST1b"])
            wo3 = r3(wo, 8)
            ybb = [kc.alloc(12 * 512, BF16) for _ in range(2)]
            gtb = [kc.alloc(24 * 512, BF16) for _ in range(2)]
            xtb = [kc.alloc(8 * 512) for _ in range(2)]
            mg = kc.alloc(8 * 512, BF16)
            mg3 = r3(mg, 8)
            ta = kc.alloc(512)
            tb = kc.alloc(512)
            tc_ = kc.alloc(512)
            for tt in range(NT):
                i2 = tt % 2
                tsl = slice(tt * 512, (tt + 1) * 512)
                yb3 = r3(ybb[i2], 12)
                gt3 = r3(gtb[i2], 24)
                xt3 = r3(xtb[i2], 8)
                ybk, gtk, xk = "E_yb%d" % i2, "E_gt%d" % i2, "E_xt%d" % i2
                kc.ld(yb3, ybT[:, tsl].rearrange("(c p) t -> p c t", p=128), ybk)
                kc.ld(gt3, gatesT[:, tsl].rearrange("(c p) t -> p c t", p=128), gtk)
                kc.ld(xt3, xsrc[:, tsl].rearrange("(k p) t -> p k t", p=128), xk)
                for m in range(8):
                    pbs_ = []
                    for n in range(3):
                        bi, pb, pk = kc.bank()
                        for k in range(4):
                            kc.mm(pb[:, :], wbr4[:, n, k, m * 128:(m + 1) * 128], yb3[:, n * 4 + k, :], k == 0, k == 3,
                                  ["E_wbr", ybk], [pk])
                        pbs_.append((pb, pk))
                    kc.tt("dve", ta, pbs_[0][0][:, :], gt3[:, m, :], ALU.mult, [pbs_[0][1], gtk], ["E_ta"])
                    kc.tt("dve", tb, pbs_[1][0][:, :], gt3[:, 8 + m, :], ALU.mult, [pbs_[1][1], gtk], ["E_tb"])
                    kc.tt("dve", tc_, pbs_[2][0][:, :], gt3[:, 16 + m, :], ALU.mult, [pbs_[2][1], gtk], ["E_tc"])
                    kc.tt("pool", ta, ta, tb, ALU.add, ["E_ta", "E_tb"], ["E_ta"])
                    kc.tt("pool", mg3[:, m, :], ta, tc_, ALU.add, ["E_ta", "E_tc"], [("E_mg", m)])
                for m in range(8):
                    bi, pb, pk = kc.bank()
                    for k in range(8):
                        kc.mm(pb[:, :], wo3[:, k, m * 128:(m + 1) * 128], mg3[:, k, :], k == 0, k == 7,
                              ["E_wo", ("E_mg", k)], [pk])
                    kc.tt("dve", xt3[:, m, :], xt3[:, m, :], pb[:, :], ALU.add, [xk, pk], [xk])
                kc.store(x1s[:, tsl].rearrange("(k p) t -> p k t", p=128), xt3, xk, writes=["x1s"])
            kc.reset()

        if "F" in stages:
            P.cur_tag = "L%dF" % l
            wdn = kc.alloc(22 * 1024, BF16)
            kc.ld(wdn, wdnb[l], "F_wdn", reads=["CAST1" if l == layers[0] else "CAST1b"])
            wdn3 = r3(wdn, 22)
            pgw = kc.alloc(8 * 1024, BF16)
            kc.ld(pgw, pgb[l], "F_pgw", reads=["CAST1" if l == layers[0] else "CAST1b"])
            pgw3 = r3(pgw, 8)
            ppw = kc.alloc(2 * 1024, BF16)
            kc.ld(ppw, ppb[l], "F_ppw", reads=["CAST1" if l == layers[0] else "CAST1b"])
            ppw3 = r3(ppw, 2)
            hal = kc.alloc(44 * 2)
            xtb = [kc.alloc(8 * 512) for _ in range(2)]
            h2 = kc.alloc(8 * 512, BF16)
            h23 = r3(h2, 8)
            sqt = kc.alloc(8 * 512, BF16)
            rstd = kc.alloc(512)
            wub = [kc.alloc(8 * 256, BF16) for _ in range(2)]
            rawb = [kc.alloc(514) for _ in range(2)]
            yb_ = [kc.alloc(512) for _ in range(2)]
            gTt = kc.alloc(22 * 512, BF16)
            gT3 = r3(gTt, 22)
            ptb_ = [kc.alloc(2 * 512, BF16) for _ in range(2)]
            sg = kc.alloc(512)
            tp = kc.alloc(512)
            last = (l == layers[-1]) and final
            if last:
                ot = kc.alloc(8 * 512)
                ot3 = r3(ot, 8)
            iw = 0
            for tt in range(NT):
                i2 = tt % 2
                tsl = slice(tt * 512, (tt + 1) * 512)
                xt = xtb[i2]
                xt3 = r3(xt, 8)
                xk = "F_xt%d" % i2
                kc.ld(xt3, x1s[:, tsl].rearrange("(k p) t -> p k t", p=128), xk)
                pt_ = ptb_[i2]
                ptk = "F_pt%d" % i2
                kc.ld(r3(pt_, 2), pT_in[l, :, tsl].rearrange("(k p) t -> p k t", p=128), ptk, q="pool")
                if tt % 4 == 0:
                    kc.memset("pool", hal, 0.0, ["F_hal"])
                rmsnorm_tile(xt, xk, V_NFFN + l * 8, lambda k: h23[:, k, :], [("F_h", k) for k in range(8)], sqt, "F_sq",
                             rstd, "F_rstd")
                HK = [("F_h", k) for k in range(8)]
                for j in range(22):
                    wu = wub[iw % 2]
                    wuk = "F_wu%d" % (iw % 2)
                    iw += 1
                    kc.ld(wu, wupb[l, j], wuk, reads=["CAST1" if l == layers[0] else "CAST1b"])
                    wu3 = r3(wu, 8)
                    for half in range(2):
                        bi, pb, pk = kc.bank()
                        for k in range(8):
                            kc.mm(pb[:, :], wu3[:, k, half * 128:(half + 1) * 128], h23[:, k, :], k == 0, k == 7,
                                  [wuk, ("F_h", k)], [pk])
                        rb = rawb[half]
                        rk = "F_raw%d" % half
                        yb = yb_[half]
                        yk = "F_y%d" % half
                        hi = (j * 2 + half) * 2
                        kc.copy("act", rb[:, 2:514], pb[:, :], [pk], [rk])
                        kc.copy("pool", rb[:, 0:2], hal[:, hi:hi + 2], ["F_hal"], [rk + "h"])
                        kc.copy("pool", hal[:, hi:hi + 2], rb[:, 512:514], [rk], ["F_hal"])
                        cwi = V_FCW + (l * 44 + half * 22 + j) * 3
                        e1 = "dve"
                        kc.ts(e1, yb, rb[:, 0:512], vcol(cwi), None, ALU.mult, None, [rk, rk + "h", "vecs"], [yk])
                        kc.sto(e1, yb, rb[:, 1:513], vcol(cwi + 1), yb, ALU.mult, ALU.add, [rk, rk + "h", yk, "vecs"], [yk])
                        kc.sto("dve", yb, rb[:, 2:514], vcol(cwi + 2), yb, ALU.mult, ALU.add, [rk, yk, "vecs"], [yk])
                    kc.act(yb_[0], yb_[0], AF.Gelu_apprx_tanh, ["F_y0"], ["F_y0"])
                    kc.tt("dve", gT3[:, j, :], yb_[0], yb_[1], ALU.mult, ["F_y0", "F_y1"], [("F_g", j)])
                for m in range(8):
                    bi, pb, pk = kc.bank()
                    for j in range(22):
                        kc.mm(pb[:, :], wdn3[:, j, m * 128:(m + 1) * 128], gT3[:, j, :], j == 0, j == 21,
                              ["F_wdn", ("F_g", j)], [pk])
                    kc.tt("dve", xt3[:, m, :], xt3[:, m, :], pb[:, :], ALU.add, [xk, pk], [xk])
                rmsnorm_tile(xt, xk, V_NPLE + l * 8, lambda k: h23[:, k, :], HK, sqt, "F_sq", rstd, "F_rstd")
                for m in range(8):
                    bi, pb, pk = kc.bank()
                    for k in range(8):
                        kc.mm(pb[:, :], pgw3[:, k, m * 128:(m + 1) * 128], h23[:, k, :], k == 0, k == 7,
                              ["F_pgw", ("F_h", k)], [pk])
                    kc.act(sg, pb[:, :], AF.Sigmoid, [pk], ["F_sg"])
                    bi, pb, pk = kc.bank()
                    for k in range(2):
                        kc.mm(pb[:, :], ppw3[:, k, m * 128:(m + 1) * 128], r3(pt_, 2)[:, k, :], k == 0, k == 1,
                              ["F_ppw", ptk], [pk])
                    kc.tt("dve", tp, pb[:, :], sg, ALU.mult, [pk, "F_sg"], ["F_tp"])
                    kc.tt("pool", xt3[:, m, :], xt3[:, m, :], tp, ALU.add, [xk, "F_tp"], [xk])
                if last:
                    rmsnorm_tile(xt, xk, V_NFIN, lambda k: ot3[:, k, :], [("F_ot", k) for k in range(8)], sqt, "F_sq", rstd, "F_rstd")
                    kc.finals.append(kc.store(outT[:, tsl].rearrange("(k p) t -> p k t", p=128), ot3, [("F_ot", k) for k in range(8)],
                                              writes=["outT"], semkey="F_ot_st"))
                else:
                    kc.store(xs[:, tsl].rearrange("(k p) t -> p k t", p=128), xt3, xk, writes=["xs"])
            kc.reset()
    nsem = P.emit(final_wait_ops=kc.finals)
    return nc, kc, nsem


def _t5_bucket_np(n):
    n = np.maximum(n, 0)
    exact = 16
    nf = np.maximum(n, 1).astype(np.float32)
    large = exact + (np.log(nf / exact) / math.log(128 / exact) * (32 - exact)).astype(np.int32)
    large = np.minimum(large, 31)
    return np.where(n < exact, n, large)


def host_consts():
    ident = np.eye(128, dtype=np.float32)
    p = np.arange(64)[:, None]
    f = np.arange(64)[None, :]
    U = (p <= f).astype(np.float32)
    mA = np.where(f >= p, 0.0, -BIG).astype(np.float32)
    mB = np.where(p > f, 0.0, -BIG).astype(np.float32)
    SU = (f > p).astype(np.float32)
    tri = np.concatenate([U, mA, mB, SU], axis=1)
    rc = np.zeros((128, 64), np.float32)
    for g in range(4):
        w = 2 << g
        for t in range(16):
            rc[:, g * 16 + t] = 1.0 / min(t + 1, w)
    ind = np.zeros((64, 64, 128), np.float32)
    for r in range(64):
        ind[r, r, :] = 1.0
    ind = ind.reshape(64, 64 * 128)
    return ident, tri, rc, np.concatenate([ind, ind], axis=0)


def host_vecs(inp):
    v = np.zeros((128, NVEC), np.float32)

    def put(base, arr):
        a = np.asarray(arr, np.float32)
        lead = int(np.prod(a.shape[:-1])) if a.ndim > 1 else 1
        a = a.reshape(lead, -1, 128)
        a = a.transpose(2, 0, 1).reshape(128, -1)
        v[:, base:base + a.shape[1]] = a

    put(V_NMIX, inp["norm_mix"])
    put(V_NFFN, inp["norm_ffn"])
    put(V_NPLE, inp["norm_ple"])
    put(V_NFIN, inp["norm_final"])
    put(V_PSCALE, inp["pool_scale"])
    cw = np.asarray(inp["dn_conv"], np.float32).reshape(2, 4, 12, 128).transpose(3, 0, 2, 1).reshape(128, 96)
    v[:, V_CW:V_CW + 96] = cw
    fc = np.asarray(inp["ffn_conv"], np.float32).reshape(2, 3, 44, 128).transpose(3, 0, 2, 1).reshape(128, 264)
    v[:, V_FCW:V_FCW + 264] = fc
    v[:, V_DNW:V_DNW + 2] = np.asarray(inp["dn_norm"], np.float32).T
    v[:, V_EPS] = EPS
    v[:, V_ONE] = 1.0
    return v


def host_band(rel_bias):
    rb = np.asarray(rel_bias, np.float32)
    p = np.arange(128)[:, None]
    c = np.arange(1152)[None, :]
    n = c - 384 - p
    bucket = _t5_bucket_np(n)
    band = np.empty((128, 8, 1152), np.float32)
    for h in range(8):
        band[:, h, :] = np.where(n >= 0, rb[bucket, h], -BIG)
    return band.reshape(128, 8 * 1152)


_CACHE = {}


def get_program(NSEQ=2, **kw):
    key = (NSEQ, tuple(sorted(kw.items())))
    if key not in _CACHE:
        _CACHE[key] = build(NSEQ=NSEQ, **kw)
    return _CACHE[key]


def make_in_maps(inp, NSEQ, ncores):
    ident, tri, rc, ind = host_consts()
    vecs = host_vecs(inp)
    band = host_band(inp["rel_bias"])
    a4 = np.stack([np.asarray(inp["dn_a_log"], np.float32)[0], np.asarray(inp["dn_dt_bias"], np.float32)[0],
                   np.asarray(inp["dn_a_log"], np.float32)[1], np.asarray(inp["dn_dt_bias"], np.float32)[1]], axis=1)
    x = np.asarray(inp["x"], np.float32)
    p = np.asarray(inp["p"], np.float32)
    shared = {
        "w_in": np.ascontiguousarray(inp["w_in"], np.float32),
        "w_branch": np.ascontiguousarray(inp["w_branch"], np.float32),
        "w_out": np.ascontiguousarray(inp["w_out"], np.float32),
        "ffn_up": np.ascontiguousarray(inp["ffn_up"], np.float32),
        "ffn_down": np.ascontiguousarray(inp["ffn_down"], np.float32),
        "ple_gate": np.ascontiguousarray(inp["ple_gate"], np.float32),
        "ple_proj": np.ascontiguousarray(inp["ple_proj"], np.float32),
        "pool_w": np.ascontiguousarray(inp["pool_w"], np.float32),
        "vecs": vecs, "ident": ident, "tri": tri, "rcnt": rc, "band": band, "ind": ind,
        "a4": np.ascontiguousarray(a4),
    }
    maps = []
    for c in range(ncores):
        xs_ = x[c * NSEQ:(c + 1) * NSEQ].reshape(NSEQ * S, D)
        ps_ = p[:, c * NSEQ:(c + 1) * NSEQ].reshape(DEPTH, NSEQ * S, 256)
        m = dict(shared)
        m["xT"] = np.ascontiguousarray(xs_.T)
        m["pT"] = np.ascontiguousarray(ps_.transpose(0, 2, 1))
        maps.append(m)
    return maps


def kernel(**inputs):
    NSEQ = 2
    nc, kc, _ = get_program(NSEQ=NSEQ)
    maps = make_in_maps(inputs, NSEQ, NCORES)
    res = run_bass_kernel_spmd(nc, maps, core_ids=list(range(NCORES)))
    outs = []
    for c in range(NCORES):
        oT = np.asarray(res.results[c]["outT"], np.float32)
        outs.append(oT.T.reshape(NSEQ, S, D))
    return np.concatenate(outs, axis=0).astype(np.float32)
```

```python
import contextlib
import math
import numpy as np
import concourse.bass as bass
import concourse.mybir as mybir
from concourse.bass_utils import run_bass_kernel_spmd

F32 = mybir.dt.float32
BF16 = mybir.dt.bfloat16
AF = mybir.ActivationFunctionType
ALU = mybir.AluOpType
AX = mybir.AxisListType

D = 1024
S = 2048
DEPTH = 2
IN_COLS = 7176
FFN = 2816
EPS = 1e-6
NCORES = 8
BIG = 30000.0


class Op:
    __slots__ = ("eng", "fn", "deps", "sem", "inc", "val", "signal", "is_dma", "tag")

    def __init__(self, eng, fn, is_dma=False):
        self.eng = eng
        self.fn = fn
        self.deps = []
        self.sem = None
        self.inc = 1
        self.val = None
        self.signal = False
        self.is_dma = is_dma


class Prog:
    ENGS = ("pe", "act", "dve", "pool", "sp")

    def __init__(self, nc):
        self.nc = nc
        self.ops = {e: [] for e in self.ENGS}
        self.last_w = {}
        self.readers = {}
        self.n_ops = 0
        self.last_dma = {}
        self.bar = {e: None for e in self.ENGS}
        self.cur_tag = "pre"
        self.scopes = False

    def barrier(self):
        deps = []
        for e in self.ENGS:
            for o in reversed(self.ops[e]):
                if not o.is_dma:
                    deps.append(o)
                    break
        deps.extend(self.last_dma.values())
        for e in self.ENGS:
            self.bar[e] = deps
        self.last_w = {}
        self.readers = {}

    def _add(self, op, reads, writes):
        deps = []
        for k in reads:
            w = self.last_w.get(k)
            if w is not None:
                deps.append((w, False))
        for k in writes:
            w = self.last_w.get(k)
            if w is not None:
                deps.append((w, False))
            for r in self.readers.get(k, ()):
                deps.append((r, True))
        if self.bar[op.eng] is not None:
            for d in self.bar[op.eng]:
                deps.append((d, False))
            self.bar[op.eng] = None
        seen = set()
        for d, war in deps:
            if d is op or id(d) in seen:
                continue
            if d.eng == op.eng and not d.is_dma and not op.is_dma:
                if op.eng == "pe" or war:
                    continue
            seen.add(id(d))
            op.deps.append(d)
            d.signal = True
        for k in reads:
            self.readers.setdefault(k, []).append(op)
        for k in writes:
            self.last_w[k] = op
            self.readers[k] = []
        op.tag = self.cur_tag
        self.ops[op.eng].append(op)
        self.n_ops += 1

    @staticmethod
    def _is_psum(k):
        return isinstance(k, str) and k.startswith("ps") and k[2:].isdigit()

    def op(self, eng, fn, reads=(), writes=()):
        o = Op(eng, fn)
        o.sem = ("eng", eng)
        ex = [k for k in reads if self._is_psum(k)]
        if ex:
            reads = [k for k in reads if not self._is_psum(k)]
            writes = list(writes) + ex
        self._add(o, reads, writes)
        return o

    def dma(self, fn, semkey, reads=(), writes=(), q="sp"):
        o = Op(q, fn, is_dma=True)
        o.sem = ("dma", semkey)
        o.inc = 16
        o.signal = True
        self._add(o, reads, writes)
        self.last_dma[semkey] = o
        return o

    def emit(self, final_wait_ops=()):
        nc = self.nc
        counts = {}
        for e in self.ENGS:
            for o in self.ops[e]:
                if o.signal:
                    c = counts.get(o.sem, 0) + o.inc
                    counts[o.sem] = c
                    o.val = c
        semkeys = list(counts.keys())
        with contextlib.ExitStack() as st:
            sems = {}
            for i, k in enumerate(semkeys):
                sems[k] = st.enter_context(nc.semaphore("s%d" % i))
            block = st.enter_context(nc.Block())
            engmap = {"pe": block.tensor, "act": block.scalar, "dve": block.vector,
                      "pool": block.gpsimd, "sp": block.sync}

            def make(e):
                oplist = self.ops[e]

                def body(eng):
                    waited = {}
                    cur = None
                    cm = None
                    for o in oplist:
                        if self.scopes and o.tag != cur:
                            if cm is not None:
                                cm.__exit__(None, None, None)
                            cm = nc.named_scope(o.tag)
                            cm.__enter__()
                            cur = o.tag
                        for d in o.deps:
                            if waited.get(d.sem, 0) >= d.val:
                                continue
                            eng.wait_ge(sems[d.sem], d.val)
                            waited[d.sem] = d.val
                        ins = o.fn(eng)
                        if o.signal:
                            ins.then_inc(sems[o.sem], o.inc)
                    if cm is not None:
                        cm.__exit__(None, None, None)
                    if e == "sp":
                        for o in final_wait_ops:
                            if waited.get(o.sem, 0) >= o.val:
                                continue
                            eng.wait_ge(sems[o.sem], o.val)
                            waited[o.sem] = o.val

                return body

            for e in self.ENGS:
                if self.ops[e] or e == "sp":
                    engmap[e](make(e))
        return len(semkeys)


ARENA_F32 = 46 * 1024


import os as _os
CSTOP = int(_os.environ.get("C_STOP", "99"))
DSTOP = int(_os.environ.get("D_STOP", "99"))


class _Stop(Exception):
    pass


class KC:
    def __init__(self, nc, NSEQ, debug):
        self.nc = nc
        self.P = Prog(nc)
        self.NSEQ = NSEQ
        self.T = NSEQ * S
        self.NT = self.T // 512
        self.debug = debug
        self.st = contextlib.ExitStack()
        self.arena = self.st.enter_context(nc.sbuf_tensor("arena", [128, ARENA_F32], F32))
        self.arena_bf = self.arena[:, :].bitcast(BF16)
        self.psb = [self.st.enter_context(nc.psum_tensor("psb%d" % i, [128, 512], F32)) for i in range(8)]
        self.bump = 0
        self.perm = 0
        self.rr = 0
        self.finals = []
        self.dram = {}
        self.uid = 0

    def alloc(self, cols, dt=F32):
        if dt == BF16:
            w = (cols + 1) // 2
            a = self.bump
            self.bump += w
            assert self.bump <= ARENA_F32, "SBUF arena overflow %d" % self.bump
            return self.arena_bf[:, 2 * a:2 * a + cols]
        a = self.bump
        self.bump += cols
        assert self.bump <= ARENA_F32, "SBUF arena overflow %d" % self.bump
        return self.arena[:, a:a + cols]

    def make_perm(self):
        self.perm = self.bump

    def reset(self):
        self.P.barrier()
        self.bump = self.perm

    def key(self, base):
        self.uid += 1
        return "%s#%d" % (base, self.uid)

    def din(self, name, shape, dt=F32):
        t = self.nc.dram_tensor(name, list(shape), dt, kind="ExternalInput").ap()
        self.dram[name] = t
        return t

    def dscr(self, name, shape, dt=F32, out=False):
        kind = "ExternalOutput" if (out or self.debug) else "Internal"
        t = self.nc.dram_tensor(name, list(shape), dt, kind=kind).ap()
        self.dram[name] = t
        return t

    def ld(self, dst, src, key, reads=(), q="sp", semkey=None, slow=False):
        if slow:
            return self.P.dma(lambda e: e.dma_start(out=dst, in_=src, allow_slow_non_contiguous=True), semkey or key,
                              reads=reads, writes=[key], q=q)
        return self.P.dma(lambda e: e.dma_start(out=dst, in_=src), semkey or key, reads=reads, writes=[key], q=q)

    def stt(self, dst, src, srckey, writes=(), q="sp", semkey=None):
        keys = list(srckey) if isinstance(srckey, list) else [srckey]
        sk = semkey or (str(keys[0]) + "_st")
        return self.P.dma(lambda e: e.dma_start(out=dst, in_=src), sk, reads=keys, writes=writes, q=q)

    def dump(self, name, ap, keys, dt=F32):
        if not self.debug:
            return
        shape = list(ap.shape)
        d = self.nc.dram_tensor("dbg_" + name, shape, dt, kind="ExternalOutput").ap()
        self.finals.append(self.P.dma(lambda e: e.dma_start(out=d, in_=ap), "dump_" + name, reads=list(keys)))

    def store(self, dst, src, srckey, writes=(), q="sp", semkey=None):
        return self.stt(dst, src, srckey, writes, q, semkey)

    def act(self, out, in_, func, reads, writes, bias=None, scale=None, accum=None):
        kw = {}
        if bias is not None:
            kw["bias"] = bias
        if scale is not None:
            kw["scale"] = scale
        if accum is not None:
            kw["accum_out"] = accum
        return self.P.op("act", lambda e: e.activation(out=out, in_=in_, func=func, **kw), reads=reads, writes=writes)

    def tt(self, eng, out, in0, in1, op, reads, writes):
        return self.P.op(eng, lambda e: e.tensor_tensor(out, in0, in1, op), reads=reads, writes=writes)

    def ts(self, eng, out, in0, s1, s2, op0, op1, reads, writes):
        if s2 is None:
            return self.P.op(eng, lambda e: e.tensor_scalar(out, in0, s1, None, op0), reads=reads, writes=writes)
        return self.P.op(eng, lambda e: e.tensor_scalar(out, in0, s1, s2, op0, op1), reads=reads, writes=writes)

    def sto(self, eng, out, in0, scalar, in1, op0, op1, reads, writes):
        return self.P.op(eng, lambda e: e.scalar_tensor_tensor(out=out, in0=in0, scalar=scalar, in1=in1, op0=op0,
                                                               op1=op1), reads=reads, writes=writes)

    def copy(self, eng, out, in_, reads, writes):
        if eng == "act":
            return self.P.op("act", lambda e: e.copy(out, in_), reads=reads, writes=writes)
        return self.P.op(eng, lambda e: e.tensor_copy(out, in_), reads=reads, writes=writes)

    def memset(self, eng, ap, val, writes):
        return self.P.op(eng, lambda e: e.memset(ap, val), writes=writes)

    def recip(self, out, in_, reads, writes):
        return self.P.op("dve", lambda e: e.reciprocal(out, in_), reads=reads, writes=writes)

    def transpose(self, out, in_, ident, reads, writes):
        return self.P.op("pe", lambda e: e.transpose(out, in_, ident), reads=reads, writes=writes)

    def mm(self, out, lhsT, rhs, start, stop, reads, writes):
        return self.P.op("pe", lambda e: e.matmul(out, lhsT, rhs, start=start, stop=stop), reads=reads, writes=writes)

    def bank(self, n=8, base=0):
        i = base + (self.rr % n)
        self.rr += 1
        return i, self.psb[i], "ps%d" % i


def r3(ap, a):
    return ap.rearrange("p (a b) -> p a b", a=a)


V_NMIX = 0
V_NFFN = 16
V_NPLE = 32
V_NFIN = 48
V_PSCALE = 56
V_CW = 64
V_FCW = 160
V_DNW = 424
V_EPS = 426
V_ONE = 427
NVEC = 428
WIN_GROUP_C0 = [0, 512, 1024, 1536, 2048, 2568, 3080, 3592] + [4104 + 512 * i for i in range(6)]
WIN_GROUP_DST = [("u", None, 0), ("dqkv", None, 0), ("dqkv", None, 512), ("dqkv", None, 1024), ("dz", None, 0),
                 ("mq", None, 0), ("mk", None, 0), ("mv", None, 0)] + [("gate", None, 512 * i) for i in range(6)]

C_POOL = 0
C_DQKV = 512
C_DZ = 2048
C_DB = 2560
C_DA = 2564
C_MQ = 2568
C_MK = 2568 + 512
C_MV = 2568 + 1024
C_GATE = 4104


def build(NSEQ=2, debug=False, layers=(0, 1), stages="ABCDEF", final=True, scopes=False):
    nc = bass.Bass("TRN2", target_bir_lowering=False)
    kc = KC(nc, NSEQ, debug)
    P = kc.P
    P.scopes = scopes
    T = kc.T
    NT = kc.NT
    xT_in = kc.din("xT", [D, T])
    pT_in = kc.din("pT", [DEPTH, 256, T])
    w_in = kc.din("w_in", [DEPTH, D, IN_COLS])
    w_branch = kc.din("w_branch", [DEPTH, 3, 512, D])
    w_out = kc.din("w_out", [DEPTH, D, D])
    ffn_up = kc.din("ffn_up", [DEPTH, D, 2 * FFN])
    ffn_down = kc.din("ffn_down", [DEPTH, FFN, D])
    ple_gate = kc.din("ple_gate", [DEPTH, D, D])
    ple_proj = kc.din("ple_proj", [DEPTH, 256, D])
    pool_w = kc.din("pool_w", [DEPTH, 4, 128, 128])
    vecs_d = kc.din("vecs", [128, NVEC])
    ident_d = kc.din("ident", [128, 128])
    tri_d = kc.din("tri", [64, 4 * 64])
    rcnt_d = kc.din("rcnt", [128, 64])
    band_d = kc.din("band", [128, 8 * 1152])
    ind_d = kc.din("ind", [128, 64 * 128])
    a4_d = kc.din("a4", [4, 4])
    outT = kc.dscr("outT", [D, T], out=True)
    xs = kc.dscr("xs", [D, T])
    x1s = kc.dscr("x1s", [D, T])
    uT = kc.dscr("uT", [512, T])
    dqkvT = kc.dscr("dqkvT", [1536, T])
    dzT = kc.dscr("dzT", [512, T])
    dbg = kc.dscr("dbg", [8, T])
    mqT = kc.dscr("mqT", [512, T], BF16)
    mkT = kc.dscr("mkT", [512, T], BF16)
    mv = kc.dscr("mv", [T, 512], BF16)
    gatesT = kc.dscr("gatesT", [3072, T], BF16)
    ybT = kc.dscr("ybT", [1536, T], BF16)
    NG_IN = 15
    winb = kc.dscr("winb", [DEPTH, 15, 128, 8 * 512], BF16)
    wbrb = kc.dscr("wbrb", [DEPTH, 128, 12 * 1024], BF16)
    woutb = kc.dscr("woutb", [DEPTH, 128, 8 * 1024], BF16)
    wupb = kc.dscr("wupb", [DEPTH, 22, 128, 8 * 256], BF16)
    wdnb = kc.dscr("wdnb", [DEPTH, 128, 22 * 1024], BF16)
    pgb = kc.dscr("pgb", [DEPTH, 128, 8 * 1024], BF16)
    ppb = kc.dscr("ppb", [DEPTH, 128, 2 * 1024], BF16)
    pwb = kc.dscr("pwb", [DEPTH, 128, 4 * 128], BF16)

    vecs = kc.alloc(NVEC)
    ident = kc.alloc(128)
    ones_bf = kc.alloc(128, BF16)
    ones_f = kc.alloc(128)
    kc.ld(vecs, vecs_d, "vecs")
    kc.ld(ident, ident_d, "ident")
    P.op("pool", lambda e: e.memset(ones_bf, 1.0), writes=["ones_bf"])
    P.op("pool", lambda e: e.memset(ones_f, 1.0), writes=["ones_f"])
    kc.make_perm()
    CONST = ["vecs", "ident", "ones_bf", "ones_f"]

    def cast(dst, src, key):
        grp = ("CAST0" if key[0] == "winb" else "CAST1") + ("" if key[1] == layers[0] else "b")
        P.dma(lambda e: e.dma_start(out=dst, in_=src), grp, writes=[grp], q="pool")

    for l in layers:
        for g in range(14):
            c0 = WIN_GROUP_C0[g]
            cast(winb[l, g].rearrange("p (k c) -> p k c", k=8),
                 w_in[l, :, c0:c0 + 512].rearrange("(k p) c -> p k c", p=128), ("winb", l, g))
        cast(winb[l, 14].rearrange("p (k c) -> p k c", k=8)[:, :, 0:8],
             w_in[l, :, C_DB:C_DB + 8].rearrange("(k p) c -> p k c", p=128), ("winb", l, 14))
        cast(pwb[l].rearrange("p (g d) -> p g d", g=4), pool_w[l].rearrange("g c d -> c g d"), ("pwb", l))
        for n in range(3):
            cast(wbrb[l].rearrange("p (n k d) -> p n k d", n=3, k=4)[:, n],
                 w_branch[l, n].rearrange("(k p) d -> p k d", p=128), ("wbrb", l, n))
        cast(woutb[l].rearrange("p (k d) -> p k d", k=8), w_out[l].rearrange("(k p) d -> p k d", p=128), ("woutb", l))
        for j in range(22):
            dstv = wupb[l, j].rearrange("p (k c) -> p k c", k=8)
            cast(dstv[:, :, 0:128], ffn_up[l, :, j * 128:(j + 1) * 128].rearrange("(k p) c -> p k c", p=128),
                 ("wupb", l, j, 0))
            cast(dstv[:, :, 128:256],
                 ffn_up[l, :, FFN + j * 128:FFN + (j + 1) * 128].rearrange("(k p) c -> p k c", p=128),
                 ("wupb", l, j, 1))
        cast(wdnb[l].rearrange("p (j d) -> p j d", j=22), ffn_down[l].rearrange("(j p) d -> p j d", p=128),
             ("wdnb", l))
        cast(pgb[l].rearrange("p (k d) -> p k d", k=8), ple_gate[l].rearrange("(k p) d -> p k d", p=128), ("pgb", l))
        cast(ppb[l].rearrange("p (k d) -> p k d", k=2), ple_proj[l].rearrange("(k p) d -> p k d", p=128), ("ppb", l))

    def vcol(i):
        return vecs[:, i:i + 1]

    def rmsnorm_tile(xt, xkey, nbase, out_fn, outkeys, sqt, sqkey, rstd, rkey, engs=("dve",)):
        P.op("act", lambda e: e.activation(out=sqt, in_=xt, func=AF.Square), reads=[xkey], writes=[sqkey])
        bi, pb, pk = kc.bank()
        for k in range(8):
            kc.mm(pb[:, :], ones_bf, sqt[:, k * 512:(k + 1) * 512], k == 0, k == 7, [sqkey, "ones_bf"], [pk])
        P.op("act", lambda e: e.activation(out=rstd, in_=pb[:, :], func=AF.Ln, bias=vcol(V_EPS), scale=1.0 / D),
             reads=[pk, "vecs"], writes=[rkey])
        P.op("act", lambda e: e.activation(out=rstd, in_=rstd, func=AF.Exp, scale=-0.5), reads=[rkey], writes=[rkey])
        for k in range(8):
            o = out_fn(k)
            eng = engs[k % len(engs)]
            P.op(eng, (lambda o=o, k=k: lambda e: e.scalar_tensor_tensor(
                out=o, in0=xt[:, k * 512:(k + 1) * 512], scalar=vcol(nbase + k), in1=rstd,
                op0=ALU.mult, op1=ALU.mult))(), reads=[xkey, rkey, "vecs"], writes=[outkeys[k]])

    stages0 = stages
    for l in layers:
        stages = stages0 if l == layers[0] else _os.environ.get("L1S", stages0)
        xsrc = xT_in if l == 0 else xs
        xsrc_key = "xs"
        if "A" in stages:
            P.cur_tag = "L%dA" % l
            hT = kc.alloc(8 * T, BF16)
            hT3 = r3(hT, 8)
            xtb = [kc.alloc(8 * 512) for _ in range(2)]
            sqt = kc.alloc(8 * 512, BF16)
            rstd = kc.alloc(512)
            for tt in range(NT):
                xt = xtb[tt % 2]
                xk = "A_xt%d" % (tt % 2)
                kc.ld(r3(xt, 8), xsrc[:, tt * 512:(tt + 1) * 512].rearrange("(k p) t -> p k t", p=128), xk,
                      reads=[xsrc_key])
                rmsnorm_tile(xt, xk, V_NMIX + l * 8, lambda k: hT3[:, k, tt * 512:(tt + 1) * 512],
                             [("hT", tt, k) for k in range(8)], sqt, "A_sq", rstd, "A_rstd")
            HT_ALL = [("hT", tt) for tt in range(NT)]
            wgb = [kc.alloc(8 * 512, BF16) for _ in range(2)]
            ob32 = [kc.alloc(T) for _ in range(2)]
            ob16 = [kc.alloc(T, BF16) for _ in range(2)]
            ovb = [kc.alloc(512, BF16) for _ in range(2)]
            cnt = {"o32": 0, "o16": 0, "ov": 0, "ev": 0}

            def evac(kind, dst, src, pk, okey):
                if kind in ("u", "dqkv"):
                    eng = "act" if cnt["ev"] % 2 == 0 else "dve"
                    cnt["ev"] += 1
                    if eng == "act":
                        P.op("act", lambda e: e.copy(dst, src), reads=[pk], writes=[okey])
                    else:
                        P.op("dve", lambda e: e.tensor_copy(dst, src), reads=[pk], writes=[okey])
                elif kind == "dz":
                    P.op("act", lambda e: e.activation(out=dst, in_=src, func=AF.Silu), reads=[pk], writes=[okey])
                elif kind == "mq":
                    P.op("dve", lambda e: e.tensor_scalar(dst, src, 0.125, None, ALU.mult), reads=[pk], writes=[okey])
                elif kind == "mk":
                    P.op("dve", lambda e: e.tensor_copy(dst, src), reads=[pk], writes=[okey])
                elif kind == "gate":
                    P.op("act", lambda e: e.activation(out=dst, in_=src, func=AF.Sigmoid), reads=[pk], writes=[okey])
                else:
                    raise ValueError(kind)

            for g in range(14):
                wg = wgb[g % 2]
                wk = "A_wg%d" % (g % 2)
                kc.ld(wg, winb[l, g], wk, reads=["CAST0b"] if l != layers[0] else ["CAST0"])
                wg3 = r3(wg, 8)
                gkind, gdst, grow0 = WIN_GROUP_DST[g]
                if gkind == "mv":
                    for i in range(T // 128):
                        bi, pb, pk = kc.bank()
                        for k in range(8):
                            kc.mm(pb[:, :], hT3[:, k, i * 128:(i + 1) * 128], wg3[:, k, :], k == 0, k == 7,
                                  [("hT", i // 4, k), wk], [pk])
                        ov = ovb[cnt["ov"] % 2]
                        ok = "A_ov%d" % (cnt["ov"] % 2)
                        cnt["ov"] += 1
                        eng = "act" if i % 2 == 0 else "dve"
                        if eng == "act":
                            P.op("act", (lambda ov=ov, pb=pb: lambda e: e.copy(ov, pb[:, :]))(), reads=[pk], writes=[ok])
                        else:
                            P.op("dve", (lambda ov=ov, pb=pb: lambda e: e.tensor_copy(ov, pb[:, :]))(), reads=[pk],
                                 writes=[ok])
                        kc.stt(mv[i * 128:(i + 1) * 128, :], ov, ok, writes=["mv"])
                    continue
                for j in range(4):
                    is16 = gkind in ("mq", "mk", "gate")
                    if is16:
                        ob = ob16[cnt["o16"] % 2]
                        okey = "A_o16_%d" % (cnt["o16"] % 2)
                        cnt["o16"] += 1
                    else:
                        ob = ob32[cnt["o32"] % 2]
                        okey = "A_o32_%d" % (cnt["o32"] % 2)
                        cnt["o32"] += 1
                    for tt in range(NT):
                        bi, pb, pk = kc.bank()
                        for k in range(8):
                            kc.mm(pb[:, :], wg3[:, k, j * 128:(j + 1) * 128], hT3[:, k, tt * 512:(tt + 1) * 512],
                                  k == 0, k == 7, [("hT", tt, k), wk], [pk])
                        evac(gkind, ob[:, tt * 512:(tt + 1) * 512], pb[:, :], pk, (okey, tt))
                    dst = {"u": uT, "dqkv": dqkvT, "dz": dzT, "mq": mqT, "mk": mkT, "gate": gatesT}[gkind]
                    r0 = grow0 + j * 128
                    kc.stt(dst[r0:r0 + 128, :], ob, [(okey, tt) for tt in range(NT)], writes=[gkind + "_d"], semkey=okey + "_st")
            wg = wgb[0]
            wk = "A_wg0"
            kc.ld(wg, winb[l, 14], wk, reads=["CAST0b"] if l != layers[0] else ["CAST0"])
            wg3 = r3(wg, 8)
            a4 = kc.alloc(4)
            P_a4 = kc.ld(a4[0:4, :], a4_d, "A_a4")
            nexpA = kc.alloc(1)
            kc.act(nexpA[0:4, :], a4[0:4, 2 * l:2 * l + 1], AF.Exp, ["A_a4"], ["A_nexpA"])
            kc.ts("dve", nexpA[0:4, :], nexpA[0:4, :], -1.0, None, ALU.mult, None, ["A_nexpA"], ["A_nexpA"])
            obb = ob32[0]
            oba = ob32[1]
            for tt in range(NT):
                sl = slice(tt * 512, (tt + 1) * 512)
                bi, pb, pk = kc.bank()
                for k in range(8):
                    kc.mm(pb[0:4, :], wg3[:, k, 0:4], hT3[:, k, sl], k == 0, k == 7, [("hT", tt, k), wk], [pk])
                kc.act(obb[0:4, sl], pb[0:4, :], AF.Sigmoid, [pk], [("A_o32_0", tt)])
                bi, pb, pk = kc.bank()
                for k in range(8):
                    kc.mm(pb[0:4, :], wg3[:, k, 4:8], hT3[:, k, sl], k == 0, k == 7, [("hT", tt, k), wk], [pk])
                kc.act(oba[0:4, sl], pb[0:4, :], AF.Exp, [pk, "A_a4"], [("A_o32_1", tt)],
                       bias=a4[0:4, 2 * l + 1:2 * l + 2])
            AK1 = [("A_o32_1", tt) for tt in range(NT)]
            kc.act(oba[0:4, :], oba[0:4, :], AF.Ln, AK1 + ["vecs"], AK1, bias=vcol(V_ONE)[0:4, :])
            kc.ts("dve", oba[0:4, :], oba[0:4, :], nexpA[0:4, 0:1], None, ALU.mult, None, AK1 + ["A_nexpA"], AK1)
            kc.stt(dbg[0:4, :], obb[0:4, :], [("A_o32_0", tt) for tt in range(NT)], writes=["dbg"], semkey="A_o32_0_st")
            kc.stt(dbg[4:8, :], oba[0:4, :], [("A_o32_1", tt) for tt in range(NT)], writes=["dbg"], semkey="A_o32_1_st")
            kc.reset()
        if "B" in stages:
            P.cur_tag = "L%dB" % l
            pw = kc.alloc(4 * 128, BF16)
            kc.ld(pw, pwb[l], "B_pw", reads=["CAST1" if l == layers[0] else "CAST1b"])
            pw3 = r3(pw, 4)
            rc = kc.alloc(64)
            kc.ld(rc, rcnt_d, "B_rc")
            ub = [kc.alloc(16 + S) for _ in range(2)]
            sab = [kc.alloc(16 + S) for _ in range(2)]
            mxb = [kc.alloc(S, BF16) for _ in range(2)]
            t16 = kc.alloc(16)
            yob = [kc.alloc(S, BF16) for _ in range(2)]
            for i in range(2):
                kc.memset("pool", ub[i][:, 0:16], 0.0, ["B_u%dz" % i])
                kc.memset("pool", sab[i][:, 0:16], 0.0, ["B_s%dz" % i])
            it = 0
            for s_ in range(NSEQ):
                for g in range(4):
                    i2 = it % 2
                    u = ub[i2]
                    uk = "B_u%d" % i2
                    kc.ld(u[:, 16:], uT[g * 128:(g + 1) * 128, s_ * S:(s_ + 1) * S], uk)
                    cur, curk = u, uk
                    for j in range(g + 1):
                        dst = sab[j % 2]
                        dk = "B_s%d" % (j % 2)
                        sh = 1 << j
                        kc.tt("dve" if j % 2 == 0 else "pool", dst[:, 16:], cur[:, 16:], cur[:, 16 - sh:16 - sh + S],
                              ALU.add, [curk, curk + "z"], [dk])
                        cur, curk = dst, dk
                    w = 1 << (g + 1)
                    m = mxb[i2]
                    mk_ = "B_mx%d" % i2
                    kc.sto("dve", m, cur[:, 16:], 1.0 / w, u[:, 16:], ALU.mult, ALU.subtract, [curk, uk], [mk_])
                    kc.tt("dve", t16, cur[:, 16:32], rc[:, g * 16:(g + 1) * 16], ALU.mult, [curk, "B_rc"], ["B_t16"])
                    kc.tt("dve", m[:, 0:16], t16, u[:, 16:32], ALU.subtract, ["B_t16", uk, mk_], [mk_])
                    y = yob[i2]
                    yk = "B_y%d" % i2
                    for j in range(4):
                        bi, pb, pk = kc.bank()
                        kc.mm(pb[:, :], pw3[:, g, :], m[:, j * 512:(j + 1) * 512], True, True, [mk_, "B_pw"], [pk])
                        kc.ts("dve" if j % 2 == 0 else "dve", y[:, j * 512:(j + 1) * 512], pb[:, :],
                              vcol(V_PSCALE + l * 4 + g), None, ALU.mult, None, [pk, "vecs"], [yk])
                    kc.store(ybT[g * 128:(g + 1) * 128, s_ * S:(s_ + 1) * S], y, yk, writes=["ybT"])
                    it += 1
            kc.reset()

        if "C" in stages:
            P.cur_tag = "L%dC" % l
            tri = kc.alloc(256)
            kc.ld(tri[0:64, :], tri_d, "C_tri")
            Ut = tri[0:64, 0:64]
            mA = tri[0:64, 64:128]
            mB = tri[0:64, 128:192]
            SU = tri[0:64, 192:256]
            identb3 = ident[0:64, 0:64].rearrange("p (o j) -> p o j", o=1).to_broadcast([64, 8, 64])
            raw = kc.alloc(3 + S)
            Xb = raw[:, 3:3 + S]
            acc = kc.alloc(S)
            sqb = kc.alloc(S, BF16)
            rn = kc.alloc(512)
            khb = kc.alloc(S, BF16)
            qhb = kc.alloc(S, BF16)
            gcrow = kc.alloc(S)
            E1 = kc.alloc(S)
            brow = kc.alloc(S)
            gT = kc.alloc(32)
            bT = kc.alloc(32)
            gcc = kc.alloc(32)
            egd = kc.alloc(32)
            Dm = kc.alloc(512)
            Gb = kc.alloc(512)
            GTi = kc.alloc(512)
            GTb = kc.alloc(512)
            tmpD = kc.alloc(512)
            Qk = [kc.alloc(512) for _ in range(2)]
            Rk = [kc.alloc(512) for _ in range(2)]
            Gk = [kc.alloc(512) for _ in range(2)]
            SETS = []
            for i_ in range(2):
                SETS.append(dict(i=i_, keT=kc.alloc(S), qdT=kc.alloc(S), kdec=kc.alloc(32 * 128, BF16),
                                 vtok=kc.alloc(32 * 128, BF16), aqkT=kc.alloc(S, BF16), TTb=kc.alloc(S, BF16),
                                 egl=kc.alloc(32)))
            oT = kc.alloc(S)
            S_ = kc.alloc(128)
            Rb = kc.alloc(128, BF16)
            vn = kc.alloc(128, BF16)
            zsb = [kc.alloc(512) for _ in range(2)]
            yout = kc.alloc(S, BF16)
            sqb2 = kc.alloc(S, BF16)
            rn2 = kc.alloc(512)
            kc.memset("pool", raw[:, 0:3], 0.0, ["C_rawz"])

            def bc_mid(ap64, n):
                return ap64.rearrange("p (o j) -> p o j", o=1).to_broadcast([64, n, 64])

            def bc_last(ap, n, w):
                return ap.rearrange("p (n o) -> p n o", o=1).to_broadcast([64, n, w])

            def phaseA(s_, h, st):
                t0 = s_ * S
                si = st["i"]
                kK = lambda nm: "C_%s%d" % (nm, si)
                keT, qdT, kdec, vtok, aqkT, TTb, egl = (st[k] for k in ("keT", "qdT", "kdec", "vtok", "aqkT", "TTb", "egl"))
                kc.ld(gcrow[0:64, :], dbg[4 + h:5 + h, t0:t0 + S].partition_broadcast(64), "C_gcrow")
                kc.ld(brow[0:64, :], dbg[h:h + 1, t0:t0 + S].partition_broadcast(64), "C_brow")
                idb32 = ident[0:64, 0:64].rearrange("p (o j) -> p o j", o=1).to_broadcast([64, 32, 64])
                kc.tt("dve", r3(Xb[0:64, :], 32), r3(gcrow[0:64, :], 32), idb32, ALU.mult, ["C_gcrow", "ident"], ["C_raw"])
                P.op("dve", (lambda: lambda e: e.tensor_reduce(gT[0:64, :], r3(Xb[0:64, :], 32), AX.X, ALU.add))(),
                     reads=["C_raw"], writes=["C_gT"])
                kc.tt("dve", r3(Xb[0:64, :], 32), r3(brow[0:64, :], 32), idb32, ALU.mult, ["C_brow", "ident"], ["C_raw"])
                P.op("dve", (lambda: lambda e: e.tensor_reduce(bT[0:64, :], r3(Xb[0:64, :], 32), AX.X, ALU.add))(),
                     reads=["C_raw"], writes=["C_bT"])
                yield
                bi, pb, pk = kc.bank()
                kc.mm(pb[0:64, 0:32], Ut, gT[0:64, :], True, True, ["C_tri", "C_gT"], [pk])
                kc.copy("act", gcc[0:64, :], pb[0:64, 0:32], [pk], ["C_gcc"])
                bi, pb, pk = kc.bank()
                kc.mm(pb[:, 0:32], ones_f[0:64, 0:128], gT[0:64, :], True, True, ["ones_f", "C_gT"], [pk])
                kc.act(egl[:, :], pb[:, 0:32], AF.Exp, [pk], [kK("egl")])
                kc.tt("dve", egd[0:64, :], pb[0:64, 0:32], gcc[0:64, :], ALU.subtract, [pk, "C_gcc"], ["C_egd"])
                kc.act(egd[0:64, :], egd[0:64, :], AF.Exp, ["C_egd"], ["C_egd"])
                kc.tt("dve", r3(Xb[0:64, :], 32), bc_last(gT[0:64, :], 32, 64), bc_mid(Ut, 32), ALU.mult,
                      ["C_gT", "C_tri"], ["C_raw"])
                yield
                for q4 in range(4):
                    sl = slice(q4 * 512, (q4 + 1) * 512)
                    bi, pb, pk = kc.bank()
                    kc.mm(pb[:, :], ones_f[0:64, 0:128], Xb[0:64, sl], True, True, ["ones_f", "C_raw"], [pk])
                    kc.copy("dve", gcrow[:, sl], pb[:, :], [pk], ["C_gcrow"])
                    kc.act(E1[:, sl], pb[:, :], AF.Exp, [pk], ["C_E1"])
                    yield

                def conv_silu(comp):
                    blk = comp * 4 + h
                    kc.ld(raw[:, 3:3 + S], dqkvT[blk * 128:(blk + 1) * 128, t0:t0 + S], "C_raw")
                    cwi = V_CW + (l * 12 + blk) * 4
                    kc.ts("dve", acc, raw[:, 0:S], vcol(cwi), None, ALU.mult, None, ["C_raw", "C_rawz", "vecs"], ["C_acc"])
                    for j in range(1, 4):
                        kc.sto("dve", acc, raw[:, j:j + S], vcol(cwi + j), acc, ALU.mult, ALU.add,
                               ["C_raw", "C_rawz", "C_acc", "vecs"], ["C_acc"])
                    kc.act(acc, acc, AF.Silu, ["C_acc"], ["C_acc"])

                def l2n_slice(q4, scale):
                    sl = slice(q4 * 512, (q4 + 1) * 512)
                    bi, pb, pk = kc.bank()
                    kc.mm(pb[:, :], ones_bf, sqb[:, sl], True, True, ["ones_bf", "C_sqb"], [pk])
                    kc.act(rn, pb[:, :], AF.Ln, [pk, "vecs"], ["C_rn"], bias=vcol(V_EPS))
                    kc.act(rn, rn, AF.Exp, ["C_rn"], ["C_rn"], scale=-0.5)
                    kc.sto("dve", acc[:, sl], acc[:, sl], scale, rn, ALU.mult, ALU.mult, ["C_acc", "C_rn"], ["C_acc"])

                def to_tok4(n4, dst, dkey, mul_egd):
                    bi, pb, pk = kc.bank()
                    for c in range(4):
                        n = n4 * 4 + c
                        kc.transpose(pb[0:64, c * 128:(c + 1) * 128], acc[:, n * 64:(n + 1) * 64], ident,
                                     ["C_acc", "ident"], [pk])
                    dsl = dst[0:64, n4 * 512:(n4 + 1) * 512]
                    if mul_egd:
                        kc.tt("dve", r3(dsl, 4), r3(pb[0:64, :], 4), bc_last(egd[0:64, n4 * 4:n4 * 4 + 4], 4, 128),
                              ALU.mult, [pk, "C_egd"], [dkey])
                    else:
                        kc.copy("act", dsl, pb[0:64, :], [pk], [dkey])

                conv_silu(1)
                yield
                kc.act(sqb, acc, AF.Square, ["C_acc"], ["C_sqb"])
                for q4 in range(4):
                    l2n_slice(q4, 1.0)
                    yield
                kc.copy("act", khb, acc, ["C_acc"], ["C_khb"])
                kc.tt("dve", keT, acc, E1, ALU.mult, ["C_acc", "C_E1"], [kK("keT")])
                yield
                for n4 in range(8):
                    to_tok4(n4, kdec, kK("kdec"), True)
                    yield
                conv_silu(0)
                yield
                kc.act(sqb, acc, AF.Square, ["C_acc"], ["C_sqb"])
                for q4 in range(4):
                    l2n_slice(q4, 128.0 ** -0.5)
                    yield
                kc.copy("act", qhb, acc, ["C_acc"], ["C_qhb"])
                kc.tt("dve", qdT, acc, E1, ALU.mult, ["C_acc", "C_E1"], [kK("qdT")])
                yield
                conv_silu(2)
                yield
                for n4 in range(8):
                    to_tok4(n4, vtok, kK("vtok"), False)
                    yield

                for half in range(2):
                    for bt in (2 * half, 2 * half + 1):
                        b2 = bt % 2
                        n0 = bt * 8
                        c0 = bt * 512
                        bsl = slice(c0, c0 + 512)
                        kc.tt("dve", r3(Dm[0:64, :], 8), r3(gcrow[0:64, bsl], 8), bc_last(gcc[0:64, n0:n0 + 8], 8, 64),
                              ALU.subtract, ["C_gcrow", "C_gcc"], ["C_Dm"])
                        kc.tt("pool", r3(tmpD[0:64, :], 8), r3(Dm[0:64, :], 8), bc_mid(mA, 8), ALU.add,
                              ["C_Dm", "C_tri"], ["C_tmpD"])
                        kc.act(GTi[0:64, :], tmpD[0:64, :], AF.Exp, ["C_tmpD"], ["C_GTi"])
                        kc.tt("dve", r3(tmpD[0:64, :], 8), bc_mid(mB, 8), r3(Dm[0:64, :], 8), ALU.subtract,
                              ["C_Dm", "C_tri", "C_tmpD"], ["C_tmpD"])
                        kc.act(Gb[0:64, :], tmpD[0:64, :], AF.Exp, ["C_tmpD"], ["C_Gb"])
                        kc.tt("dve", r3(Gb[0:64, :], 8), r3(Gb[0:64, :], 8), bc_last(bT[0:64, n0:n0 + 8], 8, 64),
                              ALU.mult, ["C_Gb", "C_bT"], ["C_Gb"])
                        kc.tt("pool", r3(GTb[0:64, :], 8), r3(GTi[0:64, :], 8), bc_mid(SU, 8), ALU.mult,
                              ["C_GTi", "C_tri"], ["C_GTb"])
                        kc.tt("pool", GTb[0:64, :], GTb[0:64, :], brow[0:64, bsl], ALU.mult, ["C_GTb", "C_brow"],
                              ["C_GTb"])
                        yield
                        Q, R, G = Qk[b2], Rk[b2], Gk[b2]
                        qk_, rk_, gk_ = "C_Q%d" % b2, "C_R%d" % b2, "C_G%d" % b2
                        bi, pb, pk = kc.bank()
                        for c in range(8):
                            cs = slice((n0 + c) * 64, (n0 + c + 1) * 64)
                            kc.mm(pb[0:64, c * 64:(c + 1) * 64], khb[:, cs], khb[:, cs], True, True, ["C_khb"], [pk])
                        kc.tt("dve", R[0:64, :], pb[0:64, :], Gb[0:64, :], ALU.mult, [pk, "C_Gb"], [rk_])
                        kc.tt("dve", Q[0:64, :], pb[0:64, :], GTb[0:64, :], ALU.mult, [pk, "C_GTb"], [qk_])
                        bi, pb, pk = kc.bank()
                        for c in range(8):
                            cs = slice((n0 + c) * 64, (n0 + c + 1) * 64)
                            kc.mm(pb[0:64, c * 64:(c + 1) * 64], khb[:, cs], qhb[:, cs], True, True,
                                  ["C_khb", "C_qhb"], [pk])
                        kc.tt("dve", aqkT[0:64, bsl], pb[0:64, :], GTi[0:64, :], ALU.mult, [pk, "C_GTi"], [kK("aqkT")])
                        kc.tt("pool", r3(G[0:64, :], 8), identb3, r3(Q[0:64, :], 8), ALU.subtract, ["ident", qk_], [gk_])
                        yield
                    for lev in range(1, 6):
                        for b2 in range(2):
                            Q, R, G = Qk[b2], Rk[b2], Gk[b2]
                            qk_, rk_, gk_ = "C_Q%d" % b2, "C_R%d" % b2, "C_G%d" % b2
                            if lev < 5:
                                bq, pbq, pkq = kc.bank()
                                for c in range(8):
                                    cs = slice(c * 64, (c + 1) * 64)
                                    kc.mm(pbq[0:64, cs], R[0:64, cs], Q[0:64, cs], True, True, [qk_, rk_], [pkq])
                            br, pbr, pkr = kc.bank()
                            for c in range(8):
                                cs = slice(c * 64, (c + 1) * 64)
                                kc.mm(pbr[0:64, cs], Q[0:64, cs], R[0:64, cs], True, True, [qk_, rk_], [pkr])
                            if lev < 5:
                                kc.copy("act", Q[0:64, :], pbq[0:64, :], [pkq], [qk_])
                            kc.copy("act", R[0:64, :], pbr[0:64, :], [pkr], [rk_])
                            yield
                            bg, pbg, pkg = kc.bank()
                            for c in range(8):
                                cs = slice(c * 64, (c + 1) * 64)
                                kc.mm(pbg[0:64, cs], R[0:64, cs], G[0:64, cs], True, True, [rk_, gk_], [pkg])
                            kc.tt("dve", G[0:64, :], G[0:64, :], pbg[0:64, :], ALU.add, [gk_, pkg], [gk_])
                            yield
                    for bt in (2 * half, 2 * half + 1):
                        n0 = bt * 8
                        kc.tt("dve", r3(TTb[0:64, bt * 512:(bt + 1) * 512], 8), r3(Gk[bt % 2][0:64, :], 8),
                              bc_last(bT[0:64, n0:n0 + 8], 8, 64), ALU.mult, ["C_G%d" % (bt % 2), "C_bT"], [kK("TTb")])
                    yield

            def phaseB(s_, h, st):
                t0 = s_ * S
                si = st["i"]
                kK = lambda nm: "C_%s%d" % (nm, si)
                keT, qdT, kdec, vtok, aqkT, TTb, egl = (st[k] for k in ("keT", "qdT", "kdec", "vtok", "aqkT", "TTb", "egl"))
                kc.memset("dve", S_, 0.0, ["C_S"])
                for n in range(32):
                    cs = slice(n * 64, (n + 1) * 64)
                    ns = slice(n * 128, (n + 1) * 128)
                    b1, pb1, pk1 = kc.bank()
                    kc.mm(pb1[0:64, 0:128], keT[:, cs], S_, True, True, [kK("keT"), "C_S"], [pk1])
                    b3, pb3, pk3 = kc.bank()
                    kc.mm(pb3[:, 0:64], S_, qdT[:, cs], True, False, [kK("qdT"), "C_S"], [pk3])
                    kc.tt("dve", Rb[0:64, :], vtok[0:64, ns], pb1[0:64, 0:128], ALU.subtract, [kK("vtok"), pk1], ["C_Rb"])
                    b2, pb2, pk2 = kc.bank()
                    kc.mm(pb2[0:64, 0:128], TTb[0:64, cs], Rb[0:64, :], True, True, [kK("TTb"), "C_Rb"], [pk2])
                    kc.copy("act", vn[0:64, :], pb2[0:64, 0:128], [pk2], ["C_vn"])
                    kc.mm(pb3[:, 0:64], vn[0:64, :], aqkT[0:64, cs], False, True, ["C_vn", kK("aqkT")], [pk3])
                    b4, pb4, pk4 = kc.bank()
                    kc.mm(pb4[:, 0:128], kdec[0:64, ns], vn[0:64, :], True, True, [kK("kdec"), "C_vn"], [pk4])
                    kc.copy("act", oT[:, cs], pb3[:, 0:64], [pk3], ["C_oT"])
                    kc.sto("dve", S_, S_, egl[:, n:n + 1], pb4[:, 0:128], ALU.mult, ALU.add, ["C_S", kK("egl"), pk4],
                           ["C_S"])
                    yield
                kc.act(sqb2, oT, AF.Square, ["C_oT"], ["C_sqb2"])
                for q4 in range(4):
                    sl = slice(q4 * 512, (q4 + 1) * 512)
                    zs = zsb[q4 % 2]
                    zk = "C_zs%d" % (q4 % 2)
                    kc.ld(zs, dzT[h * 128:(h + 1) * 128, t0 + q4 * 512:t0 + (q4 + 1) * 512], zk)
                    bi, pb, pk = kc.bank()
                    kc.mm(pb[:, :], ones_bf, sqb2[:, sl], True, True, ["ones_bf", "C_sqb2"], [pk])
                    kc.act(rn2, pb[:, :], AF.Ln, [pk, "vecs"], ["C_rn2"], bias=vcol(V_EPS), scale=1.0 / 128)
                    kc.act(rn2, rn2, AF.Exp, ["C_rn2"], ["C_rn2"], scale=-0.5)
                    kc.sto("dve", oT[:, sl], oT[:, sl], vcol(V_DNW + l), rn2, ALU.mult, ALU.mult,
                           ["C_oT", "C_rn2", "vecs"], ["C_oT"])
                    kc.tt("pool", yout[:, sl], oT[:, sl], zs, ALU.mult, ["C_oT", zk], ["C_yout"])
                    yield
                kc.store(ybT[512 + h * 128:512 + (h + 1) * 128, t0:t0 + S], yout, "C_yout", writes=["ybT"])
                yield

            heads = [(s_, h) for s_ in range(NSEQ) for h in range(4)]
            nA = 0
            for _ in phaseA(heads[0][0], heads[0][1], SETS[0]):
                nA += 1
            nB = 38
            per = max(1, -(-nA // nB))
            for i_, (s_, h) in enumerate(heads):
                gB = phaseB(s_, h, SETS[i_ % 2])
                gA = phaseA(heads[i_ + 1][0], heads[i_ + 1][1], SETS[(i_ + 1) % 2]) if i_ + 1 < len(heads) else None
                aliveA = gA is not None
                aliveB = True
                while aliveA or aliveB:
                    if aliveB:
                        try:
                            next(gB)
                        except StopIteration:
                            aliveB = False
                    if aliveA:
                        for _ in range(per):
                            try:
                                next(gA)
                            except StopIteration:
                                aliveA = False
                                break
            kc.reset()

        if "D" in stages:
            P.cur_tag = "L%dD" % l
            band = kc.alloc(8 * 1152, BF16)
            kc.ld(band, band_d, "D_band", q="pool")
            band3 = r3(band, 8)
            ind = kc.alloc(64 * 128, BF16)
            kc.ld(ind, ind_d, "D_ind", q="pool")
            ind3 = r3(ind, 64)
            identb = kc.alloc(128, BF16)
            kc.copy("dve", identb, ident, ["ident"], ["D_identb"])
            QT = kc.alloc(4 * S, BF16)
            KT = kc.alloc(4 * S, BF16)
            QT3 = r3(QT, 4)
            KT3 = r3(KT, 4)
            VP = kc.alloc(16 * 768, BF16)
            VP4 = VP.rearrange("p (i c w) -> p i c w", i=16, c=4)
            VP3 = r3(VP, 16)
            kc.memset("pool", VP, 0.0, ["D_VPz"])
            kc.memset("pool", VP4[:, :, :, 64:65], 1.0, ["D_VPz"])
            kms = kc.alloc(32)
            KM = kc.alloc(4 * 64, BF16)
            KM3 = r3(KM, 4)
            gsb = kc.alloc(128)
            gw = kc.alloc(128)
            mxt = kc.alloc(16)
            eqt = kc.alloc(128)
            Mt = kc.alloc(128)
            MallT = kc.alloc(S, BF16)
            ptb = [kc.alloc(512, BF16) for _ in range(5)]
            cfar = kc.alloc(8)
            kc.copy("dve", cfar, band3[:, :, 1151], ["D_band"], ["D_cfar"])
            rl = kc.alloc(512)
            osb = kc.alloc(512)
            ycT = kc.alloc(4 * S, BF16)
            ycT3 = r3(ycT, 4)
            kc.memset("pool", KM, 0.0, ["D_KMz"])
            KZ = kc.alloc(8 * S, BF16)
            KZ3 = r3(KZ, 8)
            kc.memset("pool", KZ, 0.0, ["D_KZz"])
            for s_ in range(NSEQ):
              try:
                t0 = s_ * S
                kc.ld(QT3, mqT[:, t0:t0 + S].rearrange("(c p) t -> p c t", p=128), "D_QT")
                kc.ld(KT3, mkT[:, t0:t0 + S].rearrange("(c p) t -> p c t", p=128), "D_KT")
                srcv = mv[t0:t0 + S, :].rearrange("(i p) (c two d) -> p i c two d", p=128, two=2, d=64)
                for c in range(4):
                    kc.ld(VP4[:, :, c, 0:64], srcv[:, :, c, 0, :], ("D_VP", c, 0), reads=["D_VPz"], semkey="D_VPa%d" % c)
                    kc.ld(VP4[:, :, c, 128:192], srcv[:, :, c, 1, :], ("D_VP", c, 1), reads=["D_VPz"],
                          semkey="D_VPb%d" % c)
                for h_ in range(8):
                    rr0 = (h_ % 2) * 64
                    kc.copy("pool" if h_ % 2 == 0 else "act", KZ3[rr0:rr0 + 64, h_, :], KT3[rr0:rr0 + 64, h_ // 2, :],
                            ["D_KT", "D_KZz"], [("D_KZ", h_)])
                P.op("dve", (lambda kms=kms, KT=KT: lambda e: e.tensor_reduce(r3(kms, 4), KT.rearrange("p (c n k) -> p c n k", c=4, n=8), AX.X,
                                                      ALU.add))(), reads=["D_KT"], writes=["D_kms"])
                kms3 = r3(kms, 4)
                for c in range(4):
                    kc.copy("dve", KM3[0:64, c, (2 * c) * 8:(2 * c) * 8 + 8], kms3[0:64, c, :], ["D_kms", "D_KMz"],
                            ["D_KM"])
                    kc.copy("dve", KM3[64:128, c, (2 * c + 1) * 8:(2 * c + 1) * 8 + 8], kms3[64:128, c, :],
                            ["D_kms", "D_KMz"], ["D_KM"])
                if DSTOP == 1:
                    kc.dump("KM", KM, ["D_KM"], BF16)
                    kc.dump("VP", VP, ["D_VPz"] + [("D_VP", c, hh) for c in range(4) for hh in range(2)], BF16)
                    kc.dump("kms", kms, ["D_kms"])
                    raise _Stop()
                for q4 in range(4):
                    bm, pbm, pkm = kc.bank()
                    for qq in range(4):
                        qt = q4 * 4 + qq
                        b = qt // 2
                        if b >= 4:
                            bi, pb, pk = kc.bank()
                            for c in range(4):
                                kc.mm(pb[:, 0:64], QT3[:, c, qt * 128:(qt + 1) * 128], KM3[:, c, :], c == 0, c == 3,
                                      ["D_QT", "D_KM"], [pk])
                            g3 = r3(gsb[:, 0:64], 8)
                            w3 = r3(gw[:, 0:64], 8)
                            e3 = r3(eqt[:, 0:64], 8)
                            kc.copy("act", gsb[:, 0:64], pb[:, 0:64], [pk], ["D_gsb"])
                            kc.memset("dve", g3[:, :, b:8], -1.0e9, ["D_gsb"])
                            src, srck = g3, "D_gsb"
                            for rnd in range(3):
                                P.op("dve", (lambda src=src, mxt=mxt: lambda e: e.tensor_reduce(mxt[:, 0:8], src, AX.X, ALU.max))(),
                                     reads=[srck], writes=["D_mxt"])
                                if rnd == 2:
                                    break
                                mb = mxt[:, 0:8].rearrange("p (h o) -> p h o", o=1).to_broadcast([128, 8, 8])
                                kc.tt("dve", e3, src, mb, ALU.is_ge, [srck, "D_mxt"], ["D_eqt"])
                                kc.sto("dve", w3, e3, -1.0e9, src, ALU.mult, ALU.add, ["D_eqt", srck], ["D_gw"])
                                src, srck = w3, "D_gw"
                            mb = mxt[:, 0:8].rearrange("p (h o) -> p h o", o=1).to_broadcast([128, 8, 8])
                            kc.tt("dve", e3, g3, mb, ALU.is_ge, ["D_gsb", "D_mxt"], ["D_eqt"])
                            kc.ts("dve", Mt[:, 0:64], eqt[:, 0:64], BIG, -BIG, ALU.mult, ALU.add, ["D_eqt"], ["D_Mt"])
                            kc.memset("dve", r3(Mt[:, 0:64], 8)[:, :, b:b + 1], 0.0, ["D_Mt"])
                        else:
                            kc.memset("dve", Mt[:, 0:64], 0.0, ["D_Mt"])
                        kc.transpose(pbm[0:64, qq * 128:(qq + 1) * 128], Mt[:, 0:64], ident, ["D_Mt", "ident"], [pkm])
                    kc.copy("act", MallT[0:64, q4 * 512:(q4 + 1) * 512], pbm[0:64, :], [pkm], ["D_MallT"])
                    kc.copy("act", MallT[64:128, q4 * 512:(q4 + 1) * 512], pbm[0:64, :], [pkm], ["D_MallT"])
                if DSTOP == 2:
                    kc.dump("MallT", MallT[0:64, :], ["D_MallT"], BF16)
                    raise _Stop()
                ipt = 0
                pend = []

                def flush(keep):
                    while len(pend) > keep:
                        pend.pop(0)()

                for h in range(8):
                    c = h // 2
                    r0 = (h % 2) * 64
                    lrow = 64 if h % 2 == 0 else 0
                    for qi in range(4):
                        q0 = qi * 512
                        nkt = (qi + 1) * 4
                        bo, pbo, pko = kc.bank(n=2, base=6)
                        for kt in range(nkt):
                            k0 = kt * 128
                            nblk = kt // 2
                            bs_, pbs, pks = kc.bank(n=5, base=0)
                            mms = [(KZ3[:, h, k0:k0 + 128], QT3[:, c, q0:q0 + 512], [("D_KZ", h), "D_KZz", "D_QT"])]
                            if qi >= 2 and nblk < 2 * qi + 1:
                                mms.append((ind3[:, h * 8 + nblk, :], MallT[:, q0:q0 + 512], ["D_ind", "D_MallT"]))
                            far = (q0 - k0) >= 256
                            if not far:
                                off = min(max(q0 - k0, -384), 256) + 384
                                mms.append((identb, band3[:, h, off:off + 512], ["D_identb", "D_band"]))
                            for i_, (a_, b_, rd_) in enumerate(mms):
                                kc.mm(pbs[:, :], a_, b_, i_ == 0, i_ == len(mms) - 1, rd_, [pks])
                            pt = ptb[ipt % 5]
                            ptk = "D_pt%d" % (ipt % 5)
                            ipt += 1
                            if far:
                                kc.act(pt, pbs[:, :], AF.Exp, [pks, "D_cfar"], [ptk], bias=cfar[:, h:h + 1])
                            else:
                                kc.act(pt, pbs[:, :], AF.Exp, [pks], [ptk])
                            lo = c * 192 + (h % 2) * 64

                            def pv(pbo=pbo, pko=pko, kt=kt, nkt=nkt, lo=lo, pt=pt, ptk=ptk, c=c, r0=r0, lrow=lrow, q0=q0):
                                kc.mm(pbo[:, :], VP3[:, kt, lo:lo + 128], pt, kt == 0, kt == nkt - 1,
                                      [("D_VP", c, 0), ("D_VP", c, 1), "D_VPz", ptk], [pko])
                                if kt == nkt - 1:
                                    kc.act(rl[lrow:lrow + 1, :], pbo[lrow:lrow + 1, :], AF.Ln, [pko], ["D_rl"])
                                    kc.act(rl[lrow:lrow + 1, :], rl[lrow:lrow + 1, :], AF.Exp, ["D_rl"], ["D_rl"], scale=-1.0)
                                    kc.copy("act", osb[r0:r0 + 64, :], pbo[r0:r0 + 64, :], [pko], ["D_osb"])
                                    br_, pbr, pkr = kc.bank(n=1, base=5)
                                    kc.mm(pbr[:, :], ones_f[lrow:lrow + 1, 0:128], rl[lrow:lrow + 1, :], True, True,
                                          ["ones_f", "D_rl"], [pkr])
                                    kc.tt("dve", ycT3[r0:r0 + 64, c, q0:q0 + 512], osb[r0:r0 + 64, :], pbr[r0:r0 + 64, :],
                                          ALU.mult, ["D_osb", pkr], ["D_ycT"])

                            pend.append(pv)
                            flush(2)
                    if DSTOP == 3 + h:
                        flush(0)
                        kc.dump("ycT", ycT, ["D_ycT"], BF16)
                        raise _Stop()
                flush(0)
                kc.store(ybT[1024:1536, t0:t0 + S].rearrange("(c p) t -> p c t", p=128), ycT3, "D_ycT", writes=["ybT"])
              except _Stop:
                pass
            kc.reset()

        if "E" in stages:
            P.cur_tag = "L%dE" % l
            wbr = kc.alloc(12 * 1024, BF16)
            kc.ld(wbr, wbrb[l], "E_wbr", reads=["CAST1" if l == layers[0] else "CAST1b"])
            wbr4 = wbr.rearrange("p (n k d) -> p n k d", n=3, k=4)
            wo = kc.alloc(8 * 1024, BF16)
            kc.ld(wo, woutb[l], "E_wo", reads=["CAST1" if l == layers[0] else "CAST1b"])
            wo3 = r3(wo, 8)
            ybb = [kc.alloc(12 * 512, BF16) for _ in range(2)]
            gtb = [kc.alloc(24 * 512, BF16) for _ in range(2)]
            xtb = [kc.alloc(8 * 512) for _ in range(2)]
            mg = kc.alloc(8 * 512, BF16)
            mg3 = r3(mg, 8)
            ta = kc.alloc(512)
            tb = kc.alloc(512)
            tc_ = kc.alloc(512)
            for tt in range(NT):
                i2 = tt % 2
                tsl = slice(tt * 512, (tt + 1) * 512)
                yb3 = r3(ybb[i2], 12)
                gt3 = r3(gtb[i2], 24)
                xt3 = r3(xtb[i2], 8)
                ybk, gtk, xk = "E_yb%d" % i2, "E_gt%d" % i2, "E_xt%d" % i2
                kc.ld(yb3, ybT[:, tsl].rearrange("(c p) t -> p c t", p=128), ybk)
                kc.ld(gt3, gatesT[:, tsl].rearrange("(c p) t -> p c t", p=128), gtk)
                kc.ld(xt3, xsrc[:, tsl].rearrange("(k p) t -> p k t", p=128), xk)
                for m in range(8):
                    pbs_ = []
                    for n in range(3):
                        bi, pb, pk = kc.bank()
                        for k in range(4):
                            kc.mm(pb[:, :], wbr4[:, n, k, m * 128:(m + 1) * 128], yb3[:, n * 4 + k, :], k == 0, k == 3,
                                  ["E_wbr", ybk], [pk])
                        pbs_.append((pb, pk))
                    kc.tt("dve", ta, pbs_[0][0][:, :], gt3[:, m, :], ALU.mult, [pbs_[0][1], gtk], ["E_ta"])
                    kc.tt("dve", tb, pbs_[1][0][:, :], gt3[:, 8 + m, :], ALU.mult, [pbs_[1][1], gtk], ["E_tb"])
                    kc.tt("dve", tc_, pbs_[2][0][:, :], gt3[:, 16 + m, :], ALU.mult, [pbs_[2][1], gtk], ["E_tc"])
                    kc.tt("pool", ta, ta, tb, ALU.add, ["E_ta", "E_tb"], ["E_ta"])
                    kc.tt("pool", mg3[:, m, :], ta, tc_, ALU.add, ["E_ta", "E_tc"], [("E_mg", m)])
                for m in range(8):
                    bi, pb, pk = kc.bank()
                    for k in range(8):
                        kc.mm(pb[:, :], wo3[:, k, m * 128:(m + 1) * 128], mg3[:, k, :], k == 0, k == 7,
                              ["E_wo", ("E_mg", k)], [pk])
                    kc.tt("dve", xt3[:, m, :], xt3[:, m, :], pb[:, :], ALU.add, [xk, pk], [xk])
                kc.store(x1s[:, tsl].rearrange("(k p) t -> p k t", p=128), xt3, xk, writes=["x1s"])
            kc.reset()

        if "F" in stages:
            P.cur_tag = "L%dF" % l
            wdn = kc.alloc(22 * 1024, BF16)
            kc.ld(wdn, wdnb[l], "F_wdn", reads=["CAST1" if l == layers[0] else "CAST1b"])
            wdn3 = r3(wdn, 22)
            pgw = kc.alloc(8 * 1024, BF16)
            kc.ld(pgw, pgb[l], "F_pgw", reads=["CAST1" if l == layers[0] else "CAST1b"])
            pgw3 = r3(pgw, 8)
            ppw = kc.alloc(2 * 1024, BF16)
            kc.ld(ppw, ppb[l], "F_ppw", reads=["CAST1" if l == layers[0] else "CAST1b"])
            ppw3 = r3(ppw, 2)
            hal = kc.alloc(44 * 2)
            xtb = [kc.alloc(8 * 512) for _ in range(2)]
            h2 = kc.alloc(8 * 512, BF16)
            h23 = r3(h2, 8)
            sqt = kc.alloc(8 * 512, BF16)
            rstd = kc.alloc(512)
            wub = [kc.alloc(8 * 256, BF16) for _ in range(2)]
            rawb = [kc.alloc(514) for _ in range(2)]
            yb_ = [kc.alloc(512) for _ in range(2)]
            gTt = kc.alloc(22 * 512, BF16)
            gT3 = r3(gTt, 22)
            ptb_ = [kc.alloc(2 * 512, BF16) for _ in range(2)]
            sg = kc.alloc(512)
            tp = kc.alloc(512)
            last = (l == layers[-1]) and final
            if last:
                ot = kc.alloc(8 * 512)
                ot3 = r3(ot, 8)
            iw = 0
            for tt in range(NT):
                i2 = tt % 2
                tsl = slice(tt * 512, (tt + 1) * 512)
                xt = xtb[i2]
                xt3 = r3(xt, 8)
                xk = "F_xt%d" % i2
                kc.ld(xt3, x1s[:, tsl].rearrange("(k p) t -> p k t", p=128), xk)
                pt_ = ptb_[i2]
                ptk = "F_pt%d" % i2
                kc.ld(r3(pt_, 2), pT_in[l, :, tsl].rearrange("(k p) t -> p k t", p=128), ptk, q="pool")
                if tt % 4 == 0:
                    kc.memset("pool", hal, 0.0, ["F_hal"])
                rmsnorm_tile(xt, xk, V_NFFN + l * 8, lambda k: h23[:, k, :], [("F_h", k) for k in range(8)], sqt, "F_sq",
                             rstd, "F_rstd")
                HK = [("F_h", k) for k in range(8)]
                for j in range(22):
                    wu = wub[iw % 2]
                    wuk = "F_wu%d" % (iw % 2)
                    iw += 1
                    kc.ld(wu, wupb[l, j], wuk, reads=["CAST1" if l == layers[0] else "CAST1b"])
                    wu3 = r3(wu, 8)
                    for half in range(2):
                        bi, pb, pk = kc.bank()
                        for k in range(8):
                            kc.mm(pb[:, :], wu3[:, k, half * 128:(half + 1) * 128], h23[:, k, :], k == 0, k == 7,
                                  [wuk, ("F_h", k)], [pk])
                        rb = rawb[half]
                        rk = "F_raw%d" % half
                        yb = yb_[half]
                        yk = "F_y%d" % half
                        hi = (j * 2 + half) * 2
                        kc.copy("act", rb[:, 2:514], pb[:, :], [pk], [rk])
                        kc.copy("pool", rb[:, 0:2], hal[:, hi:hi + 2], ["F_hal"], [rk + "h"])
                        kc.copy("pool", hal[:, hi:hi + 2], rb[:, 512:514], [rk], ["F_hal"])
                        cwi = V_FCW + (l * 44 + half * 22 + j) * 3
                        e1 = "dve"
                        kc.ts(e1, yb, rb[:, 0:512], vcol(cwi), None, ALU.mult, None, [rk, rk + "h", "vecs"], [yk])
                        kc.sto(e1, yb, rb[:, 1:513], vcol(cwi + 1), yb, ALU.mult, ALU.add, [rk, rk + "h", yk, "vecs"], [yk])
                        kc.sto("dve", yb, rb[:, 2:514], vcol(cwi + 2), yb, ALU.mult, ALU.add, [rk, yk, "vecs"], [yk])
                    kc.act(yb_[0], yb_[0], AF.Gelu_apprx_tanh, ["F_y0"], ["F_y0"])
                    kc.tt("dve", gT3[:, j, :], yb_[0], yb_[1], ALU.mult, ["F_y0", "F_y1"], [("F_g", j)])
                for m in range(8):
                    bi, pb, pk = kc.bank()
                    for j in range(22):
                        kc.mm(pb[:, :], wdn3[:, j, m * 128:(m + 1) * 128], gT3[:, j, :], j == 0, j == 21,
                              ["F_wdn", ("F_g", j)], [pk])
                    kc.tt("dve", xt3[:, m, :], xt3[:, m, :], pb[:, :], ALU.add, [xk, pk], [xk])
                rmsnorm_tile(xt, xk, V_NPLE + l * 8, lambda k: h23[:, k, :], HK, sqt, "F_sq", rstd, "F_rstd")
                for m in range(8):
                    bi, pb, pk = kc.bank()
                    for k in range(8):
                        kc.mm(pb[:, :], pgw3[:, k, m * 128:(m + 1) * 128], h23[:, k, :], k == 0, k == 7,
                              ["F_pgw", ("F_h", k)], [pk])
                    kc.act(sg, pb[:, :], AF.Sigmoid, [pk], ["F_sg"])
                    bi, pb, pk = kc.bank()
                    for k in range(2):
                        kc.mm(pb[:, :], ppw3[:, k, m * 128:(m + 1) * 128], r3(pt_, 2)[:, k, :], k == 0, k == 1,
                              ["F_ppw", ptk], [pk])
                    kc.tt("dve", tp, pb[:, :], sg, ALU.mult, [pk, "F_sg"], ["F_tp"])
                    kc.tt("pool", xt3[:, m, :], xt3[:, m, :], tp, ALU.add, [xk, "F_tp"], [xk])
                if last:
                    rmsnorm_tile(xt, xk, V_NFIN, lambda k: ot3[:, k, :], [("F_ot", k) for k in range(8)], sqt, "F_sq", rstd, "F_rstd")
                    kc.finals.append(kc.store(outT[:, tsl].rearrange("(k p) t -> p k t", p=128), ot3, [("F_ot", k) for k in range(8)],
                                              writes=["outT"], semkey="F_ot_st"))
                else:
                    kc.store(xs[:, tsl].rearrange("(k p) t -> p k t", p=128), xt3, xk, writes=["xs"])
            kc.reset()
    nsem = P.emit(final_wait_ops=kc.finals)
    return nc, kc, nsem


def _t5_bucket_np(n):
    n = np.maximum(n, 0)
    exact = 16
    nf = np.maximum(n, 1).astype(np.float32)
    large = exact + (np.log(nf / exact) / math.log(128 / exact) * (32 - exact)).astype(np.int32)
    large = np.minimum(large, 31)
    return np.where(n < exact, n, large)


def host_consts():
    ident = np.eye(128, dtype=np.float32)
    p = np.arange(64)[:, None]
    f = np.arange(64)[None, :]
    U = (p <= f).astype(np.float32)
    mA = np.where(f >= p, 0.0, -BIG).astype(np.float32)
    mB = np.where(p > f, 0.0, -BIG).astype(np.float32)
    SU = (f > p).astype(np.float32)
    tri = np.concatenate([U, mA, mB, SU], axis=1)
    rc = np.zeros((128, 64), np.float32)
    for g in range(4):
        w = 2 << g
        for t in range(16):
            rc[:, g * 16 + t] = 1.0 / min(t + 1, w)
    ind = np.zeros((64, 64, 128), np.float32)
    for r in range(64):
        ind[r, r, :] = 1.0
    ind = ind.reshape(64, 64 * 128)
    return ident, tri, rc, np.concatenate([ind, ind], axis=0)


def host_vecs(inp):
    v = np.zeros((128, NVEC), np.float32)

    def put(base, arr):
        a = np.asarray(arr, np.float32)
        lead = int(np.prod(a.shape[:-1])) if a.ndim > 1 else 1
        a = a.reshape(lead, -1, 128)
        a = a.transpose(2, 0, 1).reshape(128, -1)
        v[:, base:base + a.shape[1]] = a

    put(V_NMIX, inp["norm_mix"])
    put(V_NFFN, inp["norm_ffn"])
    put(V_NPLE, inp["norm_ple"])
    put(V_NFIN, inp["norm_final"])
    put(V_PSCALE, inp["pool_scale"])
    cw = np.asarray(inp["dn_conv"], np.float32).reshape(2, 4, 12, 128).transpose(3, 0, 2, 1).reshape(128, 96)
    v[:, V_CW:V_CW + 96] = cw
    fc = np.asarray(inp["ffn_conv"], np.float32).reshape(2, 3, 44, 128).transpose(3, 0, 2, 1).reshape(128, 264)
    v[:, V_FCW:V_FCW + 264] = fc
    v[:, V_DNW:V_DNW + 2] = np.asarray(inp["dn_norm"], np.float32).T
    v[:, V_EPS] = EPS
    v[:, V_ONE] = 1.0
    return v


def host_band(rel_bias):
    rb = np.asarray(rel_bias, np.float32)
    p = np.arange(128)[:, None]
    c = np.arange(1152)[None, :]
    n = c - 384 - p
    bucket = _t5_bucket_np(n)
    band = np.empty((128, 8, 1152), np.float32)
    for h in range(8):
        band[:, h, :] = np.where(n >= 0, rb[bucket, h], -BIG)
    return band.reshape(128, 8 * 1152)


_CACHE = {}


def get_program(NSEQ=2, **kw):
    key = (NSEQ, tuple(sorted(kw.items())))
    if key not in _CACHE:
        _CACHE[key] = build(NSEQ=NSEQ, **kw)
    return _CACHE[key]


def make_in_maps(inp, NSEQ, ncores):
    ident, tri, rc, ind = host_consts()
    vecs = host_vecs(inp)
    band = host_band(inp["rel_bias"])
    a4 = np.stack([np.asarray(inp["dn_a_log"], np.float32)[0], np.asarray(inp["dn_dt_bias"], np.float32)[0],
                   np.asarray(inp["dn_a_log"], np.float32)[1], np.asarray(inp["dn_dt_bias"], np.float32)[1]], axis=1)
    x = np.asarray(inp["x"], np.float32)
    p = np.asarray(inp["p"], np.float32)
    shared = {
        "w_in": np.ascontiguousarray(inp["w_in"], np.float32),
        "w_branch": np.ascontiguousarray(inp["w_branch"], np.float32),
        "w_out": np.ascontiguousarray(inp["w_out"], np.float32),
        "ffn_up": np.ascontiguousarray(inp["ffn_up"], np.float32),
        "ffn_down": np.ascontiguousarray(inp["ffn_down"], np.float32),
        "ple_gate": np.ascontiguousarray(inp["ple_gate"], np.float32),
        "ple_proj": np.ascontiguousarray(inp["ple_proj"], np.float32),
        "pool_w": np.ascontiguousarray(inp["pool_w"], np.float32),
        "vecs": vecs, "ident": ident, "tri": tri, "rcnt": rc, "band": band, "ind": ind,
        "a4": np.ascontiguousarray(a4),
    }
    maps = []
    for c in range(ncores):
        xs_ = x[c * NSEQ:(c + 1) * NSEQ].reshape(NSEQ * S, D)
        ps_ = p[:, c * NSEQ:(c + 1) * NSEQ].reshape(DEPTH, NSEQ * S, 256)
        m = dict(shared)
        m["xT"] = np.ascontiguousarray(xs_.T)
        m["pT"] = np.ascontiguousarray(ps_.transpose(0, 2, 1))
        maps.append(m)
    return maps


def kernel(**inputs):
    NSEQ = 2
    nc, kc, _ = get_program(NSEQ=NSEQ)
    maps = make_in_maps(inputs, NSEQ, NCORES)
    res = run_bass_kernel_spmd(nc, maps, core_ids=list(range(NCORES)))
    outs = []
    for c in range(NCORES):
        oT = np.asarray(res.results[c]["outT"], np.float32)
        outs.append(oT.T.reshape(NSEQ, S, D))
    return np.concatenate(outs, axis=0).astype(np.float32)
```

```python
import contextlib
import math
import numpy as np
import concourse.bass as bass
import concourse.mybir as mybir
from concourse.bass_utils import run_bass_kernel_spmd

F32 = mybir.dt.float32
BF16 = mybir.dt.bfloat16
AF = mybir.ActivationFunctionType
ALU = mybir.AluOpType
AX = mybir.AxisListType

D = 1024
S = 2048
DEPTH = 2
IN_COLS = 7176
FFN = 2816
EPS = 1e-6
NCORES = 8
BIG = 30000.0


class Op:
    __slots__ = ("eng", "fn", "deps", "sem", "inc", "val", "signal", "is_dma", "tag")

    def __init__(self, eng, fn, is_dma=False):
        self.eng = eng
        self.fn = fn
        self.deps = []
        self.sem = None
        self.inc = 1
        self.val = None
        self.signal = False
        self.is_dma = is_dma


class Prog:
    ENGS = ("pe", "act", "dve", "pool", "sp")

    def __init__(self, nc):
        self.nc = nc
        self.ops = {e: [] for e in self.ENGS}
        self.last_w = {}
        self.readers = {}
        self.n_ops = 0
        self.last_dma = {}
        self.bar = {e: None for e in self.ENGS}
        self.cur_tag = "pre"
        self.scopes = False

    def barrier(self):
        deps = []
        for e in self.ENGS:
            for o in reversed(self.ops[e]):
                if not o.is_dma:
                    deps.append(o)
                    break
        deps.extend(self.last_dma.values())
        for e in self.ENGS:
            self.bar[e] = deps
        self.last_w = {}
        self.readers = {}

    def _add(self, op, reads, writes):
        deps = []
        for k in reads:
            w = self.last_w.get(k)
            if w is not None:
                deps.append((w, False))
        for k in writes:
            w = self.last_w.get(k)
            if w is not None:
                deps.append((w, False))
            for r in self.readers.get(k, ()):
                deps.append((r, True))
        if self.bar[op.eng] is not None:
            for d in self.bar[op.eng]:
                deps.append((d, False))
            self.bar[op.eng] = None
        seen = set()
        for d, war in deps:
            if d is op or id(d) in seen:
                continue
            if d.eng == op.eng and not d.is_dma and not op.is_dma:
                if op.eng == "pe" or war:
                    continue
            seen.add(id(d))
            op.deps.append(d)
            d.signal = True
        for k in reads:
            self.readers.setdefault(k, []).append(op)
        for k in writes:
            self.last_w[k] = op
            self.readers[k] = []
        op.tag = self.cur_tag
        self.ops[op.eng].append(op)
        self.n_ops += 1

    @staticmethod
    def _is_psum(k):
        return isinstance(k, str) and k.startswith("ps") and k[2:].isdigit()

    def op(self, eng, fn, reads=(), writes=()):
        o = Op(eng, fn)
        o.sem = ("eng", eng)
        ex = [k for k in reads if self._is_psum(k)]
        if ex:
            reads = [k for k in reads if not self._is_psum(k)]
            writes = list(writes) + ex
        self._add(o, reads, writes)
        return o

    def dma(self, fn, semkey, reads=(), writes=(), q="sp"):
        o = Op(q, fn, is_dma=True)
        o.sem = ("dma", semkey)
        o.inc = 16
        o.signal = True
        self._add(o, reads, writes)
        self.last_dma[semkey] = o
        return o

    def emit(self, final_wait_ops=()):
        nc = self.nc
        counts = {}
        for e in self.ENGS:
            for o in self.ops[e]:
                if o.signal:
                    c = counts.get(o.sem, 0) + o.inc
                    counts[o.sem] = c
                    o.val = c
        semkeys = list(counts.keys())
        with contextlib.ExitStack() as st:
            sems = {}
            for i, k in enumerate(semkeys):
                sems[k] = st.enter_context(nc.semaphore("s%d" % i))
            block = st.enter_context(nc.Block())
            engmap = {"pe": block.tensor, "act": block.scalar, "dve": block.vector,
                      "pool": block.gpsimd, "sp": block.sync}

            def make(e):
                oplist = self.ops[e]

                def body(eng):
                    waited = {}
                    cur = None
                    cm = None
                    for o in oplist:
                        if self.scopes and o.tag != cur:
                            if cm is not None:
                                cm.__exit__(None, None, None)
                            cm = nc.named_scope(o.tag)
                            cm.__enter__()
                            cur = o.tag
                        for d in o.deps:
                            if waited.get(d.sem, 0) >= d.val:
                                continue
                            eng.wait_ge(sems[d.sem], d.val)
                            waited[d.sem] = d.val
                        ins = o.fn(eng)
                        if o.signal:
                            ins.then_inc(sems[o.sem], o.inc)
                    if cm is not None:
                        cm.__exit__(None, None, None)
                    if e == "sp":
                        for o in final_wait_ops:
                            if waited.get(o.sem, 0) >= o.val:
                                continue
                            eng.wait_ge(sems[o.sem], o.val)
                            waited[o.sem] = o.val

                return body

            for e in self.ENGS:
                if self.ops[e] or e == "sp":
                    engmap[e](make(e))
        return len(semkeys)


ARENA_F32 = 47 * 1024 + 512


import os as _os
CSTOP = int(_os.environ.get("C_STOP", "99"))
DSTOP = int(_os.environ.get("D_STOP", "99"))


class _Stop(Exception):
    pass


class KC:
    def __init__(self, nc, NSEQ, debug):
        self.nc = nc
        self.P = Prog(nc)
        self.NSEQ = NSEQ
        self.T = NSEQ * S
        self.NT = self.T // 512
        self.debug = debug
        self.st = contextlib.ExitStack()
        self.arena = self.st.enter_context(nc.sbuf_tensor("arena", [128, ARENA_F32], F32))
        self.arena_bf = self.arena[:, :].bitcast(BF16)
        self.psb = [self.st.enter_context(nc.psum_tensor("psb%d" % i, [128, 512], F32)) for i in range(8)]
        self.bump = 0
        self.perm = 0
        self.rr = 0
        self.finals = []
        self.dram = {}
        self.uid = 0

    def alloc(self, cols, dt=F32):
        if dt == BF16:
            w = (cols + 1) // 2
            a = self.bump
            self.bump += w
            assert self.bump <= ARENA_F32, "SBUF arena overflow %d" % self.bump
            return self.arena_bf[:, 2 * a:2 * a + cols]
        a = self.bump
        self.bump += cols
        assert self.bump <= ARENA_F32, "SBUF arena overflow %d" % self.bump
        return self.arena[:, a:a + cols]

    def make_perm(self):
        self.perm = self.bump

    def reset(self):
        self.P.barrier()
        self.bump = self.perm

    def key(self, base):
        self.uid += 1
        return "%s#%d" % (base, self.uid)

    def din(self, name, shape, dt=F32):
        t = self.nc.dram_tensor(name, list(shape), dt, kind="ExternalInput").ap()
        self.dram[name] = t
        return t

    def dscr(self, name, shape, dt=F32, out=False):
        kind = "ExternalOutput" if (out or self.debug) else "Internal"
        t = self.nc.dram_tensor(name, list(shape), dt, kind=kind).ap()
        self.dram[name] = t
        return t

    def ld(self, dst, src, key, reads=(), q="sp", semkey=None, slow=False):
        if slow:
            return self.P.dma(lambda e: e.dma_start(out=dst, in_=src, allow_slow_non_contiguous=True), semkey or key,
                              reads=reads, writes=[key], q=q)
        return self.P.dma(lambda e: e.dma_start(out=dst, in_=src), semkey or key, reads=reads, writes=[key], q=q)

    def stt(self, dst, src, srckey, writes=(), q="sp", semkey=None):
        keys = list(srckey) if isinstance(srckey, list) else [srckey]
        sk = semkey or (str(keys[0]) + "_st")
        return self.P.dma(lambda e: e.dma_start(out=dst, in_=src), sk, reads=keys, writes=writes, q=q)

    def dump(self, name, ap, keys, dt=F32):
        if not self.debug:
            return
        shape = list(ap.shape)
        d = self.nc.dram_tensor("dbg_" + name, shape, dt, kind="ExternalOutput").ap()
        self.finals.append(self.P.dma(lambda e: e.dma_start(out=d, in_=ap), "dump_" + name, reads=list(keys)))

    def store(self, dst, src, srckey, writes=(), q="sp", semkey=None):
        return self.stt(dst, src, srckey, writes, q, semkey)

    def act(self, out, in_, func, reads, writes, bias=None, scale=None, accum=None):
        kw = {}
        if bias is not None:
            kw["bias"] = bias
        if scale is not None:
            kw["scale"] = scale
        if accum is not None:
            kw["accum_out"] = accum
        return self.P.op("act", lambda e: e.activation(out=out, in_=in_, func=func, **kw), reads=reads, writes=writes)

    def tt(self, eng, out, in0, in1, op, reads, writes):
        return self.P.op(eng, lambda e: e.tensor_tensor(out, in0, in1, op), reads=reads, writes=writes)

    def ts(self, eng, out, in0, s1, s2, op0, op1, reads, writes):
        if s2 is None:
            return self.P.op(eng, lambda e: e.tensor_scalar(out, in0, s1, None, op0), reads=reads, writes=writes)
        return self.P.op(eng, lambda e: e.tensor_scalar(out, in0, s1, s2, op0, op1), reads=reads, writes=writes)

    def sto(self, eng, out, in0, scalar, in1, op0, op1, reads, writes):
        return self.P.op(eng, lambda e: e.scalar_tensor_tensor(out=out, in0=in0, scalar=scalar, in1=in1, op0=op0,
                                                               op1=op1), reads=reads, writes=writes)

    def copy(self, eng, out, in_, reads, writes):
        if eng == "act":
            return self.P.op("act", lambda e: e.copy(out, in_), reads=reads, writes=writes)
        return self.P.op(eng, lambda e: e.tensor_copy(out, in_), reads=reads, writes=writes)

    def memset(self, eng, ap, val, writes):
        return self.P.op(eng, lambda e: e.memset(ap, val), writes=writes)

    def recip(self, out, in_, reads, writes):
        return self.P.op("dve", lambda e: e.reciprocal(out, in_), reads=reads, writes=writes)

    def transpose(self, out, in_, ident, reads, writes):
        return self.P.op("pe", lambda e: e.transpose(out, in_, ident), reads=reads, writes=writes)

    def mm(self, out, lhsT, rhs, start, stop, reads, writes):
        return self.P.op("pe", lambda e: e.matmul(out, lhsT, rhs, start=start, stop=stop), reads=reads, writes=writes)

    def bank(self, n=8, base=0):
        i = base + (self.rr % n)
        self.rr += 1
        return i, self.psb[i], "ps%d" % i


def r3(ap, a):
    return ap.rearrange("p (a b) -> p a b", a=a)


V_NMIX = 0
V_NFFN = 16
V_NPLE = 32
V_NFIN = 48
V_PSCALE = 56
V_CW = 64
V_FCW = 160
V_DNW = 424
V_EPS = 426
V_ONE = 427
NVEC = 428
WIN_GROUP_C0 = [0, 512, 1024, 1536, 2048, 2568, 3080, 3592] + [4104 + 512 * i for i in range(6)]
WIN_GROUP_DST = [("u", None, 0), ("dqkv", None, 0), ("dqkv", None, 512), ("dqkv", None, 1024), ("dz", None, 0),
                 ("mq", None, 0), ("mk", None, 0), ("mv", None, 0)] + [("gate", None, 512 * i) for i in range(6)]

C_POOL = 0
C_DQKV = 512
C_DZ = 2048
C_DB = 2560
C_DA = 2564
C_MQ = 2568
C_MK = 2568 + 512
C_MV = 2568 + 1024
C_GATE = 4104


def build(NSEQ=2, debug=False, layers=(0, 1), stages="ABCDEF", final=True, scopes=False):
    nc = bass.Bass("TRN2", target_bir_lowering=False)
    kc = KC(nc, NSEQ, debug)
    P = kc.P
    P.scopes = scopes
    T = kc.T
    NT = kc.NT
    xT_in = kc.din("xT", [D, T])
    pT_in = kc.din("pT", [DEPTH, 256, T])
    w_in = kc.din("w_in", [DEPTH, D, IN_COLS])
    w_branch = kc.din("w_branch", [DEPTH, 3, 512, D])
    w_out = kc.din("w_out", [DEPTH, D, D])
    ffn_up = kc.din("ffn_up", [DEPTH, D, 2 * FFN])
    ffn_down = kc.din("ffn_down", [DEPTH, FFN, D])
    ple_gate = kc.din("ple_gate", [DEPTH, D, D])
    ple_proj = kc.din("ple_proj", [DEPTH, 256, D])
    pool_w = kc.din("pool_w", [DEPTH, 4, 128, 128])
    vecs_d = kc.din("vecs", [128, NVEC])
    ident_d = kc.din("ident", [128, 128])
    tri_d = kc.din("tri", [64, 4 * 64])
    rcnt_d = kc.din("rcnt", [128, 64])
    band_d = kc.din("band", [128, 8 * 1152])
    ind_d = kc.din("ind", [128, 64 * 128])
    a4_d = kc.din("a4", [4, 4])
    outT = kc.dscr("outT", [D, T], out=True)
    xs = kc.dscr("xs", [D, T])
    x1s = kc.dscr("x1s", [D, T])
    uT = kc.dscr("uT", [512, T])
    dqkvT = kc.dscr("dqkvT", [1536, T])
    dzT = kc.dscr("dzT", [512, T])
    dbg = kc.dscr("dbg", [8, T])
    mqT = kc.dscr("mqT", [512, T], BF16)
    mkT = kc.dscr("mkT", [512, T], BF16)
    mv = kc.dscr("mv", [T, 512], BF16)
    gatesT = kc.dscr("gatesT", [3072, T], BF16)
    ybT = kc.dscr("ybT", [1536, T], BF16)
    NG_IN = 15
    winb = kc.dscr("winb", [DEPTH, 15, 128, 8 * 512], BF16)
    wbrb = kc.dscr("wbrb", [DEPTH, 128, 12 * 1024], BF16)
    woutb = kc.dscr("woutb", [DEPTH, 128, 8 * 1024], BF16)
    wupb = kc.dscr("wupb", [DEPTH, 22, 128, 8 * 256], BF16)
    wdnb = kc.dscr("wdnb", [DEPTH, 128, 22 * 1024], BF16)
    pgb = kc.dscr("pgb", [DEPTH, 128, 8 * 1024], BF16)
    ppb = kc.dscr("ppb", [DEPTH, 128, 2 * 1024], BF16)
    pwb = kc.dscr("pwb", [DEPTH, 128, 4 * 128], BF16)

    vecs = kc.alloc(NVEC)
    ident = kc.alloc(128)
    ones_bf = kc.alloc(128, BF16)
    ones_f = kc.alloc(128)
    kc.ld(vecs, vecs_d, "vecs")
    kc.ld(ident, ident_d, "ident")
    P.op("pool", lambda e: e.memset(ones_bf, 1.0), writes=["ones_bf"])
    P.op("pool", lambda e: e.memset(ones_f, 1.0), writes=["ones_f"])
    kc.make_perm()
    CONST = ["vecs", "ident", "ones_bf", "ones_f"]

    def cast(dst, src, key):
        grp = ("CAST0" if key[0] == "winb" else "CAST1") + ("" if key[1] == layers[0] else "b")
        P.dma(lambda e: e.dma_start(out=dst, in_=src), grp, writes=[grp], q="pool")

    for l in layers:
        for g in range(14):
            c0 = WIN_GROUP_C0[g]
            cast(winb[l, g].rearrange("p (k c) -> p k c", k=8),
                 w_in[l, :, c0:c0 + 512].rearrange("(k p) c -> p k c", p=128), ("winb", l, g))
        cast(winb[l, 14].rearrange("p (k c) -> p k c", k=8)[:, :, 0:8],
             w_in[l, :, C_DB:C_DB + 8].rearrange("(k p) c -> p k c", p=128), ("winb", l, 14))
        cast(pwb[l].rearrange("p (g d) -> p g d", g=4), pool_w[l].rearrange("g c d -> c g d"), ("pwb", l))
        for n in range(3):
            cast(wbrb[l].rearrange("p (n k d) -> p n k d", n=3, k=4)[:, n],
                 w_branch[l, n].rearrange("(k p) d -> p k d", p=128), ("wbrb", l, n))
        cast(woutb[l].rearrange("p (k d) -> p k d", k=8), w_out[l].rearrange("(k p) d -> p k d", p=128), ("woutb", l))
        for j in range(22):
            dstv = wupb[l, j].rearrange("p (k c) -> p k c", k=8)
            cast(dstv[:, :, 0:128], ffn_up[l, :, j * 128:(j + 1) * 128].rearrange("(k p) c -> p k c", p=128),
                 ("wupb", l, j, 0))
            cast(dstv[:, :, 128:256],
                 ffn_up[l, :, FFN + j * 128:FFN + (j + 1) * 128].rearrange("(k p) c -> p k c", p=128),
                 ("wupb", l, j, 1))
        cast(wdnb[l].rearrange("p (j d) -> p j d", j=22), ffn_down[l].rearrange("(j p) d -> p j d", p=128),
             ("wdnb", l))
        cast(pgb[l].rearrange("p (k d) -> p k d", k=8), ple_gate[l].rearrange("(k p) d -> p k d", p=128), ("pgb", l))
        cast(ppb[l].rearrange("p (k d) -> p k d", k=2), ple_proj[l].rearrange("(k p) d -> p k d", p=128), ("ppb", l))

    def vcol(i):
        return vecs[:, i:i + 1]

    def rmsnorm_tile(xt, xkey, nbase, out_fn, outkeys, sqt, sqkey, rstd, rkey, engs=("dve",)):
        P.op("act", lambda e: e.activation(out=sqt, in_=xt, func=AF.Square), reads=[xkey], writes=[sqkey])
        bi, pb, pk = kc.bank()
        for k in range(8):
            kc.mm(pb[:, :], ones_bf, sqt[:, k * 512:(k + 1) * 512], k == 0, k == 7, [sqkey, "ones_bf"], [pk])
        P.op("act", lambda e: e.activation(out=rstd, in_=pb[:, :], func=AF.Ln, bias=vcol(V_EPS), scale=1.0 / D),
             reads=[pk, "vecs"], writes=[rkey])
        P.op("act", lambda e: e.activation(out=rstd, in_=rstd, func=AF.Exp, scale=-0.5), reads=[rkey], writes=[rkey])
        for k in range(8):
            o = out_fn(k)
            eng = engs[k % len(engs)]
            P.op(eng, (lambda o=o, k=k: lambda e: e.scalar_tensor_tensor(
                out=o, in0=xt[:, k * 512:(k + 1) * 512], scalar=vcol(nbase + k), in1=rstd,
                op0=ALU.mult, op1=ALU.mult))(), reads=[xkey, rkey, "vecs"], writes=[outkeys[k]])

    stages0 = stages
    for l in layers:
        stages = stages0 if l == layers[0] else _os.environ.get("L1S", stages0)
        xsrc = xT_in if l == 0 else xs
        xsrc_key = "xs"
        if "A" in stages:
            P.cur_tag = "L%dA" % l
            hT = kc.alloc(8 * T, BF16)
            hT3 = r3(hT, 8)
            xtb = [kc.alloc(8 * 512) for _ in range(2)]
            sqt = kc.alloc(8 * 512, BF16)
            rstd = kc.alloc(512)
            for tt in range(NT):
                xt = xtb[tt % 2]
                xk = "A_xt%d" % (tt % 2)
                kc.ld(r3(xt, 8), xsrc[:, tt * 512:(tt + 1) * 512].rearrange("(k p) t -> p k t", p=128), xk,
                      reads=[xsrc_key])
                rmsnorm_tile(xt, xk, V_NMIX + l * 8, lambda k: hT3[:, k, tt * 512:(tt + 1) * 512],
                             [("hT", tt, k) for k in range(8)], sqt, "A_sq", rstd, "A_rstd")
            HT_ALL = [("hT", tt) for tt in range(NT)]
            wgb = [kc.alloc(8 * 512, BF16) for _ in range(2)]
            ob32 = [kc.alloc(T) for _ in range(2)]
            ob16 = [kc.alloc(T, BF16) for _ in range(2)]
            ovb = [kc.alloc(512, BF16) for _ in range(2)]
            cnt = {"o32": 0, "o16": 0, "ov": 0, "ev": 0}

            def evac(kind, dst, src, pk, okey):
                if kind in ("u", "dqkv"):
                    eng = "act" if cnt["ev"] % 2 == 0 else "dve"
                    cnt["ev"] += 1
                    if eng == "act":
                        P.op("act", lambda e: e.copy(dst, src), reads=[pk], writes=[okey])
                    else:
                        P.op("dve", lambda e: e.tensor_copy(dst, src), reads=[pk], writes=[okey])
                elif kind == "dz":
                    P.op("act", lambda e: e.activation(out=dst, in_=src, func=AF.Silu), reads=[pk], writes=[okey])
                elif kind == "mq":
                    P.op("dve", lambda e: e.tensor_scalar(dst, src, 0.125, None, ALU.mult), reads=[pk], writes=[okey])
                elif kind == "mk":
                    P.op("dve", lambda e: e.tensor_copy(dst, src), reads=[pk], writes=[okey])
                elif kind == "gate":
                    P.op("act", lambda e: e.activation(out=dst, in_=src, func=AF.Sigmoid), reads=[pk], writes=[okey])
                else:
                    raise ValueError(kind)

            for g in range(14):
                wg = wgb[g % 2]
                wk = "A_wg%d" % (g % 2)
                kc.ld(wg, winb[l, g], wk, reads=["CAST0b"] if l != layers[0] else ["CAST0"])
                wg3 = r3(wg, 8)
                gkind, gdst, grow0 = WIN_GROUP_DST[g]
                if gkind == "mv":
                    for i in range(T // 128):
                        bi, pb, pk = kc.bank()
                        for k in range(8):
                            kc.mm(pb[:, :], hT3[:, k, i * 128:(i + 1) * 128], wg3[:, k, :], k == 0, k == 7,
                                  [("hT", i // 4, k), wk], [pk])
                        ov = ovb[cnt["ov"] % 2]
                        ok = "A_ov%d" % (cnt["ov"] % 2)
                        cnt["ov"] += 1
                        eng = "act" if i % 2 == 0 else "dve"
                        if eng == "act":
                            P.op("act", (lambda ov=ov, pb=pb: lambda e: e.copy(ov, pb[:, :]))(), reads=[pk], writes=[ok])
                        else:
                            P.op("dve", (lambda ov=ov, pb=pb: lambda e: e.tensor_copy(ov, pb[:, :]))(), reads=[pk],
                                 writes=[ok])
                        kc.stt(mv[i * 128:(i + 1) * 128, :], ov, ok, writes=["mv"])
                    continue
                for j in range(4):
                    is16 = gkind in ("mq", "mk", "gate")
                    if is16:
                        ob = ob16[cnt["o16"] % 2]
                        okey = "A_o16_%d" % (cnt["o16"] % 2)
                        cnt["o16"] += 1
                    else:
                        ob = ob32[cnt["o32"] % 2]
                        okey = "A_o32_%d" % (cnt["o32"] % 2)
                        cnt["o32"] += 1
                    for tt in range(NT):
                        bi, pb, pk = kc.bank()
                        for k in range(8):
                            kc.mm(pb[:, :], wg3[:, k, j * 128:(j + 1) * 128], hT3[:, k, tt * 512:(tt + 1) * 512],
                                  k == 0, k == 7, [("hT", tt, k), wk], [pk])
                        evac(gkind, ob[:, tt * 512:(tt + 1) * 512], pb[:, :], pk, (okey, tt))
                    dst = {"u": uT, "dqkv": dqkvT, "dz": dzT, "mq": mqT, "mk": mkT, "gate": gatesT}[gkind]
                    r0 = grow0 + j * 128
                    kc.stt(dst[r0:r0 + 128, :], ob, [(okey, tt) for tt in range(NT)], writes=[gkind + "_d"], semkey=okey + "_st")
            wg = wgb[0]
            wk = "A_wg0"
            kc.ld(wg, winb[l, 14], wk, reads=["CAST0b"] if l != layers[0] else ["CAST0"])
            wg3 = r3(wg, 8)
            a4 = kc.alloc(4)
            P_a4 = kc.ld(a4[0:4, :], a4_d, "A_a4")
            nexpA = kc.alloc(1)
            kc.act(nexpA[0:4, :], a4[0:4, 2 * l:2 * l + 1], AF.Exp, ["A_a4"], ["A_nexpA"])
            kc.ts("dve", nexpA[0:4, :], nexpA[0:4, :], -1.0, None, ALU.mult, None, ["A_nexpA"], ["A_nexpA"])
            obb = ob32[0]
            oba = ob32[1]
            for tt in range(NT):
                sl = slice(tt * 512, (tt + 1) * 512)
                bi, pb, pk = kc.bank()
                for k in range(8):
                    kc.mm(pb[0:4, :], wg3[:, k, 0:4], hT3[:, k, sl], k == 0, k == 7, [("hT", tt, k), wk], [pk])
                kc.act(obb[0:4, sl], pb[0:4, :], AF.Sigmoid, [pk], [("A_o32_0", tt)])
                bi, pb, pk = kc.bank()
                for k in range(8):
                    kc.mm(pb[0:4, :], wg3[:, k, 4:8], hT3[:, k, sl], k == 0, k == 7, [("hT", tt, k), wk], [pk])
                kc.act(oba[0:4, sl], pb[0:4, :], AF.Exp, [pk, "A_a4"], [("A_o32_1", tt)],
                       bias=a4[0:4, 2 * l + 1:2 * l + 2])
            AK1 = [("A_o32_1", tt) for tt in range(NT)]
            kc.act(oba[0:4, :], oba[0:4, :], AF.Ln, AK1 + ["vecs"], AK1, bias=vcol(V_ONE)[0:4, :])
            kc.ts("dve", oba[0:4, :], oba[0:4, :], nexpA[0:4, 0:1], None, ALU.mult, None, AK1 + ["A_nexpA"], AK1)
            kc.stt(dbg[0:4, :], obb[0:4, :], [("A_o32_0", tt) for tt in range(NT)], writes=["dbg"], semkey="A_o32_0_st")
            kc.stt(dbg[4:8, :], oba[0:4, :], [("A_o32_1", tt) for tt in range(NT)], writes=["dbg"], semkey="A_o32_1_st")
            kc.reset()
        if "B" in stages:
            P.cur_tag = "L%dB" % l
            pw = kc.alloc(4 * 128, BF16)
            kc.ld(pw, pwb[l], "B_pw", reads=["CAST1" if l == layers[0] else "CAST1b"])
            pw3 = r3(pw, 4)
            rc = kc.alloc(64)
            kc.ld(rc, rcnt_d, "B_rc")
            ub = [kc.alloc(16 + S) for _ in range(2)]
            sab = [kc.alloc(16 + S) for _ in range(2)]
            mxb = [kc.alloc(S, BF16) for _ in range(2)]
            t16 = kc.alloc(16)
            yob = [kc.alloc(S, BF16) for _ in range(2)]
            for i in range(2):
                kc.memset("pool", ub[i][:, 0:16], 0.0, ["B_u%dz" % i])
                kc.memset("pool", sab[i][:, 0:16], 0.0, ["B_s%dz" % i])
            it = 0
            for s_ in range(NSEQ):
                for g in range(4):
                    i2 = it % 2
                    u = ub[i2]
                    uk = "B_u%d" % i2
                    kc.ld(u[:, 16:], uT[g * 128:(g + 1) * 128, s_ * S:(s_ + 1) * S], uk)
                    cur, curk = u, uk
                    for j in range(g + 1):
                        dst = sab[j % 2]
                        dk = "B_s%d" % (j % 2)
                        sh = 1 << j
                        kc.tt("dve" if j % 2 == 0 else "pool", dst[:, 16:], cur[:, 16:], cur[:, 16 - sh:16 - sh + S],
                              ALU.add, [curk, curk + "z"], [dk])
                        cur, curk = dst, dk
                    w = 1 << (g + 1)
                    m = mxb[i2]
                    mk_ = "B_mx%d" % i2
                    kc.sto("dve", m, cur[:, 16:], 1.0 / w, u[:, 16:], ALU.mult, ALU.subtract, [curk, uk], [mk_])
                    kc.tt("dve", t16, cur[:, 16:32], rc[:, g * 16:(g + 1) * 16], ALU.mult, [curk, "B_rc"], ["B_t16"])
                    kc.tt("dve", m[:, 0:16], t16, u[:, 16:32], ALU.subtract, ["B_t16", uk, mk_], [mk_])
                    y = yob[i2]
                    yk = "B_y%d" % i2
                    for j in range(4):
                        bi, pb, pk = kc.bank()
                        kc.mm(pb[:, :], pw3[:, g, :], m[:, j * 512:(j + 1) * 512], True, True, [mk_, "B_pw"], [pk])
                        kc.ts("dve" if j % 2 == 0 else "dve", y[:, j * 512:(j + 1) * 512], pb[:, :],
                              vcol(V_PSCALE + l * 4 + g), None, ALU.mult, None, [pk, "vecs"], [yk])
                    kc.store(ybT[g * 128:(g + 1) * 128, s_ * S:(s_ + 1) * S], y, yk, writes=["ybT"])
                    it += 1
            kc.reset()

        if "C" in stages:
            P.cur_tag = "L%dC" % l
            tri = kc.alloc(256)
            kc.ld(tri[0:64, :], tri_d, "C_tri")
            Ut = tri[0:64, 0:64]
            mA = tri[0:64, 64:128]
            mB = tri[0:64, 128:192]
            SU = tri[0:64, 192:256]
            identb3 = ident[0:64, 0:64].rearrange("p (o j) -> p o j", o=1).to_broadcast([64, 8, 64])
            raw = kc.alloc(3 + S)
            Xb = raw[:, 3:3 + S]
            acc = kc.alloc(S)
            sqb = kc.alloc(S, BF16)
            rn = kc.alloc(512)
            khb = kc.alloc(S, BF16)
            qhb = kc.alloc(S, BF16)
            gcrow = kc.alloc(S)
            E1 = kc.alloc(S)
            brow = kc.alloc(S)
            gT = kc.alloc(32)
            bT = kc.alloc(32)
            gcc = kc.alloc(32)
            egd = kc.alloc(32)
            Dm = kc.alloc(512)
            Gb = kc.alloc(512)
            GTi = kc.alloc(512)
            GTb = kc.alloc(512)
            tmpD = kc.alloc(512)
            Qk = [kc.alloc(512) for _ in range(2)]
            Rk = [kc.alloc(512) for _ in range(2)]
            Gk = [kc.alloc(512) for _ in range(2)]
            SETS = []
            for i_ in range(2):
                SETS.append(dict(i=i_, keT=kc.alloc(S), qdT=kc.alloc(S), kdec=kc.alloc(32 * 128, BF16),
                                 vtok=kc.alloc(32 * 128, BF16), aqkT=kc.alloc(S, BF16), TTb=kc.alloc(S, BF16),
                                 egl=kc.alloc(32)))
            oT = kc.alloc(S)
            S_ = kc.alloc(128)
            Rb = kc.alloc(128, BF16)
            vn = kc.alloc(128, BF16)
            zsb = [kc.alloc(512) for _ in range(2)]
            yout = kc.alloc(S, BF16)
            sqb2 = kc.alloc(S, BF16)
            rn2 = kc.alloc(512)
            kc.memset("pool", raw[:, 0:3], 0.0, ["C_rawz"])
            identP = kc.alloc(64)
            bTe = kc.alloc(16)
            bTo = kc.alloc(16)
            kc.copy("dve", identP[0:64, :], ident[0:64, 0:64], ["ident"], ["C_identP"])
            kc.copy("dve", identP[64:128, :], ident[64:128, 64:128], ["ident"], ["C_identP"])

            def bc_mid(ap64, n):
                return ap64.rearrange("p (o j) -> p o j", o=1).to_broadcast([64, n, 64])

            def bc_last(ap, n, w):
                return ap.rearrange("p (n o) -> p n o", o=1).to_broadcast([64, n, w])

            def phaseA(s_, h, st):
                t0 = s_ * S
                si = st["i"]
                kK = lambda nm: "C_%s%d" % (nm, si)
                keT, qdT, kdec, vtok, aqkT, TTb, egl = (st[k] for k in ("keT", "qdT", "kdec", "vtok", "aqkT", "TTb", "egl"))
                kc.ld(gcrow[0:64, :], dbg[4 + h:5 + h, t0:t0 + S].partition_broadcast(64), "C_gcrow")
                kc.ld(brow[0:64, :], dbg[h:h + 1, t0:t0 + S].partition_broadcast(64), "C_brow")
                idb32 = ident[0:64, 0:64].rearrange("p (o j) -> p o j", o=1).to_broadcast([64, 32, 64])
                kc.tt("dve", r3(Xb[0:64, :], 32), r3(gcrow[0:64, :], 32), idb32, ALU.mult, ["C_gcrow", "ident"], ["C_raw"])
                P.op("dve", (lambda: lambda e: e.tensor_reduce(gT[0:64, :], r3(Xb[0:64, :], 32), AX.X, ALU.add))(),
                     reads=["C_raw"], writes=["C_gT"])
                kc.tt("dve", r3(Xb[0:64, :], 32), r3(brow[0:64, :], 32), idb32, ALU.mult, ["C_brow", "ident"], ["C_raw"])
                P.op("dve", (lambda: lambda e: e.tensor_reduce(bT[0:64, :], r3(Xb[0:64, :], 32), AX.X, ALU.add))(),
                     reads=["C_raw"], writes=["C_bT"])
                yield
                bi, pb, pk = kc.bank()
                kc.mm(pb[0:64, 0:32], Ut, gT[0:64, :], True, True, ["C_tri", "C_gT"], [pk])
                kc.copy("act", gcc[0:64, :], pb[0:64, 0:32], [pk], ["C_gcc"])
                bi, pb, pk = kc.bank()
                kc.mm(pb[:, 0:32], ones_f[0:64, 0:128], gT[0:64, :], True, True, ["ones_f", "C_gT"], [pk])
                kc.act(egl[:, :], pb[:, 0:32], AF.Exp, [pk], [kK("egl")])
                kc.tt("dve", egd[0:64, :], pb[0:64, 0:32], gcc[0:64, :], ALU.subtract, [pk, "C_gcc"], ["C_egd"])
                kc.act(egd[0:64, :], egd[0:64, :], AF.Exp, ["C_egd"], ["C_egd"])
                kc.tt("dve", r3(Xb[0:64, :], 32), bc_last(gT[0:64, :], 32, 64), bc_mid(Ut, 32), ALU.mult,
                      ["C_gT", "C_tri"], ["C_raw"])
                yield
                for q4 in range(4):
                    sl = slice(q4 * 512, (q4 + 1) * 512)
                    bi, pb, pk = kc.bank()
                    kc.mm(pb[:, :], ones_f[0:64, 0:128], Xb[0:64, sl], True, True, ["ones_f", "C_raw"], [pk])
                    kc.copy("dve", gcrow[:, sl], pb[:, :], [pk], ["C_gcrow"])
                    kc.act(E1[:, sl], pb[:, :], AF.Exp, [pk], ["C_E1"])
                    yield

                def conv_silu(comp):
                    blk = comp * 4 + h
                    kc.ld(raw[:, 3:3 + S], dqkvT[blk * 128:(blk + 1) * 128, t0:t0 + S], "C_raw")
                    cwi = V_CW + (l * 12 + blk) * 4
                    kc.ts("dve", acc, raw[:, 0:S], vcol(cwi), None, ALU.mult, None, ["C_raw", "C_rawz", "vecs"], ["C_acc"])
                    for j in range(1, 4):
                        kc.sto("dve", acc, raw[:, j:j + S], vcol(cwi + j), acc, ALU.mult, ALU.add,
                               ["C_raw", "C_rawz", "C_acc", "vecs"], ["C_acc"])
                    kc.act(acc, acc, AF.Silu, ["C_acc"], ["C_acc"])

                def l2n_slice(q4, scale):
                    sl = slice(q4 * 512, (q4 + 1) * 512)
                    bi, pb, pk = kc.bank()
                    kc.mm(pb[:, :], ones_bf, sqb[:, sl], True, True, ["ones_bf", "C_sqb"], [pk])
                    kc.act(rn, pb[:, :], AF.Ln, [pk, "vecs"], ["C_rn"], bias=vcol(V_EPS))
                    kc.act(rn, rn, AF.Exp, ["C_rn"], ["C_rn"], scale=-0.5)
                    kc.sto("dve", acc[:, sl], acc[:, sl], scale, rn, ALU.mult, ALU.mult, ["C_acc", "C_rn"], ["C_acc"])

                def to_tok4(n4, dst, dkey, mul_egd):
                    bi, pb, pk = kc.bank()
                    for c in range(4):
                        n = n4 * 4 + c
                        kc.transpose(pb[0:64, c * 128:(c + 1) * 128], acc[:, n * 64:(n + 1) * 64], ident,
                                     ["C_acc", "ident"], [pk])
                    dsl = dst[0:64, n4 * 512:(n4 + 1) * 512]
                    if mul_egd:
                        kc.tt("dve", r3(dsl, 4), r3(pb[0:64, :], 4), bc_last(egd[0:64, n4 * 4:n4 * 4 + 4], 4, 128),
                              ALU.mult, [pk, "C_egd"], [dkey])
                    else:
                        kc.copy("act", dsl, pb[0:64, :], [pk], [dkey])

                conv_silu(1)
                yield
                kc.act(sqb, acc, AF.Square, ["C_acc"], ["C_sqb"])
                for q4 in range(4):
                    l2n_slice(q4, 1.0)
                    yield
                kc.copy("act", khb, acc, ["C_acc"], ["C_khb"])
                kc.tt("dve", keT, acc, E1, ALU.mult, ["C_acc", "C_E1"], [kK("keT")])
                yield
                for n4 in range(8):
                    to_tok4(n4, kdec, kK("kdec"), True)
                    yield
                conv_silu(0)
                yield
                kc.act(sqb, acc, AF.Square, ["C_acc"], ["C_sqb"])
                for q4 in range(4):
                    l2n_slice(q4, 128.0 ** -0.5)
                    yield
                kc.copy("act", qhb, acc, ["C_acc"], ["C_qhb"])
                kc.tt("dve", qdT, acc, E1, ALU.mult, ["C_acc", "C_E1"], [kK("qdT")])
                yield
                conv_silu(2)
                yield
                for n4 in range(8):
                    to_tok4(n4, vtok, kK("vtok"), False)
                    yield

                kc.copy("dve", bTe[0:64, :], bT[0:64, :].rearrange("p (m two) -> p m two", two=2)[:, :, 0], ["C_bT"],
                        ["C_bTe"])
                kc.copy("dve", bTo[0:64, :], bT[0:64, :].rearrange("p (m two) -> p m two", two=2)[:, :, 1], ["C_bT"],
                        ["C_bTo"])

                def v4(ap):
                    return ap.rearrange("p (m two f) -> p m two f", m=4, two=2)

                for bt in range(4):
                    sb, lb = bt // 2, bt % 2
                    n0 = bt * 8
                    c0 = bt * 512
                    bsl = slice(c0, c0 + 512)
                    kc.tt("dve", r3(Dm[0:64, :], 8), r3(gcrow[0:64, bsl], 8), bc_last(gcc[0:64, n0:n0 + 8], 8, 64),
                          ALU.subtract, ["C_gcrow", "C_gcc"], ["C_Dm"])
                    kc.tt("pool", r3(tmpD[0:64, :], 8), r3(Dm[0:64, :], 8), bc_mid(mA, 8), ALU.add,
                          ["C_Dm", "C_tri"], ["C_tmpD"])
                    kc.act(GTi[0:64, :], tmpD[0:64, :], AF.Exp, ["C_tmpD"], ["C_GTi"])
                    kc.tt("dve", r3(tmpD[0:64, :], 8), bc_mid(mB, 8), r3(Dm[0:64, :], 8), ALU.subtract,
                          ["C_Dm", "C_tri", "C_tmpD"], ["C_tmpD"])
                    kc.act(Gb[0:64, :], tmpD[0:64, :], AF.Exp, ["C_tmpD"], ["C_Gb"])
                    kc.tt("dve", r3(Gb[0:64, :], 8), r3(Gb[0:64, :], 8), bc_last(bT[0:64, n0:n0 + 8], 8, 64),
                          ALU.mult, ["C_Gb", "C_bT"], ["C_Gb"])
                    kc.tt("pool", r3(GTb[0:64, :], 8), r3(GTi[0:64, :], 8), bc_mid(SU, 8), ALU.mult,
                          ["C_GTi", "C_tri"], ["C_GTb"])
                    kc.tt("pool", GTb[0:64, :], GTb[0:64, :], brow[0:64, bsl], ALU.mult, ["C_GTb", "C_brow"],
                          ["C_GTb"])
                    yield
                    Q, R, G = Qk[sb], Rk[sb], Gk[sb]
                    qk_, rk_, gk_ = "C_Q%d" % sb, "C_R%d" % sb, "C_G%d" % sb
                    lsl = slice(lb * 256, (lb + 1) * 256)
                    bi, pb, pk = kc.bank()
                    for c in range(8):
                        cs = slice((n0 + c) * 64, (n0 + c + 1) * 64)
                        kc.mm(pb[0:64, c * 64:(c + 1) * 64], khb[:, cs], khb[:, cs], True, True, ["C_khb"], [pk])
                    pv_ = v4(pb[0:64, :])
                    for par in range(2):
                        psl = slice(par * 64, (par + 1) * 64)
                        kc.tt("dve", r3(R[psl, lsl], 4), pv_[:, :, par, :], v4(Gb[0:64, :])[:, :, par, :], ALU.mult,
                              [pk, "C_Gb"], [(rk_, lb, par)])
                        kc.tt("dve", r3(Q[psl, lsl], 4), pv_[:, :, par, :], v4(GTb[0:64, :])[:, :, par, :], ALU.mult,
                              [pk, "C_GTb"], [(qk_, lb, par)])
                    bi, pb, pk = kc.bank()
                    for c in range(8):
                        cs = slice((n0 + c) * 64, (n0 + c + 1) * 64)
                        kc.mm(pb[0:64, c * 64:(c + 1) * 64], khb[:, cs], qhb[:, cs], True, True,
                              ["C_khb", "C_qhb"], [pk])
                    kc.tt("dve", aqkT[0:64, bsl], pb[0:64, :], GTi[0:64, :], ALU.mult, [pk, "C_GTi"], [kK("aqkT")])
                    kc.tt("pool", r3(G[:, lsl], 4), identP.rearrange("p (o j) -> p o j", o=1).to_broadcast([128, 4, 64]),
                          r3(Q[:, lsl], 4), ALU.subtract, ["C_identP", (qk_, lb, 0), (qk_, lb, 1)], [(gk_, lb)])
                    yield
                QRK = lambda nm: [(nm, lb_, par_) for lb_ in range(2) for par_ in range(2)]
                for lev in range(1, 6):
                    for sb in range(2):
                        Q, R, G = Qk[sb], Rk[sb], Gk[sb]
                        qk_, rk_, gk_ = "C_Q%d" % sb, "C_R%d" % sb, "C_G%d" % sb
                        qks = QRK(qk_) if lev == 1 else [qk_]
                        rks = QRK(rk_) if lev == 1 else [rk_]
                        gks = [(gk_, 0), (gk_, 1)] if lev == 1 else [gk_]

                        def pairmm(pbx, pkx, A_, B_, rd):
                            for m in range(8):
                                ms = slice(m * 64, (m + 1) * 64)
                                kc.mm(pbx[0:64, ms], A_[0:64, ms], B_[0:64, ms], True, True, rd, [pkx])
                                P.op("pe", (lambda o=pbx[64:128, ms], a_=A_[64:128, ms], b_=B_[64:128, ms]:
                                            lambda e: e.matmul(o, a_, b_, start=True, stop=True, tile_position=(64, 64)))(),
                                     reads=rd, writes=[pkx])

                        if lev < 5:
                            bq, pbq, pkq = kc.bank()
                            pairmm(pbq, pkq, R, Q, qks + rks)
                        br, pbr, pkr = kc.bank()
                        pairmm(pbr, pkr, Q, R, qks + rks)
                        if lev < 5:
                            kc.copy("act", Q[:, :], pbq[:, :], [pkq], [qk_] + QRK(qk_))
                        kc.copy("act", R[:, :], pbr[:, :], [pkr], [rk_] + QRK(rk_))
                        yield
                        bg, pbg, pkg = kc.bank()
                        pairmm(pbg, pkg, R, G, [rk_] + gks)
                        kc.tt("dve", G[:, :], G[:, :], pbg[:, :], ALU.add, gks + [pkg], [gk_, (gk_, 0), (gk_, 1)])
                        yield
                for sb in range(2):
                    G = Gk[sb]
                    gk_ = "C_G%d" % sb
                    TTv = TTb[0:64, sb * 1024:(sb + 1) * 1024].rearrange("p (m two f) -> p m two f", m=8, two=2)
                    kc.tt("dve", TTv[:, :, 0, :], r3(G[0:64, :], 8), bc_last(bTe[0:64, sb * 8:(sb + 1) * 8], 8, 64), ALU.mult,
                          [gk_, "C_bTe"], [kK("TTb")])
                    kc.copy("act", tmpD[0:64, :], G[64:128, :], [gk_], ["C_tmpD"])
                    kc.tt("dve", TTv[:, :, 1, :], r3(tmpD[0:64, :], 8), bc_last(bTo[0:64, sb * 8:(sb + 1) * 8], 8, 64), ALU.mult,
                          ["C_tmpD", "C_bTo"], [kK("TTb")])
                    yield

            def phaseB(s_, h, st):
                t0 = s_ * S
                si = st["i"]
                kK = lambda nm: "C_%s%d" % (nm, si)
                keT, qdT, kdec, vtok, aqkT, TTb, egl = (st[k] for k in ("keT", "qdT", "kdec", "vtok", "aqkT", "TTb", "egl"))
                kc.memset("dve", S_, 0.0, ["C_S"])
                for n in range(32):
                    cs = slice(n * 64, (n + 1) * 64)
                    ns = slice(n * 128, (n + 1) * 128)
                    b1, pb1, pk1 = kc.bank()
                    kc.mm(pb1[0:64, 0:128], keT[:, cs], S_, True, True, [kK("keT"), "C_S"], [pk1])
                    b3, pb3, pk3 = kc.bank()
                    kc.mm(pb3[:, 0:64], S_, qdT[:, cs], True, False, [kK("qdT"), "C_S"], [pk3])
                    kc.tt("dve", Rb[0:64, :], vtok[0:64, ns], pb1[0:64, 0:128], ALU.subtract, [kK("vtok"), pk1], ["C_Rb"])
                    b2, pb2, pk2 = kc.bank()
                    kc.mm(pb2[0:64, 0:128], TTb[0:64, cs], Rb[0:64, :], True, True, [kK("TTb"), "C_Rb"], [pk2])
                    kc.copy("act", vn[0:64, :], pb2[0:64, 0:128], [pk2], ["C_vn"])
                    kc.mm(pb3[:, 0:64], vn[0:64, :], aqkT[0:64, cs], False, True, ["C_vn", kK("aqkT")], [pk3])
                    b4, pb4, pk4 = kc.bank()
                    kc.mm(pb4[:, 0:128], kdec[0:64, ns], vn[0:64, :], True, True, [kK("kdec"), "C_vn"], [pk4])
                    kc.copy("act", oT[:, cs], pb3[:, 0:64], [pk3], ["C_oT"])
                    kc.sto("dve", S_, S_, egl[:, n:n + 1], pb4[:, 0:128], ALU.mult, ALU.add, ["C_S", kK("egl"), pk4],
                           ["C_S"])
                    yield
                kc.act(sqb2, oT, AF.Square, ["C_oT"], ["C_sqb2"])
                for q4 in range(4):
                    sl = slice(q4 * 512, (q4 + 1) * 512)
                    zs = zsb[q4 % 2]
                    zk = "C_zs%d" % (q4 % 2)
                    kc.ld(zs, dzT[h * 128:(h + 1) * 128, t0 + q4 * 512:t0 + (q4 + 1) * 512], zk)
                    bi, pb, pk = kc.bank()
                    kc.mm(pb[:, :], ones_bf, sqb2[:, sl], True, True, ["ones_bf", "C_sqb2"], [pk])
                    kc.act(rn2, pb[:, :], AF.Ln, [pk, "vecs"], ["C_rn2"], bias=vcol(V_EPS), scale=1.0 / 128)
                    kc.act(rn2, rn2, AF.Exp, ["C_rn2"], ["C_rn2"], scale=-0.5)
                    kc.sto("dve", oT[:, sl], oT[:, sl], vcol(V_DNW + l), rn2, ALU.mult, ALU.mult,
                           ["C_oT", "C_rn2", "vecs"], ["C_oT"])
                    kc.tt("pool", yout[:, sl], oT[:, sl], zs, ALU.mult, ["C_oT", zk], ["C_yout"])
                    yield
                kc.store(ybT[512 + h * 128:512 + (h + 1) * 128, t0:t0 + S], yout, "C_yout", writes=["ybT"])
                yield

            heads = [(s_, h) for s_ in range(NSEQ) for h in range(4)]
            nA = 0
            for _ in phaseA(heads[0][0], heads[0][1], SETS[0]):
                nA += 1
            nB = 38
            per = max(1, -(-nA // nB))
            for i_, (s_, h) in enumerate(heads):
                gB = phaseB(s_, h, SETS[i_ % 2])
                gA = phaseA(heads[i_ + 1][0], heads[i_ + 1][1], SETS[(i_ + 1) % 2]) if i_ + 1 < len(heads) else None
                aliveA = gA is not None
                aliveB = True
                while aliveA or aliveB:
                    if aliveB:
                        try:
                            next(gB)
                        except StopIteration:
                            aliveB = False
                    if aliveA:
                        for _ in range(per):
                            try:
                                next(gA)
                            except StopIteration:
                                aliveA = False
                                break
            kc.reset()

        if "D" in stages:
            P.cur_tag = "L%dD" % l
            band = kc.alloc(8 * 1152, BF16)
            kc.ld(band, band_d, "D_band", q="pool")
            band3 = r3(band, 8)
            ind = kc.alloc(64 * 128, BF16)
            kc.ld(ind, ind_d, "D_ind", q="pool")
            ind3 = r3(ind, 64)
            identb = kc.alloc(128, BF16)
            kc.copy("dve", identb, ident, ["ident"], ["D_identb"])
            QT = kc.alloc(4 * S, BF16)
            KT = kc.alloc(4 * S, BF16)
            QT3 = r3(QT, 4)
            KT3 = r3(KT, 4)
            VP = kc.alloc(16 * 768, BF16)
            VP4 = VP.rearrange("p (i c w) -> p i c w", i=16, c=4)
            VP3 = r3(VP, 16)
            kc.memset("pool", VP, 0.0, ["D_VPz"])
            kc.memset("pool", VP4[:, :, :, 64:65], 1.0, ["D_VPz"])
            kms = kc.alloc(32)
            KM = kc.alloc(4 * 64, BF16)
            KM3 = r3(KM, 4)
            gsb = kc.alloc(128)
            gw = kc.alloc(128)
            mxt = kc.alloc(16)
            eqt = kc.alloc(128)
            Mt = kc.alloc(128)
            MallT = kc.alloc(S, BF16)
            ptb = [kc.alloc(512, BF16) for _ in range(5)]
            cfar = kc.alloc(8)
            kc.copy("dve", cfar, band3[:, :, 1151], ["D_band"], ["D_cfar"])
            rl = kc.alloc(512)
            osb = kc.alloc(512)
            ycT = kc.alloc(4 * S, BF16)
            ycT3 = r3(ycT, 4)
            kc.memset("pool", KM, 0.0, ["D_KMz"])
            KZ = kc.alloc(8 * S, BF16)
            KZ3 = r3(KZ, 8)
            kc.memset("pool", KZ, 0.0, ["D_KZz"])
            for s_ in range(NSEQ):
              try:
                t0 = s_ * S
                kc.ld(QT3, mqT[:, t0:t0 + S].rearrange("(c p) t -> p c t", p=128), "D_QT")
                kc.ld(KT3, mkT[:, t0:t0 + S].rearrange("(c p) t -> p c t", p=128), "D_KT")
                srcv = mv[t0:t0 + S, :].rearrange("(i p) (c two d) -> p i c two d", p=128, two=2, d=64)
                for c in range(4):
                    kc.ld(VP4[:, :, c, 0:64], srcv[:, :, c, 0, :], ("D_VP", c, 0), reads=["D_VPz"], semkey="D_VPa%d" % c)
                    kc.ld(VP4[:, :, c, 128:192], srcv[:, :, c, 1, :], ("D_VP", c, 1), reads=["D_VPz"],
                          semkey="D_VPb%d" % c)
                for h_ in range(8):
                    rr0 = (h_ % 2) * 64
                    kc.copy("pool" if h_ % 2 == 0 else "act", KZ3[rr0:rr0 + 64, h_, :], KT3[rr0:rr0 + 64, h_ // 2, :],
                            ["D_KT", "D_KZz"], [("D_KZ", h_)])
                P.op("dve", (lambda kms=kms, KT=KT: lambda e: e.tensor_reduce(r3(kms, 4), KT.rearrange("p (c n k) -> p c n k", c=4, n=8), AX.X,
                                                      ALU.add))(), reads=["D_KT"], writes=["D_kms"])
                kms3 = r3(kms, 4)
                for c in range(4):
                    kc.copy("dve", KM3[0:64, c, (2 * c) * 8:(2 * c) * 8 + 8], kms3[0:64, c, :], ["D_kms", "D_KMz"],
                            ["D_KM"])
                    kc.copy("dve", KM3[64:128, c, (2 * c + 1) * 8:(2 * c + 1) * 8 + 8], kms3[64:128, c, :],
                            ["D_kms", "D_KMz"], ["D_KM"])
                if DSTOP == 1:
                    kc.dump("KM", KM, ["D_KM"], BF16)
                    kc.dump("VP", VP, ["D_VPz"] + [("D_VP", c, hh) for c in range(4) for hh in range(2)], BF16)
                    kc.dump("kms", kms, ["D_kms"])
                    raise _Stop()
                for q4 in range(4):
                    bm, pbm, pkm = kc.bank()
                    for qq in range(4):
                        qt = q4 * 4 + qq
                        b = qt // 2
                        if b >= 4:
                            bi, pb, pk = kc.bank()
                            for c in range(4):
                                kc.mm(pb[:, 0:64], QT3[:, c, qt * 128:(qt + 1) * 128], KM3[:, c, :], c == 0, c == 3,
                                      ["D_QT", "D_KM"], [pk])
                            g3 = r3(gsb[:, 0:64], 8)
                            w3 = r3(gw[:, 0:64], 8)
                            e3 = r3(eqt[:, 0:64], 8)
                            kc.copy("act", gsb[:, 0:64], pb[:, 0:64], [pk], ["D_gsb"])
                            kc.memset("dve", g3[:, :, b:8], -1.0e9, ["D_gsb"])
                            src, srck = g3, "D_gsb"
                            for rnd in range(3):
                                P.op("dve", (lambda src=src, mxt=mxt: lambda e: e.tensor_reduce(mxt[:, 0:8], src, AX.X, ALU.max))(),
                                     reads=[srck], writes=["D_mxt"])
                                if rnd == 2:
                                    break
                                mb = mxt[:, 0:8].rearrange("p (h o) -> p h o", o=1).to_broadcast([128, 8, 8])
                                kc.tt("dve", e3, src, mb, ALU.is_ge, [srck, "D_mxt"], ["D_eqt"])
                                kc.sto("dve", w3, e3, -1.0e9, src, ALU.mult, ALU.add, ["D_eqt", srck], ["D_gw"])
                                src, srck = w3, "D_gw"
                            mb = mxt[:, 0:8].rearrange("p (h o) -> p h o", o=1).to_broadcast([128, 8, 8])
                            kc.tt("dve", e3, g3, mb, ALU.is_ge, ["D_gsb", "D_mxt"], ["D_eqt"])
                            kc.ts("dve", Mt[:, 0:64], eqt[:, 0:64], BIG, -BIG, ALU.mult, ALU.add, ["D_eqt"], ["D_Mt"])
                            kc.memset("dve", r3(Mt[:, 0:64], 8)[:, :, b:b + 1], 0.0, ["D_Mt"])
                        else:
                            kc.memset("dve", Mt[:, 0:64], 0.0, ["D_Mt"])
                        kc.transpose(pbm[0:64, qq * 128:(qq + 1) * 128], Mt[:, 0:64], ident, ["D_Mt", "ident"], [pkm])
                    kc.copy("act", MallT[0:64, q4 * 512:(q4 + 1) * 512], pbm[0:64, :], [pkm], ["D_MallT"])
                    kc.copy("act", MallT[64:128, q4 * 512:(q4 + 1) * 512], pbm[0:64, :], [pkm], ["D_MallT"])
                if DSTOP == 2:
                    kc.dump("MallT", MallT[0:64, :], ["D_MallT"], BF16)
                    raise _Stop()
                ipt = 0
                pend = []

                def flush(keep):
                    while len(pend) > keep:
                        pend.pop(0)()

                for h in range(8):
                    c = h // 2
                    r0 = (h % 2) * 64
                    lrow = 64 if h % 2 == 0 else 0
                    for qi in range(4):
                        q0 = qi * 512
                        nkt = (qi + 1) * 4
                        bo, pbo, pko = kc.bank(n=2, base=6)
                        for kt in range(nkt):
                            k0 = kt * 128
                            nblk = kt // 2
                            bs_, pbs, pks = kc.bank(n=5, base=0)
                            mms = [(KZ3[:, h, k0:k0 + 128], QT3[:, c, q0:q0 + 512], [("D_KZ", h), "D_KZz", "D_QT"])]
                            if qi >= 2 and nblk < 2 * qi + 1:
                                mms.append((ind3[:, h * 8 + nblk, :], MallT[:, q0:q0 + 512], ["D_ind", "D_MallT"]))
                            far = (q0 - k0) >= 256
                            if not far:
                                off = min(max(q0 - k0, -384), 256) + 384
                                mms.append((identb, band3[:, h, off:off + 512], ["D_identb", "D_band"]))
                            for i_, (a_, b_, rd_) in enumerate(mms):
                                kc.mm(pbs[:, :], a_, b_, i_ == 0, i_ == len(mms) - 1, rd_, [pks])
                            pt = ptb[ipt % 5]
                            ptk = "D_pt%d" % (ipt % 5)
                            ipt += 1
                            if far:
                                kc.act(pt, pbs[:, :], AF.Exp, [pks, "D_cfar"], [ptk], bias=cfar[:, h:h + 1])
                            else:
                                kc.act(pt, pbs[:, :], AF.Exp, [pks], [ptk])
                            lo = c * 192 + (h % 2) * 64

                            def pv(pbo=pbo, pko=pko, kt=kt, nkt=nkt, lo=lo, pt=pt, ptk=ptk, c=c, r0=r0, lrow=lrow, q0=q0):
                                kc.mm(pbo[:, :], VP3[:, kt, lo:lo + 128], pt, kt == 0, kt == nkt - 1,
                                      [("D_VP", c, 0), ("D_VP", c, 1), "D_VPz", ptk], [pko])
                                if kt == nkt - 1:
                                    kc.act(rl[lrow:lrow + 1, :], pbo[lrow:lrow + 1, :], AF.Ln, [pko], ["D_rl"])
                                    kc.act(rl[lrow:lrow + 1, :], rl[lrow:lrow + 1, :], AF.Exp, ["D_rl"], ["D_rl"], scale=-1.0)
                                    kc.copy("act", osb[r0:r0 + 64, :], pbo[r0:r0 + 64, :], [pko], ["D_osb"])
                                    br_, pbr, pkr = kc.bank(n=1, base=5)
                                    kc.mm(pbr[:, :], ones_f[lrow:lrow + 1, 0:128], rl[lrow:lrow + 1, :], True, True,
                                          ["ones_f", "D_rl"], [pkr])
                                    kc.tt("dve", ycT3[r0:r0 + 64, c, q0:q0 + 512], osb[r0:r0 + 64, :], pbr[r0:r0 + 64, :],
                                          ALU.mult, ["D_osb", pkr], ["D_ycT"])

                            pend.append(pv)
                            flush(2)
                    if DSTOP == 3 + h:
                        flush(0)
                        kc.dump("ycT", ycT, ["D_ycT"], BF16)
                        raise _Stop()
                flush(0)
                kc.store(ybT[1024:1536, t0:t0 + S].rearrange("(c p) t -> p c t", p=128), ycT3, "D_ycT", writes=["ybT"])
              except _Stop:
                pass
            kc.reset()

        if "E" in stages:
            P.cur_tag = "L%dE" % l
            wbr = kc.alloc(12 * 1024, BF16)
            kc.ld(wbr, wbrb[l], "E_wbr", reads=["CAST1" if l == layers[0] else "CAST1b"])
            wbr4 = wbr.rearrange("p (n k d) -> p n k d", n=3, k=4)
            wo = kc.alloc(8 * 1024, BF16)
            kc.ld(wo, woutb[l], "E_wo", reads=["CAST1" if l == layers[0] else "CAST1b"])
            wo3 = r3(wo, 8)
            ybb = [kc.alloc(12 * 512, BF16) for _ in range(2)]
            gtb = [kc.alloc(24 * 512, BF16) for _ in range(2)]
            xtb = [kc.alloc(8 * 512) for _ in range(2)]
            mg = kc.alloc(8 * 512, BF16)
            mg3 = r3(mg, 8)
            ta = kc.alloc(512)
            tb = kc.alloc(512)
            tc_ = kc.alloc(512)
            for tt in range(NT):
                i2 = tt % 2
                tsl = slice(tt * 512, (tt + 1) * 512)
                yb3 = r3(ybb[i2], 12)
                gt3 = r3(gtb[i2], 24)
                xt3 = r3(xtb[i2], 8)
                ybk, gtk, xk = "E_yb%d" % i2, "E_gt%d" % i2, "E_xt%d" % i2
                kc.ld(yb3, ybT[:, tsl].rearrange("(c p) t -> p c t", p=128), ybk)
                kc.ld(gt3, gatesT[:, tsl].rearrange("(c p) t -> p c t", p=128), gtk)
                kc.ld(xt3, xsrc[:, tsl].rearrange("(k p) t -> p k t", p=128), xk)
                for m in range(8):
                    pbs_ = []
                    for n in range(3):
                        bi, pb, pk = kc.bank()
                        for k in range(4):
                            kc.mm(pb[:, :], wbr4[:, n, k, m * 128:(m + 1) * 128], yb3[:, n * 4 + k, :], k == 0, k == 3,
                                  ["E_wbr", ybk], [pk])
                        pbs_.append((pb, pk))
                    kc.tt("dve", ta, pbs_[0][0][:, :], gt3[:, m, :], ALU.mult, [pbs_[0][1], gtk], ["E_ta"])
                    kc.tt("dve", tb, pbs_[1][0][:, :], gt3[:, 8 + m, :], ALU.mult, [pbs_[1][1], gtk], ["E_tb"])
                    kc.tt("dve", tc_, pbs_[2][0][:, :], gt3[:, 16 + m, :], ALU.mult, [pbs_[2][1], gtk], ["E_tc"])
                    kc.tt("pool", ta, ta, tb, ALU.add, ["E_ta", "E_tb"], ["E_ta"])
                    kc.tt("pool", mg3[:, m, :], ta, tc_, ALU.add, ["E_ta", "E_tc"], [("E_mg", m)])
                for m in range(8):
                    bi, pb, pk = kc.bank()
                    for k in range(8):
                        kc.mm(pb[:, :], wo3[:, k, m * 128:(m + 1) * 128], mg3[:, k, :], k == 0, k == 7,
                              ["E_wo", ("E_mg", k)], [pk])
                    kc.tt("dve", xt3[:, m, :], xt3[:, m, :], pb[:, :], ALU.add, [xk, pk], [xk])
                kc.store(x1s[:, tsl].rearrange("(k p) t -> p k t", p=128), xt3, xk, writes=["x1s"])
            kc.reset()

        if "F" in stages:
            P.cur_tag = "L%dF" % l
            wdn = kc.alloc(22 * 1024, BF16)
            kc.ld(wdn, wdnb[l], "F_wdn", reads=["CAST1" if l == layers[0] else "CAST1b"])
            wdn3 = r3(wdn, 22)
            pgw = kc.alloc(8 * 1024, BF16)
            kc.ld(pgw, pgb[l], "F_pgw", reads=["CAST1" if l == layers[0] else "CAST1b"])
            pgw3 = r3(pgw, 8)
            ppw = kc.alloc(2 * 1024, BF16)
            kc.ld(ppw, ppb[l], "F_ppw", reads=["CAST1" if l == layers[0] else "CAST1b"])
            ppw3 = r3(ppw, 2)
            hal = kc.alloc(44 * 2)
            xtb = [kc.alloc(8 * 512) for _ in range(2)]
            h2 = kc.alloc(8 * 512, BF16)
            h23 = r3(h2, 8)
            sqt = kc.alloc(8 * 512, BF16)
            rstd = kc.alloc(512)
            wub = [kc.alloc(8 * 256, BF16) for _ in range(2)]
            rawb = [kc.alloc(514) for _ in range(2)]
            yb_ = [kc.alloc(512) for _ in range(2)]
            gTt = kc.alloc(22 * 512, BF16)
            gT3 = r3(gTt, 22)
            ptb_ = [kc.alloc(2 * 512, BF16) for _ in range(2)]
            sg = kc.alloc(512)
            tp = kc.alloc(512)
            last = (l == layers[-1]) and final
            if last:
                ot = kc.alloc(8 * 512)
                ot3 = r3(ot, 8)
            iw = 0
            for tt in range(NT):
                i2 = tt % 2
                tsl = slice(tt * 512, (tt + 1) * 512)
                xt = xtb[i2]
                xt3 = r3(xt, 8)
                xk = "F_xt%d" % i2
                kc.ld(xt3, x1s[:, tsl].rearrange("(k p) t -> p k t", p=128), xk)
                pt_ = ptb_[i2]
                ptk = "F_pt%d" % i2
                kc.ld(r3(pt_, 2), pT_in[l, :, tsl].rearrange("(k p) t -> p k t", p=128), ptk, q="pool")
                if tt % 4 == 0:
                    kc.memset("pool", hal, 0.0, ["F_hal"])
                rmsnorm_tile(xt, xk, V_NFFN + l * 8, lambda k: h23[:, k, :], [("F_h", k) for k in range(8)], sqt, "F_sq",
                             rstd, "F_rstd")
                HK = [("F_h", k) for k in range(8)]
                for j in range(22):
                    wu = wub[iw % 2]
                    wuk = "F_wu%d" % (iw % 2)
                    iw += 1
                    kc.ld(wu, wupb[l, j], wuk, reads=["CAST1" if l == layers[0] else "CAST1b"])
                    wu3 = r3(wu, 8)
                    for half in range(2):
                        bi, pb, pk = kc.bank()
                        for k in range(8):
                            kc.mm(pb[:, :], wu3[:, k, half * 128:(half + 1) * 128], h23[:, k, :], k == 0, k == 7,
                                  [wuk, ("F_h", k)], [pk])
                        rb = rawb[half]
                        rk = "F_raw%d" % half
                        yb = yb_[half]
                        yk = "F_y%d" % half
                        hi = (j * 2 + half) * 2
                        kc.copy("act", rb[:, 2:514], pb[:, :], [pk], [rk])
                        kc.copy("pool", rb[:, 0:2], hal[:, hi:hi + 2], ["F_hal"], [rk + "h"])
                        kc.copy("pool", hal[:, hi:hi + 2], rb[:, 512:514], [rk], ["F_hal"])
                        cwi = V_FCW + (l * 44 + half * 22 + j) * 3
                        e1 = "dve"
                        kc.ts(e1, yb, rb[:, 0:512], vcol(cwi), None, ALU.mult, None, [rk, rk + "h", "vecs"], [yk])
                        kc.sto(e1, yb, rb[:, 1:513], vcol(cwi + 1), yb, ALU.mult, ALU.add, [rk, rk + "h", yk, "vecs"], [yk])
                        kc.sto("dve", yb, rb[:, 2:514], vcol(cwi + 2), yb, ALU.mult, ALU.add, [rk, yk, "vecs"], [yk])
                    kc.act(yb_[0], yb_[0], AF.Gelu_apprx_tanh, ["F_y0"], ["F_y0"])
                    kc.tt("dve", gT3[:, j, :], yb_[0], yb_[1], ALU.mult, ["F_y0", "F_y1"], [("F_g", j)])
                for m in range(8):
                    bi, pb, pk = kc.bank()
                    for j in range(22):
                        kc.mm(pb[:, :], wdn3[:, j, m * 128:(m + 1) * 128], gT3[:, j, :], j == 0, j == 21,
                              ["F_wdn", ("F_g", j)], [pk])
                    kc.tt("dve", xt3[:, m, :], xt3[:, m, :], pb[:, :], ALU.add, [xk, pk], [xk])
                rmsnorm_tile(xt, xk, V_NPLE + l * 8, lambda k: h23[:, k, :], HK, sqt, "F_sq", rstd, "F_rstd")
                for m in range(8):
                    bi, pb, pk = kc.bank()
                    for k in range(8):
                        kc.mm(pb[:, :], pgw3[:, k, m * 128:(m + 1) * 128], h23[:, k, :], k == 0, k == 7,
                              ["F_pgw", ("F_h", k)], [pk])
                    kc.act(sg, pb[:, :], AF.Sigmoid, [pk], ["F_sg"])
                    bi, pb, pk = kc.bank()
                    for k in range(2):
                        kc.mm(pb[:, :], ppw3[:, k, m * 128:(m + 1) * 128], r3(pt_, 2)[:, k, :], k == 0, k == 1,
                              ["F_ppw", ptk], [pk])
                    kc.tt("dve", tp, pb[:, :], sg, ALU.mult, [pk, "F_sg"], ["F_tp"])
                    kc.tt("pool", xt3[:, m, :], xt3[:, m, :], tp, ALU.add, [xk, "F_tp"], [xk])
                if last:
                    rmsnorm_tile(xt, xk, V_NFIN, lambda k: ot3[:, k, :], [("F_ot", k) for k in range(8)], sqt, "F_sq", rstd, "F_rstd")
                    kc.finals.append(kc.store(outT[:, tsl].rearrange("(k p) t -> p k t", p=128), ot3, [("F_ot", k) for k in range(8)],
                                              writes=["outT"], semkey="F_ot_st"))
                else:
                    kc.store(xs[:, tsl].rearrange("(k p) t -> p k t", p=128), xt3, xk, writes=["xs"])
            kc.reset()
    nsem = P.emit(final_wait_ops=kc.finals)
    return nc, kc, nsem


def _t5_bucket_np(n):
    n = np.maximum(n, 0)
    exact = 16
    nf = np.maximum(n, 1).astype(np.float32)
    large = exact + (np.log(nf / exact) / math.log(128 / exact) * (32 - exact)).astype(np.int32)
    large = np.minimum(large, 31)
    return np.where(n < exact, n, large)


def host_consts():
    ident = np.eye(128, dtype=np.float32)
    p = np.arange(64)[:, None]
    f = np.arange(64)[None, :]
    U = (p <= f).astype(np.float32)
    mA = np.where(f >= p, 0.0, -BIG).astype(np.float32)
    mB = np.where(p > f, 0.0, -BIG).astype(np.float32)
    SU = (f > p).astype(np.float32)
    tri = np.concatenate([U, mA, mB, SU], axis=1)
    rc = np.zeros((128, 64), np.float32)
    for g in range(4):
        w = 2 << g
        for t in range(16):
            rc[:, g * 16 + t] = 1.0 / min(t + 1, w)
    ind = np.zeros((64, 64, 128), np.float32)
    for r in range(64):
        ind[r, r, :] = 1.0
    ind = ind.reshape(64, 64 * 128)
    return ident, tri, rc, np.concatenate([ind, ind], axis=0)


def host_vecs(inp):
    v = np.zeros((128, NVEC), np.float32)

    def put(base, arr):
        a = np.asarray(arr, np.float32)
        lead = int(np.prod(a.shape[:-1])) if a.ndim > 1 else 1
        a = a.reshape(lead, -1, 128)
        a = a.transpose(2, 0, 1).reshape(128, -1)
        v[:, base:base + a.shape[1]] = a

    put(V_NMIX, inp["norm_mix"])
    put(V_NFFN, inp["norm_ffn"])
    put(V_NPLE, inp["norm_ple"])
    put(V_NFIN, inp["norm_final"])
    put(V_PSCALE, inp["pool_scale"])
    cw = np.asarray(inp["dn_conv"], np.float32).reshape(2, 4, 12, 128).transpose(3, 0, 2, 1).reshape(128, 96)
    v[:, V_CW:V_CW + 96] = cw
    fc = np.asarray(inp["ffn_conv"], np.float32).reshape(2, 3, 44, 128).transpose(3, 0, 2, 1).reshape(128, 264)
    v[:, V_FCW:V_FCW + 264] = fc
    v[:, V_DNW:V_DNW + 2] = np.asarray(inp["dn_norm"], np.float32).T
    v[:, V_EPS] = EPS
    v[:, V_ONE] = 1.0
    return v


def host_band(rel_bias):
    rb = np.asarray(rel_bias, np.float32)
    p = np.arange(128)[:, None]
    c = np.arange(1152)[None, :]
    n = c - 384 - p
    bucket = _t5_bucket_np(n)
    band = np.empty((128, 8, 1152), np.float32)
    for h in range(8):
        band[:, h, :] = np.where(n >= 0, rb[bucket, h], -BIG)
    return band.reshape(128, 8 * 1152)


_CACHE = {}


def get_program(NSEQ=2, **kw):
    key = (NSEQ, tuple(sorted(kw.items())))
    if key not in _CACHE:
        _CACHE[key] = build(NSEQ=NSEQ, **kw)
    return _CACHE[key]


def make_in_maps(inp, NSEQ, ncores):
    ident, tri, rc, ind = host_consts()
    vecs = host_vecs(inp)
    band = host_band(inp["rel_bias"])
    a4 = np.stack([np.asarray(inp["dn_a_log"], np.float32)[0], np.asarray(inp["dn_dt_bias"], np.float32)[0],
                   np.asarray(inp["dn_a_log"], np.float32)[1], np.asarray(inp["dn_dt_bias"], np.float32)[1]], axis=1)
    x = np.asarray(inp["x"], np.float32)
    p = np.asarray(inp["p"], np.float32)
    shared = {
        "w_in": np.ascontiguousarray(inp["w_in"], np.float32),
        "w_branch": np.ascontiguousarray(inp["w_branch"], np.float32),
        "w_out": np.ascontiguousarray(inp["w_out"], np.float32),
        "ffn_up": np.ascontiguousarray(inp["ffn_up"], np.float32),
        "ffn_down": np.ascontiguousarray(inp["ffn_down"], np.float32),
        "ple_gate": np.ascontiguousarray(inp["ple_gate"], np.float32),
        "ple_proj": np.ascontiguousarray(inp["ple_proj"], np.float32),
        "pool_w": np.ascontiguousarray(inp["pool_w"], np.float32),
        "vecs": vecs, "ident": ident, "tri": tri, "rcnt": rc, "band": band, "ind": ind,
        "a4": np.ascontiguousarray(a4),
    }
    maps = []
    for c in range(ncores):
        xs_ = x[c * NSEQ:(c + 1) * NSEQ].reshape(NSEQ * S, D)
        ps_ = p[:, c * NSEQ:(c + 1) * NSEQ].reshape(DEPTH, NSEQ * S, 256)
        m = dict(shared)
        m["xT"] = np.ascontiguousarray(xs_.T)
        m["pT"] = np.ascontiguousarray(ps_.transpose(0, 2, 1))
        maps.append(m)
    return maps


def kernel(**inputs):
    NSEQ = 2
    nc, kc, _ = get_program(NSEQ=NSEQ)
    maps = make_in_maps(inputs, NSEQ, NCORES)
    res = run_bass_kernel_spmd(nc, maps, core_ids=list(range(NCORES)))
    outs = []
    for c in range(NCORES):
        oT = np.asarray(res.results[c]["outT"], np.float32)
        outs.append(oT.T.reshape(NSEQ, S, D))
    return np.concatenate(outs, axis=0).astype(np.float32)
```

```python
import contextlib
import math
import numpy as np
import concourse.bass as bass
import concourse.mybir as mybir
from concourse.bass_utils import run_bass_kernel_spmd

F32 = mybir.dt.float32
BF16 = mybir.dt.bfloat16
AF = mybir.ActivationFunctionType
ALU = mybir.AluOpType
AX = mybir.AxisListType

D = 1024
S = 2048
DEPTH = 2
IN_COLS = 7176
FFN = 2816
EPS = 1e-6
NCORES = 8
BIG = 30000.0


class Op:
    __slots__ = ("eng", "fn", "deps", "sem", "inc", "val", "signal", "is_dma", "tag")

    def __init__(self, eng, fn, is_dma=False):
        self.eng = eng
        self.fn = fn
        self.deps = []
        self.sem = None
        self.inc = 1
        self.val = None
        self.signal = False
        self.is_dma = is_dma


class Prog:
    ENGS = ("pe", "act", "dve", "pool", "sp")

    def __init__(self, nc):
        self.nc = nc
        self.ops = {e: [] for e in self.ENGS}
        self.last_w = {}
        self.readers = {}
        self.n_ops = 0
        self.last_dma = {}
        self.bar = {e: None for e in self.ENGS}
        self.cur_tag = "pre"
        self.scopes = False

    def barrier(self):
        deps = []
        for e in self.ENGS:
            for o in reversed(self.ops[e]):
                if not o.is_dma:
                    deps.append(o)
                    break
        deps.extend(self.last_dma.values())
        for e in self.ENGS:
            self.bar[e] = deps
        self.last_w = {}
        self.readers = {}

    def _add(self, op, reads, writes):
        deps = []
        for k in reads:
            w = self.last_w.get(k)
            if w is not None:
                deps.append((w, False))
        for k in writes:
            w = self.last_w.get(k)
            if w is not None:
                deps.append((w, False))
            for r in self.readers.get(k, ()):
                deps.append((r, True))
        if self.bar[op.eng] is not None:
            for d in self.bar[op.eng]:
                deps.append((d, False))
            self.bar[op.eng] = None
        seen = set()
        for d, war in deps:
            if d is op or id(d) in seen:
                continue
            if d.eng == op.eng and not d.is_dma and not op.is_dma:
                if op.eng == "pe" or war:
                    continue
            seen.add(id(d))
            op.deps.append(d)
            d.signal = True
        for k in reads:
            self.readers.setdefault(k, []).append(op)
        for k in writes:
            self.last_w[k] = op
            self.readers[k] = []
        op.tag = self.cur_tag
        self.ops[op.eng].append(op)
        self.n_ops += 1

    @staticmethod
    def _is_psum(k):
        return isinstance(k, str) and k.startswith("ps") and k[2:].isdigit()

    def op(self, eng, fn, reads=(), writes=()):
        o = Op(eng, fn)
        o.sem = ("eng", eng)
        ex = [k for k in reads if self._is_psum(k)]
        if ex:
            reads = [k for k in reads if not self._is_psum(k)]
            writes = list(writes) + ex
        self._add(o, reads, writes)
        return o

    def dma(self, fn, semkey, reads=(), writes=(), q="sp"):
        o = Op(q, fn, is_dma=True)
        o.sem = ("dma", semkey)
        o.inc = 16
        o.signal = True
        self._add(o, reads, writes)
        self.last_dma[semkey] = o
        return o

    def emit(self, final_wait_ops=()):
        nc = self.nc
        counts = {}
        for e in self.ENGS:
            for o in self.ops[e]:
                if o.signal:
                    c = counts.get(o.sem, 0) + o.inc
                    counts[o.sem] = c
                    o.val = c
        semkeys = list(counts.keys())
        with contextlib.ExitStack() as st:
            sems = {}
            for i, k in enumerate(semkeys):
                sems[k] = st.enter_context(nc.semaphore("s%d" % i))
            block = st.enter_context(nc.Block())
            engmap = {"pe": block.tensor, "act": block.scalar, "dve": block.vector,
                      "pool": block.gpsimd, "sp": block.sync}

            def make(e):
                oplist = self.ops[e]

                def body(eng):
                    waited = {}
                    cur = None
                    cm = None
                    for o in oplist:
                        if self.scopes and o.tag != cur:
                            if cm is not None:
                                cm.__exit__(None, None, None)
                            cm = nc.named_scope(o.tag)
                            cm.__enter__()
                            cur = o.tag
                        for d in o.deps:
                            if waited.get(d.sem, 0) >= d.val:
                                continue
                            eng.wait_ge(sems[d.sem], d.val)
                            waited[d.sem] = d.val
                        ins = o.fn(eng)
                        if o.signal:
                            ins.then_inc(sems[o.sem], o.inc)
                    if cm is not None:
                        cm.__exit__(None, None, None)
                    if e == "sp":
                        for o in final_wait_ops:
                            if waited.get(o.sem, 0) >= o.val:
                                continue
                            eng.wait_ge(sems[o.sem], o.val)
                            waited[o.sem] = o.val

                return body

            for e in self.ENGS:
                if self.ops[e] or e == "sp":
                    engmap[e](make(e))
        return len(semkeys)


ARENA_F32 = 47 * 1024 + 512


import os as _os
CSTOP = int(_os.environ.get("C_STOP", "99"))
DSTOP = int(_os.environ.get("D_STOP", "99"))


class _Stop(Exception):
    pass


class KC:
    def __init__(self, nc, NSEQ, debug):
        self.nc = nc
        self.P = Prog(nc)
        self.NSEQ = NSEQ
        self.T = NSEQ * S
        self.NT = self.T // 512
        self.debug = debug
        self.st = contextlib.ExitStack()
        self.arena = self.st.enter_context(nc.sbuf_tensor("arena", [128, ARENA_F32], F32))
        self.arena_bf = self.arena[:, :].bitcast(BF16)
        self.psb = [self.st.enter_context(nc.psum_tensor("psb%d" % i, [128, 512], F32)) for i in range(8)]
        self.bump = 0
        self.perm = 0
        self.rr = 0
        self.finals = []
        self.dram = {}
        self.uid = 0

    def alloc(self, cols, dt=F32):
        if dt == BF16:
            w = (cols + 1) // 2
            a = self.bump
            self.bump += w
            assert self.bump <= ARENA_F32, "SBUF arena overflow %d" % self.bump
            return self.arena_bf[:, 2 * a:2 * a + cols]
        a = self.bump
        self.bump += cols
        assert self.bump <= ARENA_F32, "SBUF arena overflow %d" % self.bump
        return self.arena[:, a:a + cols]

    def make_perm(self):
        self.perm = self.bump

    def reset(self):
        self.P.barrier()
        self.bump = self.perm

    def key(self, base):
        self.uid += 1
        return "%s#%d" % (base, self.uid)

    def din(self, name, shape, dt=F32):
        t = self.nc.dram_tensor(name, list(shape), dt, kind="ExternalInput").ap()
        self.dram[name] = t
        return t

    def dscr(self, name, shape, dt=F32, out=False):
        kind = "ExternalOutput" if (out or self.debug) else "Internal"
        t = self.nc.dram_tensor(name, list(shape), dt, kind=kind).ap()
        self.dram[name] = t
        return t

    def ld(self, dst, src, key, reads=(), q="sp", semkey=None, slow=False):
        if slow:
            return self.P.dma(lambda e: e.dma_start(out=dst, in_=src, allow_slow_non_contiguous=True), semkey or key,
                              reads=reads, writes=[key], q=q)
        return self.P.dma(lambda e: e.dma_start(out=dst, in_=src), semkey or key, reads=reads, writes=[key], q=q)

    def stt(self, dst, src, srckey, writes=(), q="sp", semkey=None):
        keys = list(srckey) if isinstance(srckey, list) else [srckey]
        sk = semkey or (str(keys[0]) + "_st")
        return self.P.dma(lambda e: e.dma_start(out=dst, in_=src), sk, reads=keys, writes=writes, q=q)

    def dump(self, name, ap, keys, dt=F32):
        if not self.debug:
            return
        shape = list(ap.shape)
        d = self.nc.dram_tensor("dbg_" + name, shape, dt, kind="ExternalOutput").ap()
        self.finals.append(self.P.dma(lambda e: e.dma_start(out=d, in_=ap), "dump_" + name, reads=list(keys)))

    def store(self, dst, src, srckey, writes=(), q="sp", semkey=None):
        return self.stt(dst, src, srckey, writes, q, semkey)

    def act(self, out, in_, func, reads, writes, bias=None, scale=None, accum=None):
        kw = {}
        if bias is not None:
            kw["bias"] = bias
        if scale is not None:
            kw["scale"] = scale
        if accum is not None:
            kw["accum_out"] = accum
        return self.P.op("act", lambda e: e.activation(out=out, in_=in_, func=func, **kw), reads=reads, writes=writes)

    def tt(self, eng, out, in0, in1, op, reads, writes):
        return self.P.op(eng, lambda e: e.tensor_tensor(out, in0, in1, op), reads=reads, writes=writes)

    def ts(self, eng, out, in0, s1, s2, op0, op1, reads, writes):
        if s2 is None:
            return self.P.op(eng, lambda e: e.tensor_scalar(out, in0, s1, None, op0), reads=reads, writes=writes)
        return self.P.op(eng, lambda e: e.tensor_scalar(out, in0, s1, s2, op0, op1), reads=reads, writes=writes)

    def sto(self, eng, out, in0, scalar, in1, op0, op1, reads, writes):
        return self.P.op(eng, lambda e: e.scalar_tensor_tensor(out=out, in0=in0, scalar=scalar, in1=in1, op0=op0,
                                                               op1=op1), reads=reads, writes=writes)

    def copy(self, eng, out, in_, reads, writes):
        if eng == "act":
            return self.P.op("act", lambda e: e.copy(out, in_), reads=reads, writes=writes)
        return self.P.op(eng, lambda e: e.tensor_copy(out, in_), reads=reads, writes=writes)

    def memset(self, eng, ap, val, writes):
        return self.P.op(eng, lambda e: e.memset(ap, val), writes=writes)

    def recip(self, out, in_, reads, writes):
        return self.P.op("dve", lambda e: e.reciprocal(out, in_), reads=reads, writes=writes)

    def transpose(self, out, in_, ident, reads, writes):
        return self.P.op("pe", lambda e: e.transpose(out, in_, ident), reads=reads, writes=writes)

    def mm(self, out, lhsT, rhs, start, stop, reads, writes):
        return self.P.op("pe", lambda e: e.matmul(out, lhsT, rhs, start=start, stop=stop), reads=reads, writes=writes)

    def bank(self, n=8, base=0):
        i = base + (self.rr % n)
        self.rr += 1
        return i, self.psb[i], "ps%d" % i


def r3(ap, a):
    return ap.rearrange("p (a b) -> p a b", a=a)


V_NMIX = 0
V_NFFN = 16
V_NPLE = 32
V_NFIN = 48
V_PSCALE = 56
V_CW = 64
V_FCW = 160
V_DNW = 424
V_EPS = 426
V_ONE = 427
NVEC = 428
WIN_GROUP_C0 = [0, 512, 1024, 1536, 2048, 2568, 3080, 3592] + [4104 + 512 * i for i in range(6)]
WIN_GROUP_DST = [("u", None, 0), ("dqkv", None, 0), ("dqkv", None, 512), ("dqkv", None, 1024), ("dz", None, 0),
                 ("mq", None, 0), ("mk", None, 0), ("mv", None, 0)] + [("gate", None, 512 * i) for i in range(6)]

C_POOL = 0
C_DQKV = 512
C_DZ = 2048
C_DB = 2560
C_DA = 2564
C_MQ = 2568
C_MK = 2568 + 512
C_MV = 2568 + 1024
C_GATE = 4104


def build(NSEQ=2, debug=False, layers=(0, 1), stages="ABCDEF", final=True, scopes=False):
    nc = bass.Bass("TRN2", target_bir_lowering=False)
    kc = KC(nc, NSEQ, debug)
    P = kc.P
    P.scopes = scopes
    T = kc.T
    NT = kc.NT
    xT_in = kc.din("xT", [D, T])
    pT_in = kc.din("pT", [DEPTH, 256, T])
    w_in = kc.din("w_in", [DEPTH, D, IN_COLS])
    w_branch = kc.din("w_branch", [DEPTH, 3, 512, D])
    w_out = kc.din("w_out", [DEPTH, D, D])
    ffn_up = kc.din("ffn_up", [DEPTH, D, 2 * FFN])
    ffn_down = kc.din("ffn_down", [DEPTH, FFN, D])
    ple_gate = kc.din("ple_gate", [DEPTH, D, D])
    ple_proj = kc.din("ple_proj", [DEPTH, 256, D])
    pool_w = kc.din("pool_w", [DEPTH, 4, 128, 128])
    vecs_d = kc.din("vecs", [128, NVEC])
    ident_d = kc.din("ident", [128, 128])
    tri_d = kc.din("tri", [64, 4 * 64])
    rcnt_d = kc.din("rcnt", [128, 64])
    band_d = kc.din("band", [128, 8 * 1152])
    ind_d = kc.din("ind", [128, 64 * 128])
    a4_d = kc.din("a4", [4, 4])
    outT = kc.dscr("outT", [D, T], out=True)
    xs = kc.dscr("xs", [D, T])
    x1s = kc.dscr("x1s", [D, T])
    uT = kc.dscr("uT", [512, T])
    dqkvT = kc.dscr("dqkvT", [1536, T])
    dzT = kc.dscr("dzT", [512, T])
    dbg = kc.dscr("dbg", [8, T])
    mqT = kc.dscr("mqT", [512, T], BF16)
    mkT = kc.dscr("mkT", [512, T], BF16)
    mv = kc.dscr("mv", [T, 512], BF16)
    gatesT = kc.dscr("gatesT", [3072, T], BF16)
    ybT = kc.dscr("ybT", [1536, T], BF16)
    NG_IN = 15
    winb = kc.dscr("winb", [DEPTH, 15, 128, 8 * 512], BF16)
    wbrb = kc.dscr("wbrb", [DEPTH, 128, 12 * 1024], BF16)
    woutb = kc.dscr("woutb", [DEPTH, 128, 8 * 1024], BF16)
    wupb = kc.dscr("wupb", [DEPTH, 22, 128, 8 * 256], BF16)
    wdnb = kc.dscr("wdnb", [DEPTH, 128, 22 * 1024], BF16)
    pgb = kc.dscr("pgb", [DEPTH, 128, 8 * 1024], BF16)
    ppb = kc.dscr("ppb", [DEPTH, 128, 2 * 1024], BF16)
    pwb = kc.dscr("pwb", [DEPTH, 128, 4 * 128], BF16)

    vecs = kc.alloc(NVEC)
    ident = kc.alloc(128)
    ones_bf = kc.alloc(128, BF16)
    ones_f = kc.alloc(128)
    kc.ld(vecs, vecs_d, "vecs")
    kc.ld(ident, ident_d, "ident")
    P.op("pool", lambda e: e.memset(ones_bf, 1.0), writes=["ones_bf"])
    P.op("pool", lambda e: e.memset(ones_f, 1.0), writes=["ones_f"])
    kc.make_perm()
    CONST = ["vecs", "ident", "ones_bf", "ones_f"]

    def cast(dst, src, key):
        grp = ("CAST0" if key[0] == "winb" else "CAST1") + ("" if key[1] == layers[0] else "b")
        if key[0] == "winb" and key[1] == layers[0]:
            grp = "CAST0_%d" % key[2]
        P.dma(lambda e: e.dma_start(out=dst, in_=src), grp, writes=[grp], q="pool")

    for l in layers:
        for g in range(14):
            c0 = WIN_GROUP_C0[g]
            cast(winb[l, g].rearrange("p (k c) -> p k c", k=8),
                 w_in[l, :, c0:c0 + 512].rearrange("(k p) c -> p k c", p=128), ("winb", l, g))
        cast(winb[l, 14].rearrange("p (k c) -> p k c", k=8)[:, :, 0:8],
             w_in[l, :, C_DB:C_DB + 8].rearrange("(k p) c -> p k c", p=128), ("winb", l, 14))
        cast(pwb[l].rearrange("p (g d) -> p g d", g=4), pool_w[l].rearrange("g c d -> c g d"), ("pwb", l))
        for n in range(3):
            cast(wbrb[l].rearrange("p (n k d) -> p n k d", n=3, k=4)[:, n],
                 w_branch[l, n].rearrange("(k p) d -> p k d", p=128), ("wbrb", l, n))
        cast(woutb[l].rearrange("p (k d) -> p k d", k=8), w_out[l].rearrange("(k p) d -> p k d", p=128), ("woutb", l))
        for j in range(22):
            dstv = wupb[l, j].rearrange("p (k c) -> p k c", k=8)
            cast(dstv[:, :, 0:128], ffn_up[l, :, j * 128:(j + 1) * 128].rearrange("(k p) c -> p k c", p=128),
                 ("wupb", l, j, 0))
            cast(dstv[:, :, 128:256],
                 ffn_up[l, :, FFN + j * 128:FFN + (j + 1) * 128].rearrange("(k p) c -> p k c", p=128),
                 ("wupb", l, j, 1))
        cast(wdnb[l].rearrange("p (j d) -> p j d", j=22), ffn_down[l].rearrange("(j p) d -> p j d", p=128),
             ("wdnb", l))
        cast(pgb[l].rearrange("p (k d) -> p k d", k=8), ple_gate[l].rearrange("(k p) d -> p k d", p=128), ("pgb", l))
        cast(ppb[l].rearrange("p (k d) -> p k d", k=2), ple_proj[l].rearrange("(k p) d -> p k d", p=128), ("ppb", l))

    def vcol(i):
        return vecs[:, i:i + 1]

    def rmsnorm_tile(xt, xkey, nbase, out_fn, outkeys, sqt, sqkey, rstd, rkey, engs=("dve",), sq_extra=()):
        P.op("act", lambda e: e.activation(out=sqt, in_=xt, func=AF.Square), reads=[xkey],
             writes=[sqkey] + list(sq_extra))
        bi, pb, pk = kc.bank()
        for k in range(8):
            kc.mm(pb[:, :], ones_bf, sqt[:, k * 512:(k + 1) * 512], k == 0, k == 7, [sqkey, "ones_bf"], [pk])
        P.op("act", lambda e: e.activation(out=rstd, in_=pb[:, :], func=AF.Ln, bias=vcol(V_EPS), scale=1.0 / D),
             reads=[pk, "vecs"], writes=[rkey])
        P.op("act", lambda e: e.activation(out=rstd, in_=rstd, func=AF.Exp, scale=-0.5), reads=[rkey], writes=[rkey])
        for k in range(8):
            o = out_fn(k)
            eng = engs[k % len(engs)]
            P.op(eng, (lambda o=o, k=k: lambda e: e.scalar_tensor_tensor(
                out=o, in0=xt[:, k * 512:(k + 1) * 512], scalar=vcol(nbase + k), in1=rstd,
                op0=ALU.mult, op1=ALU.mult))(), reads=[xkey, rkey, "vecs"], writes=[outkeys[k]])

    stages0 = stages
    for l in layers:
        stages = stages0 if l == layers[0] else _os.environ.get("L1S", stages0)
        xsrc = xT_in if l == 0 else xs
        xsrc_key = "xs"
        if "A" in stages:
            P.cur_tag = "L%dA" % l
            hT = kc.alloc(8 * T, BF16)
            hT3 = r3(hT, 8)
            xtb = [kc.alloc(8 * 512) for _ in range(2)]
            sqt = kc.alloc(8 * 512, BF16)
            rstd = kc.alloc(512)
            for tt in range(NT):
                xt = xtb[tt % 2]
                xk = "A_xt%d" % (tt % 2)
                kc.ld(r3(xt, 8), xsrc[:, tt * 512:(tt + 1) * 512].rearrange("(k p) t -> p k t", p=128), xk,
                      reads=[xsrc_key])
                rmsnorm_tile(xt, xk, V_NMIX + l * 8, lambda k: hT3[:, k, tt * 512:(tt + 1) * 512],
                             [("hT", tt, k) for k in range(8)], sqt, "A_sq", rstd, "A_rstd")
            HT_ALL = [("hT", tt) for tt in range(NT)]
            wgb = [kc.alloc(8 * 512, BF16) for _ in range(2)]
            ob32 = [kc.alloc(T) for _ in range(2)]
            ob16 = [kc.alloc(T, BF16) for _ in range(2)]
            ovb = [kc.alloc(512, BF16) for _ in range(2)]
            cnt = {"o32": 0, "o16": 0, "ov": 0, "ev": 0}

            def evac(kind, dst, src, pk, okey):
                if kind in ("u", "dqkv"):
                    eng = "act" if cnt["ev"] % 2 == 0 else "dve"
                    cnt["ev"] += 1
                    if eng == "act":
                        P.op("act", lambda e: e.copy(dst, src), reads=[pk], writes=[okey])
                    else:
                        P.op("dve", lambda e: e.tensor_copy(dst, src), reads=[pk], writes=[okey])
                elif kind == "dz":
                    P.op("act", lambda e: e.activation(out=dst, in_=src, func=AF.Silu), reads=[pk], writes=[okey])
                elif kind == "mq":
                    P.op("dve", lambda e: e.tensor_scalar(dst, src, 0.125, None, ALU.mult), reads=[pk], writes=[okey])
                elif kind == "mk":
                    P.op("dve", lambda e: e.tensor_copy(dst, src), reads=[pk], writes=[okey])
                elif kind == "gate":
                    P.op("act", lambda e: e.activation(out=dst, in_=src, func=AF.Sigmoid), reads=[pk], writes=[okey])
                else:
                    raise ValueError(kind)

            for g in range(14):
                wg = wgb[g % 2]
                wk = "A_wg%d" % (g % 2)
                kc.ld(wg, winb[l, g], wk, reads=["CAST0b"] if l != layers[0] else ["CAST0_%d" % g])
                wg3 = r3(wg, 8)
                gkind, gdst, grow0 = WIN_GROUP_DST[g]
                if gkind == "mv":
                    for i in range(T // 128):
                        bi, pb, pk = kc.bank()
                        for k in range(8):
                            kc.mm(pb[:, :], hT3[:, k, i * 128:(i + 1) * 128], wg3[:, k, :], k == 0, k == 7,
                                  [("hT", i // 4, k), wk], [pk])
                        ov = ovb[cnt["ov"] % 2]
                        ok = "A_ov%d" % (cnt["ov"] % 2)
                        cnt["ov"] += 1
                        eng = "act" if i % 2 == 0 else "dve"
                        if eng == "act":
                            P.op("act", (lambda ov=ov, pb=pb: lambda e: e.copy(ov, pb[:, :]))(), reads=[pk], writes=[ok])
                        else:
                            P.op("dve", (lambda ov=ov, pb=pb: lambda e: e.tensor_copy(ov, pb[:, :]))(), reads=[pk],
                                 writes=[ok])
                        kc.stt(mv[i * 128:(i + 1) * 128, :], ov, ok, writes=["mv"])
                    continue
                for j in range(4):
                    is16 = gkind in ("mq", "mk", "gate")
                    if is16:
                        ob = ob16[cnt["o16"] % 2]
                        okey = "A_o16_%d" % (cnt["o16"] % 2)
                        cnt["o16"] += 1
                    else:
                        ob = ob32[cnt["o32"] % 2]
                        okey = "A_o32_%d" % (cnt["o32"] % 2)
                        cnt["o32"] += 1
                    for tt in range(NT):
                        bi, pb, pk = kc.bank()
                        for k in range(8):
                            kc.mm(pb[:, :], wg3[:, k, j * 128:(j + 1) * 128], hT3[:, k, tt * 512:(tt + 1) * 512],
                                  k == 0, k == 7, [("hT", tt, k), wk], [pk])
                        evac(gkind, ob[:, tt * 512:(tt + 1) * 512], pb[:, :], pk, (okey, tt))
                    dst = {"u": uT, "dqkv": dqkvT, "dz": dzT, "mq": mqT, "mk": mkT, "gate": gatesT}[gkind]
                    r0 = grow0 + j * 128
                    kc.stt(dst[r0:r0 + 128, :], ob, [(okey, tt) for tt in range(NT)], writes=[gkind + "_d"], semkey=okey + "_st")
            wg = wgb[0]
            wk = "A_wg0"
            kc.ld(wg, winb[l, 14], wk, reads=["CAST0b"] if l != layers[0] else ["CAST0_14"])
            wg3 = r3(wg, 8)
            a4 = kc.alloc(4)
            P_a4 = kc.ld(a4[0:4, :], a4_d, "A_a4")
            nexpA = kc.alloc(1)
            kc.act(nexpA[0:4, :], a4[0:4, 2 * l:2 * l + 1], AF.Exp, ["A_a4"], ["A_nexpA"])
            kc.ts("dve", nexpA[0:4, :], nexpA[0:4, :], -1.0, None, ALU.mult, None, ["A_nexpA"], ["A_nexpA"])
            obb = ob32[0]
            oba = ob32[1]
            for tt in range(NT):
                sl = slice(tt * 512, (tt + 1) * 512)
                bi, pb, pk = kc.bank()
                for k in range(8):
                    kc.mm(pb[0:4, :], wg3[:, k, 0:4], hT3[:, k, sl], k == 0, k == 7, [("hT", tt, k), wk], [pk])
                kc.act(obb[0:4, sl], pb[0:4, :], AF.Sigmoid, [pk], [("A_o32_0", tt)])
                bi, pb, pk = kc.bank()
                for k in range(8):
                    kc.mm(pb[0:4, :], wg3[:, k, 4:8], hT3[:, k, sl], k == 0, k == 7, [("hT", tt, k), wk], [pk])
                kc.act(oba[0:4, sl], pb[0:4, :], AF.Exp, [pk, "A_a4"], [("A_o32_1", tt)],
                       bias=a4[0:4, 2 * l + 1:2 * l + 2])
            AK1 = [("A_o32_1", tt) for tt in range(NT)]
            kc.act(oba[0:4, :], oba[0:4, :], AF.Ln, AK1 + ["vecs"], AK1, bias=vcol(V_ONE)[0:4, :])
            kc.ts("dve", oba[0:4, :], oba[0:4, :], nexpA[0:4, 0:1], None, ALU.mult, None, AK1 + ["A_nexpA"], AK1)
            kc.stt(dbg[0:4, :], obb[0:4, :], [("A_o32_0", tt) for tt in range(NT)], writes=["dbg"], semkey="A_o32_0_st")
            kc.stt(dbg[4:8, :], oba[0:4, :], [("A_o32_1", tt) for tt in range(NT)], writes=["dbg"], semkey="A_o32_1_st")
            kc.reset()
        if "B" in stages:
            P.cur_tag = "L%dB" % l
            pw = kc.alloc(4 * 128, BF16)
            kc.ld(pw, pwb[l], "B_pw", reads=["CAST1" if l == layers[0] else "CAST1b"])
            pw3 = r3(pw, 4)
            rc = kc.alloc(64)
            kc.ld(rc, rcnt_d, "B_rc")
            ub = [kc.alloc(16 + S) for _ in range(2)]
            sab = [kc.alloc(16 + S) for _ in range(2)]
            mxb = [kc.alloc(S, BF16) for _ in range(2)]
            t16 = kc.alloc(16)
            yob = [kc.alloc(S, BF16) for _ in range(2)]
            for i in range(2):
                kc.memset("pool", ub[i][:, 0:16], 0.0, ["B_u%dz" % i])
                kc.memset("pool", sab[i][:, 0:16], 0.0, ["B_s%dz" % i])
            it = 0
            for s_ in range(NSEQ):
                for g in range(4):
                    i2 = it % 2
                    u = ub[i2]
                    uk = "B_u%d" % i2
                    kc.ld(u[:, 16:], uT[g * 128:(g + 1) * 128, s_ * S:(s_ + 1) * S], uk)
                    cur, curk = u, uk
                    for j in range(g + 1):
                        dst = sab[j % 2]
                        dk = "B_s%d" % (j % 2)
                        sh = 1 << j
                        kc.tt("dve" if j % 2 == 0 else "pool", dst[:, 16:], cur[:, 16:], cur[:, 16 - sh:16 - sh + S],
                              ALU.add, [curk, curk + "z"], [dk])
                        cur, curk = dst, dk
                    w = 1 << (g + 1)
                    m = mxb[i2]
                    mk_ = "B_mx%d" % i2
                    kc.sto("dve", m, cur[:, 16:], 1.0 / w, u[:, 16:], ALU.mult, ALU.subtract, [curk, uk], [mk_])
                    kc.tt("dve", t16, cur[:, 16:32], rc[:, g * 16:(g + 1) * 16], ALU.mult, [curk, "B_rc"], ["B_t16"])
                    kc.tt("dve", m[:, 0:16], t16, u[:, 16:32], ALU.subtract, ["B_t16", uk, mk_], [mk_])
                    y = yob[i2]
                    yk = "B_y%d" % i2
                    for j in range(4):
                        bi, pb, pk = kc.bank()
                        kc.mm(pb[:, :], pw3[:, g, :], m[:, j * 512:(j + 1) * 512], True, True, [mk_, "B_pw"], [pk])
                        kc.ts("dve" if j % 2 == 0 else "dve", y[:, j * 512:(j + 1) * 512], pb[:, :],
                              vcol(V_PSCALE + l * 4 + g), None, ALU.mult, None, [pk, "vecs"], [yk])
                    kc.store(ybT[g * 128:(g + 1) * 128, s_ * S:(s_ + 1) * S], y, yk, writes=["ybT"])
                    it += 1
            kc.reset()

        if "C" in stages:
            P.cur_tag = "L%dC" % l
            tri = kc.alloc(256)
            kc.ld(tri[0:64, :], tri_d, "C_tri")
            Ut = tri[0:64, 0:64]
            mA = tri[0:64, 64:128]
            mB = tri[0:64, 128:192]
            SU = tri[0:64, 192:256]
            identb3 = ident[0:64, 0:64].rearrange("p (o j) -> p o j", o=1).to_broadcast([64, 8, 64])
            raw = kc.alloc(3 + S)
            Xb = raw[:, 3:3 + S]
            acc = kc.alloc(S)
            sqb = kc.alloc(S, BF16)
            rn = kc.alloc(512)
            khb = kc.alloc(S, BF16)
            qhb = kc.alloc(S, BF16)
            gcrow = kc.alloc(S)
            E1 = kc.alloc(S)
            brow = kc.alloc(S)
            gT = kc.alloc(32)
            bT = kc.alloc(32)
            gcc = kc.alloc(32)
            egd = kc.alloc(32)
            Dm = kc.alloc(512)
            Gb = kc.alloc(512)
            GTi = kc.alloc(512)
            GTb = kc.alloc(512)
            tmpD = kc.alloc(512)
            Qk = [kc.alloc(512) for _ in range(2)]
            Rk = [kc.alloc(512) for _ in range(2)]
            Gk = [kc.alloc(512) for _ in range(2)]
            SETS = []
            for i_ in range(2):
                SETS.append(dict(i=i_, keT=kc.alloc(S), qdT=kc.alloc(S), kdec=kc.alloc(32 * 128, BF16),
                                 vtok=kc.alloc(32 * 128, BF16), aqkT=kc.alloc(S, BF16), TTb=kc.alloc(S, BF16),
                                 egl=kc.alloc(32)))
            oT = kc.alloc(S)
            S_ = kc.alloc(128)
            Rb = kc.alloc(128, BF16)
            vn = kc.alloc(128, BF16)
            zsb = [kc.alloc(512) for _ in range(2)]
            yout = kc.alloc(S, BF16)
            sqb2 = kc.alloc(S, BF16)
            rn2 = kc.alloc(512)
            kc.memset("pool", raw[:, 0:3], 0.0, ["C_rawz"])
            identP = kc.alloc(64)
            bTe = kc.alloc(16)
            bTo = kc.alloc(16)
            kc.copy("dve", identP[0:64, :], ident[0:64, 0:64], ["ident"], ["C_identP"])
            kc.copy("dve", identP[64:128, :], ident[64:128, 64:128], ["ident"], ["C_identP"])

            def bc_mid(ap64, n):
                return ap64.rearrange("p (o j) -> p o j", o=1).to_broadcast([64, n, 64])

            def bc_last(ap, n, w):
                return ap.rearrange("p (n o) -> p n o", o=1).to_broadcast([64, n, w])

            def phaseA(s_, h, st):
                t0 = s_ * S
                si = st["i"]
                kK = lambda nm: "C_%s%d" % (nm, si)
                keT, qdT, kdec, vtok, aqkT, TTb, egl = (st[k] for k in ("keT", "qdT", "kdec", "vtok", "aqkT", "TTb", "egl"))
                kc.ld(gcrow[0:64, :], dbg[4 + h:5 + h, t0:t0 + S].partition_broadcast(64), "C_gcrow")
                kc.ld(brow[0:64, :], dbg[h:h + 1, t0:t0 + S].partition_broadcast(64), "C_brow")
                idb32 = ident[0:64, 0:64].rearrange("p (o j) -> p o j", o=1).to_broadcast([64, 32, 64])
                kc.tt("dve", r3(Xb[0:64, :], 32), r3(gcrow[0:64, :], 32), idb32, ALU.mult, ["C_gcrow", "ident"], ["C_raw"])
                P.op("dve", (lambda: lambda e: e.tensor_reduce(gT[0:64, :], r3(Xb[0:64, :], 32), AX.X, ALU.add))(),
                     reads=["C_raw"], writes=["C_gT"])
                kc.tt("dve", r3(Xb[0:64, :], 32), r3(brow[0:64, :], 32), idb32, ALU.mult, ["C_brow", "ident"], ["C_raw"])
                P.op("dve", (lambda: lambda e: e.tensor_reduce(bT[0:64, :], r3(Xb[0:64, :], 32), AX.X, ALU.add))(),
                     reads=["C_raw"], writes=["C_bT"])
                yield
                bi, pb, pk = kc.bank()
                kc.mm(pb[0:64, 0:32], Ut, gT[0:64, :], True, True, ["C_tri", "C_gT"], [pk])
                kc.copy("act", gcc[0:64, :], pb[0:64, 0:32], [pk], ["C_gcc"])
                bi, pb, pk = kc.bank()
                kc.mm(pb[:, 0:32], ones_f[0:64, 0:128], gT[0:64, :], True, True, ["ones_f", "C_gT"], [pk])
                kc.act(egl[:, :], pb[:, 0:32], AF.Exp, [pk], [kK("egl")])
                kc.tt("dve", egd[0:64, :], pb[0:64, 0:32], gcc[0:64, :], ALU.subtract, [pk, "C_gcc"], ["C_egd"])
                kc.act(egd[0:64, :], egd[0:64, :], AF.Exp, ["C_egd"], ["C_egd"])
                kc.tt("dve", r3(Xb[0:64, :], 32), bc_last(gT[0:64, :], 32, 64), bc_mid(Ut, 32), ALU.mult,
                      ["C_gT", "C_tri"], ["C_raw"])
                yield
                for q4 in range(4):
                    sl = slice(q4 * 512, (q4 + 1) * 512)
                    bi, pb, pk = kc.bank()
                    kc.mm(pb[:, :], ones_f[0:64, 0:128], Xb[0:64, sl], True, True, ["ones_f", "C_raw"], [pk])
                    kc.copy("dve", gcrow[:, sl], pb[:, :], [pk], ["C_gcrow"])
                    kc.act(E1[:, sl], pb[:, :], AF.Exp, [pk], ["C_E1"])
                    yield

                def conv_silu(comp):
                    blk = comp * 4 + h
                    kc.ld(raw[:, 3:3 + S], dqkvT[blk * 128:(blk + 1) * 128, t0:t0 + S], "C_raw")
                    cwi = V_CW + (l * 12 + blk) * 4
                    kc.ts("dve", acc, raw[:, 0:S], vcol(cwi), None, ALU.mult, None, ["C_raw", "C_rawz", "vecs"], ["C_acc"])
                    for j in range(1, 4):
                        kc.sto("dve", acc, raw[:, j:j + S], vcol(cwi + j), acc, ALU.mult, ALU.add,
                               ["C_raw", "C_rawz", "C_acc", "vecs"], ["C_acc"])
                    kc.act(acc, acc, AF.Silu, ["C_acc"], ["C_acc"])

                def l2n_slice(q4, scale):
                    sl = slice(q4 * 512, (q4 + 1) * 512)
                    bi, pb, pk = kc.bank()
                    kc.mm(pb[:, :], ones_bf, sqb[:, sl], True, True, ["ones_bf", "C_sqb"], [pk])
                    kc.act(rn, pb[:, :], AF.Ln, [pk, "vecs"], ["C_rn"], bias=vcol(V_EPS))
                    kc.act(rn, rn, AF.Exp, ["C_rn"], ["C_rn"], scale=-0.5)
                    kc.sto("dve", acc[:, sl], acc[:, sl], scale, rn, ALU.mult, ALU.mult, ["C_acc", "C_rn"], ["C_acc"])

                def to_tok4(n4, dst, dkey, mul_egd):
                    bi, pb, pk = kc.bank()
                    for c in range(4):
                        n = n4 * 4 + c
                        kc.transpose(pb[0:64, c * 128:(c + 1) * 128], acc[:, n * 64:(n + 1) * 64], ident,
                                     ["C_acc", "ident"], [pk])
                    dsl = dst[0:64, n4 * 512:(n4 + 1) * 512]
                    if mul_egd:
                        kc.tt("dve", r3(dsl, 4), r3(pb[0:64, :], 4), bc_last(egd[0:64, n4 * 4:n4 * 4 + 4], 4, 128),
                              ALU.mult, [pk, "C_egd"], [dkey])
                    else:
                        kc.copy("act", dsl, pb[0:64, :], [pk], [dkey])

                conv_silu(1)
                yield
                kc.act(sqb, acc, AF.Square, ["C_acc"], ["C_sqb"])
                for q4 in range(4):
                    l2n_slice(q4, 1.0)
                    yield
                kc.copy("act", khb, acc, ["C_acc"], ["C_khb"])
                kc.tt("dve", keT, acc, E1, ALU.mult, ["C_acc", "C_E1"], [kK("keT")])
                yield
                for n4 in range(8):
                    to_tok4(n4, kdec, kK("kdec"), True)
                    yield
                conv_silu(0)
                yield
                kc.act(sqb, acc, AF.Square, ["C_acc"], ["C_sqb"])
                for q4 in range(4):
                    l2n_slice(q4, 128.0 ** -0.5)
                    yield
                kc.copy("act", qhb, acc, ["C_acc"], ["C_qhb"])
                kc.tt("dve", qdT, acc, E1, ALU.mult, ["C_acc", "C_E1"], [kK("qdT")])
                yield
                conv_silu(2)
                yield
                for n4 in range(8):
                    to_tok4(n4, vtok, kK("vtok"), False)
                    yield

                kc.copy("dve", bTe[0:64, :], bT[0:64, :].rearrange("p (m two) -> p m two", two=2)[:, :, 0], ["C_bT"],
                        ["C_bTe"])
                kc.copy("dve", bTo[0:64, :], bT[0:64, :].rearrange("p (m two) -> p m two", two=2)[:, :, 1], ["C_bT"],
                        ["C_bTo"])

                def v4(ap):
                    return ap.rearrange("p (m two f) -> p m two f", m=4, two=2)

                for bt in range(4):
                    sb, lb = bt // 2, bt % 2
                    n0 = bt * 8
                    c0 = bt * 512
                    bsl = slice(c0, c0 + 512)
                    kc.tt("dve", r3(Dm[0:64, :], 8), r3(gcrow[0:64, bsl], 8), bc_last(gcc[0:64, n0:n0 + 8], 8, 64),
                          ALU.subtract, ["C_gcrow", "C_gcc"], ["C_Dm"])
                    kc.tt("pool", r3(tmpD[0:64, :], 8), r3(Dm[0:64, :], 8), bc_mid(mA, 8), ALU.add,
                          ["C_Dm", "C_tri"], ["C_tmpD"])
                    kc.act(GTi[0:64, :], tmpD[0:64, :], AF.Exp, ["C_tmpD"], ["C_GTi"])
                    kc.tt("dve", r3(tmpD[0:64, :], 8), bc_mid(mB, 8), r3(Dm[0:64, :], 8), ALU.subtract,
                          ["C_Dm", "C_tri", "C_tmpD"], ["C_tmpD"])
                    kc.act(Gb[0:64, :], tmpD[0:64, :], AF.Exp, ["C_tmpD"], ["C_Gb"])
                    kc.tt("dve", r3(Gb[0:64, :], 8), r3(Gb[0:64, :], 8), bc_last(bT[0:64, n0:n0 + 8], 8, 64),
                          ALU.mult, ["C_Gb", "C_bT"], ["C_Gb"])
                    kc.tt("pool", r3(GTb[0:64, :], 8), r3(GTi[0:64, :], 8), bc_mid(SU, 8), ALU.mult,
                          ["C_GTi", "C_tri"], ["C_GTb"])
                    kc.tt("pool", GTb[0:64, :], GTb[0:64, :], brow[0:64, bsl], ALU.mult, ["C_GTb", "C_brow"],
                          ["C_GTb"])
                    yield
                    Q, R, G = Qk[sb], Rk[sb], Gk[sb]
                    qk_, rk_, gk_ = "C_Q%d" % sb, "C_R%d" % sb, "C_G%d" % sb
                    lsl = slice(lb * 256, (lb + 1) * 256)
                    bi, pb, pk = kc.bank()
                    for c in range(8):
                        cs = slice((n0 + c) * 64, (n0 + c + 1) * 64)
                        kc.mm(pb[0:64, c * 64:(c + 1) * 64], khb[:, cs], khb[:, cs], True, True, ["C_khb"], [pk])
                    pv_ = v4(pb[0:64, :])
                    for par in range(2):
                        psl = slice(par * 64, (par + 1) * 64)
                        kc.tt("dve", r3(R[psl, lsl], 4), pv_[:, :, par, :], v4(Gb[0:64, :])[:, :, par, :], ALU.mult,
                              [pk, "C_Gb"], [(rk_, lb, par)])
                        kc.tt("dve", r3(Q[psl, lsl], 4), pv_[:, :, par, :], v4(GTb[0:64, :])[:, :, par, :], ALU.mult,
                              [pk, "C_GTb"], [(qk_, lb, par)])
                    bi, pb, pk = kc.bank()
                    for c in range(8):
                        cs = slice((n0 + c) * 64, (n0 + c + 1) * 64)
                        kc.mm(pb[0:64, c * 64:(c + 1) * 64], khb[:, cs], qhb[:, cs], True, True,
                              ["C_khb", "C_qhb"], [pk])
                    kc.tt("dve", aqkT[0:64, bsl], pb[0:64, :], GTi[0:64, :], ALU.mult, [pk, "C_GTi"], [kK("aqkT")])
                    kc.tt("pool", r3(G[:, lsl], 4), identP.rearrange("p (o j) -> p o j", o=1).to_broadcast([128, 4, 64]),
                          r3(Q[:, lsl], 4), ALU.subtract, ["C_identP", (qk_, lb, 0), (qk_, lb, 1)], [(gk_, lb)])
                    yield
                QRK = lambda nm: [(nm, lb_, par_) for lb_ in range(2) for par_ in range(2)]
                for lev in range(1, 6):
                    for sb in range(2):
                        Q, R, G = Qk[sb], Rk[sb], Gk[sb]
                        qk_, rk_, gk_ = "C_Q%d" % sb, "C_R%d" % sb, "C_G%d" % sb
                        qks = QRK(qk_) if lev == 1 else [qk_]
                        rks = QRK(rk_) if lev == 1 else [rk_]
                        gks = [(gk_, 0), (gk_, 1)] if lev == 1 else [gk_]

                        def pairmm(pbx, pkx, A_, B_, rd):
                            for m in range(8):
                                ms = slice(m * 64, (m + 1) * 64)
                                kc.mm(pbx[0:64, ms], A_[0:64, ms], B_[0:64, ms], True, True, rd, [pkx])
                                P.op("pe", (lambda o=pbx[64:128, ms], a_=A_[64:128, ms], b_=B_[64:128, ms]:
                                            lambda e: e.matmul(o, a_, b_, start=True, stop=True, tile_position=(64, 64)))(),
                                     reads=rd, writes=[pkx])

                        if lev < 5:
                            bq, pbq, pkq = kc.bank()
                            pairmm(pbq, pkq, R, Q, qks + rks)
                        br, pbr, pkr = kc.bank()
                        pairmm(pbr, pkr, Q, R, qks + rks)
                        if lev < 5:
                            kc.copy("act", Q[:, :], pbq[:, :], [pkq], [qk_] + QRK(qk_))
                        kc.copy("act", R[:, :], pbr[:, :], [pkr], [rk_] + QRK(rk_))
                        yield
                        bg, pbg, pkg = kc.bank()
                        pairmm(pbg, pkg, R, G, [rk_] + gks)
                        kc.tt("dve", G[:, :], G[:, :], pbg[:, :], ALU.add, gks + [pkg], [gk_, (gk_, 0), (gk_, 1)])
                        yield
                for sb in range(2):
                    G = Gk[sb]
                    gk_ = "C_G%d" % sb
                    TTv = TTb[0:64, sb * 1024:(sb + 1) * 1024].rearrange("p (m two f) -> p m two f", m=8, two=2)
                    kc.tt("dve", TTv[:, :, 0, :], r3(G[0:64, :], 8), bc_last(bTe[0:64, sb * 8:(sb + 1) * 8], 8, 64), ALU.mult,
                          [gk_, "C_bTe"], [kK("TTb")])
                    kc.copy("act", tmpD[0:64, :], G[64:128, :], [gk_], ["C_tmpD"])
                    kc.tt("dve", TTv[:, :, 1, :], r3(tmpD[0:64, :], 8), bc_last(bTo[0:64, sb * 8:(sb + 1) * 8], 8, 64), ALU.mult,
                          ["C_tmpD", "C_bTo"], [kK("TTb")])
                    yield

            def phaseB(s_, h, st):
                t0 = s_ * S
                si = st["i"]
                kK = lambda nm: "C_%s%d" % (nm, si)
                keT, qdT, kdec, vtok, aqkT, TTb, egl = (st[k] for k in ("keT", "qdT", "kdec", "vtok", "aqkT", "TTb", "egl"))
                kc.memset("dve", S_, 0.0, ["C_S"])
                for n in range(32):
                    cs = slice(n * 64, (n + 1) * 64)
                    ns = slice(n * 128, (n + 1) * 128)
                    b1, pb1, pk1 = kc.bank()
                    kc.mm(pb1[0:64, 0:128], keT[:, cs], S_, True, True, [kK("keT"), "C_S"], [pk1])
                    b3, pb3, pk3 = kc.bank()
                    kc.mm(pb3[:, 0:64], S_, qdT[:, cs], True, False, [kK("qdT"), "C_S"], [pk3])
                    kc.tt("dve", Rb[0:64, :], vtok[0:64, ns], pb1[0:64, 0:128], ALU.subtract, [kK("vtok"), pk1], ["C_Rb"])
                    b2, pb2, pk2 = kc.bank()
                    kc.mm(pb2[0:64, 0:128], TTb[0:64, cs], Rb[0:64, :], True, True, [kK("TTb"), "C_Rb"], [pk2])
                    kc.copy("act", vn[0:64, :], pb2[0:64, 0:128], [pk2], ["C_vn"])
                    kc.mm(pb3[:, 0:64], vn[0:64, :], aqkT[0:64, cs], False, True, ["C_vn", kK("aqkT")], [pk3])
                    b4, pb4, pk4 = kc.bank()
                    kc.mm(pb4[:, 0:128], kdec[0:64, ns], vn[0:64, :], True, True, [kK("kdec"), "C_vn"], [pk4])
                    kc.copy("act", oT[:, cs], pb3[:, 0:64], [pk3], ["C_oT"])
                    kc.sto("dve", S_, S_, egl[:, n:n + 1], pb4[:, 0:128], ALU.mult, ALU.add, ["C_S", kK("egl"), pk4],
                           ["C_S"])
                    yield
                kc.act(sqb2, oT, AF.Square, ["C_oT"], ["C_sqb2"])
                for q4 in range(4):
                    sl = slice(q4 * 512, (q4 + 1) * 512)
                    zs = zsb[q4 % 2]
                    zk = "C_zs%d" % (q4 % 2)
                    kc.ld(zs, dzT[h * 128:(h + 1) * 128, t0 + q4 * 512:t0 + (q4 + 1) * 512], zk)
                    bi, pb, pk = kc.bank()
                    kc.mm(pb[:, :], ones_bf, sqb2[:, sl], True, True, ["ones_bf", "C_sqb2"], [pk])
                    kc.act(rn2, pb[:, :], AF.Ln, [pk, "vecs"], ["C_rn2"], bias=vcol(V_EPS), scale=1.0 / 128)
                    kc.act(rn2, rn2, AF.Exp, ["C_rn2"], ["C_rn2"], scale=-0.5)
                    kc.sto("dve", oT[:, sl], oT[:, sl], vcol(V_DNW + l), rn2, ALU.mult, ALU.mult,
                           ["C_oT", "C_rn2", "vecs"], ["C_oT"])
                    kc.tt("pool", yout[:, sl], oT[:, sl], zs, ALU.mult, ["C_oT", zk], ["C_yout"])
                    yield
                kc.store(ybT[512 + h * 128:512 + (h + 1) * 128, t0:t0 + S], yout, "C_yout", writes=["ybT"])
                yield

            heads = [(s_, h) for s_ in range(NSEQ) for h in range(4)]
            nA = 0
            for _ in phaseA(heads[0][0], heads[0][1], SETS[0]):
                nA += 1
            nB = 38
            per = max(1, -(-nA // nB))
            for i_, (s_, h) in enumerate(heads):
                gB = phaseB(s_, h, SETS[i_ % 2])
                gA = phaseA(heads[i_ + 1][0], heads[i_ + 1][1], SETS[(i_ + 1) % 2]) if i_ + 1 < len(heads) else None
                aliveA = gA is not None
                aliveB = True
                while aliveA or aliveB:
                    if aliveB:
                        try:
                            next(gB)
                        except StopIteration:
                            aliveB = False
                    if aliveA:
                        for _ in range(per):
                            try:
                                next(gA)
                            except StopIteration:
                                aliveA = False
                                break
            kc.reset()

        if "D" in stages:
            P.cur_tag = "L%dD" % l
            band = kc.alloc(8 * 1152, BF16)
            kc.ld(band, band_d, "D_band", q="pool")
            band3 = r3(band, 8)
            ind = kc.alloc(64 * 128, BF16)
            kc.ld(ind, ind_d, "D_ind", q="pool")
            ind3 = r3(ind, 64)
            identb = kc.alloc(128, BF16)
            kc.copy("dve", identb, ident, ["ident"], ["D_identb"])
            QT = kc.alloc(4 * S, BF16)
            KT = kc.alloc(4 * S, BF16)
            QT3 = r3(QT, 4)
            KT3 = r3(KT, 4)
            VP = kc.alloc(16 * 768, BF16)
            VP4 = VP.rearrange("p (i c w) -> p i c w", i=16, c=4)
            VP3 = r3(VP, 16)
            kc.memset("pool", VP, 0.0, ["D_VPz"])
            kc.memset("pool", VP4[:, :, :, 64:65], 1.0, ["D_VPz"])
            kms = kc.alloc(32)
            KM = kc.alloc(4 * 64, BF16)
            KM3 = r3(KM, 4)
            gsb = kc.alloc(128)
            gw = kc.alloc(128)
            mxt = kc.alloc(16)
            eqt = kc.alloc(128)
            Mt = kc.alloc(128)
            MallT = kc.alloc(S, BF16)
            ptb = [kc.alloc(512, BF16) for _ in range(5)]
            cfar = kc.alloc(8)
            kc.copy("dve", cfar, band3[:, :, 1151], ["D_band"], ["D_cfar"])
            rl = kc.alloc(512)
            osb = kc.alloc(512)
            ycT = kc.alloc(4 * S, BF16)
            ycT3 = r3(ycT, 4)
            kc.memset("pool", KM, 0.0, ["D_KMz"])
            KZ = kc.alloc(8 * S, BF16)
            KZ3 = r3(KZ, 8)
            kc.memset("pool", KZ, 0.0, ["D_KZz"])
            for s_ in range(NSEQ):
              try:
                t0 = s_ * S
                kc.ld(QT3, mqT[:, t0:t0 + S].rearrange("(c p) t -> p c t", p=128), "D_QT")
                kc.ld(KT3, mkT[:, t0:t0 + S].rearrange("(c p) t -> p c t", p=128), "D_KT")
                srcv = mv[t0:t0 + S, :].rearrange("(i p) (c two d) -> p i c two d", p=128, two=2, d=64)
                for c in range(4):
                    kc.ld(VP4[:, :, c, 0:64], srcv[:, :, c, 0, :], ("D_VP", c, 0), reads=["D_VPz"], semkey="D_VPa%d" % c)
                    kc.ld(VP4[:, :, c, 128:192], srcv[:, :, c, 1, :], ("D_VP", c, 1), reads=["D_VPz"],
                          semkey="D_VPb%d" % c)
                for h_ in range(8):
                    rr0 = (h_ % 2) * 64
                    kc.copy("pool" if h_ % 2 == 0 else "act", KZ3[rr0:rr0 + 64, h_, :], KT3[rr0:rr0 + 64, h_ // 2, :],
                            ["D_KT", "D_KZz"], [("D_KZ", h_)])
                P.op("dve", (lambda kms=kms, KT=KT: lambda e: e.tensor_reduce(r3(kms, 4), KT.rearrange("p (c n k) -> p c n k", c=4, n=8), AX.X,
                                                      ALU.add))(), reads=["D_KT"], writes=["D_kms"])
                kms3 = r3(kms, 4)
                for c in range(4):
                    kc.copy("dve", KM3[0:64, c, (2 * c) * 8:(2 * c) * 8 + 8], kms3[0:64, c, :], ["D_kms", "D_KMz"],
                            ["D_KM"])
                    kc.copy("dve", KM3[64:128, c, (2 * c + 1) * 8:(2 * c + 1) * 8 + 8], kms3[64:128, c, :],
                            ["D_kms", "D_KMz"], ["D_KM"])
                if DSTOP == 1:
                    kc.dump("KM", KM, ["D_KM"], BF16)
                    kc.dump("VP", VP, ["D_VPz"] + [("D_VP", c, hh) for c in range(4) for hh in range(2)], BF16)
                    kc.dump("kms", kms, ["D_kms"])
                    raise _Stop()
                for q4 in range(4):
                    bm, pbm, pkm = kc.bank()
                    for qq in range(4):
                        qt = q4 * 4 + qq
                        b = qt // 2
                        if b >= 4:
                            bi, pb, pk = kc.bank()
                            for c in range(4):
                                kc.mm(pb[:, 0:64], QT3[:, c, qt * 128:(qt + 1) * 128], KM3[:, c, :], c == 0, c == 3,
                                      ["D_QT", "D_KM"], [pk])
                            g3 = r3(gsb[:, 0:64], 8)
                            w3 = r3(gw[:, 0:64], 8)
                            e3 = r3(eqt[:, 0:64], 8)
                            kc.copy("act", gsb[:, 0:64], pb[:, 0:64], [pk], ["D_gsb"])
                            kc.memset("dve", g3[:, :, b:8], -1.0e9, ["D_gsb"])
                            src, srck = g3, "D_gsb"
                            for rnd in range(3):
                                P.op("dve", (lambda src=src, mxt=mxt: lambda e: e.tensor_reduce(mxt[:, 0:8], src, AX.X, ALU.max))(),
                                     reads=[srck], writes=["D_mxt"])
                                if rnd == 2:
                                    break
                                mb = mxt[:, 0:8].rearrange("p (h o) -> p h o", o=1).to_broadcast([128, 8, 8])
                                kc.tt("dve", e3, src, mb, ALU.is_ge, [srck, "D_mxt"], ["D_eqt"])
                                kc.sto("dve", w3, e3, -1.0e9, src, ALU.mult, ALU.add, ["D_eqt", srck], ["D_gw"])
                                src, srck = w3, "D_gw"
                            mb = mxt[:, 0:8].rearrange("p (h o) -> p h o", o=1).to_broadcast([128, 8, 8])
                            kc.tt("dve", e3, g3, mb, ALU.is_ge, ["D_gsb", "D_mxt"], ["D_eqt"])
                            kc.ts("dve", Mt[:, 0:64], eqt[:, 0:64], BIG, -BIG, ALU.mult, ALU.add, ["D_eqt"], ["D_Mt"])
                            kc.memset("dve", r3(Mt[:, 0:64], 8)[:, :, b:b + 1], 0.0, ["D_Mt"])
                        else:
                            kc.memset("dve", Mt[:, 0:64], 0.0, ["D_Mt"])
                        kc.transpose(pbm[0:64, qq * 128:(qq + 1) * 128], Mt[:, 0:64], ident, ["D_Mt", "ident"], [pkm])
                    kc.copy("act", MallT[0:64, q4 * 512:(q4 + 1) * 512], pbm[0:64, :], [pkm], ["D_MallT"])
                    kc.copy("act", MallT[64:128, q4 * 512:(q4 + 1) * 512], pbm[0:64, :], [pkm], ["D_MallT"])
                if DSTOP == 2:
                    kc.dump("MallT", MallT[0:64, :], ["D_MallT"], BF16)
                    raise _Stop()
                ipt = 0
                pend = []

                def flush(keep):
                    while len(pend) > keep:
                        pend.pop(0)()

                for h in range(8):
                    c = h // 2
                    r0 = (h % 2) * 64
                    lrow = 64 if h % 2 == 0 else 0
                    for qi in range(4):
                        q0 = qi * 512
                        nkt = (qi + 1) * 4
                        bo, pbo, pko = kc.bank(n=2, base=6)
                        for kt in range(nkt):
                            k0 = kt * 128
                            nblk = kt // 2
                            bs_, pbs, pks = kc.bank(n=5, base=0)
                            mms = [(KZ3[:, h, k0:k0 + 128], QT3[:, c, q0:q0 + 512], [("D_KZ", h), "D_KZz", "D_QT"])]
                            if qi >= 2 and nblk < 2 * qi + 1:
                                mms.append((ind3[:, h * 8 + nblk, :], MallT[:, q0:q0 + 512], ["D_ind", "D_MallT"]))
                            far = (q0 - k0) >= 256
                            if not far:
                                off = min(max(q0 - k0, -384), 256) + 384
                                mms.append((identb, band3[:, h, off:off + 512], ["D_identb", "D_band"]))
                            for i_, (a_, b_, rd_) in enumerate(mms):
                                kc.mm(pbs[:, :], a_, b_, i_ == 0, i_ == len(mms) - 1, rd_, [pks])
                            pt = ptb[ipt % 5]
                            ptk = "D_pt%d" % (ipt % 5)
                            ipt += 1
                            if far:
                                kc.act(pt, pbs[:, :], AF.Exp, [pks, "D_cfar"], [ptk], bias=cfar[:, h:h + 1])
                            else:
                                kc.act(pt, pbs[:, :], AF.Exp, [pks], [ptk])
                            lo = c * 192 + (h % 2) * 64

                            def pv(pbo=pbo, pko=pko, kt=kt, nkt=nkt, lo=lo, pt=pt, ptk=ptk, c=c, r0=r0, lrow=lrow, q0=q0):
                                kc.mm(pbo[:, :], VP3[:, kt, lo:lo + 128], pt, kt == 0, kt == nkt - 1,
                                      [("D_VP", c, 0), ("D_VP", c, 1), "D_VPz", ptk], [pko])
                                if kt == nkt - 1:
                                    kc.act(rl[lrow:lrow + 1, :], pbo[lrow:lrow + 1, :], AF.Ln, [pko], ["D_rl"])
                                    kc.act(rl[lrow:lrow + 1, :], rl[lrow:lrow + 1, :], AF.Exp, ["D_rl"], ["D_rl"], scale=-1.0)
                                    kc.copy("act", osb[r0:r0 + 64, :], pbo[r0:r0 + 64, :], [pko], ["D_osb"])
                                    br_, pbr, pkr = kc.bank(n=1, base=5)
                                    kc.mm(pbr[:, :], ones_f[lrow:lrow + 1, 0:128], rl[lrow:lrow + 1, :], True, True,
                                          ["ones_f", "D_rl"], [pkr])
                                    kc.tt("dve", ycT3[r0:r0 + 64, c, q0:q0 + 512], osb[r0:r0 + 64, :], pbr[r0:r0 + 64, :],
                                          ALU.mult, ["D_osb", pkr], ["D_ycT"])

                            pend.append(pv)
                            flush(2)
                    if DSTOP == 3 + h:
                        flush(0)
                        kc.dump("ycT", ycT, ["D_ycT"], BF16)
                        raise _Stop()
                flush(0)
                kc.store(ybT[1024:1536, t0:t0 + S].rearrange("(c p) t -> p c t", p=128), ycT3, "D_ycT", writes=["ybT"])
              except _Stop:
                pass
            kc.reset()

        if "E" in stages:
            P.cur_tag = "L%dE" % l
            wbr = kc.alloc(12 * 1024, BF16)
            kc.ld(wbr, wbrb[l], "E_wbr", reads=["CAST1" if l == layers[0] else "CAST1b"])
            wbr4 = wbr.rearrange("p (n k d) -> p n k d", n=3, k=4)
            wo = kc.alloc(8 * 1024, BF16)
            kc.ld(wo, woutb[l], "E_wo", reads=["CAST1" if l == layers[0] else "CAST1b"])
            wo3 = r3(wo, 8)
            ybb = [kc.alloc(12 * 512, BF16) for _ in range(2)]
            gtb = [kc.alloc(24 * 512, BF16) for _ in range(2)]
            xtb = [kc.alloc(8 * 512) for _ in range(2)]
            mg = kc.alloc(8 * 512, BF16)
            mg3 = r3(mg, 8)
            ta = kc.alloc(512)
            tb = kc.alloc(512)
            tc_ = kc.alloc(512)
            for tt in range(NT):
                i2 = tt % 2
                tsl = slice(tt * 512, (tt + 1) * 512)
                yb3 = r3(ybb[i2], 12)
                gt3 = r3(gtb[i2], 24)
                xt3 = r3(xtb[i2], 8)
                ybk, gtk, xk = "E_yb%d" % i2, "E_gt%d" % i2, "E_xt%d" % i2
                kc.ld(yb3, ybT[:, tsl].rearrange("(c p) t -> p c t", p=128), ybk)
                kc.ld(gt3, gatesT[:, tsl].rearrange("(c p) t -> p c t", p=128), gtk)
                kc.ld(xt3, xsrc[:, tsl].rearrange("(k p) t -> p k t", p=128), xk)
                for m in range(8):
                    pbs_ = []
                    for n in range(3):
                        bi, pb, pk = kc.bank()
                        for k in range(4):
                            kc.mm(pb[:, :], wbr4[:, n, k, m * 128:(m + 1) * 128], yb3[:, n * 4 + k, :], k == 0, k == 3,
                                  ["E_wbr", ybk], [pk])
                        pbs_.append((pb, pk))
                    kc.tt("dve", ta, pbs_[0][0][:, :], gt3[:, m, :], ALU.mult, [pbs_[0][1], gtk], ["E_ta"])
                    kc.tt("dve", tb, pbs_[1][0][:, :], gt3[:, 8 + m, :], ALU.mult, [pbs_[1][1], gtk], ["E_tb"])
                    kc.tt("dve", tc_, pbs_[2][0][:, :], gt3[:, 16 + m, :], ALU.mult, [pbs_[2][1], gtk], ["E_tc"])
                    kc.tt("pool", ta, ta, tb, ALU.add, ["E_ta", "E_tb"], ["E_ta"])
                    kc.tt("pool", mg3[:, m, :], ta, tc_, ALU.add, ["E_ta", "E_tc"], [("E_mg", m)])
                for m in range(8):
                    bi, pb, pk = kc.bank()
                    for k in range(8):
                        kc.mm(pb[:, :], wo3[:, k, m * 128:(m + 1) * 128], mg3[:, k, :], k == 0, k == 7,
                              ["E_wo", ("E_mg", k)], [pk])
                    kc.tt("dve", xt3[:, m, :], xt3[:, m, :], pb[:, :], ALU.add, [xk, pk], [xk])
                kc.store(x1s[:, tsl].rearrange("(k p) t -> p k t", p=128), xt3, xk, writes=["x1s"])
            kc.reset()

        if "F" in stages:
            P.cur_tag = "L%dF" % l
            wdn = kc.alloc(22 * 1024, BF16)
            kc.ld(wdn, wdnb[l], "F_wdn", reads=["CAST1" if l == layers[0] else "CAST1b"])
            wdn3 = r3(wdn, 22)
            pgw = kc.alloc(8 * 1024, BF16)
            kc.ld(pgw, pgb[l], "F_pgw", reads=["CAST1" if l == layers[0] else "CAST1b"])
            pgw3 = r3(pgw, 8)
            ppw = kc.alloc(2 * 1024, BF16)
            kc.ld(ppw, ppb[l], "F_ppw", reads=["CAST1" if l == layers[0] else "CAST1b"])
            ppw3 = r3(ppw, 2)
            hal = kc.alloc(44 * 2)
            xtb = [kc.alloc(8 * 512) for _ in range(2)]
            hb = [kc.alloc(8 * 512, BF16) for _ in range(2)]
            rstd1 = kc.alloc(512)
            rstd2 = kc.alloc(512)
            wub = [kc.alloc(8 * 256, BF16) for _ in range(2)]
            rawb = [kc.alloc(514) for _ in range(2)]
            yb_ = [kc.alloc(512) for _ in range(2)]
            gTb = [kc.alloc(22 * 512, BF16) for _ in range(2)]
            ptb_ = [kc.alloc(2 * 512, BF16) for _ in range(2)]
            sg = kc.alloc(512)
            tp = kc.alloc(512)
            last = (l == layers[-1]) and final
            iwc = [0]

            def f_phase1(tt):
                i2 = tt % 2
                tsl = slice(tt * 512, (tt + 1) * 512)
                xt = xtb[i2]
                xt3 = r3(xt, 8)
                xk = "F_xt%d" % i2
                kc.ld(xt3, x1s[:, tsl].rearrange("(k p) t -> p k t", p=128), xk)
                if tt % 4 == 0:
                    kc.memset("pool", hal, 0.0, ["F_hal"])
                h2 = hb[i2]
                h23 = r3(h2, 8)
                HK = [("F_h%d" % i2, k) for k in range(8)]
                rmsnorm_tile(xt, xk, V_NFFN + l * 8, lambda k: h23[:, k, :], HK, h2, "F_sq%d" % i2, rstd1, "F_rstd1",
                             sq_extra=HK)
                yield
                gT3 = r3(gTb[i2], 22)
                for j in range(22):
                    wu = wub[iwc[0] % 2]
                    wuk = "F_wu%d" % (iwc[0] % 2)
                    iwc[0] += 1
                    kc.ld(wu, wupb[l, j], wuk, reads=["CAST1" if l == layers[0] else "CAST1b"])
                    wu3 = r3(wu, 8)
                    for half in range(2):
                        bi, pb, pk = kc.bank()
                        for k in range(8):
                            kc.mm(pb[:, :], wu3[:, k, half * 128:(half + 1) * 128], h23[:, k, :], k == 0, k == 7,
                                  [wuk, HK[k]], [pk])
                        rb = rawb[half]
                        rk = "F_raw%d" % half
                        yb = yb_[half]
                        yk = "F_y%d" % half
                        hi = (j * 2 + half) * 2
                        kc.copy("act", rb[:, 2:514], pb[:, :], [pk], [rk])
                        kc.copy("pool", rb[:, 0:2], hal[:, hi:hi + 2], ["F_hal"], [rk + "h"])
                        kc.copy("pool", hal[:, hi:hi + 2], rb[:, 512:514], [rk], ["F_hal"])
                        cwi = V_FCW + (l * 44 + half * 22 + j) * 3
                        kc.act(yb, pb[:, :], AF.Copy, [pk, "vecs"], [yk], scale=vcol(cwi + 2))
                        kc.sto("dve", yb, rb[:, 0:512], vcol(cwi), yb, ALU.mult, ALU.add, [rk, rk + "h", yk, "vecs"], [yk])
                        kc.sto("dve", yb, rb[:, 1:513], vcol(cwi + 1), yb, ALU.mult, ALU.add, [rk, rk + "h", yk, "vecs"],
                               [yk])
                    kc.act(yb_[0], yb_[0], AF.Gelu_apprx_tanh, ["F_y0"], ["F_y0"])
                    kc.tt("dve", gT3[:, j, :], yb_[0], yb_[1], ALU.mult, ["F_y0", "F_y1"], [("F_g%d" % i2, j)])
                    yield

            def f_phase2(tt):
                i2 = tt % 2
                tsl = slice(tt * 512, (tt + 1) * 512)
                xt = xtb[i2]
                xt3 = r3(xt, 8)
                xk = "F_xt%d" % i2
                h2 = hb[i2]
                h23 = r3(h2, 8)
                HK = [("F_h%d" % i2, k) for k in range(8)]
                gT3 = r3(gTb[i2], 22)
                pt_ = ptb_[i2]
                ptk = "F_pt%d" % i2
                kc.ld(r3(pt_, 2), pT_in[l, :, tsl].rearrange("(k p) t -> p k t", p=128), ptk, q="pool")
                for m in range(8):
                    bi, pb, pk = kc.bank()
                    for j in range(22):
                        kc.mm(pb[:, :], wdn3[:, j, m * 128:(m + 1) * 128], gT3[:, j, :], j == 0, j == 21,
                              ["F_wdn", ("F_g%d" % i2, j)], [pk])
                    kc.tt("dve", xt3[:, m, :], xt3[:, m, :], pb[:, :], ALU.add, [xk, pk], [xk])
                    yield
                rmsnorm_tile(xt, xk, V_NPLE + l * 8, lambda k: h23[:, k, :], HK, h2, "F_sq%d" % i2, rstd2, "F_rstd2",
                             sq_extra=HK)
                yield
                for m in range(8):
                    bi, pb, pk = kc.bank()
                    for k in range(8):
                        kc.mm(pb[:, :], pgw3[:, k, m * 128:(m + 1) * 128], h23[:, k, :], k == 0, k == 7,
                              ["F_pgw", HK[k]], [pk])
                    kc.act(sg, pb[:, :], AF.Sigmoid, [pk], ["F_sg"])
                    bi, pb, pk = kc.bank()
                    for k in range(2):
                        kc.mm(pb[:, :], ppw3[:, k, m * 128:(m + 1) * 128], r3(pt_, 2)[:, k, :], k == 0, k == 1,
                              ["F_ppw", ptk], [pk])
                    kc.tt("dve", tp, pb[:, :], sg, ALU.mult, [pk, "F_sg"], ["F_tp"])
                    kc.tt("pool", xt3[:, m, :], xt3[:, m, :], tp, ALU.add, [xk, "F_tp"], [xk])
                    yield
                if last:
                    OK_ = [("F_ot%d" % i2, k) for k in range(8)]
                    rmsnorm_tile(xt, xk, V_NFIN, lambda k: xt3[:, k, :], OK_, h2, "F_sq%d" % i2, rstd2, "F_rstd2",
                                 sq_extra=HK)
                    kc.finals.append(kc.store(outT[:, tsl].rearrange("(k p) t -> p k t", p=128), xt3, OK_ + [xk],
                                              writes=["outT"], semkey="F_ot_st%d" % i2))
                else:
                    kc.store(xs[:, tsl].rearrange("(k p) t -> p k t", p=128), xt3, xk, writes=["xs"])
                yield

            for _ in f_phase1(0):
                pass
            for tt in range(NT):
                g2 = f_phase2(tt)
                g1 = f_phase1(tt + 1) if tt + 1 < NT else None
                a1 = g1 is not None
                a2 = True
                while a1 or a2:
                    if a1:
                        try:
                            next(g1)
                        except StopIteration:
                            a1 = False
                    if a2:
                        try:
                            next(g2)
                        except StopIteration:
                            a2 = False
            kc.reset()
    nsem = P.emit(final_wait_ops=kc.finals)
    return nc, kc, nsem


def _t5_bucket_np(n):
    n = np.maximum(n, 0)
    exact = 16
    nf = np.maximum(n, 1).astype(np.float32)
    large = exact + (np.log(nf / exact) / math.log(128 / exact) * (32 - exact)).astype(np.int32)
    large = np.minimum(large, 31)
    return np.where(n < exact, n, large)


def host_consts():
    ident = np.eye(128, dtype=np.float32)
    p = np.arange(64)[:, None]
    f = np.arange(64)[None, :]
    U = (p <= f).astype(np.float32)
    mA = np.where(f >= p, 0.0, -BIG).astype(np.float32)
    mB = np.where(p > f, 0.0, -BIG).astype(np.float32)
    SU = (f > p).astype(np.float32)
    tri = np.concatenate([U, mA, mB, SU], axis=1)
    rc = np.zeros((128, 64), np.float32)
    for g in range(4):
        w = 2 << g
        for t in range(16):
            rc[:, g * 16 + t] = 1.0 / min(t + 1, w)
    ind = np.zeros((64, 64, 128), np.float32)
    for r in range(64):
        ind[r, r, :] = 1.0
    ind = ind.reshape(64, 64 * 128)
    return ident, tri, rc, np.concatenate([ind, ind], axis=0)


def host_vecs(inp):
    v = np.zeros((128, NVEC), np.float32)

    def put(base, arr):
        a = np.asarray(arr, np.float32)
        lead = int(np.prod(a.shape[:-1])) if a.ndim > 1 else 1
        a = a.reshape(lead, -1, 128)
        a = a.transpose(2, 0, 1).reshape(128, -1)
        v[:, base:base + a.shape[1]] = a

    put(V_NMIX, inp["norm_mix"])
    put(V_NFFN, inp["norm_ffn"])
    put(V_NPLE, inp["norm_ple"])
    put(V_NFIN, inp["norm_final"])
    put(V_PSCALE, inp["pool_scale"])
    cw = np.asarray(inp["dn_conv"], np.float32).reshape(2, 4, 12, 128).transpose(3, 0, 2, 1).reshape(128, 96)
    v[:, V_CW:V_CW + 96] = cw
    fc = np.asarray(inp["ffn_conv"], np.float32).reshape(2, 3, 44, 128).transpose(3, 0, 2, 1).reshape(128, 264)
    v[:, V_FCW:V_FCW + 264] = fc
    v[:, V_DNW:V_DNW + 2] = np.asarray(inp["dn_norm"], np.float32).T
    v[:, V_EPS] = EPS
    v[:, V_ONE] = 1.0
    return v


def host_band(rel_bias):
    rb = np.asarray(rel_bias, np.float32)
    p = np.arange(128)[:, None]
    c = np.arange(1152)[None, :]
    n = c - 384 - p
    bucket = _t5_bucket_np(n)
    band = np.empty((128, 8, 1152), np.float32)
    for h in range(8):
        band[:, h, :] = np.where(n >= 0, rb[bucket, h], -BIG)
    return band.reshape(128, 8 * 1152)


_CACHE = {}


def get_program(NSEQ=2, **kw):
    key = (NSEQ, tuple(sorted(kw.items())))
    if key not in _CACHE:
        _CACHE[key] = build(NSEQ=NSEQ, **kw)
    return _CACHE[key]


def make_in_maps(inp, NSEQ, ncores):
    ident, tri, rc, ind = host_consts()
    vecs = host_vecs(inp)
    band = host_band(inp["rel_bias"])
    a4 = np.stack([np.asarray(inp["dn_a_log"], np.float32)[0], np.asarray(inp["dn_dt_bias"], np.float32)[0],
                   np.asarray(inp["dn_a_log"], np.float32)[1], np.asarray(inp["dn_dt_bias"], np.float32)[1]], axis=1)
    x = np.asarray(inp["x"], np.float32)
    p = np.asarray(inp["p"], np.float32)
    shared = {
        "w_in": np.ascontiguousarray(inp["w_in"], np.float32),
        "w_branch": np.ascontiguousarray(inp["w_branch"], np.float32),
        "w_out": np.ascontiguousarray(inp["w_out"], np.float32),
        "ffn_up": np.ascontiguousarray(inp["ffn_up"], np.float32),
        "ffn_down": np.ascontiguousarray(inp["ffn_down"], np.float32),
        "ple_gate": np.ascontiguousarray(inp["ple_gate"], np.float32),
        "ple_proj": np.ascontiguousarray(inp["ple_proj"], np.float32),
        "pool_w": np.ascontiguousarray(inp["pool_w"], np.float32),
        "vecs": vecs, "ident": ident, "tri": tri, "rcnt": rc, "band": band, "ind": ind,
        "a4": np.ascontiguousarray(a4),
    }
    maps = []
    for c in range(ncores):
        xs_ = x[c * NSEQ:(c + 1) * NSEQ].reshape(NSEQ * S, D)
        ps_ = p[:, c * NSEQ:(c + 1) * NSEQ].reshape(DEPTH, NSEQ * S, 256)
        m = dict(shared)
        m["xT"] = np.ascontiguousarray(xs_.T)
        m["pT"] = np.ascontiguousarray(ps_.transpose(0, 2, 1))
        maps.append(m)
    return maps


def kernel(**inputs):
    NSEQ = 2
    nc, kc, _ = get_program(NSEQ=NSEQ)
    maps = make_in_maps(inputs, NSEQ, NCORES)
    res = run_bass_kernel_spmd(nc, maps, core_ids=list(range(NCORES)))
    outs = []
    for c in range(NCORES):
        oT = np.asarray(res.results[c]["outT"], np.float32)
        outs.append(oT.T.reshape(NSEQ, S, D))
    return np.concatenate(outs, axis=0).astype(np.float32)
```

```python
import contextlib
import math
import numpy as np
import concourse.bass as bass
import concourse.mybir as mybir
from concourse.bass_utils import run_bass_kernel_spmd

F32 = mybir.dt.float32
BF16 = mybir.dt.bfloat16
AF = mybir.ActivationFunctionType
ALU = mybir.AluOpType
AX = mybir.AxisListType

D = 1024
S = 2048
DEPTH = 2
IN_COLS = 7176
FFN = 2816
EPS = 1e-6
NCORES = 8
BIG = 30000.0


class Op:
    __slots__ = ("eng", "fn", "deps", "sem", "inc", "val", "signal", "is_dma", "tag")

    def __init__(self, eng, fn, is_dma=False):
        self.eng = eng
        self.fn = fn
        self.deps = []
        self.sem = None
        self.inc = 1
        self.val = None
        self.signal = False
        self.is_dma = is_dma


class Prog:
    ENGS = ("pe", "act", "dve", "pool", "sp")

    def __init__(self, nc):
        self.nc = nc
        self.ops = {e: [] for e in self.ENGS}
        self.last_w = {}
        self.readers = {}
        self.n_ops = 0
        self.last_dma = {}
        self.bar = {e: None for e in self.ENGS}
        self.cur_tag = "pre"
        self.scopes = False

    def barrier(self):
        deps = []
        for e in self.ENGS:
            for o in reversed(self.ops[e]):
                if not o.is_dma:
                    deps.append(o)
                    break
        deps.extend(self.last_dma.values())
        for e in self.ENGS:
            self.bar[e] = deps
        self.last_w = {}
        self.readers = {}

    def _add(self, op, reads, writes):
        deps = []
        for k in reads:
            w = self.last_w.get(k)
            if w is not None:
                deps.append((w, False))
        for k in writes:
            w = self.last_w.get(k)
            if w is not None:
                deps.append((w, False))
            for r in self.readers.get(k, ()):
                deps.append((r, True))
        if self.bar[op.eng] is not None:
            for d in self.bar[op.eng]:
                deps.append((d, False))
            self.bar[op.eng] = None
        seen = set()
        for d, war in deps:
            if d is op or id(d) in seen:
                continue
            if d.eng == op.eng and not d.is_dma and not op.is_dma:
                if op.eng == "pe" or war:
                    continue
            seen.add(id(d))
            op.deps.append(d)
            d.signal = True
        for k in reads:
            self.readers.setdefault(k, []).append(op)
        for k in writes:
            self.last_w[k] = op
            self.readers[k] = []
        op.tag = self.cur_tag
        self.ops[op.eng].append(op)
        self.n_ops += 1

    @staticmethod
    def _is_psum(k):
        return isinstance(k, str) and k.startswith("ps") and k[2:].isdigit()

    def op(self, eng, fn, reads=(), writes=()):
        o = Op(eng, fn)
        o.sem = ("eng", eng)
        ex = [k for k in reads if self._is_psum(k)]
        if ex:
            reads = [k for k in reads if not self._is_psum(k)]
            writes = list(writes) + ex
        self._add(o, reads, writes)
        return o

    def dma(self, fn, semkey, reads=(), writes=(), q="sp"):
        o = Op(q, fn, is_dma=True)
        o.sem = ("dma", semkey)
        o.inc = 16
        o.signal = True
        self._add(o, reads, writes)
        self.last_dma[semkey] = o
        return o

    def emit(self, final_wait_ops=()):
        nc = self.nc
        counts = {}
        for e in self.ENGS:
            for o in self.ops[e]:
                if o.signal:
                    c = counts.get(o.sem, 0) + o.inc
                    counts[o.sem] = c
                    o.val = c
        semkeys = list(counts.keys())
        with contextlib.ExitStack() as st:
            sems = {}
            for i, k in enumerate(semkeys):
                sems[k] = st.enter_context(nc.semaphore("s%d" % i))
            block = st.enter_context(nc.Block())
            engmap = {"pe": block.tensor, "act": block.scalar, "dve": block.vector,
                      "pool": block.gpsimd, "sp": block.sync}

            def make(e):
                oplist = self.ops[e]

                def body(eng):
                    waited = {}
                    cur = None
                    cm = None
                    for o in oplist:
                        if self.scopes and o.tag != cur:
                            if cm is not None:
                                cm.__exit__(None, None, None)
                            cm = nc.named_scope(o.tag)
                            cm.__enter__()
                            cur = o.tag
                        for d in o.deps:
                            if waited.get(d.sem, 0) >= d.val:
                                continue
                            eng.wait_ge(sems[d.sem], d.val)
                            waited[d.sem] = d.val
                        ins = o.fn(eng)
                        if o.signal:
                            ins.then_inc(sems[o.sem], o.inc)
                    if cm is not None:
                        cm.__exit__(None, None, None)
                    if e == "sp":
                        for o in final_wait_ops:
                            if waited.get(o.sem, 0) >= o.val:
                                continue
                            eng.wait_ge(sems[o.sem], o.val)
                            waited[o.sem] = o.val

                return body

            for e in self.ENGS:
                if self.ops[e] or e == "sp":
                    engmap[e](make(e))
        return len(semkeys)


ARENA_F32 = 47 * 1024 + 512


import os as _os
CSTOP = int(_os.environ.get("C_STOP", "99"))
DSTOP = int(_os.environ.get("D_STOP", "99"))


class _Stop(Exception):
    pass


class KC:
    def __init__(self, nc, NSEQ, debug):
        self.nc = nc
        self.P = Prog(nc)
        self.NSEQ = NSEQ
        self.T = NSEQ * S
        self.NT = self.T // 512
        self.debug = debug
        self.st = contextlib.ExitStack()
        self.arena = self.st.enter_context(nc.sbuf_tensor("arena", [128, ARENA_F32], F32))
        self.arena_bf = self.arena[:, :].bitcast(BF16)
        self.psb = [self.st.enter_context(nc.psum_tensor("psb%d" % i, [128, 512], F32)) for i in range(8)]
        self.bump = 0
        self.perm = 0
        self.rr = 0
        self.finals = []
        self.dram = {}
        self.uid = 0

    def alloc(self, cols, dt=F32):
        if dt == BF16:
            w = (cols + 1) // 2
            a = self.bump
            self.bump += w
            assert self.bump <= ARENA_F32, "SBUF arena overflow %d" % self.bump
            return self.arena_bf[:, 2 * a:2 * a + cols]
        a = self.bump
        self.bump += cols
        assert self.bump <= ARENA_F32, "SBUF arena overflow %d" % self.bump
        return self.arena[:, a:a + cols]

    def make_perm(self):
        self.perm = self.bump

    def reset(self):
        self.P.barrier()
        self.bump = self.perm

    def key(self, base):
        self.uid += 1
        return "%s#%d" % (base, self.uid)

    def din(self, name, shape, dt=F32):
        t = self.nc.dram_tensor(name, list(shape), dt, kind="ExternalInput").ap()
        self.dram[name] = t
        return t

    def dscr(self, name, shape, dt=F32, out=False):
        kind = "ExternalOutput" if (out or self.debug) else "Internal"
        t = self.nc.dram_tensor(name, list(shape), dt, kind=kind).ap()
        self.dram[name] = t
        return t

    def ld(self, dst, src, key, reads=(), q="sp", semkey=None, slow=False):
        if slow:
            return self.P.dma(lambda e: e.dma_start(out=dst, in_=src, allow_slow_non_contiguous=True), semkey or key,
                              reads=reads, writes=[key], q=q)
        return self.P.dma(lambda e: e.dma_start(out=dst, in_=src), semkey or key, reads=reads, writes=[key], q=q)

    def stt(self, dst, src, srckey, writes=(), q="sp", semkey=None):
        keys = list(srckey) if isinstance(srckey, list) else [srckey]
        sk = semkey or (str(keys[0]) + "_st")
        return self.P.dma(lambda e: e.dma_start(out=dst, in_=src), sk, reads=keys, writes=writes, q=q)

    def dump(self, name, ap, keys, dt=F32):
        if not self.debug:
            return
        shape = list(ap.shape)
        d = self.nc.dram_tensor("dbg_" + name, shape, dt, kind="ExternalOutput").ap()
        self.finals.append(self.P.dma(lambda e: e.dma_start(out=d, in_=ap), "dump_" + name, reads=list(keys)))

    def store(self, dst, src, srckey, writes=(), q="sp", semkey=None):
        return self.stt(dst, src, srckey, writes, q, semkey)

    def act(self, out, in_, func, reads, writes, bias=None, scale=None, accum=None):
        kw = {}
        if bias is not None:
            kw["bias"] = bias
        if scale is not None:
            kw["scale"] = scale
        if accum is not None:
            kw["accum_out"] = accum
        return self.P.op("act", lambda e: e.activation(out=out, in_=in_, func=func, **kw), reads=reads, writes=writes)

    def tt(self, eng, out, in0, in1, op, reads, writes):
        return self.P.op(eng, lambda e: e.tensor_tensor(out, in0, in1, op), reads=reads, writes=writes)

    def ts(self, eng, out, in0, s1, s2, op0, op1, reads, writes):
        if s2 is None:
            return self.P.op(eng, lambda e: e.tensor_scalar(out, in0, s1, None, op0), reads=reads, writes=writes)
        return self.P.op(eng, lambda e: e.tensor_scalar(out, in0, s1, s2, op0, op1), reads=reads, writes=writes)

    def sto(self, eng, out, in0, scalar, in1, op0, op1, reads, writes):
        return self.P.op(eng, lambda e: e.scalar_tensor_tensor(out=out, in0=in0, scalar=scalar, in1=in1, op0=op0,
                                                               op1=op1), reads=reads, writes=writes)

    def copy(self, eng, out, in_, reads, writes):
        if eng == "act":
            return self.P.op("act", lambda e: e.copy(out, in_), reads=reads, writes=writes)
        return self.P.op(eng, lambda e: e.tensor_copy(out, in_), reads=reads, writes=writes)

    def memset(self, eng, ap, val, writes):
        return self.P.op(eng, lambda e: e.memset(ap, val), writes=writes)

    def recip(self, out, in_, reads, writes):
        return self.P.op("dve", lambda e: e.reciprocal(out, in_), reads=reads, writes=writes)

    def transpose(self, out, in_, ident, reads, writes):
        return self.P.op("pe", lambda e: e.transpose(out, in_, ident), reads=reads, writes=writes)

    def mm(self, out, lhsT, rhs, start, stop, reads, writes):
        return self.P.op("pe", lambda e: e.matmul(out, lhsT, rhs, start=start, stop=stop), reads=reads, writes=writes)

    def bank(self, n=8, base=0):
        i = base + (self.rr % n)
        self.rr += 1
        return i, self.psb[i], "ps%d" % i


def r3(ap, a):
    return ap.rearrange("p (a b) -> p a b", a=a)


V_NMIX = 0
V_NFFN = 16
V_NPLE = 32
V_NFIN = 48
V_PSCALE = 56
V_CW = 64
V_FCW = 160
V_DNW = 424
V_EPS = 426
V_ONE = 427
NVEC = 428
WIN_GROUP_C0 = [0, 512, 1024, 1536, 2048, 2568, 3080, 3592] + [4104 + 512 * i for i in range(6)]
WIN_GROUP_DST = [("u", None, 0), ("dqkv", None, 0), ("dqkv", None, 512), ("dqkv", None, 1024), ("dz", None, 0),
                 ("mq", None, 0), ("mk", None, 0), ("mv", None, 0)] + [("gate", None, 512 * i) for i in range(6)]

C_POOL = 0
C_DQKV = 512
C_DZ = 2048
C_DB = 2560
C_DA = 2564
C_MQ = 2568
C_MK = 2568 + 512
C_MV = 2568 + 1024
C_GATE = 4104


def build(NSEQ=2, debug=False, layers=(0, 1), stages="ABCDEF", final=True, scopes=False):
    nc = bass.Bass("TRN2", target_bir_lowering=False)
    kc = KC(nc, NSEQ, debug)
    P = kc.P
    P.scopes = scopes
    T = kc.T
    NT = kc.NT
    xT_in = kc.din("xT", [D, T])
    pT_in = kc.din("pT", [DEPTH, 256, T])
    w_in = kc.din("w_in", [DEPTH, D, IN_COLS])
    w_branch = kc.din("w_branch", [DEPTH, 3, 512, D])
    w_out = kc.din("w_out", [DEPTH, D, D])
    ffn_up = kc.din("ffn_up", [DEPTH, D, 2 * FFN])
    ffn_down = kc.din("ffn_down", [DEPTH, FFN, D])
    ple_gate = kc.din("ple_gate", [DEPTH, D, D])
    ple_proj = kc.din("ple_proj", [DEPTH, 256, D])
    pool_w = kc.din("pool_w", [DEPTH, 4, 128, 128])
    vecs_d = kc.din("vecs", [128, NVEC])
    ident_d = kc.din("ident", [128, 128])
    tri_d = kc.din("tri", [64, 4 * 64])
    rcnt_d = kc.din("rcnt", [128, 64])
    band_d = kc.din("band", [128, 8 * 1152])
    ind_d = kc.din("ind", [128, 64 * 128])
    a4_d = kc.din("a4", [4, 4])
    outT = kc.dscr("outT", [D, T], out=True)
    xs = kc.dscr("xs", [D, T])
    x1s = kc.dscr("x1s", [D, T])
    uT = kc.dscr("uT", [512, T])
    dqkvT = kc.dscr("dqkvT", [1536, T])
    dzT = kc.dscr("dzT", [512, T])
    dbg = kc.dscr("dbg", [8, T])
    mqT = kc.dscr("mqT", [512, T], BF16)
    mkT = kc.dscr("mkT", [512, T], BF16)
    mv = kc.dscr("mv", [T, 512], BF16)
    gatesT = kc.dscr("gatesT", [3072, T], BF16)
    ybT = kc.dscr("ybT", [1536, T], BF16)
    NG_IN = 15
    winb = kc.dscr("winb", [DEPTH, 15, 128, 8 * 512], BF16)
    wbrb = kc.dscr("wbrb", [DEPTH, 128, 12 * 1024], BF16)
    woutb = kc.dscr("woutb", [DEPTH, 128, 8 * 1024], BF16)
    wupb = kc.dscr("wupb", [DEPTH, 22, 128, 8 * 256], BF16)
    wdnb = kc.dscr("wdnb", [DEPTH, 128, 22 * 1024], BF16)
    pgb = kc.dscr("pgb", [DEPTH, 128, 8 * 1024], BF16)
    ppb = kc.dscr("ppb", [DEPTH, 128, 2 * 1024], BF16)
    pwb = kc.dscr("pwb", [DEPTH, 128, 4 * 128], BF16)

    vecs = kc.alloc(NVEC)
    ident = kc.alloc(128)
    ones_bf = kc.alloc(128, BF16)
    ones_f = kc.alloc(128)
    kc.ld(vecs, vecs_d, "vecs")
    kc.ld(ident, ident_d, "ident")
    P.op("pool", lambda e: e.memset(ones_bf, 1.0), writes=["ones_bf"])
    P.op("pool", lambda e: e.memset(ones_f, 1.0), writes=["ones_f"])
    kc.make_perm()
    CONST = ["vecs", "ident", "ones_bf", "ones_f"]

    def cast(dst, src, key):
        grp = ("CAST0" if key[0] == "winb" else "CAST1") + ("" if key[1] == layers[0] else "b")
        if key[0] == "winb" and key[1] == layers[0]:
            grp = "CAST0_%d" % key[2]
        P.dma(lambda e: e.dma_start(out=dst, in_=src), grp, writes=[grp], q="pool")

    for l in layers:
        for g in range(14):
            c0 = WIN_GROUP_C0[g]
            cast(winb[l, g].rearrange("p (k c) -> p k c", k=8),
                 w_in[l, :, c0:c0 + 512].rearrange("(k p) c -> p k c", p=128), ("winb", l, g))
        cast(winb[l, 14].rearrange("p (k c) -> p k c", k=8)[:, :, 0:8],
             w_in[l, :, C_DB:C_DB + 8].rearrange("(k p) c -> p k c", p=128), ("winb", l, 14))
        cast(pwb[l].rearrange("p (g d) -> p g d", g=4), pool_w[l].rearrange("g c d -> c g d"), ("pwb", l))
        for n in range(3):
            cast(wbrb[l].rearrange("p (n k d) -> p n k d", n=3, k=4)[:, n],
                 w_branch[l, n].rearrange("(k p) d -> p k d", p=128), ("wbrb", l, n))
        cast(woutb[l].rearrange("p (k d) -> p k d", k=8), w_out[l].rearrange("(k p) d -> p k d", p=128), ("woutb", l))
        for j in range(22):
            dstv = wupb[l, j].rearrange("p (k c) -> p k c", k=8)
            cast(dstv[:, :, 0:128], ffn_up[l, :, j * 128:(j + 1) * 128].rearrange("(k p) c -> p k c", p=128),
                 ("wupb", l, j, 0))
            cast(dstv[:, :, 128:256],
                 ffn_up[l, :, FFN + j * 128:FFN + (j + 1) * 128].rearrange("(k p) c -> p k c", p=128),
                 ("wupb", l, j, 1))
        cast(wdnb[l].rearrange("p (j d) -> p j d", j=22), ffn_down[l].rearrange("(j p) d -> p j d", p=128),
             ("wdnb", l))
        cast(pgb[l].rearrange("p (k d) -> p k d", k=8), ple_gate[l].rearrange("(k p) d -> p k d", p=128), ("pgb", l))
        cast(ppb[l].rearrange("p (k d) -> p k d", k=2), ple_proj[l].rearrange("(k p) d -> p k d", p=128), ("ppb", l))

    def vcol(i):
        return vecs[:, i:i + 1]

    def rmsnorm_tile(xt, xkey, nbase, out_fn, outkeys, sqt, sqkey, rstd, rkey, engs=("dve",), sq_extra=()):
        P.op("act", lambda e: e.activation(out=sqt, in_=xt, func=AF.Square), reads=[xkey],
             writes=[sqkey] + list(sq_extra))
        bi, pb, pk = kc.bank()
        for k in range(8):
            kc.mm(pb[:, :], ones_bf, sqt[:, k * 512:(k + 1) * 512], k == 0, k == 7, [sqkey, "ones_bf"], [pk])
        P.op("act", lambda e: e.activation(out=rstd, in_=pb[:, :], func=AF.Ln, bias=vcol(V_EPS), scale=1.0 / D),
             reads=[pk, "vecs"], writes=[rkey])
        P.op("act", lambda e: e.activation(out=rstd, in_=rstd, func=AF.Exp, scale=-0.5), reads=[rkey], writes=[rkey])
        for k in range(8):
            o = out_fn(k)
            eng = engs[k % len(engs)]
            P.op(eng, (lambda o=o, k=k: lambda e: e.scalar_tensor_tensor(
                out=o, in0=xt[:, k * 512:(k + 1) * 512], scalar=vcol(nbase + k), in1=rstd,
                op0=ALU.mult, op1=ALU.mult))(), reads=[xkey, rkey, "vecs"], writes=[outkeys[k]])

    stages0 = stages
    for l in layers:
        stages = stages0 if l == layers[0] else _os.environ.get("L1S", stages0)
        xsrc = xT_in if l == 0 else xs
        xsrc_key = "xs"
        if "A" in stages:
            P.cur_tag = "L%dA" % l
            hT = kc.alloc(8 * T, BF16)
            hT3 = r3(hT, 8)
            xtb = [kc.alloc(8 * 512) for _ in range(2)]
            sqt = kc.alloc(8 * 512, BF16)
            rstd = kc.alloc(512)
            for tt in range(NT):
                xt = xtb[tt % 2]
                xk = "A_xt%d" % (tt % 2)
                kc.ld(r3(xt, 8), xsrc[:, tt * 512:(tt + 1) * 512].rearrange("(k p) t -> p k t", p=128), xk,
                      reads=[xsrc_key])
                rmsnorm_tile(xt, xk, V_NMIX + l * 8, lambda k: hT3[:, k, tt * 512:(tt + 1) * 512],
                             [("hT", tt, k) for k in range(8)], sqt, "A_sq", rstd, "A_rstd")
            HT_ALL = [("hT", tt) for tt in range(NT)]
            wgb = [kc.alloc(8 * 512, BF16) for _ in range(2)]
            ob32 = [kc.alloc(T) for _ in range(2)]
            ob16 = [kc.alloc(T, BF16) for _ in range(2)]
            ovb = [kc.alloc(512, BF16) for _ in range(2)]
            cnt = {"o32": 0, "o16": 0, "ov": 0, "ev": 0}

            def evac(kind, dst, src, pk, okey):
                if kind in ("u", "dqkv"):
                    eng = "act" if cnt["ev"] % 2 == 0 else "dve"
                    cnt["ev"] += 1
                    if eng == "act":
                        P.op("act", lambda e: e.copy(dst, src), reads=[pk], writes=[okey])
                    else:
                        P.op("dve", lambda e: e.tensor_copy(dst, src), reads=[pk], writes=[okey])
                elif kind == "dz":
                    P.op("act", lambda e: e.activation(out=dst, in_=src, func=AF.Silu), reads=[pk], writes=[okey])
                elif kind == "mq":
                    P.op("dve", lambda e: e.tensor_scalar(dst, src, 0.125, None, ALU.mult), reads=[pk], writes=[okey])
                elif kind == "mk":
                    P.op("dve", lambda e: e.tensor_copy(dst, src), reads=[pk], writes=[okey])
                elif kind == "gate":
                    P.op("act", lambda e: e.activation(out=dst, in_=src, func=AF.Sigmoid), reads=[pk], writes=[okey])
                else:
                    raise ValueError(kind)

            for g in range(14):
                wg = wgb[g % 2]
                wk = "A_wg%d" % (g % 2)
                kc.ld(wg, winb[l, g], wk, reads=["CAST0b"] if l != layers[0] else ["CAST0_%d" % g])
                wg3 = r3(wg, 8)
                gkind, gdst, grow0 = WIN_GROUP_DST[g]
                if gkind == "mv":
                    for i in range(T // 128):
                        bi, pb, pk = kc.bank()
                        for k in range(8):
                            kc.mm(pb[:, :], hT3[:, k, i * 128:(i + 1) * 128], wg3[:, k, :], k == 0, k == 7,
                                  [("hT", i // 4, k), wk], [pk])
                        ov = ovb[cnt["ov"] % 2]
                        ok = "A_ov%d" % (cnt["ov"] % 2)
                        cnt["ov"] += 1
                        eng = "act" if i % 2 == 0 else "dve"
                        if eng == "act":
                            P.op("act", (lambda ov=ov, pb=pb: lambda e: e.copy(ov, pb[:, :]))(), reads=[pk], writes=[ok])
                        else:
                            P.op("dve", (lambda ov=ov, pb=pb: lambda e: e.tensor_copy(ov, pb[:, :]))(), reads=[pk],
                                 writes=[ok])
                        kc.stt(mv[i * 128:(i + 1) * 128, :], ov, ok, writes=["mv"])
                    continue
                for j in range(4):
                    is16 = gkind in ("mq", "mk", "gate")
                    if is16:
                        ob = ob16[cnt["o16"] % 2]
                        okey = "A_o16_%d" % (cnt["o16"] % 2)
                        cnt["o16"] += 1
                    else:
                        ob = ob32[cnt["o32"] % 2]
                        okey = "A_o32_%d" % (cnt["o32"] % 2)
                        cnt["o32"] += 1
                    for tt in range(NT):
                        bi, pb, pk = kc.bank()
                        for k in range(8):
                            kc.mm(pb[:, :], wg3[:, k, j * 128:(j + 1) * 128], hT3[:, k, tt * 512:(tt + 1) * 512],
                                  k == 0, k == 7, [("hT", tt, k), wk], [pk])
                        evac(gkind, ob[:, tt * 512:(tt + 1) * 512], pb[:, :], pk, (okey, tt))
                    dst = {"u": uT, "dqkv": dqkvT, "dz": dzT, "mq": mqT, "mk": mkT, "gate": gatesT}[gkind]
                    r0 = grow0 + j * 128
                    kc.stt(dst[r0:r0 + 128, :], ob, [(okey, tt) for tt in range(NT)], writes=[gkind + "_d"], semkey=okey + "_st")
            wg = wgb[0]
            wk = "A_wg0"
            kc.ld(wg, winb[l, 14], wk, reads=["CAST0b"] if l != layers[0] else ["CAST0_14"])
            wg3 = r3(wg, 8)
            a4 = kc.alloc(4)
            P_a4 = kc.ld(a4[0:4, :], a4_d, "A_a4")
            nexpA = kc.alloc(1)
            kc.act(nexpA[0:4, :], a4[0:4, 2 * l:2 * l + 1], AF.Exp, ["A_a4"], ["A_nexpA"])
            kc.ts("dve", nexpA[0:4, :], nexpA[0:4, :], -1.0, None, ALU.mult, None, ["A_nexpA"], ["A_nexpA"])
            obb = ob32[0]
            oba = ob32[1]
            for tt in range(NT):
                sl = slice(tt * 512, (tt + 1) * 512)
                bi, pb, pk = kc.bank()
                for k in range(8):
                    kc.mm(pb[0:4, :], wg3[:, k, 0:4], hT3[:, k, sl], k == 0, k == 7, [("hT", tt, k), wk], [pk])
                kc.act(obb[0:4, sl], pb[0:4, :], AF.Sigmoid, [pk], [("A_o32_0", tt)])
                bi, pb, pk = kc.bank()
                for k in range(8):
                    kc.mm(pb[0:4, :], wg3[:, k, 4:8], hT3[:, k, sl], k == 0, k == 7, [("hT", tt, k), wk], [pk])
                kc.act(oba[0:4, sl], pb[0:4, :], AF.Exp, [pk, "A_a4"], [("A_o32_1", tt)],
                       bias=a4[0:4, 2 * l + 1:2 * l + 2])
            AK1 = [("A_o32_1", tt) for tt in range(NT)]
            kc.act(oba[0:4, :], oba[0:4, :], AF.Ln, AK1 + ["vecs"], AK1, bias=vcol(V_ONE)[0:4, :])
            kc.ts("dve", oba[0:4, :], oba[0:4, :], nexpA[0:4, 0:1], None, ALU.mult, None, AK1 + ["A_nexpA"], AK1)
            kc.stt(dbg[0:4, :], obb[0:4, :], [("A_o32_0", tt) for tt in range(NT)], writes=["dbg"], semkey="A_o32_0_st")
            kc.stt(dbg[4:8, :], oba[0:4, :], [("A_o32_1", tt) for tt in range(NT)], writes=["dbg"], semkey="A_o32_1_st")
            kc.reset()
        if "B" in stages:
            P.cur_tag = "L%dB" % l
            pw = kc.alloc(4 * 128, BF16)
            kc.ld(pw, pwb[l], "B_pw", reads=["CAST1" if l == layers[0] else "CAST1b"])
            pw3 = r3(pw, 4)
            rc = kc.alloc(64)
            kc.ld(rc, rcnt_d, "B_rc")
            ub = [kc.alloc(16 + S) for _ in range(2)]
            sab = [kc.alloc(16 + S) for _ in range(2)]
            mxb = [kc.alloc(S, BF16) for _ in range(2)]
            t16 = kc.alloc(16)
            yob = [kc.alloc(S, BF16) for _ in range(2)]
            for i in range(2):
                kc.memset("pool", ub[i][:, 0:16], 0.0, ["B_u%dz" % i])
                kc.memset("pool", sab[i][:, 0:16], 0.0, ["B_s%dz" % i])
            it = 0
            for s_ in range(NSEQ):
                for g in range(4):
                    i2 = it % 2
                    u = ub[i2]
                    uk = "B_u%d" % i2
                    kc.ld(u[:, 16:], uT[g * 128:(g + 1) * 128, s_ * S:(s_ + 1) * S], uk)
                    cur, curk = u, uk
                    for j in range(g + 1):
                        dst = sab[j % 2]
                        dk = "B_s%d" % (j % 2)
                        sh = 1 << j
                        kc.tt("dve" if j % 2 == 0 else "pool", dst[:, 16:], cur[:, 16:], cur[:, 16 - sh:16 - sh + S],
                              ALU.add, [curk, curk + "z"], [dk])
                        cur, curk = dst, dk
                    w = 1 << (g + 1)
                    m = mxb[i2]
                    mk_ = "B_mx%d" % i2
                    kc.sto("dve", m, cur[:, 16:], 1.0 / w, u[:, 16:], ALU.mult, ALU.subtract, [curk, uk], [mk_])
                    kc.tt("dve", t16, cur[:, 16:32], rc[:, g * 16:(g + 1) * 16], ALU.mult, [curk, "B_rc"], ["B_t16"])
                    kc.tt("dve", m[:, 0:16], t16, u[:, 16:32], ALU.subtract, ["B_t16", uk, mk_], [mk_])
                    y = yob[i2]
                    yk = "B_y%d" % i2
                    for j in range(4):
                        bi, pb, pk = kc.bank()
                        kc.mm(pb[:, :], pw3[:, g, :], m[:, j * 512:(j + 1) * 512], True, True, [mk_, "B_pw"], [pk])
                        kc.ts("dve" if j % 2 == 0 else "dve", y[:, j * 512:(j + 1) * 512], pb[:, :],
                              vcol(V_PSCALE + l * 4 + g), None, ALU.mult, None, [pk, "vecs"], [yk])
                    kc.store(ybT[g * 128:(g + 1) * 128, s_ * S:(s_ + 1) * S], y, yk, writes=["ybT"])
                    it += 1
            kc.reset()

        if "C" in stages:
            P.cur_tag = "L%dC" % l
            tri = kc.alloc(256)
            kc.ld(tri[0:64, :], tri_d, "C_tri")
            Ut = tri[0:64, 0:64]
            mA = tri[0:64, 64:128]
            mB = tri[0:64, 128:192]
            SU = tri[0:64, 192:256]
            identb3 = ident[0:64, 0:64].rearrange("p (o j) -> p o j", o=1).to_broadcast([64, 8, 64])
            raw = kc.alloc(3 + S)
            Xb = raw[:, 3:3 + S]
            acc = kc.alloc(S)
            sqb = kc.alloc(S, BF16)
            rn = kc.alloc(512)
            khb = kc.alloc(S, BF16)
            qhb = kc.alloc(S, BF16)
            gcrow = kc.alloc(S)
            E1 = kc.alloc(S)
            brow = kc.alloc(S)
            gT = kc.alloc(32)
            bT = kc.alloc(32)
            gcc = kc.alloc(32)
            egd = kc.alloc(32)
            Dm = kc.alloc(512)
            Gb = kc.alloc(512)
            GTi = kc.alloc(512)
            GTb = kc.alloc(512)
            tmpD = kc.alloc(512)
            Qk = [kc.alloc(512) for _ in range(2)]
            Rk = [kc.alloc(512) for _ in range(2)]
            Gk = [kc.alloc(512) for _ in range(2)]
            SETS = []
            for i_ in range(2):
                SETS.append(dict(i=i_, keT=kc.alloc(S), qdT=kc.alloc(S), kdec=kc.alloc(32 * 128, BF16),
                                 vtok=kc.alloc(32 * 128, BF16), aqkT=kc.alloc(S, BF16), TTb=kc.alloc(S, BF16),
                                 egl=kc.alloc(32)))
            oT = kc.alloc(S)
            S_ = kc.alloc(128)
            Rb = kc.alloc(128, BF16)
            vn = kc.alloc(128, BF16)
            zsb = [kc.alloc(512) for _ in range(2)]
            yout = kc.alloc(S, BF16)
            sqb2 = kc.alloc(S, BF16)
            rn2 = kc.alloc(512)
            kc.memset("pool", raw[:, 0:3], 0.0, ["C_rawz"])
            identP = kc.alloc(64)
            bTe = kc.alloc(16)
            bTo = kc.alloc(16)
            kc.copy("dve", identP[0:64, :], ident[0:64, 0:64], ["ident"], ["C_identP"])
            kc.copy("dve", identP[64:128, :], ident[64:128, 64:128], ["ident"], ["C_identP"])

            def bc_mid(ap64, n):
                return ap64.rearrange("p (o j) -> p o j", o=1).to_broadcast([64, n, 64])

            def bc_last(ap, n, w):
                return ap.rearrange("p (n o) -> p n o", o=1).to_broadcast([64, n, w])

            def phaseA(s_, h, st):
                t0 = s_ * S
                si = st["i"]
                kK = lambda nm: "C_%s%d" % (nm, si)
                keT, qdT, kdec, vtok, aqkT, TTb, egl = (st[k] for k in ("keT", "qdT", "kdec", "vtok", "aqkT", "TTb", "egl"))
                kc.ld(gcrow[0:64, :], dbg[4 + h:5 + h, t0:t0 + S].partition_broadcast(64), "C_gcrow")
                kc.ld(brow[0:64, :], dbg[h:h + 1, t0:t0 + S].partition_broadcast(64), "C_brow")
                idb32 = ident[0:64, 0:64].rearrange("p (o j) -> p o j", o=1).to_broadcast([64, 32, 64])
                kc.tt("dve", r3(Xb[0:64, :], 32), r3(gcrow[0:64, :], 32), idb32, ALU.mult, ["C_gcrow", "ident"], ["C_raw"])
                P.op("dve", (lambda: lambda e: e.tensor_reduce(gT[0:64, :], r3(Xb[0:64, :], 32), AX.X, ALU.add))(),
                     reads=["C_raw"], writes=["C_gT"])
                kc.tt("dve", r3(Xb[0:64, :], 32), r3(brow[0:64, :], 32), idb32, ALU.mult, ["C_brow", "ident"], ["C_raw"])
                P.op("dve", (lambda: lambda e: e.tensor_reduce(bT[0:64, :], r3(Xb[0:64, :], 32), AX.X, ALU.add))(),
                     reads=["C_raw"], writes=["C_bT"])
                yield
                bi, pb, pk = kc.bank()
                kc.mm(pb[0:64, 0:32], Ut, gT[0:64, :], True, True, ["C_tri", "C_gT"], [pk])
                kc.copy("act", gcc[0:64, :], pb[0:64, 0:32], [pk], ["C_gcc"])
                bi, pb, pk = kc.bank()
                kc.mm(pb[:, 0:32], ones_f[0:64, 0:128], gT[0:64, :], True, True, ["ones_f", "C_gT"], [pk])
                kc.act(egl[:, :], pb[:, 0:32], AF.Exp, [pk], [kK("egl")])
                kc.tt("dve", egd[0:64, :], pb[0:64, 0:32], gcc[0:64, :], ALU.subtract, [pk, "C_gcc"], ["C_egd"])
                kc.act(egd[0:64, :], egd[0:64, :], AF.Exp, ["C_egd"], ["C_egd"])
                kc.tt("dve", r3(Xb[0:64, :], 32), bc_last(gT[0:64, :], 32, 64), bc_mid(Ut, 32), ALU.mult,
                      ["C_gT", "C_tri"], ["C_raw"])
                yield
                for q4 in range(4):
                    sl = slice(q4 * 512, (q4 + 1) * 512)
                    bi, pb, pk = kc.bank()
                    kc.mm(pb[:, :], ones_f[0:64, 0:128], Xb[0:64, sl], True, True, ["ones_f", "C_raw"], [pk])
                    kc.copy("dve", gcrow[:, sl], pb[:, :], [pk], ["C_gcrow"])
                    kc.act(E1[:, sl], pb[:, :], AF.Exp, [pk], ["C_E1"])
                    yield

                def conv_silu(comp):
                    blk = comp * 4 + h
                    kc.ld(raw[:, 3:3 + S], dqkvT[blk * 128:(blk + 1) * 128, t0:t0 + S], "C_raw")
                    cwi = V_CW + (l * 12 + blk) * 4
                    kc.ts("dve", acc, raw[:, 0:S], vcol(cwi), None, ALU.mult, None, ["C_raw", "C_rawz", "vecs"], ["C_acc"])
                    for j in range(1, 4):
                        kc.sto("dve", acc, raw[:, j:j + S], vcol(cwi + j), acc, ALU.mult, ALU.add,
                               ["C_raw", "C_rawz", "C_acc", "vecs"], ["C_acc"])
                    kc.act(acc, acc, AF.Silu, ["C_acc"], ["C_acc"])

                def l2n_slice(q4, scale):
                    sl = slice(q4 * 512, (q4 + 1) * 512)
                    bi, pb, pk = kc.bank()
                    kc.mm(pb[:, :], ones_bf, sqb[:, sl], True, True, ["ones_bf", "C_sqb"], [pk])
                    kc.act(rn, pb[:, :], AF.Ln, [pk, "vecs"], ["C_rn"], bias=vcol(V_EPS))
                    kc.act(rn, rn, AF.Exp, ["C_rn"], ["C_rn"], scale=-0.5)
                    kc.sto("dve", acc[:, sl], acc[:, sl], scale, rn, ALU.mult, ALU.mult, ["C_acc", "C_rn"], ["C_acc"])

                def to_tok4(n4, dst, dkey, mul_egd):
                    bi, pb, pk = kc.bank()
                    for c in range(4):
                        n = n4 * 4 + c
                        kc.transpose(pb[0:64, c * 128:(c + 1) * 128], acc[:, n * 64:(n + 1) * 64], ident,
                                     ["C_acc", "ident"], [pk])
                    dsl = dst[0:64, n4 * 512:(n4 + 1) * 512]
                    if mul_egd:
                        kc.tt("dve", r3(dsl, 4), r3(pb[0:64, :], 4), bc_last(egd[0:64, n4 * 4:n4 * 4 + 4], 4, 128),
                              ALU.mult, [pk, "C_egd"], [dkey])
                    else:
                        kc.copy("act", dsl, pb[0:64, :], [pk], [dkey])

                conv_silu(1)
                yield
                kc.act(sqb, acc, AF.Square, ["C_acc"], ["C_sqb"])
                for q4 in range(4):
                    l2n_slice(q4, 1.0)
                    yield
                kc.copy("act", khb, acc, ["C_acc"], ["C_khb"])
                kc.tt("dve", keT, acc, E1, ALU.mult, ["C_acc", "C_E1"], [kK("keT")])
                yield
                for n4 in range(8):
                    to_tok4(n4, kdec, kK("kdec"), True)
                    yield
                conv_silu(0)
                yield
                kc.act(sqb, acc, AF.Square, ["C_acc"], ["C_sqb"])
                for q4 in range(4):
                    l2n_slice(q4, 128.0 ** -0.5)
                    yield
                kc.copy("act", qhb, acc, ["C_acc"], ["C_qhb"])
                kc.tt("dve", qdT, acc, E1, ALU.mult, ["C_acc", "C_E1"], [kK("qdT")])
                yield
                conv_silu(2)
                yield
                for n4 in range(8):
                    to_tok4(n4, vtok, kK("vtok"), False)
                    yield

                kc.copy("dve", bTe[0:64, :], bT[0:64, :].rearrange("p (m two) -> p m two", two=2)[:, :, 0], ["C_bT"],
                        ["C_bTe"])
                kc.copy("dve", bTo[0:64, :], bT[0:64, :].rearrange("p (m two) -> p m two", two=2)[:, :, 1], ["C_bT"],
                        ["C_bTo"])

                def v4(ap):
                    return ap.rearrange("p (m two f) -> p m two f", m=4, two=2)

                for bt in range(4):
                    sb, lb = bt // 2, bt % 2
                    n0 = bt * 8
                    c0 = bt * 512
                    bsl = slice(c0, c0 + 512)
                    kc.tt("dve", r3(Dm[0:64, :], 8), r3(gcrow[0:64, bsl], 8), bc_last(gcc[0:64, n0:n0 + 8], 8, 64),
                          ALU.subtract, ["C_gcrow", "C_gcc"], ["C_Dm"])
                    kc.tt("pool", r3(tmpD[0:64, :], 8), r3(Dm[0:64, :], 8), bc_mid(mA, 8), ALU.add,
                          ["C_Dm", "C_tri"], ["C_tmpD"])
                    kc.act(GTi[0:64, :], tmpD[0:64, :], AF.Exp, ["C_tmpD"], ["C_GTi"])
                    kc.tt("dve", r3(tmpD[0:64, :], 8), bc_mid(mB, 8), r3(Dm[0:64, :], 8), ALU.subtract,
                          ["C_Dm", "C_tri", "C_tmpD"], ["C_tmpD"])
                    kc.act(Gb[0:64, :], tmpD[0:64, :], AF.Exp, ["C_tmpD"], ["C_Gb"])
                    kc.tt("dve", r3(Gb[0:64, :], 8), r3(Gb[0:64, :], 8), bc_last(bT[0:64, n0:n0 + 8], 8, 64),
                          ALU.mult, ["C_Gb", "C_bT"], ["C_Gb"])
                    kc.tt("pool", r3(GTb[0:64, :], 8), r3(GTi[0:64, :], 8), bc_mid(SU, 8), ALU.mult,
                          ["C_GTi", "C_tri"], ["C_GTb"])
                    kc.tt("pool", GTb[0:64, :], GTb[0:64, :], brow[0:64, bsl], ALU.mult, ["C_GTb", "C_brow"],
                          ["C_GTb"])
                    yield
                    Q, R, G = Qk[sb], Rk[sb], Gk[sb]
                    qk_, rk_, gk_ = "C_Q%d" % sb, "C_R%d" % sb, "C_G%d" % sb
                    lsl = slice(lb * 256, (lb + 1) * 256)
                    bi, pb, pk = kc.bank()
                    for c in range(8):
                        cs = slice((n0 + c) * 64, (n0 + c + 1) * 64)
                        kc.mm(pb[0:64, c * 64:(c + 1) * 64], khb[:, cs], khb[:, cs], True, True, ["C_khb"], [pk])
                    pv_ = v4(pb[0:64, :])
                    for par in range(2):
                        psl = slice(par * 64, (par + 1) * 64)
                        kc.tt("dve", r3(R[psl, lsl], 4), pv_[:, :, par, :], v4(Gb[0:64, :])[:, :, par, :], ALU.mult,
                              [pk, "C_Gb"], [(rk_, lb, par)])
                        kc.tt("dve", r3(Q[psl, lsl], 4), pv_[:, :, par, :], v4(GTb[0:64, :])[:, :, par, :], ALU.mult,
                              [pk, "C_GTb"], [(qk_, lb, par)])
                    bi, pb, pk = kc.bank()
                    for c in range(8):
                        cs = slice((n0 + c) * 64, (n0 + c + 1) * 64)
                        kc.mm(pb[0:64, c * 64:(c + 1) * 64], khb[:, cs], qhb[:, cs], True, True,
                              ["C_khb", "C_qhb"], [pk])
                    kc.tt("dve", aqkT[0:64, bsl], pb[0:64, :], GTi[0:64, :], ALU.mult, [pk, "C_GTi"], [kK("aqkT")])
                    kc.tt("pool", r3(G[:, lsl], 4), identP.rearrange("p (o j) -> p o j", o=1).to_broadcast([128, 4, 64]),
                          r3(Q[:, lsl], 4), ALU.subtract, ["C_identP", (qk_, lb, 0), (qk_, lb, 1)], [(gk_, lb)])
                    yield
                QRK = lambda nm: [(nm, lb_, par_) for lb_ in range(2) for par_ in range(2)]
                for lev in range(1, 6):
                    for sb in range(2):
                        Q, R, G = Qk[sb], Rk[sb], Gk[sb]
                        qk_, rk_, gk_ = "C_Q%d" % sb, "C_R%d" % sb, "C_G%d" % sb
                        qks = QRK(qk_) if lev == 1 else [qk_]
                        rks = QRK(rk_) if lev == 1 else [rk_]
                        gks = [(gk_, 0), (gk_, 1)] if lev == 1 else [gk_]

                        def pairmm(pbx, pkx, A_, B_, rd):
                            for m in range(8):
                                ms = slice(m * 64, (m + 1) * 64)
                                kc.mm(pbx[0:64, ms], A_[0:64, ms], B_[0:64, ms], True, True, rd, [pkx])
                                P.op("pe", (lambda o=pbx[64:128, ms], a_=A_[64:128, ms], b_=B_[64:128, ms]:
                                            lambda e: e.matmul(o, a_, b_, start=True, stop=True, tile_position=(64, 64)))(),
                                     reads=rd, writes=[pkx])

                        if lev < 5:
                            bq, pbq, pkq = kc.bank()
                            pairmm(pbq, pkq, R, Q, qks + rks)
                        br, pbr, pkr = kc.bank()
                        pairmm(pbr, pkr, Q, R, qks + rks)
                        if lev < 5:
                            kc.copy("act", Q[:, :], pbq[:, :], [pkq], [qk_] + QRK(qk_))
                        kc.copy("act", R[:, :], pbr[:, :], [pkr], [rk_] + QRK(rk_))
                        yield
                        bg, pbg, pkg = kc.bank()
                        pairmm(pbg, pkg, R, G, [rk_] + gks)
                        kc.tt("dve", G[:, :], G[:, :], pbg[:, :], ALU.add, gks + [pkg], [gk_, (gk_, 0), (gk_, 1)])
                        yield
                for sb in range(2):
                    G = Gk[sb]
                    gk_ = "C_G%d" % sb
                    TTv = TTb[0:64, sb * 1024:(sb + 1) * 1024].rearrange("p (m two f) -> p m two f", m=8, two=2)
                    kc.tt("dve", TTv[:, :, 0, :], r3(G[0:64, :], 8), bc_last(bTe[0:64, sb * 8:(sb + 1) * 8], 8, 64), ALU.mult,
                          [gk_, "C_bTe"], [kK("TTb")])
                    kc.copy("act", tmpD[0:64, :], G[64:128, :], [gk_], ["C_tmpD"])
                    kc.tt("dve", TTv[:, :, 1, :], r3(tmpD[0:64, :], 8), bc_last(bTo[0:64, sb * 8:(sb + 1) * 8], 8, 64), ALU.mult,
                          ["C_tmpD", "C_bTo"], [kK("TTb")])
                    yield

            def phaseB(s_, h, st):
                t0 = s_ * S
                si = st["i"]
                kK = lambda nm: "C_%s%d" % (nm, si)
                keT, qdT, kdec, vtok, aqkT, TTb, egl = (st[k] for k in ("keT", "qdT", "kdec", "vtok", "aqkT", "TTb", "egl"))
                kc.memset("dve", S_, 0.0, ["C_S"])
                for n in range(32):
                    cs = slice(n * 64, (n + 1) * 64)
                    ns = slice(n * 128, (n + 1) * 128)
                    b1, pb1, pk1 = kc.bank()
                    kc.mm(pb1[0:64, 0:128], keT[:, cs], S_, True, True, [kK("keT"), "C_S"], [pk1])
                    b3, pb3, pk3 = kc.bank()
                    kc.mm(pb3[:, 0:64], S_, qdT[:, cs], True, False, [kK("qdT"), "C_S"], [pk3])
                    kc.tt("dve", Rb[0:64, :], vtok[0:64, ns], pb1[0:64, 0:128], ALU.subtract, [kK("vtok"), pk1], ["C_Rb"])
                    b2, pb2, pk2 = kc.bank()
                    kc.mm(pb2[0:64, 0:128], TTb[0:64, cs], Rb[0:64, :], True, True, [kK("TTb"), "C_Rb"], [pk2])
                    kc.copy("act", vn[0:64, :], pb2[0:64, 0:128], [pk2], ["C_vn"])
                    kc.mm(pb3[:, 0:64], vn[0:64, :], aqkT[0:64, cs], False, True, ["C_vn", kK("aqkT")], [pk3])
                    b4, pb4, pk4 = kc.bank()
                    kc.mm(pb4[:, 0:128], kdec[0:64, ns], vn[0:64, :], True, True, [kK("kdec"), "C_vn"], [pk4])
                    kc.copy("act", oT[:, cs], pb3[:, 0:64], [pk3], ["C_oT"])
                    kc.sto("dve", S_, S_, egl[:, n:n + 1], pb4[:, 0:128], ALU.mult, ALU.add, ["C_S", kK("egl"), pk4],
                           ["C_S"])
                    yield
                kc.act(sqb2, oT, AF.Square, ["C_oT"], ["C_sqb2"])
                for q4 in range(4):
                    sl = slice(q4 * 512, (q4 + 1) * 512)
                    zs = zsb[q4 % 2]
                    zk = "C_zs%d" % (q4 % 2)
                    kc.ld(zs, dzT[h * 128:(h + 1) * 128, t0 + q4 * 512:t0 + (q4 + 1) * 512], zk)
                    bi, pb, pk = kc.bank()
                    kc.mm(pb[:, :], ones_bf, sqb2[:, sl], True, True, ["ones_bf", "C_sqb2"], [pk])
                    kc.act(rn2, pb[:, :], AF.Ln, [pk, "vecs"], ["C_rn2"], bias=vcol(V_EPS), scale=1.0 / 128)
                    kc.act(rn2, rn2, AF.Exp, ["C_rn2"], ["C_rn2"], scale=-0.5)
                    kc.sto("dve", oT[:, sl], oT[:, sl], vcol(V_DNW + l), rn2, ALU.mult, ALU.mult,
                           ["C_oT", "C_rn2", "vecs"], ["C_oT"])
                    kc.tt("pool", yout[:, sl], oT[:, sl], zs, ALU.mult, ["C_oT", zk], ["C_yout"])
                    yield
                kc.store(ybT[512 + h * 128:512 + (h + 1) * 128, t0:t0 + S], yout, "C_yout", writes=["ybT"])
                yield

            heads = [(s_, h) for s_ in range(NSEQ) for h in range(4)]
            nA = 0
            P.cur_tag = "L%dCa" % l
            for _ in phaseA(heads[0][0], heads[0][1], SETS[0]):
                nA += 1
            nB = 38
            per = max(1, -(-nA // nB))
            for i_, (s_, h) in enumerate(heads):
                gB = phaseB(s_, h, SETS[i_ % 2])
                gA = phaseA(heads[i_ + 1][0], heads[i_ + 1][1], SETS[(i_ + 1) % 2]) if i_ + 1 < len(heads) else None
                aliveA = gA is not None
                aliveB = True
                while aliveA or aliveB:
                    if aliveB:
                        P.cur_tag = "L%dCb" % l
                        try:
                            next(gB)
                        except StopIteration:
                            aliveB = False
                    if aliveA:
                        P.cur_tag = "L%dCa" % l
                        for _ in range(per):
                            try:
                                next(gA)
                            except StopIteration:
                                aliveA = False
                                break
            kc.reset()

        if "D" in stages:
            P.cur_tag = "L%dD" % l
            band = kc.alloc(8 * 1152, BF16)
            kc.ld(band, band_d, "D_band", q="pool")
            band3 = r3(band, 8)
            ind = kc.alloc(64 * 128, BF16)
            kc.ld(ind, ind_d, "D_ind", q="pool")
            ind3 = r3(ind, 64)
            identb = kc.alloc(128, BF16)
            kc.copy("dve", identb, ident, ["ident"], ["D_identb"])
            QT = kc.alloc(4 * S, BF16)
            KT = kc.alloc(4 * S, BF16)
            QT3 = r3(QT, 4)
            KT3 = r3(KT, 4)
            VP = kc.alloc(16 * 768, BF16)
            VP4 = VP.rearrange("p (i c w) -> p i c w", i=16, c=4)
            VP3 = r3(VP, 16)
            kc.memset("pool", VP, 0.0, ["D_VPz"])
            kc.memset("pool", VP4[:, :, :, 64:65], 1.0, ["D_VPz"])
            kms = kc.alloc(32)
            KM = kc.alloc(4 * 64, BF16)
            KM3 = r3(KM, 4)
            gsb = kc.alloc(128)
            gw = kc.alloc(128)
            mxt = kc.alloc(16)
            eqt = kc.alloc(128)
            Mt = kc.alloc(128)
            MallT = kc.alloc(S, BF16)
            ptb = [kc.alloc(512, BF16) for _ in range(5)]
            cfar = kc.alloc(8)
            kc.copy("dve", cfar, band3[:, :, 1151], ["D_band"], ["D_cfar"])
            rl = kc.alloc(512)
            osb = kc.alloc(512)
            ycT = kc.alloc(4 * S, BF16)
            ycT3 = r3(ycT, 4)
            kc.memset("pool", KM, 0.0, ["D_KMz"])
            KZ = kc.alloc(8 * S, BF16)
            KZ3 = r3(KZ, 8)
            kc.memset("pool", KZ, 0.0, ["D_KZz"])
            for s_ in range(NSEQ):
              try:
                t0 = s_ * S
                kc.ld(QT3, mqT[:, t0:t0 + S].rearrange("(c p) t -> p c t", p=128), "D_QT")
                kc.ld(KT3, mkT[:, t0:t0 + S].rearrange("(c p) t -> p c t", p=128), "D_KT")
                srcv = mv[t0:t0 + S, :].rearrange("(i p) (c two d) -> p i c two d", p=128, two=2, d=64)
                for c in range(4):
                    kc.ld(VP4[:, :, c, 0:64], srcv[:, :, c, 0, :], ("D_VP", c, 0), reads=["D_VPz"], semkey="D_VPa%d" % c)
                    kc.ld(VP4[:, :, c, 128:192], srcv[:, :, c, 1, :], ("D_VP", c, 1), reads=["D_VPz"],
                          semkey="D_VPb%d" % c)
                for h_ in range(8):
                    rr0 = (h_ % 2) * 64
                    kc.copy("pool" if h_ % 2 == 0 else "act", KZ3[rr0:rr0 + 64, h_, :], KT3[rr0:rr0 + 64, h_ // 2, :],
                            ["D_KT", "D_KZz"], [("D_KZ", h_)])
                P.op("dve", (lambda kms=kms, KT=KT: lambda e: e.tensor_reduce(r3(kms, 4), KT.rearrange("p (c n k) -> p c n k", c=4, n=8), AX.X,
                                                      ALU.add))(), reads=["D_KT"], writes=["D_kms"])
                kms3 = r3(kms, 4)
                for c in range(4):
                    kc.copy("dve", KM3[0:64, c, (2 * c) * 8:(2 * c) * 8 + 8], kms3[0:64, c, :], ["D_kms", "D_KMz"],
                            ["D_KM"])
                    kc.copy("dve", KM3[64:128, c, (2 * c + 1) * 8:(2 * c + 1) * 8 + 8], kms3[64:128, c, :],
                            ["D_kms", "D_KMz"], ["D_KM"])
                if DSTOP == 1:
                    kc.dump("KM", KM, ["D_KM"], BF16)
                    kc.dump("VP", VP, ["D_VPz"] + [("D_VP", c, hh) for c in range(4) for hh in range(2)], BF16)
                    kc.dump("kms", kms, ["D_kms"])
                    raise _Stop()
                for q4 in range(4):
                    bm, pbm, pkm = kc.bank()
                    for qq in range(4):
                        qt = q4 * 4 + qq
                        b = qt // 2
                        if b >= 4:
                            bi, pb, pk = kc.bank()
                            for c in range(4):
                                kc.mm(pb[:, 0:64], QT3[:, c, qt * 128:(qt + 1) * 128], KM3[:, c, :], c == 0, c == 3,
                                      ["D_QT", "D_KM"], [pk])
                            g3 = r3(gsb[:, 0:64], 8)
                            w3 = r3(gw[:, 0:64], 8)
                            e3 = r3(eqt[:, 0:64], 8)
                            kc.copy("act", gsb[:, 0:64], pb[:, 0:64], [pk], ["D_gsb"])
                            kc.memset("dve", g3[:, :, b:8], -1.0e9, ["D_gsb"])
                            src, srck = g3, "D_gsb"
                            for rnd in range(3):
                                P.op("dve", (lambda src=src, mxt=mxt: lambda e: e.tensor_reduce(mxt[:, 0:8], src, AX.X, ALU.max))(),
                                     reads=[srck], writes=["D_mxt"])
                                if rnd == 2:
                                    break
                                mb = mxt[:, 0:8].rearrange("p (h o) -> p h o", o=1).to_broadcast([128, 8, 8])
                                kc.tt("dve", e3, src, mb, ALU.is_ge, [srck, "D_mxt"], ["D_eqt"])
                                kc.sto("dve", w3, e3, -1.0e9, src, ALU.mult, ALU.add, ["D_eqt", srck], ["D_gw"])
                                src, srck = w3, "D_gw"
                            mb = mxt[:, 0:8].rearrange("p (h o) -> p h o", o=1).to_broadcast([128, 8, 8])
                            kc.tt("dve", e3, g3, mb, ALU.is_ge, ["D_gsb", "D_mxt"], ["D_eqt"])
                            kc.ts("dve", Mt[:, 0:64], eqt[:, 0:64], BIG, -BIG, ALU.mult, ALU.add, ["D_eqt"], ["D_Mt"])
                            kc.memset("dve", r3(Mt[:, 0:64], 8)[:, :, b:b + 1], 0.0, ["D_Mt"])
                        else:
                            kc.memset("dve", Mt[:, 0:64], 0.0, ["D_Mt"])
                        kc.transpose(pbm[0:64, qq * 128:(qq + 1) * 128], Mt[:, 0:64], ident, ["D_Mt", "ident"], [pkm])
                    kc.copy("act", MallT[0:64, q4 * 512:(q4 + 1) * 512], pbm[0:64, :], [pkm], ["D_MallT"])
                    kc.copy("act", MallT[64:128, q4 * 512:(q4 + 1) * 512], pbm[0:64, :], [pkm], ["D_MallT"])
                if DSTOP == 2:
                    kc.dump("MallT", MallT[0:64, :], ["D_MallT"], BF16)
                    raise _Stop()
                ipt = 0
                pend = []

                def flush(keep):
                    while len(pend) > keep:
                        pend.pop(0)()

                for h in range(8):
                    c = h // 2
                    r0 = (h % 2) * 64
                    lrow = 64 if h % 2 == 0 else 0
                    for qi in range(4):
                        q0 = qi * 512
                        nkt = (qi + 1) * 4
                        bo, pbo, pko = kc.bank(n=2, base=6)
                        for kt in range(nkt):
                            k0 = kt * 128
                            nblk = kt // 2
                            bs_, pbs, pks = kc.bank(n=5, base=0)
                            mms = [(KZ3[:, h, k0:k0 + 128], QT3[:, c, q0:q0 + 512], [("D_KZ", h), "D_KZz", "D_QT"])]
                            if qi >= 2 and nblk < 2 * qi + 1:
                                mms.append((ind3[:, h * 8 + nblk, :], MallT[:, q0:q0 + 512], ["D_ind", "D_MallT"]))
                            far = (q0 - k0) >= 256
                            if not far:
                                off = min(max(q0 - k0, -384), 256) + 384
                                mms.append((identb, band3[:, h, off:off + 512], ["D_identb", "D_band"]))
                            for i_, (a_, b_, rd_) in enumerate(mms):
                                kc.mm(pbs[:, :], a_, b_, i_ == 0, i_ == len(mms) - 1, rd_, [pks])
                            pt = ptb[ipt % 5]
                            ptk = "D_pt%d" % (ipt % 5)
                            ipt += 1
                            if far:
                                kc.act(pt, pbs[:, :], AF.Exp, [pks, "D_cfar"], [ptk], bias=cfar[:, h:h + 1])
                            else:
                                kc.act(pt, pbs[:, :], AF.Exp, [pks], [ptk])
                            lo = c * 192 + (h % 2) * 64

                            def pv(pbo=pbo, pko=pko, kt=kt, nkt=nkt, lo=lo, pt=pt, ptk=ptk, c=c, r0=r0, lrow=lrow, q0=q0):
                                kc.mm(pbo[:, :], VP3[:, kt, lo:lo + 128], pt, kt == 0, kt == nkt - 1,
                                      [("D_VP", c, 0), ("D_VP", c, 1), "D_VPz", ptk], [pko])
                                if kt == nkt - 1:
                                    kc.act(rl[lrow:lrow + 1, :], pbo[lrow:lrow + 1, :], AF.Ln, [pko], ["D_rl"])
                                    kc.act(rl[lrow:lrow + 1, :], rl[lrow:lrow + 1, :], AF.Exp, ["D_rl"], ["D_rl"], scale=-1.0)
                                    kc.copy("act", osb[r0:r0 + 64, :], pbo[r0:r0 + 64, :], [pko], ["D_osb"])
                                    br_, pbr, pkr = kc.bank(n=1, base=5)
                                    kc.mm(pbr[:, :], ones_f[lrow:lrow + 1, 0:128], rl[lrow:lrow + 1, :], True, True,
                                          ["ones_f", "D_rl"], [pkr])
                                    kc.tt("dve", ycT3[r0:r0 + 64, c, q0:q0 + 512], osb[r0:r0 + 64, :], pbr[r0:r0 + 64, :],
                                          ALU.mult, ["D_osb", pkr], ["D_ycT"])

                            pend.append(pv)
                            flush(2)
                    if DSTOP == 3 + h:
                        flush(0)
                        kc.dump("ycT", ycT, ["D_ycT"], BF16)
                        raise _Stop()
                flush(0)
                kc.store(ybT[1024:1536, t0:t0 + S].rearrange("(c p) t -> p c t", p=128), ycT3, "D_ycT", writes=["ybT"])
              except _Stop:
                pass
            kc.reset()

        if "E" in stages:
            P.cur_tag = "L%dE" % l
            wbr = kc.alloc(12 * 1024, BF16)
            kc.ld(wbr, wbrb[l], "E_wbr", reads=["CAST1" if l == layers[0] else "CAST1b"])
            wbr4 = wbr.rearrange("p (n k d) -> p n k d", n=3, k=4)
            wo = kc.alloc(8 * 1024, BF16)
            kc.ld(wo, woutb[l], "E_wo", reads=["CAST1" if l == layers[0] else "CAST1b"])
            wo3 = r3(wo, 8)
            ybb = [kc.alloc(12 * 512, BF16) for _ in range(2)]
            gtb = [kc.alloc(24 * 512, BF16) for _ in range(2)]
            xtb = [kc.alloc(8 * 512) for _ in range(2)]
            mg = kc.alloc(8 * 512, BF16)
            mg3 = r3(mg, 8)
            ta = kc.alloc(512)
            tb = kc.alloc(512)
            tc_ = kc.alloc(512)
            mgb = [mg, kc.alloc(8 * 512, BF16)]

            def e_phase1(tt):
                i2 = tt % 2
                tsl = slice(tt * 512, (tt + 1) * 512)
                yb3 = r3(ybb[i2], 12)
                gt3 = r3(gtb[i2], 24)
                xt3 = r3(xtb[i2], 8)
                mg3_ = r3(mgb[i2], 8)
                ybk, gtk, xk = "E_yb%d" % i2, "E_gt%d" % i2, "E_xt%d" % i2
                kc.ld(yb3, ybT[:, tsl].rearrange("(c p) t -> p c t", p=128), ybk)
                kc.ld(gt3, gatesT[:, tsl].rearrange("(c p) t -> p c t", p=128), gtk)
                kc.ld(xt3, xsrc[:, tsl].rearrange("(k p) t -> p k t", p=128), xk)
                yield
                for m in range(8):
                    pbs_ = []
                    for n in range(3):
                        bi, pb, pk = kc.bank()
                        for k in range(4):
                            kc.mm(pb[:, :], wbr4[:, n, k, m * 128:(m + 1) * 128], yb3[:, n * 4 + k, :], k == 0, k == 3,
                                  ["E_wbr", ybk], [pk])
                        pbs_.append((pb, pk))
                    kc.tt("dve", ta, pbs_[0][0][:, :], gt3[:, m, :], ALU.mult, [pbs_[0][1], gtk], ["E_ta"])
                    kc.tt("dve", tb, pbs_[1][0][:, :], gt3[:, 8 + m, :], ALU.mult, [pbs_[1][1], gtk], ["E_tb"])
                    kc.tt("dve", tc_, pbs_[2][0][:, :], gt3[:, 16 + m, :], ALU.mult, [pbs_[2][1], gtk], ["E_tc"])
                    kc.tt("pool", ta, ta, tb, ALU.add, ["E_ta", "E_tb"], ["E_ta"])
                    kc.tt("pool", mg3_[:, m, :], ta, tc_, ALU.add, ["E_ta", "E_tc"], [("E_mg%d" % i2, m)])
                    yield

            def e_phase2(tt):
                i2 = tt % 2
                tsl = slice(tt * 512, (tt + 1) * 512)
                xt3 = r3(xtb[i2], 8)
                mg3_ = r3(mgb[i2], 8)
                xk = "E_xt%d" % i2
                for m in range(8):
                    bi, pb, pk = kc.bank()
                    for k in range(8):
                        kc.mm(pb[:, :], wo3[:, k, m * 128:(m + 1) * 128], mg3_[:, k, :], k == 0, k == 7,
                              ["E_wo", ("E_mg%d" % i2, k)], [pk])
                    kc.tt("dve", xt3[:, m, :], xt3[:, m, :], pb[:, :], ALU.add, [xk, pk], [xk])
                    yield
                kc.store(x1s[:, tsl].rearrange("(k p) t -> p k t", p=128), xt3, xk, writes=["x1s"])
                yield

            for _ in e_phase1(0):
                pass
            for tt in range(NT):
                g2 = e_phase2(tt)
                g1 = e_phase1(tt + 1) if tt + 1 < NT else None
                a1 = g1 is not None
                a2 = True
                while a1 or a2:
                    if a1:
                        try:
                            next(g1)
                        except StopIteration:
                            a1 = False
                    if a2:
                        try:
                            next(g2)
                        except StopIteration:
                            a2 = False
            kc.reset()

        if "F" in stages:
            P.cur_tag = "L%dF" % l
            wdn = kc.alloc(22 * 1024, BF16)
            kc.ld(wdn, wdnb[l], "F_wdn", reads=["CAST1" if l == layers[0] else "CAST1b"])
            wdn3 = r3(wdn, 22)
            pgw = kc.alloc(8 * 1024, BF16)
            kc.ld(pgw, pgb[l], "F_pgw", reads=["CAST1" if l == layers[0] else "CAST1b"])
            pgw3 = r3(pgw, 8)
            ppw = kc.alloc(2 * 1024, BF16)
            kc.ld(ppw, ppb[l], "F_ppw", reads=["CAST1" if l == layers[0] else "CAST1b"])
            ppw3 = r3(ppw, 2)
            hal = kc.alloc(44 * 2)
            xtb = [kc.alloc(8 * 512) for _ in range(2)]
            hb = [kc.alloc(8 * 512, BF16) for _ in range(2)]
            rstd1 = kc.alloc(512)
            rstd2 = kc.alloc(512)
            wub = [kc.alloc(8 * 256, BF16) for _ in range(2)]
            rawb = [kc.alloc(514) for _ in range(2)]
            yb_ = [kc.alloc(512) for _ in range(2)]
            gTb = [kc.alloc(22 * 512, BF16) for _ in range(2)]
            ptb_ = [kc.alloc(2 * 512, BF16) for _ in range(2)]
            sg = kc.alloc(512)
            tp = kc.alloc(512)
            last = (l == layers[-1]) and final
            iwc = [0]

            def f_phase1(tt):
                i2 = tt % 2
                tsl = slice(tt * 512, (tt + 1) * 512)
                xt = xtb[i2]
                xt3 = r3(xt, 8)
                xk = "F_xt%d" % i2
                kc.ld(xt3, x1s[:, tsl].rearrange("(k p) t -> p k t", p=128), xk)
                if tt % 4 == 0:
                    kc.memset("pool", hal, 0.0, ["F_hal"])
                h2 = hb[i2]
                h23 = r3(h2, 8)
                HK = [("F_h%d" % i2, k) for k in range(8)]
                rmsnorm_tile(xt, xk, V_NFFN + l * 8, lambda k: h23[:, k, :], HK, h2, "F_sq%d" % i2, rstd1, "F_rstd1",
                             sq_extra=HK)
                yield
                gT3 = r3(gTb[i2], 22)
                for j in range(22):
                    wu = wub[iwc[0] % 2]
                    wuk = "F_wu%d" % (iwc[0] % 2)
                    iwc[0] += 1
                    kc.ld(wu, wupb[l, j], wuk, reads=["CAST1" if l == layers[0] else "CAST1b"])
                    wu3 = r3(wu, 8)
                    for half in range(2):
                        bi, pb, pk = kc.bank()
                        for k in range(8):
                            kc.mm(pb[:, :], wu3[:, k, half * 128:(half + 1) * 128], h23[:, k, :], k == 0, k == 7,
                                  [wuk, HK[k]], [pk])
                        rb = rawb[half]
                        rk = "F_raw%d" % half
                        yb = yb_[half]
                        yk = "F_y%d" % half
                        hi = (j * 2 + half) * 2
                        kc.copy("act", rb[:, 2:514], pb[:, :], [pk], [rk])
                        kc.copy("pool", rb[:, 0:2], hal[:, hi:hi + 2], ["F_hal"], [rk + "h"])
                        kc.copy("pool", hal[:, hi:hi + 2], rb[:, 512:514], [rk], ["F_hal"])
                        cwi = V_FCW + (l * 44 + half * 22 + j) * 3
                        kc.act(yb, pb[:, :], AF.Copy, [pk, "vecs"], [yk], scale=vcol(cwi + 2))
                        kc.sto("dve", yb, rb[:, 0:512], vcol(cwi), yb, ALU.mult, ALU.add, [rk, rk + "h", yk, "vecs"], [yk])
                        kc.sto("dve", yb, rb[:, 1:513], vcol(cwi + 1), yb, ALU.mult, ALU.add, [rk, rk + "h", yk, "vecs"],
                               [yk])
                    kc.act(yb_[0], yb_[0], AF.Gelu_apprx_tanh, ["F_y0"], ["F_y0"])
                    kc.tt("dve", gT3[:, j, :], yb_[0], yb_[1], ALU.mult, ["F_y0", "F_y1"], [("F_g%d" % i2, j)])
                    yield

            def f_phase2(tt):
                i2 = tt % 2
                tsl = slice(tt * 512, (tt + 1) * 512)
                xt = xtb[i2]
                xt3 = r3(xt, 8)
                xk = "F_xt%d" % i2
                h2 = hb[i2]
                h23 = r3(h2, 8)
                HK = [("F_h%d" % i2, k) for k in range(8)]
                gT3 = r3(gTb[i2], 22)
                pt_ = ptb_[i2]
                ptk = "F_pt%d" % i2
                kc.ld(r3(pt_, 2), pT_in[l, :, tsl].rearrange("(k p) t -> p k t", p=128), ptk, q="pool")
                for m in range(8):
                    bi, pb, pk = kc.bank()
                    for j in range(22):
                        kc.mm(pb[:, :], wdn3[:, j, m * 128:(m + 1) * 128], gT3[:, j, :], j == 0, j == 21,
                              ["F_wdn", ("F_g%d" % i2, j)], [pk])
                    kc.tt("dve", xt3[:, m, :], xt3[:, m, :], pb[:, :], ALU.add, [xk, pk], [xk])
                    yield
                rmsnorm_tile(xt, xk, V_NPLE + l * 8, lambda k: h23[:, k, :], HK, h2, "F_sq%d" % i2, rstd2, "F_rstd2",
                             sq_extra=HK)
                yield
                for m in range(8):
                    bi, pb, pk = kc.bank()
                    for k in range(8):
                        kc.mm(pb[:, :], pgw3[:, k, m * 128:(m + 1) * 128], h23[:, k, :], k == 0, k == 7,
                              ["F_pgw", HK[k]], [pk])
                    kc.act(sg, pb[:, :], AF.Sigmoid, [pk], ["F_sg"])
                    bi, pb, pk = kc.bank()
                    for k in range(2):
                        kc.mm(pb[:, :], ppw3[:, k, m * 128:(m + 1) * 128], r3(pt_, 2)[:, k, :], k == 0, k == 1,
                              ["F_ppw", ptk], [pk])
                    kc.tt("dve", tp, pb[:, :], sg, ALU.mult, [pk, "F_sg"], ["F_tp"])
                    kc.tt("pool", xt3[:, m, :], xt3[:, m, :], tp, ALU.add, [xk, "F_tp"], [xk])
                    yield
                if last:
                    OK_ = [("F_ot%d" % i2, k) for k in range(8)]
                    rmsnorm_tile(xt, xk, V_NFIN, lambda k: xt3[:, k, :], OK_, h2, "F_sq%d" % i2, rstd2, "F_rstd2",
                                 sq_extra=HK)
                    kc.finals.append(kc.store(outT[:, tsl].rearrange("(k p) t -> p k t", p=128), xt3, OK_ + [xk],
                                              writes=["outT"], semkey="F_ot_st%d" % i2))
                else:
                    kc.store(xs[:, tsl].rearrange("(k p) t -> p k t", p=128), xt3, xk, writes=["xs"])
                yield

            for _ in f_phase1(0):
                pass
            for tt in range(NT):
                g2 = f_phase2(tt)
                g1 = f_phase1(tt + 1) if tt + 1 < NT else None
                a1 = g1 is not None
                a2 = True
                while a1 or a2:
                    if a1:
                        try:
                            next(g1)
                        except StopIteration:
                            a1 = False
                    if a2:
                        try:
                            next(g2)
                        except StopIteration:
                            a2 = False
            kc.reset()
    nsem = P.emit(final_wait_ops=kc.finals)
    return nc, kc, nsem


def _t5_bucket_np(n):
    n = np.maximum(n, 0)
    exact = 16
    nf = np.maximum(n, 1).astype(np.float32)
    large = exact + (np.log(nf / exact) / math.log(128 / exact) * (32 - exact)).astype(np.int32)
    large = np.minimum(large, 31)
    return np.where(n < exact, n, large)


def host_consts():
    ident = np.eye(128, dtype=np.float32)
    p = np.arange(64)[:, None]
    f = np.arange(64)[None, :]
    U = (p <= f).astype(np.float32)
    mA = np.where(f >= p, 0.0, -BIG).astype(np.float32)
    mB = np.where(p > f, 0.0, -BIG).astype(np.float32)
    SU = (f > p).astype(np.float32)
    tri = np.concatenate([U, mA, mB, SU], axis=1)
    rc = np.zeros((128, 64), np.float32)
    for g in range(4):
        w = 2 << g
        for t in range(16):
            rc[:, g * 16 + t] = 1.0 / min(t + 1, w)
    ind = np.zeros((64, 64, 128), np.float32)
    for r in range(64):
        ind[r, r, :] = 1.0
    ind = ind.reshape(64, 64 * 128)
    return ident, tri, rc, np.concatenate([ind, ind], axis=0)


def host_vecs(inp):
    v = np.zeros((128, NVEC), np.float32)

    def put(base, arr):
        a = np.asarray(arr, np.float32)
        lead = int(np.prod(a.shape[:-1])) if a.ndim > 1 else 1
        a = a.reshape(lead, -1, 128)
        a = a.transpose(2, 0, 1).reshape(128, -1)
        v[:, base:base + a.shape[1]] = a

    put(V_NMIX, inp["norm_mix"])
    put(V_NFFN, inp["norm_ffn"])
    put(V_NPLE, inp["norm_ple"])
    put(V_NFIN, inp["norm_final"])
    put(V_PSCALE, inp["pool_scale"])
    cw = np.asarray(inp["dn_conv"], np.float32).reshape(2, 4, 12, 128).transpose(3, 0, 2, 1).reshape(128, 96)
    v[:, V_CW:V_CW + 96] = cw
    fc = np.asarray(inp["ffn_conv"], np.float32).reshape(2, 3, 44, 128).transpose(3, 0, 2, 1).reshape(128, 264)
    v[:, V_FCW:V_FCW + 264] = fc
    v[:, V_DNW:V_DNW + 2] = np.asarray(inp["dn_norm"], np.float32).T
    v[:, V_EPS] = EPS
    v[:, V_ONE] = 1.0
    return v


def host_band(rel_bias):
    rb = np.asarray(rel_bias, np.float32)
    p = np.arange(128)[:, None]
    c = np.arange(1152)[None, :]
    n = c - 384 - p
    bucket = _t5_bucket_np(n)
    band = np.empty((128, 8, 1152), np.float32)
    for h in range(8):
        band[:, h, :] = np.where(n >= 0, rb[bucket, h], -BIG)
    return band.reshape(128, 8 * 1152)


_CACHE = {}


def get_program(NSEQ=2, **kw):
    key = (NSEQ, tuple(sorted(kw.items())))
    if key not in _CACHE:
        _CACHE[key] = build(NSEQ=NSEQ, **kw)
    return _CACHE[key]


def make_in_maps(inp, NSEQ, ncores):
    ident, tri, rc, ind = host_consts()
    vecs = host_vecs(inp)
    band = host_band(inp["rel_bias"])
    a4 = np.stack([np.asarray(inp["dn_a_log"], np.float32)[0], np.asarray(inp["dn_dt_bias"], np.float32)[0],
                   np.asarray(inp["dn_a_log"], np.float32)[1], np.asarray(inp["dn_dt_bias"], np.float32)[1]], axis=1)
    x = np.asarray(inp["x"], np.float32)
    p = np.asarray(inp["p"], np.float32)
    shared = {
        "w_in": np.ascontiguousarray(inp["w_in"], np.float32),
        "w_branch": np.ascontiguousarray(inp["w_branch"], np.float32),
        "w_out": np.ascontiguousarray(inp["w_out"], np.float32),
        "ffn_up": np.ascontiguousarray(inp["ffn_up"], np.float32),
        "ffn_down": np.ascontiguousarray(inp["ffn_down"], np.float32),
        "ple_gate": np.ascontiguousarray(inp["ple_gate"], np.float32),
        "ple_proj": np.ascontiguousarray(inp["ple_proj"], np.float32),
        "pool_w": np.ascontiguousarray(inp["pool_w"], np.float32),
        "vecs": vecs, "ident": ident, "tri": tri, "rcnt": rc, "band": band, "ind": ind,
        "a4": np.ascontiguousarray(a4),
    }
    maps = []
    for c in range(ncores):
        xs_ = x[c * NSEQ:(c + 1) * NSEQ].reshape(NSEQ * S, D)
        ps_ = p[:, c * NSEQ:(c + 1) * NSEQ].reshape(DEPTH, NSEQ * S, 256)
        m = dict(shared)
        m["xT"] = np.ascontiguousarray(xs_.T)
        m["pT"] = np.ascontiguousarray(ps_.transpose(0, 2, 1))
        maps.append(m)
    return maps


def kernel(**inputs):
    NSEQ = 2
    nc, kc, _ = get_program(NSEQ=NSEQ)
    maps = make_in_maps(inputs, NSEQ, NCORES)
    res = run_bass_kernel_spmd(nc, maps, core_ids=list(range(NCORES)))
    outs = []
    for c in range(NCORES):
        oT = np.asarray(res.results[c]["outT"], np.float32)
        outs.append(oT.T.reshape(NSEQ, S, D))
    return np.concatenate(outs, axis=0).astype(np.float32)
```

```python
import contextlib
import math
import numpy as np
import concourse.bass as bass
import concourse.mybir as mybir
from concourse.bass_utils import run_bass_kernel_spmd

F32 = mybir.dt.float32
BF16 = mybir.dt.bfloat16
AF = mybir.ActivationFunctionType
ALU = mybir.AluOpType
AX = mybir.AxisListType

D = 1024
S = 2048
DEPTH = 2
IN_COLS = 7176
FFN = 2816
EPS = 1e-6
NCORES = 8
BIG = 30000.0


class Op:
    __slots__ = ("eng", "fn", "deps", "sem", "inc", "val", "signal", "is_dma", "tag")

    def __init__(self, eng, fn, is_dma=False):
        self.eng = eng
        self.fn = fn
        self.deps = []
        self.sem = None
        self.inc = 1
        self.val = None
        self.signal = False
        self.is_dma = is_dma


class Prog:
    ENGS = ("pe", "act", "dve", "pool", "sp")

    def __init__(self, nc):
        self.nc = nc
        self.ops = {e: [] for e in self.ENGS}
        self.last_w = {}
        self.readers = {}
        self.n_ops = 0
        self.last_dma = {}
        self.bar = {e: None for e in self.ENGS}
        self.cur_tag = "pre"
        self.scopes = False

    def barrier(self):
        deps = []
        for e in self.ENGS:
            for o in reversed(self.ops[e]):
                if not o.is_dma:
                    deps.append(o)
                    break
        deps.extend(self.last_dma.values())
        for e in self.ENGS:
            self.bar[e] = deps
        self.last_w = {}
        self.readers = {}

    def _add(self, op, reads, writes):
        deps = []
        for k in reads:
            w = self.last_w.get(k)
            if w is not None:
                deps.append((w, False))
        for k in writes:
            w = self.last_w.get(k)
            if w is not None:
                deps.append((w, False))
            for r in self.readers.get(k, ()):
                deps.append((r, True))
        if self.bar[op.eng] is not None:
            for d in self.bar[op.eng]:
                deps.append((d, False))
            self.bar[op.eng] = None
        seen = set()
        for d, war in deps:
            if d is op or id(d) in seen:
                continue
            if d.eng == op.eng and not d.is_dma and not op.is_dma:
                if op.eng == "pe" or war:
                    continue
            seen.add(id(d))
            op.deps.append(d)
            d.signal = True
        for k in reads:
            self.readers.setdefault(k, []).append(op)
        for k in writes:
            self.last_w[k] = op
            self.readers[k] = []
        op.tag = self.cur_tag
        self.ops[op.eng].append(op)
        self.n_ops += 1

    @staticmethod
    def _is_psum(k):
        return isinstance(k, str) and k.startswith("ps") and k[2:].isdigit()

    def op(self, eng, fn, reads=(), writes=()):
        o = Op(eng, fn)
        o.sem = ("eng", eng)
        ex = [k for k in reads if self._is_psum(k)]
        if ex:
            reads = [k for k in reads if not self._is_psum(k)]
            writes = list(writes) + ex
        self._add(o, reads, writes)
        return o

    def dma(self, fn, semkey, reads=(), writes=(), q="sp"):
        o = Op(q, fn, is_dma=True)
        o.sem = ("dma", semkey)
        o.inc = 16
        o.signal = True
        self._add(o, reads, writes)
        self.last_dma[semkey] = o
        return o

    def emit(self, final_wait_ops=()):
        nc = self.nc
        counts = {}
        for e in self.ENGS:
            for o in self.ops[e]:
                if o.signal:
                    c = counts.get(o.sem, 0) + o.inc
                    counts[o.sem] = c
                    o.val = c
        semkeys = list(counts.keys())
        with contextlib.ExitStack() as st:
            sems = {}
            for i, k in enumerate(semkeys):
                sems[k] = st.enter_context(nc.semaphore("s%d" % i))
            block = st.enter_context(nc.Block())
            engmap = {"pe": block.tensor, "act": block.scalar, "dve": block.vector,
                      "pool": block.gpsimd, "sp": block.sync}

            def make(e):
                oplist = self.ops[e]

                def body(eng):
                    waited = {}
                    cur = None
                    cm = None
                    for o in oplist:
                        if self.scopes and o.tag != cur:
                            if cm is not None:
                                cm.__exit__(None, None, None)
                            cm = nc.named_scope(o.tag)
                            cm.__enter__()
                            cur = o.tag
                        for d in o.deps:
                            if waited.get(d.sem, 0) >= d.val:
                                continue
                            eng.wait_ge(sems[d.sem], d.val)
                            waited[d.sem] = d.val
                        ins = o.fn(eng)
                        if o.signal:
                            ins.then_inc(sems[o.sem], o.inc)
                    if cm is not None:
                        cm.__exit__(None, None, None)
                    if e == "sp":
                        for o in final_wait_ops:
                            if waited.get(o.sem, 0) >= o.val:
                                continue
                            eng.wait_ge(sems[o.sem], o.val)
                            waited[o.sem] = o.val

                return body

            for e in self.ENGS:
                if self.ops[e] or e == "sp":
                    engmap[e](make(e))
        return len(semkeys)


ARENA_F32 = 47 * 1024 + 512


import os as _os
CSTOP = int(_os.environ.get("C_STOP", "99"))
DSTOP = int(_os.environ.get("D_STOP", "99"))


class _Stop(Exception):
    pass


class KC:
    def __init__(self, nc, NSEQ, debug):
        self.nc = nc
        self.P = Prog(nc)
        self.NSEQ = NSEQ
        self.T = NSEQ * S
        self.NT = self.T // 512
        self.debug = debug
        self.st = contextlib.ExitStack()
        self.arena = self.st.enter_context(nc.sbuf_tensor("arena", [128, ARENA_F32], F32))
        self.arena_bf = self.arena[:, :].bitcast(BF16)
        self.psb = [self.st.enter_context(nc.psum_tensor("psb%d" % i, [128, 512], F32)) for i in range(8)]
        self.bump = 0
        self.perm = 0
        self.rr = 0
        self.finals = []
        self.dram = {}
        self.uid = 0

    def alloc(self, cols, dt=F32):
        if dt == BF16:
            w = (cols + 1) // 2
            a = self.bump
            self.bump += w
            assert self.bump <= ARENA_F32, "SBUF arena overflow %d" % self.bump
            return self.arena_bf[:, 2 * a:2 * a + cols]
        a = self.bump
        self.bump += cols
        assert self.bump <= ARENA_F32, "SBUF arena overflow %d" % self.bump
        return self.arena[:, a:a + cols]

    def make_perm(self):
        self.perm = self.bump

    def reset(self):
        self.P.barrier()
        self.bump = self.perm

    def key(self, base):
        self.uid += 1
        return "%s#%d" % (base, self.uid)

    def din(self, name, shape, dt=F32):
        t = self.nc.dram_tensor(name, list(shape), dt, kind="ExternalInput").ap()
        self.dram[name] = t
        return t

    def dscr(self, name, shape, dt=F32, out=False):
        kind = "ExternalOutput" if (out or self.debug) else "Internal"
        t = self.nc.dram_tensor(name, list(shape), dt, kind=kind).ap()
        self.dram[name] = t
        return t

    def ld(self, dst, src, key, reads=(), q="sp", semkey=None, slow=False):
        if slow:
            return self.P.dma(lambda e: e.dma_start(out=dst, in_=src, allow_slow_non_contiguous=True), semkey or key,
                              reads=reads, writes=[key], q=q)
        return self.P.dma(lambda e: e.dma_start(out=dst, in_=src), semkey or key, reads=reads, writes=[key], q=q)

    def stt(self, dst, src, srckey, writes=(), q="sp", semkey=None):
        keys = list(srckey) if isinstance(srckey, list) else [srckey]
        sk = semkey or (str(keys[0]) + "_st")
        return self.P.dma(lambda e: e.dma_start(out=dst, in_=src), sk, reads=keys, writes=writes, q=q)

    def dump(self, name, ap, keys, dt=F32):
        if not self.debug:
            return
        shape = list(ap.shape)
        d = self.nc.dram_tensor("dbg_" + name, shape, dt, kind="ExternalOutput").ap()
        self.finals.append(self.P.dma(lambda e: e.dma_start(out=d, in_=ap), "dump_" + name, reads=list(keys)))

    def store(self, dst, src, srckey, writes=(), q="sp", semkey=None):
        return self.stt(dst, src, srckey, writes, q, semkey)

    def act(self, out, in_, func, reads, writes, bias=None, scale=None, accum=None):
        kw = {}
        if bias is not None:
            kw["bias"] = bias
        if scale is not None:
            kw["scale"] = scale
        if accum is not None:
            kw["accum_out"] = accum
        return self.P.op("act", lambda e: e.activation(out=out, in_=in_, func=func, **kw), reads=reads, writes=writes)

    def tt(self, eng, out, in0, in1, op, reads, writes):
        return self.P.op(eng, lambda e: e.tensor_tensor(out, in0, in1, op), reads=reads, writes=writes)

    def ts(self, eng, out, in0, s1, s2, op0, op1, reads, writes):
        if s2 is None:
            return self.P.op(eng, lambda e: e.tensor_scalar(out, in0, s1, None, op0), reads=reads, writes=writes)
        return self.P.op(eng, lambda e: e.tensor_scalar(out, in0, s1, s2, op0, op1), reads=reads, writes=writes)

    def sto(self, eng, out, in0, scalar, in1, op0, op1, reads, writes):
        return self.P.op(eng, lambda e: e.scalar_tensor_tensor(out=out, in0=in0, scalar=scalar, in1=in1, op0=op0,
                                                               op1=op1), reads=reads, writes=writes)

    def copy(self, eng, out, in_, reads, writes):
        if eng == "act":
            return self.P.op("act", lambda e: e.copy(out, in_), reads=reads, writes=writes)
        return self.P.op(eng, lambda e: e.tensor_copy(out, in_), reads=reads, writes=writes)

    def memset(self, eng, ap, val, writes):
        return self.P.op(eng, lambda e: e.memset(ap, val), writes=writes)

    def recip(self, out, in_, reads, writes):
        return self.P.op("dve", lambda e: e.reciprocal(out, in_), reads=reads, writes=writes)

    def transpose(self, out, in_, ident, reads, writes):
        return self.P.op("pe", lambda e: e.transpose(out, in_, ident), reads=reads, writes=writes)

    def mm(self, out, lhsT, rhs, start, stop, reads, writes):
        return self.P.op("pe", lambda e: e.matmul(out, lhsT, rhs, start=start, stop=stop), reads=reads, writes=writes)

    def bank(self, n=8, base=0):
        i = base + (self.rr % n)
        self.rr += 1
        return i, self.psb[i], "ps%d" % i


def r3(ap, a):
    return ap.rearrange("p (a b) -> p a b", a=a)


V_NMIX = 0
V_NFFN = 16
V_NPLE = 32
V_NFIN = 48
V_PSCALE = 56
V_CW = 64
V_FCW = 160
V_DNW = 424
V_EPS = 426
V_ONE = 427
NVEC = 428
WIN_GROUP_C0 = [0, 512, 1024, 1536, 2048, 2568, 3080, 3592] + [4104 + 512 * i for i in range(6)]
WIN_GROUP_DST = [("u", None, 0), ("dqkv", None, 0), ("dqkv", None, 512), ("dqkv", None, 1024), ("dz", None, 0),
                 ("mq", None, 0), ("mk", None, 0), ("mv", None, 0)] + [("gate", None, 512 * i) for i in range(6)]

C_POOL = 0
C_DQKV = 512
C_DZ = 2048
C_DB = 2560
C_DA = 2564
C_MQ = 2568
C_MK = 2568 + 512
C_MV = 2568 + 1024
C_GATE = 4104


def build(NSEQ=2, debug=False, layers=(0, 1), stages="ABCDEF", final=True, scopes=False):
    nc = bass.Bass("TRN2", target_bir_lowering=False)
    kc = KC(nc, NSEQ, debug)
    P = kc.P
    P.scopes = scopes
    T = kc.T
    NT = kc.NT
    xT_in = kc.din("xT", [D, T])
    pT_in = kc.din("pT", [DEPTH, 256, T])
    w_in = kc.din("w_in", [DEPTH, D, IN_COLS])
    w_branch = kc.din("w_branch", [DEPTH, 3, 512, D])
    w_out = kc.din("w_out", [DEPTH, D, D])
    ffn_up = kc.din("ffn_up", [DEPTH, D, 2 * FFN])
    ffn_down = kc.din("ffn_down", [DEPTH, FFN, D])
    ple_gate = kc.din("ple_gate", [DEPTH, D, D])
    ple_proj = kc.din("ple_proj", [DEPTH, 256, D])
    pool_w = kc.din("pool_w", [DEPTH, 4, 128, 128])
    vecs_d = kc.din("vecs", [128, NVEC])
    ident_d = kc.din("ident", [128, 128])
    tri_d = kc.din("tri", [64, 4 * 64])
    rcnt_d = kc.din("rcnt", [128, 64])
    band_d = kc.din("band", [128, 8 * 1152])
    ind_d = kc.din("ind", [128, 64 * 128])
    a4_d = kc.din("a4", [4, 4])
    outT = kc.dscr("outT", [D, T], out=True)
    xs = kc.dscr("xs", [D, T])
    x1s = kc.dscr("x1s", [D, T])
    uT = kc.dscr("uT", [512, T])
    dqkvT = kc.dscr("dqkvT", [1536, T])
    dzT = kc.dscr("dzT", [512, T])
    dbg = kc.dscr("dbg", [8, T])
    mqT = kc.dscr("mqT", [512, T], BF16)
    mkT = kc.dscr("mkT", [512, T], BF16)
    mv = kc.dscr("mv", [T, 512], BF16)
    gatesT = kc.dscr("gatesT", [3072, T], BF16)
    ybT = kc.dscr("ybT", [1536, T], BF16)
    NG_IN = 15
    winb = kc.dscr("winb", [DEPTH, 15, 128, 8 * 512], BF16)
    wbrb = kc.dscr("wbrb", [DEPTH, 128, 12 * 1024], BF16)
    woutb = kc.dscr("woutb", [DEPTH, 128, 8 * 1024], BF16)
    wupb = kc.dscr("wupb", [DEPTH, 22, 128, 8 * 256], BF16)
    wdnb = kc.dscr("wdnb", [DEPTH, 128, 22 * 1024], BF16)
    pgb = kc.dscr("pgb", [DEPTH, 128, 8 * 1024], BF16)
    ppb = kc.dscr("ppb", [DEPTH, 128, 2 * 1024], BF16)
    pwb = kc.dscr("pwb", [DEPTH, 128, 4 * 128], BF16)

    vecs = kc.alloc(NVEC)
    ident = kc.alloc(128)
    ones_bf = kc.alloc(128, BF16)
    ones_f = kc.alloc(128)
    kc.ld(vecs, vecs_d, "vecs")
    kc.ld(ident, ident_d, "ident")
    P.op("pool", lambda e: e.memset(ones_bf, 1.0), writes=["ones_bf"])
    P.op("pool", lambda e: e.memset(ones_f, 1.0), writes=["ones_f"])
    kc.make_perm()
    CONST = ["vecs", "ident", "ones_bf", "ones_f"]

    def cast(dst, src, key):
        grp = ("CAST0" if key[0] == "winb" else "CAST1") + ("" if key[1] == layers[0] else "b")
        if key[0] == "winb" and key[1] == layers[0]:
            grp = "CAST0_%d" % key[2]
        P.dma(lambda e: e.dma_start(out=dst, in_=src), grp, writes=[grp], q="pool")

    for l in layers:
        for g in range(14):
            c0 = WIN_GROUP_C0[g]
            cast(winb[l, g].rearrange("p (k c) -> p k c", k=8),
                 w_in[l, :, c0:c0 + 512].rearrange("(k p) c -> p k c", p=128), ("winb", l, g))
        cast(winb[l, 14].rearrange("p (k c) -> p k c", k=8)[:, :, 0:8],
             w_in[l, :, C_DB:C_DB + 8].rearrange("(k p) c -> p k c", p=128), ("winb", l, 14))
        cast(pwb[l].rearrange("p (g d) -> p g d", g=4), pool_w[l].rearrange("g c d -> c g d"), ("pwb", l))
        for n in range(3):
            cast(wbrb[l].rearrange("p (n k d) -> p n k d", n=3, k=4)[:, n],
                 w_branch[l, n].rearrange("(k p) d -> p k d", p=128), ("wbrb", l, n))
        cast(woutb[l].rearrange("p (k d) -> p k d", k=8), w_out[l].rearrange("(k p) d -> p k d", p=128), ("woutb", l))
        for j in range(22):
            dstv = wupb[l, j].rearrange("p (k c) -> p k c", k=8)
            cast(dstv[:, :, 0:128], ffn_up[l, :, j * 128:(j + 1) * 128].rearrange("(k p) c -> p k c", p=128),
                 ("wupb", l, j, 0))
            cast(dstv[:, :, 128:256],
                 ffn_up[l, :, FFN + j * 128:FFN + (j + 1) * 128].rearrange("(k p) c -> p k c", p=128),
                 ("wupb", l, j, 1))
        cast(wdnb[l].rearrange("p (j d) -> p j d", j=22), ffn_down[l].rearrange("(j p) d -> p j d", p=128),
             ("wdnb", l))
        cast(pgb[l].rearrange("p (k d) -> p k d", k=8), ple_gate[l].rearrange("(k p) d -> p k d", p=128), ("pgb", l))
        cast(ppb[l].rearrange("p (k d) -> p k d", k=2), ple_proj[l].rearrange("(k p) d -> p k d", p=128), ("ppb", l))

    def vcol(i):
        return vecs[:, i:i + 1]

    def rmsnorm_tile(xt, xkey, nbase, out_fn, outkeys, sqt, sqkey, rstd, rkey, engs=("dve",), sq_extra=()):
        P.op("act", lambda e: e.activation(out=sqt, in_=xt, func=AF.Square), reads=[xkey],
             writes=[sqkey] + list(sq_extra))
        bi, pb, pk = kc.bank()
        for k in range(8):
            kc.mm(pb[:, :], ones_bf, sqt[:, k * 512:(k + 1) * 512], k == 0, k == 7, [sqkey, "ones_bf"], [pk])
        P.op("act", lambda e: e.activation(out=rstd, in_=pb[:, :], func=AF.Ln, bias=vcol(V_EPS), scale=1.0 / D),
             reads=[pk, "vecs"], writes=[rkey])
        P.op("act", lambda e: e.activation(out=rstd, in_=rstd, func=AF.Exp, scale=-0.5), reads=[rkey], writes=[rkey])
        for k in range(8):
            o = out_fn(k)
            eng = engs[k % len(engs)]
            P.op(eng, (lambda o=o, k=k: lambda e: e.scalar_tensor_tensor(
                out=o, in0=xt[:, k * 512:(k + 1) * 512], scalar=vcol(nbase + k), in1=rstd,
                op0=ALU.mult, op1=ALU.mult))(), reads=[xkey, rkey, "vecs"], writes=[outkeys[k]])

    stages0 = stages
    for l in layers:
        stages = stages0 if l == layers[0] else _os.environ.get("L1S", stages0)
        xsrc = xT_in if l == 0 else xs
        xsrc_key = "xs"
        if "A" in stages:
            P.cur_tag = "L%dA" % l
            hT = kc.alloc(8 * T, BF16)
            hT3 = r3(hT, 8)
            xtb = [kc.alloc(8 * 512) for _ in range(2)]
            sqt = kc.alloc(8 * 512, BF16)
            rstd = kc.alloc(512)
            for tt in range(NT):
                xt = xtb[tt % 2]
                xk = "A_xt%d" % (tt % 2)
                kc.ld(r3(xt, 8), xsrc[:, tt * 512:(tt + 1) * 512].rearrange("(k p) t -> p k t", p=128), xk,
                      reads=[xsrc_key])
                rmsnorm_tile(xt, xk, V_NMIX + l * 8, lambda k: hT3[:, k, tt * 512:(tt + 1) * 512],
                             [("hT", tt, k) for k in range(8)], sqt, "A_sq", rstd, "A_rstd")
            HT_ALL = [("hT", tt) for tt in range(NT)]
            wgb = [kc.alloc(8 * 512, BF16) for _ in range(2)]
            ob32 = [kc.alloc(T) for _ in range(2)]
            ob16 = [kc.alloc(T, BF16) for _ in range(2)]
            ovb = [kc.alloc(512, BF16) for _ in range(2)]
            cnt = {"o32": 0, "o16": 0, "ov": 0, "ev": 0}

            def evac(kind, dst, src, pk, okey):
                if kind in ("u", "dqkv"):
                    eng = "act" if cnt["ev"] % 2 == 0 else "dve"
                    cnt["ev"] += 1
                    if eng == "act":
                        P.op("act", lambda e: e.copy(dst, src), reads=[pk], writes=[okey])
                    else:
                        P.op("dve", lambda e: e.tensor_copy(dst, src), reads=[pk], writes=[okey])
                elif kind == "dz":
                    P.op("act", lambda e: e.activation(out=dst, in_=src, func=AF.Silu), reads=[pk], writes=[okey])
                elif kind == "mq":
                    P.op("dve", lambda e: e.tensor_scalar(dst, src, 0.125, None, ALU.mult), reads=[pk], writes=[okey])
                elif kind == "mk":
                    P.op("dve", lambda e: e.tensor_copy(dst, src), reads=[pk], writes=[okey])
                elif kind == "gate":
                    P.op("act", lambda e: e.activation(out=dst, in_=src, func=AF.Sigmoid), reads=[pk], writes=[okey])
                else:
                    raise ValueError(kind)

            for g in range(14):
                wg = wgb[g % 2]
                wk = "A_wg%d" % (g % 2)
                kc.ld(wg, winb[l, g], wk, reads=["CAST0b"] if l != layers[0] else ["CAST0_%d" % g])
                wg3 = r3(wg, 8)
                gkind, gdst, grow0 = WIN_GROUP_DST[g]
                if gkind == "mv":
                    for i in range(T // 128):
                        bi, pb, pk = kc.bank()
                        for k in range(8):
                            kc.mm(pb[:, :], hT3[:, k, i * 128:(i + 1) * 128], wg3[:, k, :], k == 0, k == 7,
                                  [("hT", i // 4, k), wk], [pk])
                        ov = ovb[cnt["ov"] % 2]
                        ok = "A_ov%d" % (cnt["ov"] % 2)
                        cnt["ov"] += 1
                        eng = "act" if i % 2 == 0 else "dve"
                        if eng == "act":
                            P.op("act", (lambda ov=ov, pb=pb: lambda e: e.copy(ov, pb[:, :]))(), reads=[pk], writes=[ok])
                        else:
                            P.op("dve", (lambda ov=ov, pb=pb: lambda e: e.tensor_copy(ov, pb[:, :]))(), reads=[pk],
                                 writes=[ok])
                        kc.stt(mv[i * 128:(i + 1) * 128, :], ov, ok, writes=["mv"])
                    continue
                for j in range(4):
                    is16 = gkind in ("mq", "mk", "gate")
                    if is16:
                        ob = ob16[cnt["o16"] % 2]
                        okey = "A_o16_%d" % (cnt["o16"] % 2)
                        cnt["o16"] += 1
                    else:
                        ob = ob32[cnt["o32"] % 2]
                        okey = "A_o32_%d" % (cnt["o32"] % 2)
                        cnt["o32"] += 1
                    for tt in range(NT):
                        bi, pb, pk = kc.bank()
                        for k in range(8):
                            kc.mm(pb[:, :], wg3[:, k, j * 128:(j + 1) * 128], hT3[:, k, tt * 512:(tt + 1) * 512],
                                  k == 0, k == 7, [("hT", tt, k), wk], [pk])
                        evac(gkind, ob[:, tt * 512:(tt + 1) * 512], pb[:, :], pk, (okey, tt))
                    dst = {"u": uT, "dqkv": dqkvT, "dz": dzT, "mq": mqT, "mk": mkT, "gate": gatesT}[gkind]
                    r0 = grow0 + j * 128
                    kc.stt(dst[r0:r0 + 128, :], ob, [(okey, tt) for tt in range(NT)], writes=[gkind + "_d"], semkey=okey + "_st")
            wg = wgb[0]
            wk = "A_wg0"
            kc.ld(wg, winb[l, 14], wk, reads=["CAST0b"] if l != layers[0] else ["CAST0_14"])
            wg3 = r3(wg, 8)
            a4 = kc.alloc(4)
            P_a4 = kc.ld(a4[0:4, :], a4_d, "A_a4")
            nexpA = kc.alloc(1)
            kc.act(nexpA[0:4, :], a4[0:4, 2 * l:2 * l + 1], AF.Exp, ["A_a4"], ["A_nexpA"])
            kc.ts("dve", nexpA[0:4, :], nexpA[0:4, :], -1.0, None, ALU.mult, None, ["A_nexpA"], ["A_nexpA"])
            obb = ob32[0]
            oba = ob32[1]
            for tt in range(NT):
                sl = slice(tt * 512, (tt + 1) * 512)
                bi, pb, pk = kc.bank()
                for k in range(8):
                    kc.mm(pb[0:4, :], wg3[:, k, 0:4], hT3[:, k, sl], k == 0, k == 7, [("hT", tt, k), wk], [pk])
                kc.act(obb[0:4, sl], pb[0:4, :], AF.Sigmoid, [pk], [("A_o32_0", tt)])
                bi, pb, pk = kc.bank()
                for k in range(8):
                    kc.mm(pb[0:4, :], wg3[:, k, 4:8], hT3[:, k, sl], k == 0, k == 7, [("hT", tt, k), wk], [pk])
                kc.act(oba[0:4, sl], pb[0:4, :], AF.Exp, [pk, "A_a4"], [("A_o32_1", tt)],
                       bias=a4[0:4, 2 * l + 1:2 * l + 2])
            AK1 = [("A_o32_1", tt) for tt in range(NT)]
            kc.act(oba[0:4, :], oba[0:4, :], AF.Ln, AK1 + ["vecs"], AK1, bias=vcol(V_ONE)[0:4, :])
            kc.ts("dve", oba[0:4, :], oba[0:4, :], nexpA[0:4, 0:1], None, ALU.mult, None, AK1 + ["A_nexpA"], AK1)
            kc.stt(dbg[0:4, :], obb[0:4, :], [("A_o32_0", tt) for tt in range(NT)], writes=["dbg"], semkey="A_o32_0_st")
            kc.stt(dbg[4:8, :], oba[0:4, :], [("A_o32_1", tt) for tt in range(NT)], writes=["dbg"], semkey="A_o32_1_st")
            kc.reset()
        if "B" in stages and "D" not in stages:
            P.cur_tag = "L%dB" % l
            pw = kc.alloc(4 * 128, BF16)
            kc.ld(pw, pwb[l], "B_pw", reads=["CAST1" if l == layers[0] else "CAST1b"])
            pw3 = r3(pw, 4)
            rc = kc.alloc(64)
            kc.ld(rc, rcnt_d, "B_rc")
            ub = [kc.alloc(16 + S) for _ in range(2)]
            sab = [kc.alloc(16 + S) for _ in range(2)]
            mxb = [kc.alloc(S, BF16) for _ in range(2)]
            t16 = kc.alloc(16)
            yob = [kc.alloc(S, BF16) for _ in range(2)]
            for i in range(2):
                kc.memset("pool", ub[i][:, 0:16], 0.0, ["B_u%dz" % i])
                kc.memset("pool", sab[i][:, 0:16], 0.0, ["B_s%dz" % i])
            it = 0
            for s_ in range(NSEQ):
                for g in range(4):
                    i2 = it % 2
                    u = ub[i2]
                    uk = "B_u%d" % i2
                    kc.ld(u[:, 16:], uT[g * 128:(g + 1) * 128, s_ * S:(s_ + 1) * S], uk)
                    cur, curk = u, uk
                    for j in range(g + 1):
                        dst = sab[j % 2]
                        dk = "B_s%d" % (j % 2)
                        sh = 1 << j
                        kc.tt("dve" if j % 2 == 0 else "pool", dst[:, 16:], cur[:, 16:], cur[:, 16 - sh:16 - sh + S],
                              ALU.add, [curk, curk + "z"], [dk])
                        cur, curk = dst, dk
                    w = 1 << (g + 1)
                    m = mxb[i2]
                    mk_ = "B_mx%d" % i2
                    kc.sto("dve", m, cur[:, 16:], 1.0 / w, u[:, 16:], ALU.mult, ALU.subtract, [curk, uk], [mk_])
                    kc.tt("dve", t16, cur[:, 16:32], rc[:, g * 16:(g + 1) * 16], ALU.mult, [curk, "B_rc"], ["B_t16"])
                    kc.tt("dve", m[:, 0:16], t16, u[:, 16:32], ALU.subtract, ["B_t16", uk, mk_], [mk_])
                    y = yob[i2]
                    yk = "B_y%d" % i2
                    for j in range(4):
                        bi, pb, pk = kc.bank()
                        kc.mm(pb[:, :], pw3[:, g, :], m[:, j * 512:(j + 1) * 512], True, True, [mk_, "B_pw"], [pk])
                        kc.ts("dve" if j % 2 == 0 else "dve", y[:, j * 512:(j + 1) * 512], pb[:, :],
                              vcol(V_PSCALE + l * 4 + g), None, ALU.mult, None, [pk, "vecs"], [yk])
                    kc.store(ybT[g * 128:(g + 1) * 128, s_ * S:(s_ + 1) * S], y, yk, writes=["ybT"])
                    it += 1
            kc.reset()

        if "C" in stages:
            P.cur_tag = "L%dC" % l
            tri = kc.alloc(256)
            kc.ld(tri[0:64, :], tri_d, "C_tri")
            Ut = tri[0:64, 0:64]
            mA = tri[0:64, 64:128]
            mB = tri[0:64, 128:192]
            SU = tri[0:64, 192:256]
            identb3 = ident[0:64, 0:64].rearrange("p (o j) -> p o j", o=1).to_broadcast([64, 8, 64])
            raw = kc.alloc(3 + S)
            Xb = raw[:, 3:3 + S]
            acc = kc.alloc(S)
            sqb = kc.alloc(S, BF16)
            rn = kc.alloc(512)
            khb = kc.alloc(S, BF16)
            qhb = kc.alloc(S, BF16)
            gcrow = kc.alloc(S)
            E1 = kc.alloc(S)
            brow = kc.alloc(S)
            gT = kc.alloc(32)
            bT = kc.alloc(32)
            gcc = kc.alloc(32)
            egd = kc.alloc(32)
            Dm = kc.alloc(512)
            Gb = kc.alloc(512)
            GTi = kc.alloc(512)
            GTb = kc.alloc(512)
            tmpD = kc.alloc(512)
            Qk = [kc.alloc(512) for _ in range(2)]
            Rk = [kc.alloc(512) for _ in range(2)]
            Gk = [kc.alloc(512) for _ in range(2)]
            SETS = []
            for i_ in range(2):
                SETS.append(dict(i=i_, keT=kc.alloc(S), qdT=kc.alloc(S), kdec=kc.alloc(32 * 128, BF16),
                                 vtok=kc.alloc(32 * 128, BF16), aqkT=kc.alloc(S, BF16), TTb=kc.alloc(S, BF16),
                                 egl=kc.alloc(32)))
            oT = kc.alloc(S)
            S_ = kc.alloc(128)
            Rb = kc.alloc(128, BF16)
            vn = kc.alloc(128, BF16)
            zsb = [kc.alloc(512) for _ in range(2)]
            yout = kc.alloc(S, BF16)
            sqb2 = kc.alloc(S, BF16)
            rn2 = kc.alloc(512)
            kc.memset("pool", raw[:, 0:3], 0.0, ["C_rawz"])
            identP = kc.alloc(64)
            bTe = kc.alloc(16)
            bTo = kc.alloc(16)
            kc.copy("dve", identP[0:64, :], ident[0:64, 0:64], ["ident"], ["C_identP"])
            kc.copy("dve", identP[64:128, :], ident[64:128, 64:128], ["ident"], ["C_identP"])

            def bc_mid(ap64, n):
                return ap64.rearrange("p (o j) -> p o j", o=1).to_broadcast([64, n, 64])

            def bc_last(ap, n, w):
                return ap.rearrange("p (n o) -> p n o", o=1).to_broadcast([64, n, w])

            def phaseA(s_, h, st):
                t0 = s_ * S
                si = st["i"]
                kK = lambda nm: "C_%s%d" % (nm, si)
                keT, qdT, kdec, vtok, aqkT, TTb, egl = (st[k] for k in ("keT", "qdT", "kdec", "vtok", "aqkT", "TTb", "egl"))
                kc.ld(gcrow[0:64, :], dbg[4 + h:5 + h, t0:t0 + S].partition_broadcast(64), "C_gcrow")
                kc.ld(brow[0:64, :], dbg[h:h + 1, t0:t0 + S].partition_broadcast(64), "C_brow")
                idb32 = ident[0:64, 0:64].rearrange("p (o j) -> p o j", o=1).to_broadcast([64, 32, 64])
                kc.tt("dve", r3(Xb[0:64, :], 32), r3(gcrow[0:64, :], 32), idb32, ALU.mult, ["C_gcrow", "ident"], ["C_raw"])
                P.op("dve", (lambda: lambda e: e.tensor_reduce(gT[0:64, :], r3(Xb[0:64, :], 32), AX.X, ALU.add))(),
                     reads=["C_raw"], writes=["C_gT"])
                kc.tt("dve", r3(Xb[0:64, :], 32), r3(brow[0:64, :], 32), idb32, ALU.mult, ["C_brow", "ident"], ["C_raw"])
                P.op("dve", (lambda: lambda e: e.tensor_reduce(bT[0:64, :], r3(Xb[0:64, :], 32), AX.X, ALU.add))(),
                     reads=["C_raw"], writes=["C_bT"])
                yield
                bi, pb, pk = kc.bank()
                kc.mm(pb[0:64, 0:32], Ut, gT[0:64, :], True, True, ["C_tri", "C_gT"], [pk])
                kc.copy("act", gcc[0:64, :], pb[0:64, 0:32], [pk], ["C_gcc"])
                bi, pb, pk = kc.bank()
                kc.mm(pb[:, 0:32], ones_f[0:64, 0:128], gT[0:64, :], True, True, ["ones_f", "C_gT"], [pk])
                kc.act(egl[:, :], pb[:, 0:32], AF.Exp, [pk], [kK("egl")])
                kc.tt("dve", egd[0:64, :], pb[0:64, 0:32], gcc[0:64, :], ALU.subtract, [pk, "C_gcc"], ["C_egd"])
                kc.act(egd[0:64, :], egd[0:64, :], AF.Exp, ["C_egd"], ["C_egd"])
                kc.tt("dve", r3(Xb[0:64, :], 32), bc_last(gT[0:64, :], 32, 64), bc_mid(Ut, 32), ALU.mult,
                      ["C_gT", "C_tri"], ["C_raw"])
                yield
                for q4 in range(4):
                    sl = slice(q4 * 512, (q4 + 1) * 512)
                    bi, pb, pk = kc.bank()
                    kc.mm(pb[:, :], ones_f[0:64, 0:128], Xb[0:64, sl], True, True, ["ones_f", "C_raw"], [pk])
                    kc.copy("dve", gcrow[:, sl], pb[:, :], [pk], ["C_gcrow"])
                    kc.act(E1[:, sl], pb[:, :], AF.Exp, [pk], ["C_E1"])
                    yield

                def conv_silu(comp):
                    blk = comp * 4 + h
                    kc.ld(raw[:, 3:3 + S], dqkvT[blk * 128:(blk + 1) * 128, t0:t0 + S], "C_raw")
                    cwi = V_CW + (l * 12 + blk) * 4
                    kc.ts("dve", acc, raw[:, 0:S], vcol(cwi), None, ALU.mult, None, ["C_raw", "C_rawz", "vecs"], ["C_acc"])
                    for j in range(1, 4):
                        kc.sto("dve", acc, raw[:, j:j + S], vcol(cwi + j), acc, ALU.mult, ALU.add,
                               ["C_raw", "C_rawz", "C_acc", "vecs"], ["C_acc"])
                    kc.act(acc, acc, AF.Silu, ["C_acc"], ["C_acc"])

                def l2n_slice(q4, scale):
                    sl = slice(q4 * 512, (q4 + 1) * 512)
                    bi, pb, pk = kc.bank()
                    kc.mm(pb[:, :], ones_bf, sqb[:, sl], True, True, ["ones_bf", "C_sqb"], [pk])
                    kc.act(rn, pb[:, :], AF.Ln, [pk, "vecs"], ["C_rn"], bias=vcol(V_EPS))
                    kc.act(rn, rn, AF.Exp, ["C_rn"], ["C_rn"], scale=-0.5)
                    kc.sto("dve", acc[:, sl], acc[:, sl], scale, rn, ALU.mult, ALU.mult, ["C_acc", "C_rn"], ["C_acc"])

                def to_tok4(n4, dst, dkey, mul_egd):
                    bi, pb, pk = kc.bank()
                    for c in range(4):
                        n = n4 * 4 + c
                        kc.transpose(pb[0:64, c * 128:(c + 1) * 128], acc[:, n * 64:(n + 1) * 64], ident,
                                     ["C_acc", "ident"], [pk])
                    dsl = dst[0:64, n4 * 512:(n4 + 1) * 512]
                    if mul_egd:
                        kc.tt("dve", r3(dsl, 4), r3(pb[0:64, :], 4), bc_last(egd[0:64, n4 * 4:n4 * 4 + 4], 4, 128),
                              ALU.mult, [pk, "C_egd"], [dkey])
                    else:
                        kc.copy("act", dsl, pb[0:64, :], [pk], [dkey])

                conv_silu(1)
                yield
                kc.act(sqb, acc, AF.Square, ["C_acc"], ["C_sqb"])
                for q4 in range(4):
                    l2n_slice(q4, 1.0)
                    yield
                kc.copy("act", khb, acc, ["C_acc"], ["C_khb"])
                kc.tt("dve", keT, acc, E1, ALU.mult, ["C_acc", "C_E1"], [kK("keT")])
                yield
                for n4 in range(8):
                    to_tok4(n4, kdec, kK("kdec"), True)
                    yield
                conv_silu(0)
                yield
                kc.act(sqb, acc, AF.Square, ["C_acc"], ["C_sqb"])
                for q4 in range(4):
                    l2n_slice(q4, 128.0 ** -0.5)
                    yield
                kc.copy("act", qhb, acc, ["C_acc"], ["C_qhb"])
                kc.tt("dve", qdT, acc, E1, ALU.mult, ["C_acc", "C_E1"], [kK("qdT")])
                yield
                conv_silu(2)
                yield
                for n4 in range(8):
                    to_tok4(n4, vtok, kK("vtok"), False)
                    yield

                kc.copy("dve", bTe[0:64, :], bT[0:64, :].rearrange("p (m two) -> p m two", two=2)[:, :, 0], ["C_bT"],
                        ["C_bTe"])
                kc.copy("dve", bTo[0:64, :], bT[0:64, :].rearrange("p (m two) -> p m two", two=2)[:, :, 1], ["C_bT"],
                        ["C_bTo"])

                def v4(ap):
                    return ap.rearrange("p (m two f) -> p m two f", m=4, two=2)

                for bt in range(4):
                    sb, lb = bt // 2, bt % 2
                    n0 = bt * 8
                    c0 = bt * 512
                    bsl = slice(c0, c0 + 512)
                    kc.tt("dve", r3(Dm[0:64, :], 8), r3(gcrow[0:64, bsl], 8), bc_last(gcc[0:64, n0:n0 + 8], 8, 64),
                          ALU.subtract, ["C_gcrow", "C_gcc"], ["C_Dm"])
                    kc.tt("pool", r3(tmpD[0:64, :], 8), r3(Dm[0:64, :], 8), bc_mid(mA, 8), ALU.add,
                          ["C_Dm", "C_tri"], ["C_tmpD"])
                    kc.act(GTi[0:64, :], tmpD[0:64, :], AF.Exp, ["C_tmpD"], ["C_GTi"])
                    kc.tt("dve", r3(tmpD[0:64, :], 8), bc_mid(mB, 8), r3(Dm[0:64, :], 8), ALU.subtract,
                          ["C_Dm", "C_tri", "C_tmpD"], ["C_tmpD"])
                    kc.act(Gb[0:64, :], tmpD[0:64, :], AF.Exp, ["C_tmpD"], ["C_Gb"])
                    kc.tt("dve", r3(Gb[0:64, :], 8), r3(Gb[0:64, :], 8), bc_last(bT[0:64, n0:n0 + 8], 8, 64),
                          ALU.mult, ["C_Gb", "C_bT"], ["C_Gb"])
                    kc.tt("pool", r3(GTb[0:64, :], 8), r3(GTi[0:64, :], 8), bc_mid(SU, 8), ALU.mult,
                          ["C_GTi", "C_tri"], ["C_GTb"])
                    kc.tt("pool", GTb[0:64, :], GTb[0:64, :], brow[0:64, bsl], ALU.mult, ["C_GTb", "C_brow"],
                          ["C_GTb"])
                    yield
                    Q, R, G = Qk[sb], Rk[sb], Gk[sb]
                    qk_, rk_, gk_ = "C_Q%d" % sb, "C_R%d" % sb, "C_G%d" % sb
                    lsl = slice(lb * 256, (lb + 1) * 256)
                    bi, pb, pk = kc.bank()
                    for c in range(8):
                        cs = slice((n0 + c) * 64, (n0 + c + 1) * 64)
                        kc.mm(pb[0:64, c * 64:(c + 1) * 64], khb[:, cs], khb[:, cs], True, True, ["C_khb"], [pk])
                    pv_ = v4(pb[0:64, :])
                    for par in range(2):
                        psl = slice(par * 64, (par + 1) * 64)
                        kc.tt("dve", r3(R[psl, lsl], 4), pv_[:, :, par, :], v4(Gb[0:64, :])[:, :, par, :], ALU.mult,
                              [pk, "C_Gb"], [(rk_, lb, par)])
                        kc.tt("dve", r3(Q[psl, lsl], 4), pv_[:, :, par, :], v4(GTb[0:64, :])[:, :, par, :], ALU.mult,
                              [pk, "C_GTb"], [(qk_, lb, par)])
                    bi, pb, pk = kc.bank()
                    for c in range(8):
                        cs = slice((n0 + c) * 64, (n0 + c + 1) * 64)
                        kc.mm(pb[0:64, c * 64:(c + 1) * 64], khb[:, cs], qhb[:, cs], True, True,
                              ["C_khb", "C_qhb"], [pk])
                    kc.tt("dve", aqkT[0:64, bsl], pb[0:64, :], GTi[0:64, :], ALU.mult, [pk, "C_GTi"], [kK("aqkT")])
                    kc.tt("pool", r3(G[:, lsl], 4), identP.rearrange("p (o j) -> p o j", o=1).to_broadcast([128, 4, 64]),
                          r3(Q[:, lsl], 4), ALU.subtract, ["C_identP", (qk_, lb, 0), (qk_, lb, 1)], [(gk_, lb)])
                    yield
                QRK = lambda nm: [(nm, lb_, par_) for lb_ in range(2) for par_ in range(2)]
                for lev in range(1, 6):
                    for sb in range(2):
                        Q, R, G = Qk[sb], Rk[sb], Gk[sb]
                        qk_, rk_, gk_ = "C_Q%d" % sb, "C_R%d" % sb, "C_G%d" % sb
                        qks = QRK(qk_) if lev == 1 else [qk_]
                        rks = QRK(rk_) if lev == 1 else [rk_]
                        gks = [(gk_, 0), (gk_, 1)] if lev == 1 else [gk_]

                        def pairmm(pbx, pkx, A_, B_, rd):
                            for m in range(8):
                                ms = slice(m * 64, (m + 1) * 64)
                                kc.mm(pbx[0:64, ms], A_[0:64, ms], B_[0:64, ms], True, True, rd, [pkx])
                                P.op("pe", (lambda o=pbx[64:128, ms], a_=A_[64:128, ms], b_=B_[64:128, ms]:
                                            lambda e: e.matmul(o, a_, b_, start=True, stop=True, tile_position=(64, 64)))(),
                                     reads=rd, writes=[pkx])

                        if lev < 5:
                            bq, pbq, pkq = kc.bank()
                            pairmm(pbq, pkq, R, Q, qks + rks)
                        br, pbr, pkr = kc.bank()
                        pairmm(pbr, pkr, Q, R, qks + rks)
                        if lev < 5:
                            kc.copy("act", Q[:, :], pbq[:, :], [pkq], [qk_] + QRK(qk_))
                        kc.copy("act", R[:, :], pbr[:, :], [pkr], [rk_] + QRK(rk_))
                        yield
                        bg, pbg, pkg = kc.bank()
                        pairmm(pbg, pkg, R, G, [rk_] + gks)
                        kc.tt("dve", G[:, :], G[:, :], pbg[:, :], ALU.add, gks + [pkg], [gk_, (gk_, 0), (gk_, 1)])
                        yield
                for sb in range(2):
                    G = Gk[sb]
                    gk_ = "C_G%d" % sb
                    TTv = TTb[0:64, sb * 1024:(sb + 1) * 1024].rearrange("p (m two f) -> p m two f", m=8, two=2)
                    kc.tt("dve", TTv[:, :, 0, :], r3(G[0:64, :], 8), bc_last(bTe[0:64, sb * 8:(sb + 1) * 8], 8, 64), ALU.mult,
                          [gk_, "C_bTe"], [kK("TTb")])
                    kc.copy("act", tmpD[0:64, :], G[64:128, :], [gk_], ["C_tmpD"])
                    kc.tt("dve", TTv[:, :, 1, :], r3(tmpD[0:64, :], 8), bc_last(bTo[0:64, sb * 8:(sb + 1) * 8], 8, 64), ALU.mult,
                          ["C_tmpD", "C_bTo"], [kK("TTb")])
                    yield

            def phaseB(s_, h, st):
                t0 = s_ * S
                si = st["i"]
                kK = lambda nm: "C_%s%d" % (nm, si)
                keT, qdT, kdec, vtok, aqkT, TTb, egl = (st[k] for k in ("keT", "qdT", "kdec", "vtok", "aqkT", "TTb", "egl"))
                kc.memset("dve", S_, 0.0, ["C_S"])
                for n in range(32):
                    cs = slice(n * 64, (n + 1) * 64)
                    ns = slice(n * 128, (n + 1) * 128)
                    b1, pb1, pk1 = kc.bank()
                    kc.mm(pb1[0:64, 0:128], keT[:, cs], S_, True, True, [kK("keT"), "C_S"], [pk1])
                    b3, pb3, pk3 = kc.bank()
                    kc.mm(pb3[:, 0:64], S_, qdT[:, cs], True, False, [kK("qdT"), "C_S"], [pk3])
                    kc.tt("dve", Rb[0:64, :], vtok[0:64, ns], pb1[0:64, 0:128], ALU.subtract, [kK("vtok"), pk1], ["C_Rb"])
                    b2, pb2, pk2 = kc.bank()
                    kc.mm(pb2[0:64, 0:128], TTb[0:64, cs], Rb[0:64, :], True, True, [kK("TTb"), "C_Rb"], [pk2])
                    kc.copy("act", vn[0:64, :], pb2[0:64, 0:128], [pk2], ["C_vn"])
                    kc.mm(pb3[:, 0:64], vn[0:64, :], aqkT[0:64, cs], False, True, ["C_vn", kK("aqkT")], [pk3])
                    b4, pb4, pk4 = kc.bank()
                    kc.mm(pb4[:, 0:128], kdec[0:64, ns], vn[0:64, :], True, True, [kK("kdec"), "C_vn"], [pk4])
                    kc.copy("act", oT[:, cs], pb3[:, 0:64], [pk3], ["C_oT"])
                    kc.sto("dve", S_, S_, egl[:, n:n + 1], pb4[:, 0:128], ALU.mult, ALU.add, ["C_S", kK("egl"), pk4],
                           ["C_S"])
                    yield
                kc.act(sqb2, oT, AF.Square, ["C_oT"], ["C_sqb2"])
                for q4 in range(4):
                    sl = slice(q4 * 512, (q4 + 1) * 512)
                    zs = zsb[q4 % 2]
                    zk = "C_zs%d" % (q4 % 2)
                    kc.ld(zs, dzT[h * 128:(h + 1) * 128, t0 + q4 * 512:t0 + (q4 + 1) * 512], zk)
                    bi, pb, pk = kc.bank()
                    kc.mm(pb[:, :], ones_bf, sqb2[:, sl], True, True, ["ones_bf", "C_sqb2"], [pk])
                    kc.act(rn2, pb[:, :], AF.Ln, [pk, "vecs"], ["C_rn2"], bias=vcol(V_EPS), scale=1.0 / 128)
                    kc.act(rn2, rn2, AF.Exp, ["C_rn2"], ["C_rn2"], scale=-0.5)
                    kc.sto("dve", oT[:, sl], oT[:, sl], vcol(V_DNW + l), rn2, ALU.mult, ALU.mult,
                           ["C_oT", "C_rn2", "vecs"], ["C_oT"])
                    kc.tt("pool", yout[:, sl], oT[:, sl], zs, ALU.mult, ["C_oT", zk], ["C_yout"])
                    yield
                kc.store(ybT[512 + h * 128:512 + (h + 1) * 128, t0:t0 + S], yout, "C_yout", writes=["ybT"])
                yield

            heads = [(s_, h) for s_ in range(NSEQ) for h in range(4)]
            nA = 0
            P.cur_tag = "L%dCa" % l
            for _ in phaseA(heads[0][0], heads[0][1], SETS[0]):
                nA += 1
            nB = 38
            per = max(1, -(-nA // nB))
            for i_, (s_, h) in enumerate(heads):
                gB = phaseB(s_, h, SETS[i_ % 2])
                gA = phaseA(heads[i_ + 1][0], heads[i_ + 1][1], SETS[(i_ + 1) % 2]) if i_ + 1 < len(heads) else None
                aliveA = gA is not None
                aliveB = True
                while aliveA or aliveB:
                    if aliveB:
                        P.cur_tag = "L%dCb" % l
                        try:
                            next(gB)
                        except StopIteration:
                            aliveB = False
                    if aliveA:
                        P.cur_tag = "L%dCa" % l
                        for _ in range(per):
                            try:
                                next(gA)
                            except StopIteration:
                                aliveA = False
                                break
            kc.reset()

        if "D" in stages:
            P.cur_tag = "L%dD" % l
            band = kc.alloc(8 * 1152, BF16)
            kc.ld(band, band_d, "D_band", q="pool")
            band3 = r3(band, 8)
            ind = kc.alloc(64 * 128, BF16)
            kc.ld(ind, ind_d, "D_ind", q="pool")
            ind3 = r3(ind, 64)
            identb = kc.alloc(128, BF16)
            kc.copy("dve", identb, ident, ["ident"], ["D_identb"])
            QT = kc.alloc(4 * S, BF16)
            KT = kc.alloc(4 * S, BF16)
            QT3 = r3(QT, 4)
            KT3 = r3(KT, 4)
            VP = kc.alloc(16 * 768, BF16)
            VP4 = VP.rearrange("p (i c w) -> p i c w", i=16, c=4)
            VP3 = r3(VP, 16)
            kc.memset("pool", VP, 0.0, ["D_VPz"])
            kc.memset("pool", VP4[:, :, :, 64:65], 1.0, ["D_VPz"])
            kms = kc.alloc(32)
            KM = kc.alloc(4 * 64, BF16)
            KM3 = r3(KM, 4)
            gsb = kc.alloc(128)
            gw = kc.alloc(128)
            mxt = kc.alloc(16)
            eqt = kc.alloc(128)
            Mt = kc.alloc(128)
            MallT = kc.alloc(S, BF16)
            ptb = [kc.alloc(512, BF16) for _ in range(4)]
            cfar = kc.alloc(8)
            kc.copy("dve", cfar, band3[:, :, 1151], ["D_band"], ["D_cfar"])
            rl = kc.alloc(512)
            osb = kc.alloc(512)
            ycT = kc.alloc(4 * S, BF16)
            ycT3 = r3(ycT, 4)
            kc.memset("pool", KM, 0.0, ["D_KMz"])
            foldB = "B" in stages
            if foldB:
                b_pw = kc.alloc(4 * 128, BF16)
                kc.ld(b_pw, pwb[l], "B_pw", reads=["CAST1" if l == layers[0] else "CAST1b"])
                b_pw3 = r3(b_pw, 4)
                b_rc = kc.alloc(64)
                kc.ld(b_rc, rcnt_d, "B_rc")
                b_u = kc.alloc(16 + S)
                b_s = [kc.alloc(16 + S) for _ in range(2)]
                b_m = kc.alloc(S, BF16)
                b_t16 = kc.alloc(16)
                b_y = kc.alloc(S, BF16)
                kc.memset("pool", b_u[:, 0:16], 0.0, ["B_uz"])
                kc.memset("pool", b_s[0][:, 0:16], 0.0, ["B_s0z"])
                kc.memset("pool", b_s[1][:, 0:16], 0.0, ["B_s1z"])

                def b_gen(s_):
                    for g in range(4):
                        kc.ld(b_u[:, 16:], uT[g * 128:(g + 1) * 128, s_ * S:(s_ + 1) * S], "B_u")
                        yield
                        cur, curk = b_u, "B_u"
                        for j in range(g + 1):
                            dst = b_s[j % 2]
                            dk = "B_s%d" % (j % 2)
                            sh = 1 << j
                            kc.tt("pool", dst[:, 16:], cur[:, 16:], cur[:, 16 - sh:16 - sh + S], ALU.add,
                                  [curk, curk + "z"], [dk])
                            cur, curk = dst, dk
                            yield
                        w = 1 << (g + 1)
                        kc.sto("dve", b_m, cur[:, 16:], 1.0 / w, b_u[:, 16:], ALU.mult, ALU.subtract, [curk, "B_u"],
                               ["B_mx"])
                        kc.tt("dve", b_t16, cur[:, 16:32], b_rc[:, g * 16:(g + 1) * 16], ALU.mult, [curk, "B_rc"],
                              ["B_t16"])
                        kc.tt("dve", b_m[:, 0:16], b_t16, b_u[:, 16:32], ALU.subtract, ["B_t16", "B_u", "B_mx"],
                              ["B_mx"])
                        yield
                        for j in range(4):
                            kc.mm(kc.psb[4][:, :], b_pw3[:, g, :], b_m[:, j * 512:(j + 1) * 512], True, True,
                                  ["B_mx", "B_pw"], ["ps4"])
                            kc.ts("dve", b_y[:, j * 512:(j + 1) * 512], kc.psb[4][:, :], vcol(V_PSCALE + l * 4 + g), None,
                                  ALU.mult, None, ["ps4", "vecs"], ["B_y"])
                            yield
                        kc.store(ybT[g * 128:(g + 1) * 128, s_ * S:(s_ + 1) * S], b_y, "B_y", writes=["ybT_a"])
                        yield
            KZ = kc.alloc(8 * S, BF16)
            KZ3 = r3(KZ, 8)
            kc.memset("pool", KZ, 0.0, ["D_KZz"])
            for s_ in range(NSEQ):
              try:
                t0 = s_ * S
                kc.ld(QT3, mqT[:, t0:t0 + S].rearrange("(c p) t -> p c t", p=128), "D_QT")
                kc.ld(KT3, mkT[:, t0:t0 + S].rearrange("(c p) t -> p c t", p=128), "D_KT")
                srcv = mv[t0:t0 + S, :].rearrange("(i p) (c two d) -> p i c two d", p=128, two=2, d=64)
                for c in range(4):
                    kc.ld(VP4[:, :, c, 0:64], srcv[:, :, c, 0, :], ("D_VP", c, 0), reads=["D_VPz"], semkey="D_VPa%d" % c)
                    kc.ld(VP4[:, :, c, 128:192], srcv[:, :, c, 1, :], ("D_VP", c, 1), reads=["D_VPz"],
                          semkey="D_VPb%d" % c)
                for h_ in range(8):
                    rr0 = (h_ % 2) * 64
                    kc.copy("pool" if h_ % 2 == 0 else "act", KZ3[rr0:rr0 + 64, h_, :], KT3[rr0:rr0 + 64, h_ // 2, :],
                            ["D_KT", "D_KZz"], [("D_KZ", h_)])
                P.op("dve", (lambda kms=kms, KT=KT: lambda e: e.tensor_reduce(r3(kms, 4), KT.rearrange("p (c n k) -> p c n k", c=4, n=8), AX.X,
                                                      ALU.add))(), reads=["D_KT"], writes=["D_kms"])
                kms3 = r3(kms, 4)
                for c in range(4):
                    kc.copy("dve", KM3[0:64, c, (2 * c) * 8:(2 * c) * 8 + 8], kms3[0:64, c, :], ["D_kms", "D_KMz"],
                            ["D_KM"])
                    kc.copy("dve", KM3[64:128, c, (2 * c + 1) * 8:(2 * c + 1) * 8 + 8], kms3[64:128, c, :],
                            ["D_kms", "D_KMz"], ["D_KM"])
                if DSTOP == 1:
                    kc.dump("KM", KM, ["D_KM"], BF16)
                    kc.dump("VP", VP, ["D_VPz"] + [("D_VP", c, hh) for c in range(4) for hh in range(2)], BF16)
                    kc.dump("kms", kms, ["D_kms"])
                    raise _Stop()
                for q4 in range(4):
                    bm, pbm, pkm = kc.bank()
                    for qq in range(4):
                        qt = q4 * 4 + qq
                        b = qt // 2
                        if b >= 4:
                            bi, pb, pk = kc.bank()
                            for c in range(4):
                                kc.mm(pb[:, 0:64], QT3[:, c, qt * 128:(qt + 1) * 128], KM3[:, c, :], c == 0, c == 3,
                                      ["D_QT", "D_KM"], [pk])
                            g3 = r3(gsb[:, 0:64], 8)
                            w3 = r3(gw[:, 0:64], 8)
                            e3 = r3(eqt[:, 0:64], 8)
                            kc.copy("act", gsb[:, 0:64], pb[:, 0:64], [pk], ["D_gsb"])
                            kc.memset("dve", g3[:, :, b:8], -1.0e9, ["D_gsb"])
                            src, srck = g3, "D_gsb"
                            for rnd in range(3):
                                P.op("dve", (lambda src=src, mxt=mxt: lambda e: e.tensor_reduce(mxt[:, 0:8], src, AX.X, ALU.max))(),
                                     reads=[srck], writes=["D_mxt"])
                                if rnd == 2:
                                    break
                                mb = mxt[:, 0:8].rearrange("p (h o) -> p h o", o=1).to_broadcast([128, 8, 8])
                                kc.tt("dve", e3, src, mb, ALU.is_ge, [srck, "D_mxt"], ["D_eqt"])
                                kc.sto("dve", w3, e3, -1.0e9, src, ALU.mult, ALU.add, ["D_eqt", srck], ["D_gw"])
                                src, srck = w3, "D_gw"
                            mb = mxt[:, 0:8].rearrange("p (h o) -> p h o", o=1).to_broadcast([128, 8, 8])
                            kc.tt("dve", e3, g3, mb, ALU.is_ge, ["D_gsb", "D_mxt"], ["D_eqt"])
                            kc.ts("dve", Mt[:, 0:64], eqt[:, 0:64], BIG, -BIG, ALU.mult, ALU.add, ["D_eqt"], ["D_Mt"])
                            kc.memset("dve", r3(Mt[:, 0:64], 8)[:, :, b:b + 1], 0.0, ["D_Mt"])
                        else:
                            kc.memset("dve", Mt[:, 0:64], 0.0, ["D_Mt"])
                        kc.transpose(pbm[0:64, qq * 128:(qq + 1) * 128], Mt[:, 0:64], ident, ["D_Mt", "ident"], [pkm])
                    kc.copy("act", MallT[0:64, q4 * 512:(q4 + 1) * 512], pbm[0:64, :], [pkm], ["D_MallT"])
                    kc.copy("act", MallT[64:128, q4 * 512:(q4 + 1) * 512], pbm[0:64, :], [pkm], ["D_MallT"])
                if DSTOP == 2:
                    kc.dump("MallT", MallT[0:64, :], ["D_MallT"], BF16)
                    raise _Stop()
                ipt = 0
                pend = []
                bg_ = [b_gen(s_) if foldB else None]

                def b_step(n=1):
                    for _ in range(n):
                        if bg_[0] is None:
                            return
                        try:
                            next(bg_[0])
                        except StopIteration:
                            bg_[0] = None

                def flush(keep):
                    while len(pend) > keep:
                        pend.pop(0)()

                for h in range(8):
                    c = h // 2
                    r0 = (h % 2) * 64
                    lrow = 64 if h % 2 == 0 else 0
                    for qi in range(4):
                        q0 = qi * 512
                        nkt = (qi + 1) * 4
                        bo, pbo, pko = kc.bank(n=2, base=6)
                        for kt in range(nkt):
                            k0 = kt * 128
                            nblk = kt // 2
                            bs_, pbs, pks = kc.bank(n=4 if foldB else 5, base=0)
                            mms = [(KZ3[:, h, k0:k0 + 128], QT3[:, c, q0:q0 + 512], [("D_KZ", h), "D_KZz", "D_QT"])]
                            if qi >= 2 and nblk < 2 * qi + 1:
                                mms.append((ind3[:, h * 8 + nblk, :], MallT[:, q0:q0 + 512], ["D_ind", "D_MallT"]))
                            far = (q0 - k0) >= 256
                            if not far:
                                off = min(max(q0 - k0, -384), 256) + 384
                                mms.append((identb, band3[:, h, off:off + 512], ["D_identb", "D_band"]))
                            for i_, (a_, b_, rd_) in enumerate(mms):
                                kc.mm(pbs[:, :], a_, b_, i_ == 0, i_ == len(mms) - 1, rd_, [pks])
                            pt = ptb[ipt % 4]
                            ptk = "D_pt%d" % (ipt % 4)
                            ipt += 1
                            if far:
                                kc.act(pt, pbs[:, :], AF.Exp, [pks, "D_cfar"], [ptk], bias=cfar[:, h:h + 1])
                            else:
                                kc.act(pt, pbs[:, :], AF.Exp, [pks], [ptk])
                            lo = c * 192 + (h % 2) * 64

                            def pv(pbo=pbo, pko=pko, kt=kt, nkt=nkt, lo=lo, pt=pt, ptk=ptk, c=c, r0=r0, lrow=lrow, q0=q0):
                                kc.mm(pbo[:, :], VP3[:, kt, lo:lo + 128], pt, kt == 0, kt == nkt - 1,
                                      [("D_VP", c, 0), ("D_VP", c, 1), "D_VPz", ptk], [pko])
                                if kt == nkt - 1:
                                    kc.act(rl[lrow:lrow + 1, :], pbo[lrow:lrow + 1, :], AF.Ln, [pko], ["D_rl"])
                                    kc.act(rl[lrow:lrow + 1, :], rl[lrow:lrow + 1, :], AF.Exp, ["D_rl"], ["D_rl"], scale=-1.0)
                                    kc.copy("act", osb[r0:r0 + 64, :], pbo[r0:r0 + 64, :], [pko], ["D_osb"])
                                    br_, pbr, pkr = kc.bank(n=1, base=5)
                                    kc.mm(pbr[:, :], ones_f[lrow:lrow + 1, 0:128], rl[lrow:lrow + 1, :], True, True,
                                          ["ones_f", "D_rl"], [pkr])
                                    kc.tt("dve", ycT3[r0:r0 + 64, c, q0:q0 + 512], osb[r0:r0 + 64, :], pbr[r0:r0 + 64, :],
                                          ALU.mult, ["D_osb", pkr], ["D_ycT"])

                            pend.append(pv)
                            flush(2)
                            b_step()
                    if DSTOP == 3 + h:
                        flush(0)
                        kc.dump("ycT", ycT, ["D_ycT"], BF16)
                        raise _Stop()
                flush(0)
                b_step(1000)
                kc.store(ybT[1024:1536, t0:t0 + S].rearrange("(c p) t -> p c t", p=128), ycT3, "D_ycT", writes=["ybT"])
              except _Stop:
                pass
            kc.reset()

        if "E" in stages:
            P.cur_tag = "L%dE" % l
            wbr = kc.alloc(12 * 1024, BF16)
            kc.ld(wbr, wbrb[l], "E_wbr", reads=["CAST1" if l == layers[0] else "CAST1b"])
            wbr4 = wbr.rearrange("p (n k d) -> p n k d", n=3, k=4)
            wo = kc.alloc(8 * 1024, BF16)
            kc.ld(wo, woutb[l], "E_wo", reads=["CAST1" if l == layers[0] else "CAST1b"])
            wo3 = r3(wo, 8)
            ybb = [kc.alloc(12 * 512, BF16) for _ in range(2)]
            gtb = [kc.alloc(24 * 512, BF16) for _ in range(2)]
            xtb = [kc.alloc(8 * 512) for _ in range(2)]
            mg = kc.alloc(8 * 512, BF16)
            mg3 = r3(mg, 8)
            ta = kc.alloc(512)
            tb = kc.alloc(512)
            tc_ = kc.alloc(512)
            mgb = [mg, kc.alloc(8 * 512, BF16)]

            def e_phase1(tt):
                i2 = tt % 2
                tsl = slice(tt * 512, (tt + 1) * 512)
                yb3 = r3(ybb[i2], 12)
                gt3 = r3(gtb[i2], 24)
                xt3 = r3(xtb[i2], 8)
                mg3_ = r3(mgb[i2], 8)
                ybk, gtk, xk = "E_yb%d" % i2, "E_gt%d" % i2, "E_xt%d" % i2
                kc.ld(yb3, ybT[:, tsl].rearrange("(c p) t -> p c t", p=128), ybk)
                kc.ld(gt3, gatesT[:, tsl].rearrange("(c p) t -> p c t", p=128), gtk)
                kc.ld(xt3, xsrc[:, tsl].rearrange("(k p) t -> p k t", p=128), xk)
                yield
                for m in range(8):
                    pbs_ = []
                    for n in range(3):
                        bi, pb, pk = kc.bank()
                        for k in range(4):
                            kc.mm(pb[:, :], wbr4[:, n, k, m * 128:(m + 1) * 128], yb3[:, n * 4 + k, :], k == 0, k == 3,
                                  ["E_wbr", ybk], [pk])
                        pbs_.append((pb, pk))
                    kc.tt("dve", ta, pbs_[0][0][:, :], gt3[:, m, :], ALU.mult, [pbs_[0][1], gtk], ["E_ta"])
                    kc.tt("dve", tb, pbs_[1][0][:, :], gt3[:, 8 + m, :], ALU.mult, [pbs_[1][1], gtk], ["E_tb"])
                    kc.tt("dve", tc_, pbs_[2][0][:, :], gt3[:, 16 + m, :], ALU.mult, [pbs_[2][1], gtk], ["E_tc"])
                    kc.tt("pool", ta, ta, tb, ALU.add, ["E_ta", "E_tb"], ["E_ta"])
                    kc.tt("pool", mg3_[:, m, :], ta, tc_, ALU.add, ["E_ta", "E_tc"], [("E_mg%d" % i2, m)])
                    yield

            def e_phase2(tt):
                i2 = tt % 2
                tsl = slice(tt * 512, (tt + 1) * 512)
                xt3 = r3(xtb[i2], 8)
                mg3_ = r3(mgb[i2], 8)
                xk = "E_xt%d" % i2
                for m in range(8):
                    bi, pb, pk = kc.bank()
                    for k in range(8):
                        kc.mm(pb[:, :], wo3[:, k, m * 128:(m + 1) * 128], mg3_[:, k, :], k == 0, k == 7,
                              ["E_wo", ("E_mg%d" % i2, k)], [pk])
                    kc.tt("dve", xt3[:, m, :], xt3[:, m, :], pb[:, :], ALU.add, [xk, pk], [xk])
                    yield
                kc.store(x1s[:, tsl].rearrange("(k p) t -> p k t", p=128), xt3, xk, writes=["x1s"])
                yield

            for _ in e_phase1(0):
                pass
            for tt in range(NT):
                g2 = e_phase2(tt)
                g1 = e_phase1(tt + 1) if tt + 1 < NT else None
                a1 = g1 is not None
                a2 = True
                while a1 or a2:
                    if a1:
                        try:
                            next(g1)
                        except StopIteration:
                            a1 = False
                    if a2:
                        try:
                            next(g2)
                        except StopIteration:
                            a2 = False
            kc.reset()

        if "F" in stages:
            P.cur_tag = "L%dF" % l
            wdn = kc.alloc(22 * 1024, BF16)
            kc.ld(wdn, wdnb[l], "F_wdn", reads=["CAST1" if l == layers[0] else "CAST1b"])
            wdn3 = r3(wdn, 22)
            pgw = kc.alloc(8 * 1024, BF16)
            kc.ld(pgw, pgb[l], "F_pgw", reads=["CAST1" if l == layers[0] else "CAST1b"])
            pgw3 = r3(pgw, 8)
            ppw = kc.alloc(2 * 1024, BF16)
            kc.ld(ppw, ppb[l], "F_ppw", reads=["CAST1" if l == layers[0] else "CAST1b"])
            ppw3 = r3(ppw, 2)
            hal = kc.alloc(44 * 2)
            xtb = [kc.alloc(8 * 512) for _ in range(2)]
            hb = [kc.alloc(8 * 512, BF16) for _ in range(2)]
            rstd1 = kc.alloc(512)
            rstd2 = kc.alloc(512)
            wub = [kc.alloc(8 * 256, BF16) for _ in range(2)]
            rawb = [kc.alloc(514) for _ in range(2)]
            yb_ = [kc.alloc(512) for _ in range(2)]
            gTb = [kc.alloc(22 * 512, BF16) for _ in range(2)]
            ptb_ = [kc.alloc(2 * 512, BF16) for _ in range(2)]
            sg = kc.alloc(512)
            tp = kc.alloc(512)
            last = (l == layers[-1]) and final
            iwc = [0]

            def f_phase1(tt):
                i2 = tt % 2
                tsl = slice(tt * 512, (tt + 1) * 512)
                xt = xtb[i2]
                xt3 = r3(xt, 8)
                xk = "F_xt%d" % i2
                kc.ld(xt3, x1s[:, tsl].rearrange("(k p) t -> p k t", p=128), xk)
                if tt % 4 == 0:
                    kc.memset("pool", hal, 0.0, ["F_hal"])
                h2 = hb[i2]
                h23 = r3(h2, 8)
                HK = [("F_h%d" % i2, k) for k in range(8)]
                rmsnorm_tile(xt, xk, V_NFFN + l * 8, lambda k: h23[:, k, :], HK, h2, "F_sq%d" % i2, rstd1, "F_rstd1",
                             sq_extra=HK)
                yield
                gT3 = r3(gTb[i2], 22)
                for j in range(22):
                    wu = wub[iwc[0] % 2]
                    wuk = "F_wu%d" % (iwc[0] % 2)
                    iwc[0] += 1
                    kc.ld(wu, wupb[l, j], wuk, reads=["CAST1" if l == layers[0] else "CAST1b"])
                    wu3 = r3(wu, 8)
                    for half in range(2):
                        bi, pb, pk = kc.bank()
                        for k in range(8):
                            kc.mm(pb[:, :], wu3[:, k, half * 128:(half + 1) * 128], h23[:, k, :], k == 0, k == 7,
                                  [wuk, HK[k]], [pk])
                        rb = rawb[half]
                        rk = "F_raw%d" % half
                        yb = yb_[half]
                        yk = "F_y%d" % half
                        hi = (j * 2 + half) * 2
                        kc.copy("act", rb[:, 2:514], pb[:, :], [pk], [rk])
                        kc.copy("pool", rb[:, 0:2], hal[:, hi:hi + 2], ["F_hal"], [rk + "h"])
                        kc.copy("pool", hal[:, hi:hi + 2], rb[:, 512:514], [rk], ["F_hal"])
                        cwi = V_FCW + (l * 44 + half * 22 + j) * 3
                        kc.act(yb, pb[:, :], AF.Copy, [pk, "vecs"], [yk], scale=vcol(cwi + 2))
                        kc.sto("dve", yb, rb[:, 0:512], vcol(cwi), yb, ALU.mult, ALU.add, [rk, rk + "h", yk, "vecs"], [yk])
                        kc.sto("dve", yb, rb[:, 1:513], vcol(cwi + 1), yb, ALU.mult, ALU.add, [rk, rk + "h", yk, "vecs"],
                               [yk])
                    kc.act(yb_[0], yb_[0], AF.Gelu_apprx_tanh, ["F_y0"], ["F_y0"])
                    kc.tt("dve", gT3[:, j, :], yb_[0], yb_[1], ALU.mult, ["F_y0", "F_y1"], [("F_g%d" % i2, j)])
                    yield

            def f_phase2(tt):
                i2 = tt % 2
                tsl = slice(tt * 512, (tt + 1) * 512)
                xt = xtb[i2]
                xt3 = r3(xt, 8)
                xk = "F_xt%d" % i2
                h2 = hb[i2]
                h23 = r3(h2, 8)
                HK = [("F_h%d" % i2, k) for k in range(8)]
                gT3 = r3(gTb[i2], 22)
                pt_ = ptb_[i2]
                ptk = "F_pt%d" % i2
                kc.ld(r3(pt_, 2), pT_in[l, :, tsl].rearrange("(k p) t -> p k t", p=128), ptk, q="pool")
                for m in range(8):
                    bi, pb, pk = kc.bank()
                    for j in range(22):
                        kc.mm(pb[:, :], wdn3[:, j, m * 128:(m + 1) * 128], gT3[:, j, :], j == 0, j == 21,
                              ["F_wdn", ("F_g%d" % i2, j)], [pk])
                    kc.tt("dve", xt3[:, m, :], xt3[:, m, :], pb[:, :], ALU.add, [xk, pk], [xk])
                    yield
                rmsnorm_tile(xt, xk, V_NPLE + l * 8, lambda k: h23[:, k, :], HK, h2, "F_sq%d" % i2, rstd2, "F_rstd2",
                             sq_extra=HK)
                yield
                for m in range(8):
                    bi, pb, pk = kc.bank()
                    for k in range(8):
                        kc.mm(pb[:, :], pgw3[:, k, m * 128:(m + 1) * 128], h23[:, k, :], k == 0, k == 7,
                              ["F_pgw", HK[k]], [pk])
                    kc.act(sg, pb[:, :], AF.Sigmoid, [pk], ["F_sg"])
                    bi, pb, pk = kc.bank()
                    for k in range(2):
                        kc.mm(pb[:, :], ppw3[:, k, m * 128:(m + 1) * 128], r3(pt_, 2)[:, k, :], k == 0, k == 1,
                              ["F_ppw", ptk], [pk])
                    kc.tt("dve", tp, pb[:, :], sg, ALU.mult, [pk, "F_sg"], ["F_tp"])
                    kc.tt("pool", xt3[:, m, :], xt3[:, m, :], tp, ALU.add, [xk, "F_tp"], [xk])
                    yield
                if last:
                    OK_ = [("F_ot%d" % i2, k) for k in range(8)]
                    rmsnorm_tile(xt, xk, V_NFIN, lambda k: xt3[:, k, :], OK_, h2, "F_sq%d" % i2, rstd2, "F_rstd2",
                                 sq_extra=HK)
                    kc.finals.append(kc.store(outT[:, tsl].rearrange("(k p) t -> p k t", p=128), xt3, OK_ + [xk],
                                              writes=["outT"], semkey="F_ot_st%d" % i2))
                else:
                    kc.store(xs[:, tsl].rearrange("(k p) t -> p k t", p=128), xt3, xk, writes=["xs"])
                yield

            for _ in f_phase1(0):
                pass
            for tt in range(NT):
                g2 = f_phase2(tt)
                g1 = f_phase1(tt + 1) if tt + 1 < NT else None
                a1 = g1 is not None
                a2 = True
                while a1 or a2:
                    if a1:
                        try:
                            next(g1)
                        except StopIteration:
                            a1 = False
                    if a2:
                        try:
                            next(g2)
                        except StopIteration:
                            a2 = False
            kc.reset()
    nsem = P.emit(final_wait_ops=kc.finals)
    return nc, kc, nsem


def _t5_bucket_np(n):
    n = np.maximum(n, 0)
    exact = 16
    nf = np.maximum(n, 1).astype(np.float32)
    large = exact + (np.log(nf / exact) / math.log(128 / exact) * (32 - exact)).astype(np.int32)
    large = np.minimum(large, 31)
    return np.where(n < exact, n, large)


def host_consts():
    ident = np.eye(128, dtype=np.float32)
    p = np.arange(64)[:, None]
    f = np.arange(64)[None, :]
    U = (p <= f).astype(np.float32)
    mA = np.where(f >= p, 0.0, -BIG).astype(np.float32)
    mB = np.where(p > f, 0.0, -BIG).astype(np.float32)
    SU = (f > p).astype(np.float32)
    tri = np.concatenate([U, mA, mB, SU], axis=1)
    rc = np.zeros((128, 64), np.float32)
    for g in range(4):
        w = 2 << g
        for t in range(16):
            rc[:, g * 16 + t] = 1.0 / min(t + 1, w)
    ind = np.zeros((64, 64, 128), np.float32)
    for r in range(64):
        ind[r, r, :] = 1.0
    ind = ind.reshape(64, 64 * 128)
    return ident, tri, rc, np.concatenate([ind, ind], axis=0)


def host_vecs(inp):
    v = np.zeros((128, NVEC), np.float32)

    def put(base, arr):
        a = np.asarray(arr, np.float32)
        lead = int(np.prod(a.shape[:-1])) if a.ndim > 1 else 1
        a = a.reshape(lead, -1, 128)
        a = a.transpose(2, 0, 1).reshape(128, -1)
        v[:, base:base + a.shape[1]] = a

    put(V_NMIX, inp["norm_mix"])
    put(V_NFFN, inp["norm_ffn"])
    put(V_NPLE, inp["norm_ple"])
    put(V_NFIN, inp["norm_final"])
    put(V_PSCALE, inp["pool_scale"])
    cw = np.asarray(inp["dn_conv"], np.float32).reshape(2, 4, 12, 128).transpose(3, 0, 2, 1).reshape(128, 96)
    v[:, V_CW:V_CW + 96] = cw
    fc = np.asarray(inp["ffn_conv"], np.float32).reshape(2, 3, 44, 128).transpose(3, 0, 2, 1).reshape(128, 264)
    v[:, V_FCW:V_FCW + 264] = fc
    v[:, V_DNW:V_DNW + 2] = np.asarray(inp["dn_norm"], np.float32).T
    v[:, V_EPS] = EPS
    v[:, V_ONE] = 1.0
    return v


def host_band(rel_bias):
    rb = np.asarray(rel_bias, np.float32)
    p = np.arange(128)[:, None]
    c = np.arange(1152)[None, :]
    n = c - 384 - p
    bucket = _t5_bucket_np(n)
    band = np.empty((128, 8, 1152), np.float32)
    for h in range(8):
        band[:, h, :] = np.where(n >= 0, rb[bucket, h], -BIG)
    return band.reshape(128, 8 * 1152)


_CACHE = {}


def get_program(NSEQ=2, **kw):
    key = (NSEQ, tuple(sorted(kw.items())))
    if key not in _CACHE:
        _CACHE[key] = build(NSEQ=NSEQ, **kw)
    return _CACHE[key]


def make_in_maps(inp, NSEQ, ncores):
    ident, tri, rc, ind = host_consts()
    vecs = host_vecs(inp)
    band = host_band(inp["rel_bias"])
    a4 = np.stack([np.asarray(inp["dn_a_log"], np.float32)[0], np.asarray(inp["dn_dt_bias"], np.float32)[0],
                   np.asarray(inp["dn_a_log"], np.float32)[1], np.asarray(inp["dn_dt_bias"], np.float32)[1]], axis=1)
    x = np.asarray(inp["x"], np.float32)
    p = np.asarray(inp["p"], np.float32)
    shared = {
        "w_in": np.ascontiguousarray(inp["w_in"], np.float32),
        "w_branch": np.ascontiguousarray(inp["w_branch"], np.float32),
        "w_out": np.ascontiguousarray(inp["w_out"], np.float32),
        "ffn_up": np.ascontiguousarray(inp["ffn_up"], np.float32),
        "ffn_down": np.ascontiguousarray(inp["ffn_down"], np.float32),
        "ple_gate": np.ascontiguousarray(inp["ple_gate"], np.float32),
        "ple_proj": np.ascontiguousarray(inp["ple_proj"], np.float32),
        "pool_w": np.ascontiguousarray(inp["pool_w"], np.float32),
        "vecs": vecs, "ident": ident, "tri": tri, "rcnt": rc, "band": band, "ind": ind,
        "a4": np.ascontiguousarray(a4),
    }
    maps = []
    for c in range(ncores):
        xs_ = x[c * NSEQ:(c + 1) * NSEQ].reshape(NSEQ * S, D)
        ps_ = p[:, c * NSEQ:(c + 1) * NSEQ].reshape(DEPTH, NSEQ * S, 256)
        m = dict(shared)
        m["xT"] = np.ascontiguousarray(xs_.T)
        m["pT"] = np.ascontiguousarray(ps_.transpose(0, 2, 1))
        maps.append(m)
    return maps


def kernel(**inputs):
    NSEQ = 2
    nc, kc, _ = get_program(NSEQ=NSEQ)
    maps = make_in_maps(inputs, NSEQ, NCORES)
    res = run_bass_kernel_spmd(nc, maps, core_ids=list(range(NCORES)))
    outs = []
    for c in range(NCORES):
        oT = np.asarray(res.results[c]["outT"], np.float32)
        outs.append(oT.T.reshape(NSEQ, S, D))
    return np.concatenate(outs, axis=0).astype(np.float32)
```

```python
import contextlib
import math
import numpy as np
import concourse.bass as bass
import concourse.mybir as mybir
from concourse.bass_utils import run_bass_kernel_spmd

F32 = mybir.dt.float32
BF16 = mybir.dt.bfloat16
AF = mybir.ActivationFunctionType
ALU = mybir.AluOpType
AX = mybir.AxisListType

D = 1024
S = 2048
DEPTH = 2
IN_COLS = 7176
FFN = 2816
EPS = 1e-6
NCORES = 8
BIG = 30000.0


class Op:
    __slots__ = ("eng", "fn", "deps", "sem", "inc", "val", "signal", "is_dma", "tag")

    def __init__(self, eng, fn, is_dma=False):
        self.eng = eng
        self.fn = fn
        self.deps = []
        self.sem = None
        self.inc = 1
        self.val = None
        self.signal = False
        self.is_dma = is_dma


class Prog:
    ENGS = ("pe", "act", "dve", "pool", "sp")

    def __init__(self, nc):
        self.nc = nc
        self.ops = {e: [] for e in self.ENGS}
        self.last_w = {}
        self.readers = {}
        self.n_ops = 0
        self.last_dma = {}
        self.bar = {e: None for e in self.ENGS}
        self.cur_tag = "pre"
        self.scopes = False

    def barrier(self):
        deps = []
        for e in self.ENGS:
            for o in reversed(self.ops[e]):
                if not o.is_dma:
                    deps.append(o)
                    break
        deps.extend(self.last_dma.values())
        for e in self.ENGS:
            self.bar[e] = deps
        self.last_w = {}
        self.readers = {}

    def _add(self, op, reads, writes):
        deps = []
        for k in reads:
            w = self.last_w.get(k)
            if w is not None:
                deps.append((w, False))
        for k in writes:
            w = self.last_w.get(k)
            if w is not None:
                deps.append((w, False))
            for r in self.readers.get(k, ()):
                deps.append((r, True))
        if self.bar[op.eng] is not None:
            for d in self.bar[op.eng]:
                deps.append((d, False))
            self.bar[op.eng] = None
        seen = set()
        for d, war in deps:
            if d is op or id(d) in seen:
                continue
            if d.eng == op.eng and not d.is_dma and not op.is_dma:
                if op.eng == "pe" or war:
                    continue
            seen.add(id(d))
            op.deps.append(d)
            d.signal = True
        for k in reads:
            self.readers.setdefault(k, []).append(op)
        for k in writes:
            self.last_w[k] = op
            self.readers[k] = []
        op.tag = self.cur_tag
        self.ops[op.eng].append(op)
        self.n_ops += 1

    @staticmethod
    def _is_psum(k):
        return isinstance(k, str) and k.startswith("ps") and k[2:].isdigit()

    def op(self, eng, fn, reads=(), writes=()):
        o = Op(eng, fn)
        o.sem = ("eng", eng)
        ex = [k for k in reads if self._is_psum(k)]
        if ex:
            reads = [k for k in reads if not self._is_psum(k)]
            writes = list(writes) + ex
        self._add(o, reads, writes)
        return o

    def dma(self, fn, semkey, reads=(), writes=(), q="sp"):
        o = Op(q, fn, is_dma=True)
        o.sem = ("dma", semkey)
        o.inc = 16
        o.signal = True
        self._add(o, reads, writes)
        self.last_dma[semkey] = o
        return o

    def emit(self, final_wait_ops=()):
        nc = self.nc
        counts = {}
        for e in self.ENGS:
            for o in self.ops[e]:
                if o.signal:
                    c = counts.get(o.sem, 0) + o.inc
                    counts[o.sem] = c
                    o.val = c
        semkeys = list(counts.keys())
        with contextlib.ExitStack() as st:
            sems = {}
            for i, k in enumerate(semkeys):
                sems[k] = st.enter_context(nc.semaphore("s%d" % i))
            block = st.enter_context(nc.Block())
            engmap = {"pe": block.tensor, "act": block.scalar, "dve": block.vector,
                      "pool": block.gpsimd, "sp": block.sync}

            def make(e):
                oplist = self.ops[e]

                def body(eng):
                    waited = {}
                    cur = None
                    cm = None
                    for o in oplist:
                        if self.scopes and o.tag != cur:
                            if cm is not None:
                                cm.__exit__(None, None, None)
                            cm = nc.named_scope(o.tag)
                            cm.__enter__()
                            cur = o.tag
                        for d in o.deps:
                            if waited.get(d.sem, 0) >= d.val:
                                continue
                            eng.wait_ge(sems[d.sem], d.val)
                            waited[d.sem] = d.val
                        ins = o.fn(eng)
                        if o.signal:
                            ins.then_inc(sems[o.sem], o.inc)
                    if cm is not None:
                        cm.__exit__(None, None, None)
                    if e == "sp":
                        for o in final_wait_ops:
                            if waited.get(o.sem, 0) >= o.val:
                                continue
                            eng.wait_ge(sems[o.sem], o.val)
                            waited[o.sem] = o.val

                return body

            for e in self.ENGS:
                if self.ops[e] or e == "sp":
                    engmap[e](make(e))
        return len(semkeys)


ARENA_F32 = 47 * 1024 + 512


import os as _os
CSTOP = int(_os.environ.get("C_STOP", "99"))
DSTOP = int(_os.environ.get("D_STOP", "99"))


class _Stop(Exception):
    pass


class KC:
    def __init__(self, nc, NSEQ, debug):
        self.nc = nc
        self.P = Prog(nc)
        self.NSEQ = NSEQ
        self.T = NSEQ * S
        self.NT = self.T // 512
        self.debug = debug
        self.st = contextlib.ExitStack()
        self.arena = self.st.enter_context(nc.sbuf_tensor("arena", [128, ARENA_F32], F32))
        self.arena_bf = self.arena[:, :].bitcast(BF16)
        self.psb = [self.st.enter_context(nc.psum_tensor("psb%d" % i, [128, 512], F32)) for i in range(8)]
        self.bump = 0
        self.perm = 0
        self.rr = 0
        self.finals = []
        self.dram = {}
        self.uid = 0

    def alloc(self, cols, dt=F32):
        if dt == BF16:
            w = (cols + 1) // 2
            a = self.bump
            self.bump += w
            assert self.bump <= ARENA_F32, "SBUF arena overflow %d" % self.bump
            return self.arena_bf[:, 2 * a:2 * a + cols]
        a = self.bump
        self.bump += cols
        assert self.bump <= ARENA_F32, "SBUF arena overflow %d" % self.bump
        return self.arena[:, a:a + cols]

    def make_perm(self):
        self.perm = self.bump

    def reset(self):
        self.P.barrier()
        self.bump = self.perm

    def key(self, base):
        self.uid += 1
        return "%s#%d" % (base, self.uid)

    def din(self, name, shape, dt=F32):
        t = self.nc.dram_tensor(name, list(shape), dt, kind="ExternalInput").ap()
        self.dram[name] = t
        return t

    def dscr(self, name, shape, dt=F32, out=False):
        kind = "ExternalOutput" if (out or self.debug) else "Internal"
        t = self.nc.dram_tensor(name, list(shape), dt, kind=kind).ap()
        self.dram[name] = t
        return t

    def ld(self, dst, src, key, reads=(), q="sp", semkey=None, slow=False):
        if slow:
            return self.P.dma(lambda e: e.dma_start(out=dst, in_=src, allow_slow_non_contiguous=True), semkey or key,
                              reads=reads, writes=[key], q=q)
        return self.P.dma(lambda e: e.dma_start(out=dst, in_=src), semkey or key, reads=reads, writes=[key], q=q)

    def stt(self, dst, src, srckey, writes=(), q="sp", semkey=None):
        keys = list(srckey) if isinstance(srckey, list) else [srckey]
        sk = semkey or (str(keys[0]) + "_st")
        return self.P.dma(lambda e: e.dma_start(out=dst, in_=src), sk, reads=keys, writes=writes, q=q)

    def dump(self, name, ap, keys, dt=F32):
        if not self.debug:
            return
        shape = list(ap.shape)
        d = self.nc.dram_tensor("dbg_" + name, shape, dt, kind="ExternalOutput").ap()
        self.finals.append(self.P.dma(lambda e: e.dma_start(out=d, in_=ap), "dump_" + name, reads=list(keys)))

    def store(self, dst, src, srckey, writes=(), q="sp", semkey=None):
        return self.stt(dst, src, srckey, writes, q, semkey)

    def act(self, out, in_, func, reads, writes, bias=None, scale=None, accum=None):
        kw = {}
        if bias is not None:
            kw["bias"] = bias
        if scale is not None:
            kw["scale"] = scale
        if accum is not None:
            kw["accum_out"] = accum
        return self.P.op("act", lambda e: e.activation(out=out, in_=in_, func=func, **kw), reads=reads, writes=writes)

    def tt(self, eng, out, in0, in1, op, reads, writes):
        return self.P.op(eng, lambda e: e.tensor_tensor(out, in0, in1, op), reads=reads, writes=writes)

    def ts(self, eng, out, in0, s1, s2, op0, op1, reads, writes):
        if s2 is None:
            return self.P.op(eng, lambda e: e.tensor_scalar(out, in0, s1, None, op0), reads=reads, writes=writes)
        return self.P.op(eng, lambda e: e.tensor_scalar(out, in0, s1, s2, op0, op1), reads=reads, writes=writes)

    def sto(self, eng, out, in0, scalar, in1, op0, op1, reads, writes):
        return self.P.op(eng, lambda e: e.scalar_tensor_tensor(out=out, in0=in0, scalar=scalar, in1=in1, op0=op0,
                                                               op1=op1), reads=reads, writes=writes)

    def copy(self, eng, out, in_, reads, writes):
        if eng == "act":
            return self.P.op("act", lambda e: e.copy(out, in_), reads=reads, writes=writes)
        return self.P.op(eng, lambda e: e.tensor_copy(out, in_), reads=reads, writes=writes)

    def memset(self, eng, ap, val, writes):
        return self.P.op(eng, lambda e: e.memset(ap, val), writes=writes)

    def recip(self, out, in_, reads, writes):
        return self.P.op("dve", lambda e: e.reciprocal(out, in_), reads=reads, writes=writes)

    def transpose(self, out, in_, ident, reads, writes):
        return self.P.op("pe", lambda e: e.transpose(out, in_, ident), reads=reads, writes=writes)

    def mm(self, out, lhsT, rhs, start, stop, reads, writes):
        return self.P.op("pe", lambda e: e.matmul(out, lhsT, rhs, start=start, stop=stop), reads=reads, writes=writes)

    def bank(self, n=8, base=0):
        i = base + (self.rr % n)
        self.rr += 1
        return i, self.psb[i], "ps%d" % i


def r3(ap, a):
    return ap.rearrange("p (a b) -> p a b", a=a)


V_NMIX = 0
V_NFFN = 16
V_NPLE = 32
V_NFIN = 48
V_PSCALE = 56
V_CW = 64
V_FCW = 160
V_DNW = 424
V_EPS = 426
V_ONE = 427
NVEC = 428
WIN_GROUP_C0 = [0, 512, 1024, 1536, 2048, 2568, 3080, 3592] + [4104 + 512 * i for i in range(6)]
WIN_GROUP_DST = [("u", None, 0), ("dqkv", None, 0), ("dqkv", None, 512), ("dqkv", None, 1024), ("dz", None, 0),
                 ("mq", None, 0), ("mk", None, 0), ("mv", None, 0)] + [("gate", None, 512 * i) for i in range(6)]

C_POOL = 0
C_DQKV = 512
C_DZ = 2048
C_DB = 2560
C_DA = 2564
C_MQ = 2568
C_MK = 2568 + 512
C_MV = 2568 + 1024
C_GATE = 4104


def build(NSEQ=2, debug=False, layers=(0, 1), stages="ABCDEF", final=True, scopes=False):
    nc = bass.Bass("TRN2", target_bir_lowering=False)
    kc = KC(nc, NSEQ, debug)
    P = kc.P
    P.scopes = scopes
    T = kc.T
    NT = kc.NT
    xT_in = kc.din("xT", [D, T])
    pT_in = kc.din("pT", [DEPTH, 256, T])
    w_in = kc.din("w_in", [DEPTH, D, IN_COLS])
    w_branch = kc.din("w_branch", [DEPTH, 3, 512, D])
    w_out = kc.din("w_out", [DEPTH, D, D])
    ffn_up = kc.din("ffn_up", [DEPTH, D, 2 * FFN])
    ffn_down = kc.din("ffn_down", [DEPTH, FFN, D])
    ple_gate = kc.din("ple_gate", [DEPTH, D, D])
    ple_proj = kc.din("ple_proj", [DEPTH, 256, D])
    pool_w = kc.din("pool_w", [DEPTH, 4, 128, 128])
    vecs_d = kc.din("vecs", [128, NVEC])
    ident_d = kc.din("ident", [128, 128])
    tri_d = kc.din("tri", [64, 4 * 64])
    rcnt_d = kc.din("rcnt", [128, 64])
    band_d = kc.din("band", [128, 8 * 1152])
    ind_d = kc.din("ind", [128, 64 * 128])
    a4_d = kc.din("a4", [4, 4])
    outT = kc.dscr("outT", [D, T], out=True)
    xs = kc.dscr("xs", [D, T])
    x1s = kc.dscr("x1s", [D, T])
    uT = kc.dscr("uT", [512, T])
    dqkvT = kc.dscr("dqkvT", [1536, T])
    dzT = kc.dscr("dzT", [512, T])
    dbg = kc.dscr("dbg", [8, T])
    mqT = kc.dscr("mqT", [512, T], BF16)
    mkT = kc.dscr("mkT", [512, T], BF16)
    mv = kc.dscr("mv", [T, 512], BF16)
    gatesT = kc.dscr("gatesT", [3072, T], BF16)
    ybT = kc.dscr("ybT", [1536, T], BF16)
    NG_IN = 15
    winb = kc.dscr("winb", [DEPTH, 15, 128, 8 * 512], BF16)
    wbrb = kc.dscr("wbrb", [DEPTH, 128, 12 * 1024], BF16)
    woutb = kc.dscr("woutb", [DEPTH, 128, 8 * 1024], BF16)
    wupb = kc.dscr("wupb", [DEPTH, 22, 128, 8 * 256], BF16)
    wdnb = kc.dscr("wdnb", [DEPTH, 128, 22 * 1024], BF16)
    pgb = kc.dscr("pgb", [DEPTH, 128, 8 * 1024], BF16)
    ppb = kc.dscr("ppb", [DEPTH, 128, 2 * 1024], BF16)
    pwb = kc.dscr("pwb", [DEPTH, 128, 4 * 128], BF16)

    vecs = kc.alloc(NVEC)
    ident = kc.alloc(128)
    ones_bf = kc.alloc(128, BF16)
    ones_f = kc.alloc(128)
    kc.ld(vecs, vecs_d, "vecs")
    kc.ld(ident, ident_d, "ident")
    P.op("pool", lambda e: e.memset(ones_bf, 1.0), writes=["ones_bf"])
    P.op("pool", lambda e: e.memset(ones_f, 1.0), writes=["ones_f"])
    kc.make_perm()
    CONST = ["vecs", "ident", "ones_bf", "ones_f"]

    def cast(dst, src, key):
        grp = ("CAST0" if key[0] == "winb" else "CAST1") + ("" if key[1] == layers[0] else "b")
        if key[0] == "winb" and key[1] == layers[0]:
            grp = "CAST0_%d" % key[2]
        P.dma(lambda e: e.dma_start(out=dst, in_=src), grp, writes=[grp], q="pool")

    for l in layers:
        for g in range(14):
            c0 = WIN_GROUP_C0[g]
            cast(winb[l, g].rearrange("p (k c) -> p k c", k=8),
                 w_in[l, :, c0:c0 + 512].rearrange("(k p) c -> p k c", p=128), ("winb", l, g))
        cast(winb[l, 14].rearrange("p (k c) -> p k c", k=8)[:, :, 0:8],
             w_in[l, :, C_DB:C_DB + 8].rearrange("(k p) c -> p k c", p=128), ("winb", l, 14))
        cast(pwb[l].rearrange("p (g d) -> p g d", g=4), pool_w[l].rearrange("g c d -> c g d"), ("pwb", l))
        for n in range(3):
            cast(wbrb[l].rearrange("p (n k d) -> p n k d", n=3, k=4)[:, n],
                 w_branch[l, n].rearrange("(k p) d -> p k d", p=128), ("wbrb", l, n))
        cast(woutb[l].rearrange("p (k d) -> p k d", k=8), w_out[l].rearrange("(k p) d -> p k d", p=128), ("woutb", l))
        for j in range(22):
            dstv = wupb[l, j].rearrange("p (k c) -> p k c", k=8)
            cast(dstv[:, :, 0:128], ffn_up[l, :, j * 128:(j + 1) * 128].rearrange("(k p) c -> p k c", p=128),
                 ("wupb", l, j, 0))
            cast(dstv[:, :, 128:256],
                 ffn_up[l, :, FFN + j * 128:FFN + (j + 1) * 128].rearrange("(k p) c -> p k c", p=128),
                 ("wupb", l, j, 1))
        cast(wdnb[l].rearrange("p (j d) -> p j d", j=22), ffn_down[l].rearrange("(j p) d -> p j d", p=128),
             ("wdnb", l))
        cast(pgb[l].rearrange("p (k d) -> p k d", k=8), ple_gate[l].rearrange("(k p) d -> p k d", p=128), ("pgb", l))
        cast(ppb[l].rearrange("p (k d) -> p k d", k=2), ple_proj[l].rearrange("(k p) d -> p k d", p=128), ("ppb", l))

    def vcol(i):
        return vecs[:, i:i + 1]

    def rmsnorm_tile(xt, xkey, nbase, out_fn, outkeys, sqt, sqkey, rstd, rkey, engs=("dve",), sq_extra=()):
        P.op("act", lambda e: e.activation(out=sqt, in_=xt, func=AF.Square), reads=[xkey],
             writes=[sqkey] + list(sq_extra))
        bi, pb, pk = kc.bank()
        for k in range(8):
            kc.mm(pb[:, :], ones_bf, sqt[:, k * 512:(k + 1) * 512], k == 0, k == 7, [sqkey, "ones_bf"], [pk])
        P.op("act", lambda e: e.activation(out=rstd, in_=pb[:, :], func=AF.Ln, bias=vcol(V_EPS), scale=1.0 / D),
             reads=[pk, "vecs"], writes=[rkey])
        P.op("act", lambda e: e.activation(out=rstd, in_=rstd, func=AF.Exp, scale=-0.5), reads=[rkey], writes=[rkey])
        for k in range(8):
            o = out_fn(k)
            eng = engs[k % len(engs)]
            P.op(eng, (lambda o=o, k=k: lambda e: e.scalar_tensor_tensor(
                out=o, in0=xt[:, k * 512:(k + 1) * 512], scalar=vcol(nbase + k), in1=rstd,
                op0=ALU.mult, op1=ALU.mult))(), reads=[xkey, rkey, "vecs"], writes=[outkeys[k]])

    stages0 = stages
    for l in layers:
        stages = stages0 if l == layers[0] else _os.environ.get("L1S", stages0)
        xsrc = xT_in if l == 0 else xs
        xsrc_key = "xs"
        if "A" in stages:
            P.cur_tag = "L%dA" % l
            hT = kc.alloc(8 * T, BF16)
            hT3 = r3(hT, 8)
            xtb = [kc.alloc(8 * 512) for _ in range(2)]
            sqt = kc.alloc(8 * 512, BF16)
            rstd = kc.alloc(512)
            for tt in range(NT):
                xt = xtb[tt % 2]
                xk = "A_xt%d" % (tt % 2)
                kc.ld(r3(xt, 8), xsrc[:, tt * 512:(tt + 1) * 512].rearrange("(k p) t -> p k t", p=128), xk,
                      reads=[xsrc_key])
                rmsnorm_tile(xt, xk, V_NMIX + l * 8, lambda k: hT3[:, k, tt * 512:(tt + 1) * 512],
                             [("hT", tt, k) for k in range(8)], sqt, "A_sq", rstd, "A_rstd")
            HT_ALL = [("hT", tt) for tt in range(NT)]
            wgb = [kc.alloc(8 * 512, BF16) for _ in range(2)]
            ob32 = [kc.alloc(T) for _ in range(2)]
            ob16 = [kc.alloc(T, BF16) for _ in range(2)]
            ovb = [kc.alloc(512, BF16) for _ in range(2)]
            cnt = {"o32": 0, "o16": 0, "ov": 0, "ev": 0}

            def evac(kind, dst, src, pk, okey):
                if kind in ("u", "dqkv"):
                    eng = "act" if cnt["ev"] % 2 == 0 else "dve"
                    cnt["ev"] += 1
                    if eng == "act":
                        P.op("act", lambda e: e.copy(dst, src), reads=[pk], writes=[okey])
                    else:
                        P.op("dve", lambda e: e.tensor_copy(dst, src), reads=[pk], writes=[okey])
                elif kind == "dz":
                    P.op("act", lambda e: e.activation(out=dst, in_=src, func=AF.Silu), reads=[pk], writes=[okey])
                elif kind == "mq":
                    P.op("dve", lambda e: e.tensor_scalar(dst, src, 0.125, None, ALU.mult), reads=[pk], writes=[okey])
                elif kind == "mk":
                    P.op("dve", lambda e: e.tensor_copy(dst, src), reads=[pk], writes=[okey])
                elif kind == "gate":
                    P.op("act", lambda e: e.activation(out=dst, in_=src, func=AF.Sigmoid), reads=[pk], writes=[okey])
                else:
                    raise ValueError(kind)

            for g in range(14):
                wg = wgb[g % 2]
                wk = "A_wg%d" % (g % 2)
                kc.ld(wg, winb[l, g], wk, reads=["CAST0b"] if l != layers[0] else ["CAST0_%d" % g])
                wg3 = r3(wg, 8)
                gkind, gdst, grow0 = WIN_GROUP_DST[g]
                if gkind == "mv":
                    for i in range(T // 128):
                        bi, pb, pk = kc.bank()
                        for k in range(8):
                            kc.mm(pb[:, :], hT3[:, k, i * 128:(i + 1) * 128], wg3[:, k, :], k == 0, k == 7,
                                  [("hT", i // 4, k), wk], [pk])
                        ov = ovb[cnt["ov"] % 2]
                        ok = "A_ov%d" % (cnt["ov"] % 2)
                        cnt["ov"] += 1
                        eng = "act" if i % 2 == 0 else "dve"
                        if eng == "act":
                            P.op("act", (lambda ov=ov, pb=pb: lambda e: e.copy(ov, pb[:, :]))(), reads=[pk], writes=[ok])
                        else:
                            P.op("dve", (lambda ov=ov, pb=pb: lambda e: e.tensor_copy(ov, pb[:, :]))(), reads=[pk],
                                 writes=[ok])
                        kc.stt(mv[i * 128:(i + 1) * 128, :], ov, ok, writes=["mv"])
                    continue
                for j in range(4):
                    is16 = gkind in ("mq", "mk", "gate")
                    if is16:
                        ob = ob16[cnt["o16"] % 2]
                        okey = "A_o16_%d" % (cnt["o16"] % 2)
                        cnt["o16"] += 1
                    else:
                        ob = ob32[cnt["o32"] % 2]
                        okey = "A_o32_%d" % (cnt["o32"] % 2)
                        cnt["o32"] += 1
                    for tt in range(NT):
                        bi, pb, pk = kc.bank()
                        for k in range(8):
                            kc.mm(pb[:, :], wg3[:, k, j * 128:(j + 1) * 128], hT3[:, k, tt * 512:(tt + 1) * 512],
                                  k == 0, k == 7, [("hT", tt, k), wk], [pk])
                        evac(gkind, ob[:, tt * 512:(tt + 1) * 512], pb[:, :], pk, (okey, tt))
                    dst = {"u": uT, "dqkv": dqkvT, "dz": dzT, "mq": mqT, "mk": mkT, "gate": gatesT}[gkind]
                    r0 = grow0 + j * 128
                    kc.stt(dst[r0:r0 + 128, :], ob, [(okey, tt) for tt in range(NT)], writes=[gkind + "_d"], semkey=okey + "_st")
            wg = wgb[0]
            wk = "A_wg0"
            kc.ld(wg, winb[l, 14], wk, reads=["CAST0b"] if l != layers[0] else ["CAST0_14"])
            wg3 = r3(wg, 8)
            a4 = kc.alloc(4)
            P_a4 = kc.ld(a4[0:4, :], a4_d, "A_a4")
            nexpA = kc.alloc(1)
            kc.act(nexpA[0:4, :], a4[0:4, 2 * l:2 * l + 1], AF.Exp, ["A_a4"], ["A_nexpA"])
            kc.ts("dve", nexpA[0:4, :], nexpA[0:4, :], -1.0, None, ALU.mult, None, ["A_nexpA"], ["A_nexpA"])
            obb = ob32[0]
            oba = ob32[1]
            for tt in range(NT):
                sl = slice(tt * 512, (tt + 1) * 512)
                bi, pb, pk = kc.bank()
                for k in range(8):
                    kc.mm(pb[0:4, :], wg3[:, k, 0:4], hT3[:, k, sl], k == 0, k == 7, [("hT", tt, k), wk], [pk])
                kc.act(obb[0:4, sl], pb[0:4, :], AF.Sigmoid, [pk], [("A_o32_0", tt)])
                bi, pb, pk = kc.bank()
                for k in range(8):
                    kc.mm(pb[0:4, :], wg3[:, k, 4:8], hT3[:, k, sl], k == 0, k == 7, [("hT", tt, k), wk], [pk])
                kc.act(oba[0:4, sl], pb[0:4, :], AF.Exp, [pk, "A_a4"], [("A_o32_1", tt)],
                       bias=a4[0:4, 2 * l + 1:2 * l + 2])
            AK1 = [("A_o32_1", tt) for tt in range(NT)]
            kc.act(oba[0:4, :], oba[0:4, :], AF.Ln, AK1 + ["vecs"], AK1, bias=vcol(V_ONE)[0:4, :])
            kc.ts("dve", oba[0:4, :], oba[0:4, :], nexpA[0:4, 0:1], None, ALU.mult, None, AK1 + ["A_nexpA"], AK1)
            kc.stt(dbg[0:4, :], obb[0:4, :], [("A_o32_0", tt) for tt in range(NT)], writes=["dbg"], semkey="A_o32_0_st")
            kc.stt(dbg[4:8, :], oba[0:4, :], [("A_o32_1", tt) for tt in range(NT)], writes=["dbg"], semkey="A_o32_1_st")
            kc.reset()
        if "B" in stages:
            P.cur_tag = "L%dB" % l
            pw = kc.alloc(4 * 128, BF16)
            kc.ld(pw, pwb[l], "B_pw", reads=["CAST1" if l == layers[0] else "CAST1b"])
            pw3 = r3(pw, 4)
            rc = kc.alloc(64)
            kc.ld(rc, rcnt_d, "B_rc")
            ub = [kc.alloc(16 + S) for _ in range(2)]
            sab = [[kc.alloc(16 + S) for _ in range(2)] for _ in range(2)]
            mxb = [kc.alloc(S, BF16) for _ in range(2)]
            t16b = [kc.alloc(16) for _ in range(2)]
            yob = [kc.alloc(S, BF16) for _ in range(2)]
            for i in range(2):
                kc.memset("pool", ub[i][:, 0:16], 0.0, ["B_u%dz" % i])
                for j in range(2):
                    kc.memset("pool", sab[i][j][:, 0:16], 0.0, ["B_s%d_%dz" % (i, j)])

            def b_iter(it, s_, g):
                i2 = it % 2
                u = ub[i2]
                uk = "B_u%d" % i2
                kc.ld(u[:, 16:], uT[g * 128:(g + 1) * 128, s_ * S:(s_ + 1) * S], uk)
                yield
                cur, curk = u, uk
                for j in range(g + 1):
                    dst = sab[i2][j % 2]
                    dk = "B_s%d_%d" % (i2, j % 2)
                    sh = 1 << j
                    kc.tt("dve" if (j + i2) % 2 == 0 else "pool", dst[:, 16:], cur[:, 16:], cur[:, 16 - sh:16 - sh + S],
                          ALU.add, [curk, curk + "z"], [dk])
                    cur, curk = dst, dk
                    yield
                w = 1 << (g + 1)
                m = mxb[i2]
                mk_ = "B_mx%d" % i2
                t16 = t16b[i2]
                kc.sto("dve", m, cur[:, 16:], 1.0 / w, u[:, 16:], ALU.mult, ALU.subtract, [curk, uk], [mk_])
                kc.tt("dve", t16, cur[:, 16:32], rc[:, g * 16:(g + 1) * 16], ALU.mult, [curk, "B_rc"], ["B_t16_%d" % i2])
                kc.tt("dve", m[:, 0:16], t16, u[:, 16:32], ALU.subtract, ["B_t16_%d" % i2, uk, mk_], [mk_])
                yield
                y = yob[i2]
                yk = "B_y%d" % i2
                for j in range(4):
                    bi, pb, pk = kc.bank()
                    kc.mm(pb[:, :], pw3[:, g, :], m[:, j * 512:(j + 1) * 512], True, True, [mk_, "B_pw"], [pk])
                    kc.act(y[:, j * 512:(j + 1) * 512], pb[:, :], AF.Copy, [pk, "vecs"], [yk],
                           scale=vcol(V_PSCALE + l * 4 + g))
                    yield
                kc.store(ybT[g * 128:(g + 1) * 128, s_ * S:(s_ + 1) * S], y, yk, writes=["ybT"])
                yield

            gens = [b_iter(i_, s_, g) for i_, (s_, g) in enumerate([(s_, g) for s_ in range(NSEQ) for g in range(4)])]
            active = []
            nxt = 0
            while active or nxt < len(gens):
                while len(active) < 2 and nxt < len(gens):
                    active.append(gens[nxt])
                    nxt += 1
                for gnr in list(active):
                    try:
                        next(gnr)
                    except StopIteration:
                        active.remove(gnr)
            kc.reset()

        if "C" in stages:
            P.cur_tag = "L%dC" % l
            tri = kc.alloc(256)
            kc.ld(tri[0:64, :], tri_d, "C_tri")
            Ut = tri[0:64, 0:64]
            mA = tri[0:64, 64:128]
            mB = tri[0:64, 128:192]
            SU = tri[0:64, 192:256]
            identb3 = ident[0:64, 0:64].rearrange("p (o j) -> p o j", o=1).to_broadcast([64, 8, 64])
            raw = kc.alloc(3 + S)
            Xb = raw[:, 3:3 + S]
            acc = kc.alloc(S)
            sqb = kc.alloc(S, BF16)
            rn = kc.alloc(512)
            khb = kc.alloc(S, BF16)
            qhb = kc.alloc(S, BF16)
            gcrow = kc.alloc(S)
            E1 = kc.alloc(S)
            brow = kc.alloc(S)
            gT = kc.alloc(32)
            bT = kc.alloc(32)
            gcc = kc.alloc(32)
            egd = kc.alloc(32)
            Dm = kc.alloc(512)
            Gb = kc.alloc(512)
            GTi = kc.alloc(512)
            GTb = kc.alloc(512)
            tmpD = kc.alloc(512)
            Qk = [kc.alloc(512) for _ in range(2)]
            Rk = [kc.alloc(512) for _ in range(2)]
            Gk = [kc.alloc(512) for _ in range(2)]
            SETS = []
            for i_ in range(2):
                SETS.append(dict(i=i_, keT=kc.alloc(S), qdT=kc.alloc(S), kdec=kc.alloc(32 * 128, BF16),
                                 vtok=kc.alloc(32 * 128, BF16), aqkT=kc.alloc(S, BF16), TTb=kc.alloc(S, BF16),
                                 egl=kc.alloc(32)))
            oT = kc.alloc(S)
            S_ = kc.alloc(128)
            Rb = kc.alloc(128, BF16)
            vn = kc.alloc(128, BF16)
            zsb = [kc.alloc(512) for _ in range(2)]
            yout = kc.alloc(S, BF16)
            sqb2 = kc.alloc(S, BF16)
            rn2 = kc.alloc(512)
            kc.memset("pool", raw[:, 0:3], 0.0, ["C_rawz"])
            identP = kc.alloc(64)
            bTe = kc.alloc(16)
            bTo = kc.alloc(16)
            kc.copy("dve", identP[0:64, :], ident[0:64, 0:64], ["ident"], ["C_identP"])
            kc.copy("dve", identP[64:128, :], ident[64:128, 64:128], ["ident"], ["C_identP"])

            def bc_mid(ap64, n):
                return ap64.rearrange("p (o j) -> p o j", o=1).to_broadcast([64, n, 64])

            def bc_last(ap, n, w):
                return ap.rearrange("p (n o) -> p n o", o=1).to_broadcast([64, n, w])

            def phaseA(s_, h, st):
                t0 = s_ * S
                si = st["i"]
                kK = lambda nm: "C_%s%d" % (nm, si)
                keT, qdT, kdec, vtok, aqkT, TTb, egl = (st[k] for k in ("keT", "qdT", "kdec", "vtok", "aqkT", "TTb", "egl"))
                kc.ld(gcrow[0:64, :], dbg[4 + h:5 + h, t0:t0 + S].partition_broadcast(64), "C_gcrow")
                kc.ld(brow[0:64, :], dbg[h:h + 1, t0:t0 + S].partition_broadcast(64), "C_brow")
                idb32 = ident[0:64, 0:64].rearrange("p (o j) -> p o j", o=1).to_broadcast([64, 32, 64])
                kc.tt("dve", r3(Xb[0:64, :], 32), r3(gcrow[0:64, :], 32), idb32, ALU.mult, ["C_gcrow", "ident"], ["C_raw"])
                P.op("dve", (lambda: lambda e: e.tensor_reduce(gT[0:64, :], r3(Xb[0:64, :], 32), AX.X, ALU.add))(),
                     reads=["C_raw"], writes=["C_gT"])
                kc.tt("dve", r3(Xb[0:64, :], 32), r3(brow[0:64, :], 32), idb32, ALU.mult, ["C_brow", "ident"], ["C_raw"])
                P.op("dve", (lambda: lambda e: e.tensor_reduce(bT[0:64, :], r3(Xb[0:64, :], 32), AX.X, ALU.add))(),
                     reads=["C_raw"], writes=["C_bT"])
                yield
                bi, pb, pk = kc.bank()
                kc.mm(pb[0:64, 0:32], Ut, gT[0:64, :], True, True, ["C_tri", "C_gT"], [pk])
                kc.copy("act", gcc[0:64, :], pb[0:64, 0:32], [pk], ["C_gcc"])
                bi, pb, pk = kc.bank()
                kc.mm(pb[:, 0:32], ones_f[0:64, 0:128], gT[0:64, :], True, True, ["ones_f", "C_gT"], [pk])
                kc.act(egl[:, :], pb[:, 0:32], AF.Exp, [pk], [kK("egl")])
                kc.tt("dve", egd[0:64, :], pb[0:64, 0:32], gcc[0:64, :], ALU.subtract, [pk, "C_gcc"], ["C_egd"])
                kc.act(egd[0:64, :], egd[0:64, :], AF.Exp, ["C_egd"], ["C_egd"])
                kc.tt("dve", r3(Xb[0:64, :], 32), bc_last(gT[0:64, :], 32, 64), bc_mid(Ut, 32), ALU.mult,
                      ["C_gT", "C_tri"], ["C_raw"])
                yield
                for q4 in range(4):
                    sl = slice(q4 * 512, (q4 + 1) * 512)
                    bi, pb, pk = kc.bank()
                    kc.mm(pb[:, :], ones_f[0:64, 0:128], Xb[0:64, sl], True, True, ["ones_f", "C_raw"], [pk])
                    kc.copy("dve", gcrow[:, sl], pb[:, :], [pk], ["C_gcrow"])
                    kc.act(E1[:, sl], pb[:, :], AF.Exp, [pk], ["C_E1"])
                    yield

                def conv_silu(comp):
                    blk = comp * 4 + h
                    kc.ld(raw[:, 3:3 + S], dqkvT[blk * 128:(blk + 1) * 128, t0:t0 + S], "C_raw")
                    cwi = V_CW + (l * 12 + blk) * 4
                    kc.ts("dve", acc, raw[:, 0:S], vcol(cwi), None, ALU.mult, None, ["C_raw", "C_rawz", "vecs"], ["C_acc"])
                    for j in range(1, 4):
                        kc.sto("dve", acc, raw[:, j:j + S], vcol(cwi + j), acc, ALU.mult, ALU.add,
                               ["C_raw", "C_rawz", "C_acc", "vecs"], ["C_acc"])
                    kc.act(acc, acc, AF.Silu, ["C_acc"], ["C_acc"])

                def l2n_slice(q4, scale):
                    sl = slice(q4 * 512, (q4 + 1) * 512)
                    bi, pb, pk = kc.bank()
                    kc.mm(pb[:, :], ones_bf, sqb[:, sl], True, True, ["ones_bf", "C_sqb"], [pk])
                    kc.act(rn, pb[:, :], AF.Ln, [pk, "vecs"], ["C_rn"], bias=vcol(V_EPS))
                    kc.act(rn, rn, AF.Exp, ["C_rn"], ["C_rn"], scale=-0.5)
                    kc.sto("dve", acc[:, sl], acc[:, sl], scale, rn, ALU.mult, ALU.mult, ["C_acc", "C_rn"], ["C_acc"])

                def to_tok4(n4, dst, dkey, mul_egd):
                    bi, pb, pk = kc.bank()
                    for c in range(4):
                        n = n4 * 4 + c
                        kc.transpose(pb[0:64, c * 128:(c + 1) * 128], acc[:, n * 64:(n + 1) * 64], ident,
                                     ["C_acc", "ident"], [pk])
                    dsl = dst[0:64, n4 * 512:(n4 + 1) * 512]
                    if mul_egd:
                        kc.tt("dve", r3(dsl, 4), r3(pb[0:64, :], 4), bc_last(egd[0:64, n4 * 4:n4 * 4 + 4], 4, 128),
                              ALU.mult, [pk, "C_egd"], [dkey])
                    else:
                        kc.copy("act", dsl, pb[0:64, :], [pk], [dkey])

                conv_silu(1)
                yield
                kc.act(sqb, acc, AF.Square, ["C_acc"], ["C_sqb"])
                for q4 in range(4):
                    l2n_slice(q4, 1.0)
                    yield
                kc.copy("act", khb, acc, ["C_acc"], ["C_khb"])
                kc.tt("dve", keT, acc, E1, ALU.mult, ["C_acc", "C_E1"], [kK("keT")])
                yield
                for n4 in range(8):
                    to_tok4(n4, kdec, kK("kdec"), True)
                    yield
                conv_silu(0)
                yield
                kc.act(sqb, acc, AF.Square, ["C_acc"], ["C_sqb"])
                for q4 in range(4):
                    l2n_slice(q4, 128.0 ** -0.5)
                    yield
                kc.copy("act", qhb, acc, ["C_acc"], ["C_qhb"])
                kc.tt("dve", qdT, acc, E1, ALU.mult, ["C_acc", "C_E1"], [kK("qdT")])
                yield
                conv_silu(2)
                yield
                for n4 in range(8):
                    to_tok4(n4, vtok, kK("vtok"), False)
                    yield

                kc.copy("dve", bTe[0:64, :], bT[0:64, :].rearrange("p (m two) -> p m two", two=2)[:, :, 0], ["C_bT"],
                        ["C_bTe"])
                kc.copy("dve", bTo[0:64, :], bT[0:64, :].rearrange("p (m two) -> p m two", two=2)[:, :, 1], ["C_bT"],
                        ["C_bTo"])

                def v4(ap):
                    return ap.rearrange("p (m two f) -> p m two f", m=4, two=2)

                for bt in range(4):
                    sb, lb = bt // 2, bt % 2
                    n0 = bt * 8
                    c0 = bt * 512
                    bsl = slice(c0, c0 + 512)
                    kc.tt("dve", r3(Dm[0:64, :], 8), r3(gcrow[0:64, bsl], 8), bc_last(gcc[0:64, n0:n0 + 8], 8, 64),
                          ALU.subtract, ["C_gcrow", "C_gcc"], ["C_Dm"])
                    kc.tt("pool", r3(tmpD[0:64, :], 8), r3(Dm[0:64, :], 8), bc_mid(mA, 8), ALU.add,
                          ["C_Dm", "C_tri"], ["C_tmpD"])
                    kc.act(GTi[0:64, :], tmpD[0:64, :], AF.Exp, ["C_tmpD"], ["C_GTi"])
                    kc.tt("dve", r3(tmpD[0:64, :], 8), bc_mid(mB, 8), r3(Dm[0:64, :], 8), ALU.subtract,
                          ["C_Dm", "C_tri", "C_tmpD"], ["C_tmpD"])
                    kc.act(Gb[0:64, :], tmpD[0:64, :], AF.Exp, ["C_tmpD"], ["C_Gb"])
                    kc.tt("dve", r3(Gb[0:64, :], 8), r3(Gb[0:64, :], 8), bc_last(bT[0:64, n0:n0 + 8], 8, 64),
                          ALU.mult, ["C_Gb", "C_bT"], ["C_Gb"])
                    kc.tt("pool", r3(GTb[0:64, :], 8), r3(GTi[0:64, :], 8), bc_mid(SU, 8), ALU.mult,
                          ["C_GTi", "C_tri"], ["C_GTb"])
                    kc.tt("pool", GTb[0:64, :], GTb[0:64, :], brow[0:64, bsl], ALU.mult, ["C_GTb", "C_brow"],
                          ["C_GTb"])
                    yield
                    Q, R, G = Qk[sb], Rk[sb], Gk[sb]
                    qk_, rk_, gk_ = "C_Q%d" % sb, "C_R%d" % sb, "C_G%d" % sb
                    lsl = slice(lb * 256, (lb + 1) * 256)
                    bi, pb, pk = kc.bank()
                    for c in range(8):
                        cs = slice((n0 + c) * 64, (n0 + c + 1) * 64)
                        kc.mm(pb[0:64, c * 64:(c + 1) * 64], khb[:, cs], khb[:, cs], True, True, ["C_khb"], [pk])
                    pv_ = v4(pb[0:64, :])
                    for par in range(2):
                        psl = slice(par * 64, (par + 1) * 64)
                        kc.tt("dve", r3(R[psl, lsl], 4), pv_[:, :, par, :], v4(Gb[0:64, :])[:, :, par, :], ALU.mult,
                              [pk, "C_Gb"], [(rk_, lb, par)])
                        kc.tt("dve", r3(Q[psl, lsl], 4), pv_[:, :, par, :], v4(GTb[0:64, :])[:, :, par, :], ALU.mult,
                              [pk, "C_GTb"], [(qk_, lb, par)])
                    bi, pb, pk = kc.bank()
                    for c in range(8):
                        cs = slice((n0 + c) * 64, (n0 + c + 1) * 64)
                        kc.mm(pb[0:64, c * 64:(c + 1) * 64], khb[:, cs], qhb[:, cs], True, True,
                              ["C_khb", "C_qhb"], [pk])
                    kc.tt("dve", aqkT[0:64, bsl], pb[0:64, :], GTi[0:64, :], ALU.mult, [pk, "C_GTi"], [kK("aqkT")])
                    kc.tt("pool", r3(G[:, lsl], 4), identP.rearrange("p (o j) -> p o j", o=1).to_broadcast([128, 4, 64]),
                          r3(Q[:, lsl], 4), ALU.subtract, ["C_identP", (qk_, lb, 0), (qk_, lb, 1)], [(gk_, lb)])
                    yield
                QRK = lambda nm: [(nm, lb_, par_) for lb_ in range(2) for par_ in range(2)]
                for lev in range(1, 6):
                    for sb in range(2):
                        Q, R, G = Qk[sb], Rk[sb], Gk[sb]
                        qk_, rk_, gk_ = "C_Q%d" % sb, "C_R%d" % sb, "C_G%d" % sb
                        qks = QRK(qk_) if lev == 1 else [qk_]
                        rks = QRK(rk_) if lev == 1 else [rk_]
                        gks = [(gk_, 0), (gk_, 1)] if lev == 1 else [gk_]

                        def pairmm(pbx, pkx, A_, B_, rd):
                            for m in range(8):
                                ms = slice(m * 64, (m + 1) * 64)
                                kc.mm(pbx[0:64, ms], A_[0:64, ms], B_[0:64, ms], True, True, rd, [pkx])
                                P.op("pe", (lambda o=pbx[64:128, ms], a_=A_[64:128, ms], b_=B_[64:128, ms]:
                                            lambda e: e.matmul(o, a_, b_, start=True, stop=True, tile_position=(64, 64)))(),
                                     reads=rd, writes=[pkx])

                        if lev < 5:
                            bq, pbq, pkq = kc.bank()
                            pairmm(pbq, pkq, R, Q, qks + rks)
                        br, pbr, pkr = kc.bank()
                        pairmm(pbr, pkr, Q, R, qks + rks)
                        if lev < 5:
                            kc.copy("act", Q[:, :], pbq[:, :], [pkq], [qk_] + QRK(qk_))
                        kc.copy("act", R[:, :], pbr[:, :], [pkr], [rk_] + QRK(rk_))
                        yield
                        bg, pbg, pkg = kc.bank()
                        pairmm(pbg, pkg, R, G, [rk_] + gks)
                        kc.tt("dve", G[:, :], G[:, :], pbg[:, :], ALU.add, gks + [pkg], [gk_, (gk_, 0), (gk_, 1)])
                        yield
                for sb in range(2):
                    G = Gk[sb]
                    gk_ = "C_G%d" % sb
                    TTv = TTb[0:64, sb * 1024:(sb + 1) * 1024].rearrange("p (m two f) -> p m two f", m=8, two=2)
                    kc.tt("dve", TTv[:, :, 0, :], r3(G[0:64, :], 8), bc_last(bTe[0:64, sb * 8:(sb + 1) * 8], 8, 64), ALU.mult,
                          [gk_, "C_bTe"], [kK("TTb")])
                    kc.copy("act", tmpD[0:64, :], G[64:128, :], [gk_], ["C_tmpD"])
                    kc.tt("dve", TTv[:, :, 1, :], r3(tmpD[0:64, :], 8), bc_last(bTo[0:64, sb * 8:(sb + 1) * 8], 8, 64), ALU.mult,
                          ["C_tmpD", "C_bTo"], [kK("TTb")])
                    yield

            def phaseB(s_, h, st):
                t0 = s_ * S
                si = st["i"]
                kK = lambda nm: "C_%s%d" % (nm, si)
                keT, qdT, kdec, vtok, aqkT, TTb, egl = (st[k] for k in ("keT", "qdT", "kdec", "vtok", "aqkT", "TTb", "egl"))
                kc.memset("dve", S_, 0.0, ["C_S"])
                for n in range(32):
                    cs = slice(n * 64, (n + 1) * 64)
                    ns = slice(n * 128, (n + 1) * 128)
                    b1, pb1, pk1 = kc.bank()
                    kc.mm(pb1[0:64, 0:128], keT[:, cs], S_, True, True, [kK("keT"), "C_S"], [pk1])
                    b3, pb3, pk3 = kc.bank()
                    kc.mm(pb3[:, 0:64], S_, qdT[:, cs], True, False, [kK("qdT"), "C_S"], [pk3])
                    kc.tt("dve", Rb[0:64, :], vtok[0:64, ns], pb1[0:64, 0:128], ALU.subtract, [kK("vtok"), pk1], ["C_Rb"])
                    b2, pb2, pk2 = kc.bank()
                    kc.mm(pb2[0:64, 0:128], TTb[0:64, cs], Rb[0:64, :], True, True, [kK("TTb"), "C_Rb"], [pk2])
                    kc.copy("act", vn[0:64, :], pb2[0:64, 0:128], [pk2], ["C_vn"])
                    kc.mm(pb3[:, 0:64], vn[0:64, :], aqkT[0:64, cs], False, True, ["C_vn", kK("aqkT")], [pk3])
                    b4, pb4, pk4 = kc.bank()
                    kc.mm(pb4[:, 0:128], kdec[0:64, ns], vn[0:64, :], True, True, [kK("kdec"), "C_vn"], [pk4])
                    kc.copy("act", oT[:, cs], pb3[:, 0:64], [pk3], ["C_oT"])
                    kc.sto("dve", S_, S_, egl[:, n:n + 1], pb4[:, 0:128], ALU.mult, ALU.add, ["C_S", kK("egl"), pk4],
                           ["C_S"])
                    yield
                kc.act(sqb2, oT, AF.Square, ["C_oT"], ["C_sqb2"])
                for q4 in range(4):
                    sl = slice(q4 * 512, (q4 + 1) * 512)
                    zs = zsb[q4 % 2]
                    zk = "C_zs%d" % (q4 % 2)
                    kc.ld(zs, dzT[h * 128:(h + 1) * 128, t0 + q4 * 512:t0 + (q4 + 1) * 512], zk)
                    bi, pb, pk = kc.bank()
                    kc.mm(pb[:, :], ones_bf, sqb2[:, sl], True, True, ["ones_bf", "C_sqb2"], [pk])
                    kc.act(rn2, pb[:, :], AF.Ln, [pk, "vecs"], ["C_rn2"], bias=vcol(V_EPS), scale=1.0 / 128)
                    kc.act(rn2, rn2, AF.Exp, ["C_rn2"], ["C_rn2"], scale=-0.5)
                    kc.sto("dve", oT[:, sl], oT[:, sl], vcol(V_DNW + l), rn2, ALU.mult, ALU.mult,
                           ["C_oT", "C_rn2", "vecs"], ["C_oT"])
                    kc.tt("pool", yout[:, sl], oT[:, sl], zs, ALU.mult, ["C_oT", zk], ["C_yout"])
                    yield
                kc.store(ybT[512 + h * 128:512 + (h + 1) * 128, t0:t0 + S], yout, "C_yout", writes=["ybT"])
                yield

            heads = [(s_, h) for s_ in range(NSEQ) for h in range(4)]
            nA = 0
            P.cur_tag = "L%dCa" % l
            for _ in phaseA(heads[0][0], heads[0][1], SETS[0]):
                nA += 1
            nB = 38
            per = max(1, -(-nA // nB))
            for i_, (s_, h) in enumerate(heads):
                gB = phaseB(s_, h, SETS[i_ % 2])
                gA = phaseA(heads[i_ + 1][0], heads[i_ + 1][1], SETS[(i_ + 1) % 2]) if i_ + 1 < len(heads) else None
                aliveA = gA is not None
                aliveB = True
                while aliveA or aliveB:
                    if aliveB:
                        P.cur_tag = "L%dCb" % l
                        try:
                            next(gB)
                        except StopIteration:
                            aliveB = False
                    if aliveA:
                        P.cur_tag = "L%dCa" % l
                        for _ in range(per):
                            try:
                                next(gA)
                            except StopIteration:
                                aliveA = False
                                break
            kc.reset()

        if "D" in stages:
            P.cur_tag = "L%dD" % l
            band = kc.alloc(8 * 1152, BF16)
            kc.ld(band, band_d, "D_band", q="pool")
            band3 = r3(band, 8)
            ind = kc.alloc(64 * 128, BF16)
            kc.ld(ind, ind_d, "D_ind", q="pool")
            ind3 = r3(ind, 64)
            identb = kc.alloc(128, BF16)
            kc.copy("dve", identb, ident, ["ident"], ["D_identb"])
            QT = kc.alloc(4 * S, BF16)
            KT = kc.alloc(4 * S, BF16)
            QT3 = r3(QT, 4)
            KT3 = r3(KT, 4)
            VP = kc.alloc(16 * 768, BF16)
            VP4 = VP.rearrange("p (i c w) -> p i c w", i=16, c=4)
            VP3 = r3(VP, 16)
            kc.memset("pool", VP, 0.0, ["D_VPz"])
            kc.memset("pool", VP4[:, :, :, 64:65], 1.0, ["D_VPz"])
            kms = kc.alloc(32)
            KM = kc.alloc(4 * 64, BF16)
            KM3 = r3(KM, 4)
            gsb = kc.alloc(128)
            gw = kc.alloc(128)
            mxt = kc.alloc(16)
            eqt = kc.alloc(128)
            Mt = kc.alloc(128)
            MallT = kc.alloc(S, BF16)
            ptb = [kc.alloc(512, BF16) for _ in range(5)]
            cfar = kc.alloc(8)
            kc.copy("dve", cfar, band3[:, :, 1151], ["D_band"], ["D_cfar"])
            rl = kc.alloc(512)
            osb = kc.alloc(512)
            ycT = kc.alloc(4 * S, BF16)
            ycT3 = r3(ycT, 4)
            kc.memset("pool", KM, 0.0, ["D_KMz"])
            KZ = kc.alloc(8 * S, BF16)
            KZ3 = r3(KZ, 8)
            kc.memset("pool", KZ, 0.0, ["D_KZz"])
            for s_ in range(NSEQ):
              try:
                t0 = s_ * S
                kc.ld(QT3, mqT[:, t0:t0 + S].rearrange("(c p) t -> p c t", p=128), "D_QT")
                kc.ld(KT3, mkT[:, t0:t0 + S].rearrange("(c p) t -> p c t", p=128), "D_KT")
                srcv = mv[t0:t0 + S, :].rearrange("(i p) (c two d) -> p i c two d", p=128, two=2, d=64)
                for c in range(4):
                    kc.ld(VP4[:, :, c, 0:64], srcv[:, :, c, 0, :], ("D_VP", c, 0), reads=["D_VPz"], semkey="D_VPa%d" % c)
                    kc.ld(VP4[:, :, c, 128:192], srcv[:, :, c, 1, :], ("D_VP", c, 1), reads=["D_VPz"],
                          semkey="D_VPb%d" % c)
                for h_ in range(8):
                    rr0 = (h_ % 2) * 64
                    kc.copy("pool" if h_ % 2 == 0 else "act", KZ3[rr0:rr0 + 64, h_, :], KT3[rr0:rr0 + 64, h_ // 2, :],
                            ["D_KT", "D_KZz"], [("D_KZ", h_)])
                P.op("dve", (lambda kms=kms, KT=KT: lambda e: e.tensor_reduce(r3(kms, 4), KT.rearrange("p (c n k) -> p c n k", c=4, n=8), AX.X,
                                                      ALU.add))(), reads=["D_KT"], writes=["D_kms"])
                kms3 = r3(kms, 4)
                for c in range(4):
                    kc.copy("dve", KM3[0:64, c, (2 * c) * 8:(2 * c) * 8 + 8], kms3[0:64, c, :], ["D_kms", "D_KMz"],
                            ["D_KM"])
                    kc.copy("dve", KM3[64:128, c, (2 * c + 1) * 8:(2 * c + 1) * 8 + 8], kms3[64:128, c, :],
                            ["D_kms", "D_KMz"], ["D_KM"])
                if DSTOP == 1:
                    kc.dump("KM", KM, ["D_KM"], BF16)
                    kc.dump("VP", VP, ["D_VPz"] + [("D_VP", c, hh) for c in range(4) for hh in range(2)], BF16)
                    kc.dump("kms", kms, ["D_kms"])
                    raise _Stop()
                for q4 in range(4):
                    bm, pbm, pkm = kc.bank()
                    for qq in range(4):
                        qt = q4 * 4 + qq
                        b = qt // 2
                        if b >= 4:
                            bi, pb, pk = kc.bank()
                            for c in range(4):
                                kc.mm(pb[:, 0:64], QT3[:, c, qt * 128:(qt + 1) * 128], KM3[:, c, :], c == 0, c == 3,
                                      ["D_QT", "D_KM"], [pk])
                            g3 = r3(gsb[:, 0:64], 8)
                            w3 = r3(gw[:, 0:64], 8)
                            e3 = r3(eqt[:, 0:64], 8)
                            kc.copy("act", gsb[:, 0:64], pb[:, 0:64], [pk], ["D_gsb"])
                            kc.memset("dve", g3[:, :, b:8], -1.0e9, ["D_gsb"])
                            src, srck = g3, "D_gsb"
                            for rnd in range(3):
                                P.op("dve", (lambda src=src, mxt=mxt: lambda e: e.tensor_reduce(mxt[:, 0:8], src, AX.X, ALU.max))(),
                                     reads=[srck], writes=["D_mxt"])
                                if rnd == 2:
                                    break
                                mb = mxt[:, 0:8].rearrange("p (h o) -> p h o", o=1).to_broadcast([128, 8, 8])
                                kc.tt("dve", e3, src, mb, ALU.is_ge, [srck, "D_mxt"], ["D_eqt"])
                                kc.sto("dve", w3, e3, -1.0e9, src, ALU.mult, ALU.add, ["D_eqt", srck], ["D_gw"])
                                src, srck = w3, "D_gw"
                            mb = mxt[:, 0:8].rearrange("p (h o) -> p h o", o=1).to_broadcast([128, 8, 8])
                            kc.tt("dve", e3, g3, mb, ALU.is_ge, ["D_gsb", "D_mxt"], ["D_eqt"])
                            kc.ts("dve", Mt[:, 0:64], eqt[:, 0:64], BIG, -BIG, ALU.mult, ALU.add, ["D_eqt"], ["D_Mt"])
                            kc.memset("dve", r3(Mt[:, 0:64], 8)[:, :, b:b + 1], 0.0, ["D_Mt"])
                        else:
                            kc.memset("dve", Mt[:, 0:64], 0.0, ["D_Mt"])
                        kc.transpose(pbm[0:64, qq * 128:(qq + 1) * 128], Mt[:, 0:64], ident, ["D_Mt", "ident"], [pkm])
                    kc.copy("act", MallT[0:64, q4 * 512:(q4 + 1) * 512], pbm[0:64, :], [pkm], ["D_MallT"])
                    kc.copy("act", MallT[64:128, q4 * 512:(q4 + 1) * 512], pbm[0:64, :], [pkm], ["D_MallT"])
                if DSTOP == 2:
                    kc.dump("MallT", MallT[0:64, :], ["D_MallT"], BF16)
                    raise _Stop()
                ipt = 0
                pend = []

                def flush(keep):
                    while len(pend) > keep:
                        pend.pop(0)()

                for h in range(8):
                    c = h // 2
                    r0 = (h % 2) * 64
                    lrow = 64 if h % 2 == 0 else 0
                    for qi in range(4):
                        q0 = qi * 512
                        nkt = (qi + 1) * 4
                        bo, pbo, pko = kc.bank(n=2, base=6)
                        for kt in range(nkt):
                            k0 = kt * 128
                            nblk = kt // 2
                            bs_, pbs, pks = kc.bank(n=5, base=0)
                            mms = [(KZ3[:, h, k0:k0 + 128], QT3[:, c, q0:q0 + 512], [("D_KZ", h), "D_KZz", "D_QT"])]
                            if qi >= 2 and nblk < 2 * qi + 1:
                                mms.append((ind3[:, h * 8 + nblk, :], MallT[:, q0:q0 + 512], ["D_ind", "D_MallT"]))
                            far = (q0 - k0) >= 256
                            if not far:
                                off = min(max(q0 - k0, -384), 256) + 384
                                mms.append((identb, band3[:, h, off:off + 512], ["D_identb", "D_band"]))
                            for i_, (a_, b_, rd_) in enumerate(mms):
                                kc.mm(pbs[:, :], a_, b_, i_ == 0, i_ == len(mms) - 1, rd_, [pks])
                            pt = ptb[ipt % 5]
                            ptk = "D_pt%d" % (ipt % 5)
                            ipt += 1
                            if far:
                                kc.act(pt, pbs[:, :], AF.Exp, [pks, "D_cfar"], [ptk], bias=cfar[:, h:h + 1])
                            else:
                                kc.act(pt, pbs[:, :], AF.Exp, [pks], [ptk])
                            lo = c * 192 + (h % 2) * 64

                            def pv(pbo=pbo, pko=pko, kt=kt, nkt=nkt, lo=lo, pt=pt, ptk=ptk, c=c, r0=r0, lrow=lrow, q0=q0):
                                kc.mm(pbo[:, :], VP3[:, kt, lo:lo + 128], pt, kt == 0, kt == nkt - 1,
                                      [("D_VP", c, 0), ("D_VP", c, 1), "D_VPz", ptk], [pko])
                                if kt == nkt - 1:
                                    kc.act(rl[lrow:lrow + 1, :], pbo[lrow:lrow + 1, :], AF.Ln, [pko], ["D_rl"])
                                    kc.act(rl[lrow:lrow + 1, :], rl[lrow:lrow + 1, :], AF.Exp, ["D_rl"], ["D_rl"], scale=-1.0)
                                    kc.copy("act", osb[r0:r0 + 64, :], pbo[r0:r0 + 64, :], [pko], ["D_osb"])
                                    br_, pbr, pkr = kc.bank(n=1, base=5)
                                    kc.mm(pbr[:, :], ones_f[lrow:lrow + 1, 0:128], rl[lrow:lrow + 1, :], True, True,
                                          ["ones_f", "D_rl"], [pkr])
                                    kc.tt("dve", ycT3[r0:r0 + 64, c, q0:q0 + 512], osb[r0:r0 + 64, :], pbr[r0:r0 + 64, :],
                                          ALU.mult, ["D_osb", pkr], ["D_ycT"])

                            pend.append(pv)
                            flush(2)
                    if DSTOP == 3 + h:
                        flush(0)
                        kc.dump("ycT", ycT, ["D_ycT"], BF16)
                        raise _Stop()
                flush(0)
                kc.store(ybT[1024:1536, t0:t0 + S].rearrange("(c p) t -> p c t", p=128), ycT3, "D_ycT", writes=["ybT"])
              except _Stop:
                pass
            kc.reset()

        if "E" in stages:
            P.cur_tag = "L%dE" % l
            wbr = kc.alloc(12 * 1024, BF16)
            kc.ld(wbr, wbrb[l], "E_wbr", reads=["CAST1" if l == layers[0] else "CAST1b"])
            wbr4 = wbr.rearrange("p (n k d) -> p n k d", n=3, k=4)
            wo = kc.alloc(8 * 1024, BF16)
            kc.ld(wo, woutb[l], "E_wo", reads=["CAST1" if l == layers[0] else "CAST1b"])
            wo3 = r3(wo, 8)
            ybb = [kc.alloc(12 * 512, BF16) for _ in range(2)]
            gtb = [kc.alloc(24 * 512, BF16) for _ in range(2)]
            xtb = [kc.alloc(8 * 512) for _ in range(2)]
            mg = kc.alloc(8 * 512, BF16)
            mg3 = r3(mg, 8)
            ta = kc.alloc(512)
            tb = kc.alloc(512)
            tc_ = kc.alloc(512)
            mgb = [mg, kc.alloc(8 * 512, BF16)]

            def e_phase1(tt):
                i2 = tt % 2
                tsl = slice(tt * 512, (tt + 1) * 512)
                yb3 = r3(ybb[i2], 12)
                gt3 = r3(gtb[i2], 24)
                xt3 = r3(xtb[i2], 8)
                mg3_ = r3(mgb[i2], 8)
                ybk, gtk, xk = "E_yb%d" % i2, "E_gt%d" % i2, "E_xt%d" % i2
                kc.ld(yb3, ybT[:, tsl].rearrange("(c p) t -> p c t", p=128), ybk)
                kc.ld(gt3, gatesT[:, tsl].rearrange("(c p) t -> p c t", p=128), gtk)
                kc.ld(xt3, xsrc[:, tsl].rearrange("(k p) t -> p k t", p=128), xk)
                yield
                for m in range(8):
                    pbs_ = []
                    for n in range(3):
                        bi, pb, pk = kc.bank()
                        for k in range(4):
                            kc.mm(pb[:, :], wbr4[:, n, k, m * 128:(m + 1) * 128], yb3[:, n * 4 + k, :], k == 0, k == 3,
                                  ["E_wbr", ybk], [pk])
                        pbs_.append((pb, pk))
                    kc.tt("dve", ta, pbs_[0][0][:, :], gt3[:, m, :], ALU.mult, [pbs_[0][1], gtk], ["E_ta"])
                    kc.tt("dve", tb, pbs_[1][0][:, :], gt3[:, 8 + m, :], ALU.mult, [pbs_[1][1], gtk], ["E_tb"])
                    kc.tt("dve", tc_, pbs_[2][0][:, :], gt3[:, 16 + m, :], ALU.mult, [pbs_[2][1], gtk], ["E_tc"])
                    kc.tt("pool", ta, ta, tb, ALU.add, ["E_ta", "E_tb"], ["E_ta"])
                    kc.tt("pool", mg3_[:, m, :], ta, tc_, ALU.add, ["E_ta", "E_tc"], [("E_mg%d" % i2, m)])
                    yield

            def e_phase2(tt):
                i2 = tt % 2
                tsl = slice(tt * 512, (tt + 1) * 512)
                xt3 = r3(xtb[i2], 8)
                mg3_ = r3(mgb[i2], 8)
                xk = "E_xt%d" % i2
                for m in range(8):
                    bi, pb, pk = kc.bank()
                    for k in range(8):
                        kc.mm(pb[:, :], wo3[:, k, m * 128:(m + 1) * 128], mg3_[:, k, :], k == 0, k == 7,
                              ["E_wo", ("E_mg%d" % i2, k)], [pk])
                    kc.tt("dve", xt3[:, m, :], xt3[:, m, :], pb[:, :], ALU.add, [xk, pk], [xk])
                    yield
                kc.store(x1s[:, tsl].rearrange("(k p) t -> p k t", p=128), xt3, xk, writes=["x1s"])
                yield

            for _ in e_phase1(0):
                pass
            for tt in range(NT):
                g2 = e_phase2(tt)
                g1 = e_phase1(tt + 1) if tt + 1 < NT else None
                a1 = g1 is not None
                a2 = True
                while a1 or a2:
                    if a1:
                        try:
                            next(g1)
                        except StopIteration:
                            a1 = False
                    if a2:
                        try:
                            next(g2)
                        except StopIteration:
                            a2 = False
            kc.reset()

        if "F" in stages:
            P.cur_tag = "L%dF" % l
            wdn = kc.alloc(22 * 1024, BF16)
            kc.ld(wdn, wdnb[l], "F_wdn", reads=["CAST1" if l == layers[0] else "CAST1b"])
            wdn3 = r3(wdn, 22)
            pgw = kc.alloc(8 * 1024, BF16)
            kc.ld(pgw, pgb[l], "F_pgw", reads=["CAST1" if l == layers[0] else "CAST1b"])
            pgw3 = r3(pgw, 8)
            ppw = kc.alloc(2 * 1024, BF16)
            kc.ld(ppw, ppb[l], "F_ppw", reads=["CAST1" if l == layers[0] else "CAST1b"])
            ppw3 = r3(ppw, 2)
            hal = kc.alloc(44 * 2)
            xtb = [kc.alloc(8 * 512) for _ in range(2)]
            hb = [kc.alloc(8 * 512, BF16) for _ in range(2)]
            rstd1 = kc.alloc(512)
            rstd2 = kc.alloc(512)
            wub = [kc.alloc(8 * 256, BF16) for _ in range(2)]
            rawb = [kc.alloc(514) for _ in range(2)]
            yb_ = [kc.alloc(512) for _ in range(2)]
            gTb = [kc.alloc(22 * 512, BF16) for _ in range(2)]
            ptb_ = [kc.alloc(2 * 512, BF16) for _ in range(2)]
            sg = kc.alloc(512)
            tp = kc.alloc(512)
            last = (l == layers[-1]) and final
            iwc = [0]

            def f_phase1(tt):
                i2 = tt % 2
                tsl = slice(tt * 512, (tt + 1) * 512)
                xt = xtb[i2]
                xt3 = r3(xt, 8)
                xk = "F_xt%d" % i2
                kc.ld(xt3, x1s[:, tsl].rearrange("(k p) t -> p k t", p=128), xk)
                if tt % 4 == 0:
                    kc.memset("pool", hal, 0.0, ["F_hal"])
                h2 = hb[i2]
                h23 = r3(h2, 8)
                HK = [("F_h%d" % i2, k) for k in range(8)]
                rmsnorm_tile(xt, xk, V_NFFN + l * 8, lambda k: h23[:, k, :], HK, h2, "F_sq%d" % i2, rstd1, "F_rstd1",
                             sq_extra=HK)
                yield
                gT3 = r3(gTb[i2], 22)
                for j in range(22):
                    wu = wub[iwc[0] % 2]
                    wuk = "F_wu%d" % (iwc[0] % 2)
                    iwc[0] += 1
                    kc.ld(wu, wupb[l, j], wuk, reads=["CAST1" if l == layers[0] else "CAST1b"])
                    wu3 = r3(wu, 8)
                    for half in range(2):
                        bi, pb, pk = kc.bank()
                        for k in range(8):
                            kc.mm(pb[:, :], wu3[:, k, half * 128:(half + 1) * 128], h23[:, k, :], k == 0, k == 7,
                                  [wuk, HK[k]], [pk])
                        rb = rawb[half]
                        rk = "F_raw%d" % half
                        yb = yb_[half]
                        yk = "F_y%d" % half
                        hi = (j * 2 + half) * 2
                        kc.copy("act", rb[:, 2:514], pb[:, :], [pk], [rk])
                        kc.copy("pool", rb[:, 0:2], hal[:, hi:hi + 2], ["F_hal"], [rk + "h"])
                        kc.copy("pool", hal[:, hi:hi + 2], rb[:, 512:514], [rk], ["F_hal"])
                        cwi = V_FCW + (l * 44 + half * 22 + j) * 3
                        kc.act(yb, pb[:, :], AF.Copy, [pk, "vecs"], [yk], scale=vcol(cwi + 2))
                        kc.sto("dve", yb, rb[:, 0:512], vcol(cwi), yb, ALU.mult, ALU.add, [rk, rk + "h", yk, "vecs"], [yk])
                        kc.sto("dve", yb, rb[:, 1:513], vcol(cwi + 1), yb, ALU.mult, ALU.add, [rk, rk + "h", yk, "vecs"],
                               [yk])
                    kc.act(yb_[0], yb_[0], AF.Gelu_apprx_tanh, ["F_y0"], ["F_y0"])
                    kc.tt("dve", gT3[:, j, :], yb_[0], yb_[1], ALU.mult, ["F_y0", "F_y1"], [("F_g%d" % i2, j)])
                    yield

            def f_phase2(tt):
                i2 = tt % 2
                tsl = slice(tt * 512, (tt + 1) * 512)
                xt = xtb[i2]
                xt3 = r3(xt, 8)
                xk = "F_xt%d" % i2
                h2 = hb[i2]
                h23 = r3(h2, 8)
                HK = [("F_h%d" % i2, k) for k in range(8)]
                gT3 = r3(gTb[i2], 22)
                pt_ = ptb_[i2]
                ptk = "F_pt%d" % i2
                kc.ld(r3(pt_, 2), pT_in[l, :, tsl].rearrange("(k p) t -> p k t", p=128), ptk, q="pool")
                for m in range(8):
                    bi, pb, pk = kc.bank()
                    for j in range(22):
                        kc.mm(pb[:, :], wdn3[:, j, m * 128:(m + 1) * 128], gT3[:, j, :], j == 0, j == 21,
                              ["F_wdn", ("F_g%d" % i2, j)], [pk])
                    kc.tt("dve", xt3[:, m, :], xt3[:, m, :], pb[:, :], ALU.add, [xk, pk], [xk])
                    yield
                rmsnorm_tile(xt, xk, V_NPLE + l * 8, lambda k: h23[:, k, :], HK, h2, "F_sq%d" % i2, rstd2, "F_rstd2",
                             sq_extra=HK)
                yield
                for m in range(8):
                    bi, pb, pk = kc.bank()
                    for k in range(8):
                        kc.mm(pb[:, :], pgw3[:, k, m * 128:(m + 1) * 128], h23[:, k, :], k == 0, k == 7,
                              ["F_pgw", HK[k]], [pk])
                    kc.act(sg, pb[:, :], AF.Sigmoid, [pk], ["F_sg"])
                    bi, pb, pk = kc.bank()
                    for k in range(2):
                        kc.mm(pb[:, :], ppw3[:, k, m * 128:(m + 1) * 128], r3(pt_, 2)[:, k, :], k == 0, k == 1,
                              ["F_ppw", ptk], [pk])
                    kc.tt("dve", tp, pb[:, :], sg, ALU.mult, [pk, "F_sg"], ["F_tp"])
                    kc.tt("pool", xt3[:, m, :], xt3[:, m, :], tp, ALU.add, [xk, "F_tp"], [xk])
                    yield
                if last:
                    OK_ = [("F_ot%d" % i2, k) for k in range(8)]
                    rmsnorm_tile(xt, xk, V_NFIN, lambda k: xt3[:, k, :], OK_, h2, "F_sq%d" % i2, rstd2, "F_rstd2",
                                 sq_extra=HK)
                    kc.finals.append(kc.store(outT[:, tsl].rearrange("(k p) t -> p k t", p=128), xt3, OK_ + [xk],
                                              writes=["outT"], semkey="F_ot_st%d" % i2))
                else:
                    kc.store(xs[:, tsl].rearrange("(k p) t -> p k t", p=128), xt3, xk, writes=["xs"])
                yield

            for _ in f_phase1(0):
                pass
            for tt in range(NT):
                g2 = f_phase2(tt)
                g1 = f_phase1(tt + 1) if tt + 1 < NT else None
                a1 = g1 is not None
                a2 = True
                while a1 or a2:
                    if a1:
                        try:
                            next(g1)
                        except StopIteration:
                            a1 = False
                    if a2:
                        try:
                            next(g2)
                        except StopIteration:
                            a2 = False
            kc.reset()
    nsem = P.emit(final_wait_ops=kc.finals)
    return nc, kc, nsem


def _t5_bucket_np(n):
    n = np.maximum(n, 0)
    exact = 16
    nf = np.maximum(n, 1).astype(np.float32)
    large = exact + (np.log(nf / exact) / math.log(128 / exact) * (32 - exact)).astype(np.int32)
    large = np.minimum(large, 31)
    return np.where(n < exact, n, large)


def host_consts():
    ident = np.eye(128, dtype=np.float32)
    p = np.arange(64)[:, None]
    f = np.arange(64)[None, :]
    U = (p <= f).astype(np.float32)
    mA = np.where(f >= p, 0.0, -BIG).astype(np.float32)
    mB = np.where(p > f, 0.0, -BIG).astype(np.float32)
    SU = (f > p).astype(np.float32)
    tri = np.concatenate([U, mA, mB, SU], axis=1)
    rc = np.zeros((128, 64), np.float32)
    for g in range(4):
        w = 2 << g
        for t in range(16):
            rc[:, g * 16 + t] = 1.0 / min(t + 1, w)
    ind = np.zeros((64, 64, 128), np.float32)
    for r in range(64):
        ind[r, r, :] = 1.0
    ind = ind.reshape(64, 64 * 128)
    return ident, tri, rc, np.concatenate([ind, ind], axis=0)


def host_vecs(inp):
    v = np.zeros((128, NVEC), np.float32)

    def put(base, arr):
        a = np.asarray(arr, np.float32)
        lead = int(np.prod(a.shape[:-1])) if a.ndim > 1 else 1
        a = a.reshape(lead, -1, 128)
        a = a.transpose(2, 0, 1).reshape(128, -1)
        v[:, base:base + a.shape[1]] = a

    put(V_NMIX, inp["norm_mix"])
    put(V_NFFN, inp["norm_ffn"])
    put(V_NPLE, inp["norm_ple"])
    put(V_NFIN, inp["norm_final"])
    put(V_PSCALE, inp["pool_scale"])
    cw = np.asarray(inp["dn_conv"], np.float32).reshape(2, 4, 12, 128).transpose(3, 0, 2, 1).reshape(128, 96)
    v[:, V_CW:V_CW + 96] = cw
    fc = np.asarray(inp["ffn_conv"], np.float32).reshape(2, 3, 44, 128).transpose(3, 0, 2, 1).reshape(128, 264)
    v[:, V_FCW:V_FCW + 264] = fc
    v[:, V_DNW:V_DNW + 2] = np.asarray(inp["dn_norm"], np.float32).T
    v[:, V_EPS] = EPS
    v[:, V_ONE] = 1.0
    return v


def host_band(rel_bias):
    rb = np.asarray(rel_bias, np.float32)
    p = np.arange(128)[:, None]
    c = np.arange(1152)[None, :]
    n = c - 384 - p
    bucket = _t5_bucket_np(n)
    band = np.empty((128, 8, 1152), np.float32)
    for h in range(8):
        band[:, h, :] = np.where(n >= 0, rb[bucket, h], -BIG)
    return band.reshape(128, 8 * 1152)


_CACHE = {}


def get_program(NSEQ=2, **kw):
    key = (NSEQ, tuple(sorted(kw.items())))
    if key not in _CACHE:
        _CACHE[key] = build(NSEQ=NSEQ, **kw)
    return _CACHE[key]


def make_in_maps(inp, NSEQ, ncores):
    ident, tri, rc, ind = host_consts()
    vecs = host_vecs(inp)
    band = host_band(inp["rel_bias"])
    a4 = np.stack([np.asarray(inp["dn_a_log"], np.float32)[0], np.asarray(inp["dn_dt_bias"], np.float32)[0],
                   np.asarray(inp["dn_a_log"], np.float32)[1], np.asarray(inp["dn_dt_bias"], np.float32)[1]], axis=1)
    x = np.asarray(inp["x"], np.float32)
    p = np.asarray(inp["p"], np.float32)
    shared = {
        "w_in": np.ascontiguousarray(inp["w_in"], np.float32),
        "w_branch": np.ascontiguousarray(inp["w_branch"], np.float32),
        "w_out": np.ascontiguousarray(inp["w_out"], np.float32),
        "ffn_up": np.ascontiguousarray(inp["ffn_up"], np.float32),
        "ffn_down": np.ascontiguousarray(inp["ffn_down"], np.float32),
        "ple_gate": np.ascontiguousarray(inp["ple_gate"], np.float32),
        "ple_proj": np.ascontiguousarray(inp["ple_proj"], np.float32),
        "pool_w": np.ascontiguousarray(inp["pool_w"], np.float32),
        "vecs": vecs, "ident": ident, "tri": tri, "rcnt": rc, "band": band, "ind": ind,
        "a4": np.ascontiguousarray(a4),
    }
    maps = []
    for c in range(ncores):
        xs_ = x[c * NSEQ:(c + 1) * NSEQ].reshape(NSEQ * S, D)
        ps_ = p[:, c * NSEQ:(c + 1) * NSEQ].reshape(DEPTH, NSEQ * S, 256)
        m = dict(shared)
        m["xT"] = np.ascontiguousarray(xs_.T)
        m["pT"] = np.ascontiguousarray(ps_.transpose(0, 2, 1))
        maps.append(m)
    return maps


def kernel(**inputs):
    NSEQ = 2
    nc, kc, _ = get_program(NSEQ=NSEQ)
    maps = make_in_maps(inputs, NSEQ, NCORES)
    res = run_bass_kernel_spmd(nc, maps, core_ids=list(range(NCORES)))
    outs = []
    for c in range(NCORES):
        oT = np.asarray(res.results[c]["outT"], np.float32)
        outs.append(oT.T.reshape(NSEQ, S, D))
    return np.concatenate(outs, axis=0).astype(np.float32)
```
